# Optimizing a Trainium2 kernel written in Bass

```python
import jax
import jax.numpy as jnp
from jax import lax
import numpy as np

D_MODEL = 1024
BATCH = 2
SEQ = 16384
DEPTH = 2

GRID_W = 64
CTX_LEN = 256
HEAD_DIM = 64
MIX_W = D_MODEL // 4
N_BRANCH = 4
CONV_K = 3
WIN_HEADS = MIX_W // HEAD_DIM
WIN_KV_HEADS = WIN_HEADS // 2
WINDOW = 128
WIN_BLOCK = 128
LRU_BLOCKS = 4
LRU_CONV_K = 4
LRU_C = 8.0
NA_HEADS = MIX_W // HEAD_DIM
NA_ROWS = 8
NA_COLS = 16
D_FF = 256 * ((8 * D_MODEL // 3 + 255) // 256)
ROPE_BASE = 10000.0
EPS = 1e-6
NEG_INF = -1e30
N_MOD = 9
IN_SIZES = (MIX_W, MIX_W, MIX_W,
            WIN_HEADS * HEAD_DIM, WIN_KV_HEADS * HEAD_DIM, WIN_KV_HEADS * HEAD_DIM,
            MIX_W, MIX_W,
            NA_HEADS * HEAD_DIM, NA_HEADS * HEAD_DIM, NA_HEADS * HEAD_DIM,
            N_BRANCH * D_MODEL)
IN_W = sum(IN_SIZES)

kernel_name = 'hybrid_dit_prefix_gated_mixers'


def rms_norm(x, g):
    xf = x.astype(jnp.float32)
    y = xf * lax.rsqrt(jnp.mean(xf * xf, axis=-1, keepdims=True) + EPS)
    return (y * g.astype(jnp.float32)).astype(x.dtype)


def adaln(x, g, shift, scale):
    return rms_norm(x, g) * (1 + scale) + shift


def swiglu(h, w_gate, w_up, w_down):
    return (jax.nn.silu(h @ w_gate) * (h @ w_up)) @ w_down


def split_columns(u):
    points, acc = [], 0
    for size in IN_SIZES[:-1]:
        acc += size
        points.append(acc)
    return jnp.split(u, points, axis=-1)


def split_heads(t, n_heads):
    return t.reshape(t.shape[0], t.shape[1], n_heads, HEAD_DIM)


def dw_conv(x, w, pad_left):
    k, ch = w.shape
    return lax.conv_general_dilated(
        x, w[:, None, :].astype(x.dtype), window_strides=(1,),
        padding=[(pad_left, k - 1 - pad_left)],
        dimension_numbers=('NWC', 'WIO', 'NWC'), feature_group_count=ch)


def axial_rope(t, rows, cols):
    half = HEAD_DIM // 2
    nf = half // 2
    inv_freq = ROPE_BASE ** (-jnp.arange(nf, dtype=jnp.float32) / nf)

    def rotate(u, pos):
        ang = pos.astype(jnp.float32)[:, None] * inv_freq[None, :]
        cos = jnp.cos(ang)[None, :, None, :].astype(u.dtype)
        sin = jnp.sin(ang)[None, :, None, :].astype(u.dtype)
        u1, u2 = u[..., :nf], u[..., nf:]
        return jnp.concatenate([u1 * cos - u2 * sin, u1 * sin + u2 * cos], axis=-1)

    return jnp.concatenate([rotate(t[..., :half], rows), rotate(t[..., half:], cols)], axis=-1)


def context_attention(q, k, v, sink):
    b, l, hq, d = q.shape
    hkv = k.shape[2]
    g = hq // hkv
    qg = q.reshape(b, l, hkv, g, d)
    s = jnp.einsum('bqhgd,bkhd->bhgqk', qg, k).astype(jnp.float32) * (d ** -0.5)
    if sink is not None:
        s_sink = jnp.broadcast_to(sink.astype(jnp.float32).reshape(1, hkv, g, 1, 1), s.shape[:-1] + (1,))
        s = jnp.concatenate([s, s_sink], axis=-1)
    p = jax.nn.softmax(s, axis=-1)[..., :l].astype(v.dtype)
    o = jnp.einsum('bhgqk,bkhd->bqhgd', p, v)
    return o.reshape(b, l, hq * d)


def window_attention(q_rot, q_plain, k_rot, v, kc, vc, sink):
    b, n_tok, hq, d = q_rot.shape
    hkv = k_rot.shape[2]
    g = hq // hkv
    nb = n_tok // WIN_BLOCK
    scale = d ** -0.5
    qb = q_rot.reshape(b, nb, WIN_BLOCK, hkv, g, d)
    qcb = q_plain.reshape(b, nb, WIN_BLOCK, hkv, g, d)

    def band(t):
        tp = jnp.pad(t, ((0, 0), (WIN_BLOCK, WIN_BLOCK), (0, 0), (0, 0)))
        tp = tp.reshape(b, nb + 2, WIN_BLOCK, hkv, d)
        return jnp.concatenate([tp[:, :-2], tp[:, 1:-1], tp[:, 2:]], axis=2)

    kb, vb = band(k_rot), band(v)
    blk = jnp.arange(nb)[:, None, None]
    q_pos = blk * WIN_BLOCK + jnp.arange(WIN_BLOCK)[None, :, None]
    k_pos = (blk - 1) * WIN_BLOCK + jnp.arange(3 * WIN_BLOCK)[None, None, :]
    valid = (jnp.abs(k_pos - q_pos) <= WINDOW) & (k_pos >= 0) & (k_pos < n_tok)
    s_loc = jnp.einsum('bnqhgd,bnkhd->bnhgqk', qb, kb).astype(jnp.float32) * scale
    s_loc = jnp.where(valid[None, :, None, None], s_loc, NEG_INF)
    s_ctx = jnp.einsum('bnqhgd,blhd->bnhgql', qcb, kc).astype(jnp.float32) * scale
    s_sink = jnp.broadcast_to(sink.astype(jnp.float32).reshape(1, 1, hkv, g, 1, 1), s_loc.shape[:-1] + (1,))
    n_loc = 3 * WIN_BLOCK
    n_ctx = kc.shape[1]
    probs = jax.nn.softmax(jnp.concatenate([s_loc, s_ctx, s_sink], axis=-1), axis=-1).astype(v.dtype)
    o = (jnp.einsum('bnhgqk,bnkhd->bnqhgd', probs[..., :n_loc], vb)
         + jnp.einsum('bnhgql,blhd->bnqhgd', probs[..., n_loc:n_loc + n_ctx], vc))
    return o.reshape(b, n_tok, hq * d)


def neighbourhood_attention(q, k, v, kc, vc, rel_bias):
    b, n_tok, h, d = q.shape
    rows = n_tok // GRID_W
    kr = min(NA_ROWS, rows)
    ncb = GRID_W // NA_COLS
    cb = 2 * NA_COLS
    r = jnp.arange(rows)
    row_idx = jnp.clip(r - kr // 2, 0, rows - kr)[:, None] + jnp.arange(kr)[None, :]
    col_idx = (jnp.clip(jnp.arange(ncb) * NA_COLS - NA_COLS // 2, 0, GRID_W - cb)[:, None]
               + jnp.arange(cb)[None, :])
    q_col = jnp.arange(ncb)[:, None] * NA_COLS + jnp.arange(NA_COLS)[None, :]
    c_start = jnp.clip(q_col - NA_COLS // 2, 0, GRID_W - NA_COLS)[:, :, None]
    col_valid = (col_idx[:, None, :] >= c_start) & (col_idx[:, None, :] < c_start + NA_COLS)
    dr = row_idx - r[:, None] + NA_ROWS - 1
    dc = jnp.clip(col_idx[:, None, :] - q_col[:, :, None], 1 - NA_COLS, NA_COLS - 1) + NA_COLS - 1
    bias = rel_bias[:, dr[:, None, None, :, None], dc[None, :, :, None, :]]
    bias = jnp.transpose(bias, (1, 2, 0, 3, 4, 5)).astype(jnp.float32)
    qg = q.reshape(b, rows, ncb, NA_COLS, h, d)
    gi, gj = row_idx[:, None, :, None], col_idx[None, :, None, :]
    kg = k.reshape(b, rows, GRID_W, h, d)[:, gi, gj]
    vg = v.reshape(b, rows, GRID_W, h, d)[:, gi, gj]
    scale = d ** -0.5
    s_loc = jnp.einsum('brnqhd,brnkchd->brnhqkc', qg, kg).astype(jnp.float32) * scale + bias
    s_loc = jnp.where(col_valid[None, None, :, None, :, None, :], s_loc, NEG_INF)
    s_ctx = jnp.einsum('brnqhd,blhd->brnhql', qg, kc).astype(jnp.float32) * scale
    n_loc = kr * cb
    scores = jnp.concatenate([s_loc.reshape(s_ctx.shape[:-1] + (n_loc,)), s_ctx], axis=-1)
    probs = jax.nn.softmax(scores, axis=-1).astype(v.dtype)
    p_loc = probs[..., :n_loc].reshape(s_loc.shape)
    o = (jnp.einsum('brnhqkc,brnkchd->brnqhd', p_loc, vg)
         + jnp.einsum('brnhql,blhd->brnqhd', probs[..., n_loc:], vc))
    return o.reshape(b, n_tok, h * d)


def short_conv_mixer(u, gate_b, gate_c, w_conv):
    return gate_b * dw_conv(gate_c * u, w_conv, CONV_K // 2)


def rglru_coeffs(u, w_a, b_a, w_x, b_x, lam):
    b, t, w = u.shape
    ub = u.reshape(b, t, LRU_BLOCKS, w // LRU_BLOCKS)
    r = jax.nn.sigmoid(jnp.einsum('btnc,nce->btne', ub, w_a).reshape(b, t, w) + b_a).astype(jnp.float32)
    i = jax.nn.sigmoid(jnp.einsum('btnc,nce->btne', ub, w_x).reshape(b, t, w) + b_x)
    log_a = -LRU_C * r * jax.nn.softplus(-lam.astype(jnp.float32))
    return jnp.exp(log_a), jnp.sqrt(-jnp.expm1(2.0 * log_a)) * (i * u).astype(jnp.float32)


def linear_scan(a, b, h0, reverse):
    if h0 is not None:
        t0 = a.shape[1] - 1 if reverse else 0
        b = b.at[:, t0].add(a[:, t0] * h0)

    def combine(e1, e2):
        return e1[0] * e2[0], e2[0] * e1[1] + e2[1]

    _, h = lax.associative_scan(combine, (a, b), reverse=reverse, axis=1)
    return h


def rglru_mixer(xr, gr, xrc, grc, p, ctx_out):
    u = dw_conv(xr, p['lru_conv_w'], LRU_CONV_K // 2) + p['lru_conv_b']
    uc = dw_conv(xrc, p['lru_conv_w'], LRU_CONV_K // 2) + p['lru_conv_b']
    h_lat, h_ctx = [], []
    for direction, reverse in ((0, False), (1, True)):
        prm = (p['lru_w_a'][direction], p['lru_b_a'][direction], p['lru_w_x'][direction],
               p['lru_b_x'][direction], p['lru_lam'][direction])
        a_c, b_c = rglru_coeffs(uc, *prm)
        hc = linear_scan(a_c, b_c, None, reverse)
        h0 = hc[:, 0] if reverse else hc[:, -1]
        a_l, b_l = rglru_coeffs(u, *prm)
        h_lat.append(linear_scan(a_l, b_l, h0, reverse))
        h_ctx.append(hc)
    y = (h_lat[0] + h_lat[1]).astype(xr.dtype) * jax.nn.gelu(gr)
    if not ctx_out:
        return y, None
    return y, (h_ctx[0] + h_ctx[1]).astype(xrc.dtype) * jax.nn.gelu(grc)


def merge_branches(ys, gate_logits, w_branch, b_gate, w_out):
    b, t, _ = gate_logits.shape
    g = jax.nn.sigmoid(gate_logits + b_gate).reshape(b, t, N_BRANCH, D_MODEL)
    merged = g[:, :, 0] * (ys[0] @ w_branch[0])
    for n in range(1, N_BRANCH):
        merged = merged + g[:, :, n] * (ys[n] @ w_branch[n])
    return merged @ w_out


def token_mixer(h, hc, p, ctx_out):
    n_tok = h.shape[1]
    (ax, ab, ac, bq, bk, bv, cx, cg, dq, dk, dv, gl) = split_columns(h @ p['w_in'])
    (axc, abc, acc, bqc, bkc, bvc, cxc, cgc, dqc, dkc, dvc, glc) = split_columns(hc @ p['w_in'])
    gq_b, gk_b, gq_d, gk_d = p['qk_norm_g'][0], p['qk_norm_g'][1], p['qk_norm_g'][2], p['qk_norm_g'][3]
    pos = jnp.arange(n_tok)
    rows, cols = pos // GRID_W, pos % GRID_W
    qb = rms_norm(split_heads(bq, WIN_HEADS), gq_b)
    kb = rms_norm(split_heads(bk, WIN_KV_HEADS), gk_b)
    vb = split_heads(bv, WIN_KV_HEADS)
    kbc = rms_norm(split_heads(bkc, WIN_KV_HEADS), gk_b)
    vbc = split_heads(bvc, WIN_KV_HEADS)
    y_b = window_attention(axial_rope(qb, rows, cols), qb, axial_rope(kb, rows, cols), vb, kbc, vbc,
                           p['attn_sink'])
    qd = rms_norm(split_heads(dq, NA_HEADS), gq_d)
    kd = rms_norm(split_heads(dk, NA_HEADS), gk_d)
    vd = split_heads(dv, NA_HEADS)
    kdc = rms_norm(split_heads(dkc, NA_HEADS), gk_d)
    vdc = split_heads(dvc, NA_HEADS)
    y_d = neighbourhood_attention(qd, kd, vd, kdc, vdc, p['na_rel_bias'])
    y_a = short_conv_mixer(ax, ab, ac, p['conv_a_w'])
    y_c, y_c_ctx = rglru_mixer(cx, cg, cxc, cgc, p, ctx_out)
    out = merge_branches((y_a, y_b, y_c, y_d), gl, p['w_branch'], p['b_gate'], p['w_out'])
    if not ctx_out:
        return out, None
    qbc = rms_norm(split_heads(bqc, WIN_HEADS), gq_b)
    qdc = rms_norm(split_heads(dqc, NA_HEADS), gq_d)
    ys_ctx = (short_conv_mixer(axc, abc, acc, p['conv_a_w']),
              context_attention(qbc, kbc, vbc, p['attn_sink']),
              y_c_ctx,
              context_attention(qdc, kdc, vdc, None))
    return out, merge_branches(ys_ctx, glc, p['w_branch'], p['b_gate'], p['w_out'])


def layer(x, xc, mod_x, mod_c, p, ctx_out):
    mx = jnp.split(mod_x[:, None, :], N_MOD, axis=-1)
    mc = jnp.split(mod_c[:, None, :], N_MOD, axis=-1)
    g = p['norm_g']

    def half_ffn(t, m, i):
        j = 6 * i
        hh = adaln(t, g[2 * i], m[j], m[j + 1])
        return t + 0.5 * m[j + 2] * swiglu(hh, p['ffn_w_gate'][i], p['ffn_w_up'][i], p['ffn_w_down'][i])

    x = half_ffn(x, mx, 0)
    xc = half_ffn(xc, mc, 0)
    y, yc = token_mixer(adaln(x, g[1], mx[3], mx[4]), adaln(xc, g[1], mc[3], mc[4]), p, ctx_out)
    x = half_ffn(x + mx[5] * y, mx, 1)
    if not ctx_out:
        return x, None
    xc = half_ffn(xc + mc[5] * yc, mc, 1)
    return x, xc


def setup_inputs(seed: int = 0) -> dict:
    key = jax.random.key(seed)
    ks = jax.random.split(key, 26)
    f32 = jnp.float32

    def nrm(k, shape, scale):
        return scale * jax.random.normal(k, shape, f32)

    blk = MIX_W // LRU_BLOCKS
    u = jax.random.uniform(ks[20], (DEPTH, 2, MIX_W), f32, 0.9, 0.999)
    s = u ** (1.0 / LRU_C)
    lam = jnp.log(s) - jnp.log1p(-s)
    return {
        'x': nrm(ks[0], (BATCH, SEQ, D_MODEL), 1.0),
        'c': nrm(ks[1], (BATCH, D_MODEL), 1.0),
        'ctx': nrm(ks[2], (BATCH, CTX_LEN, D_MODEL), 1.0),
        'c_ctx': nrm(ks[3], (D_MODEL,), 1.0),
        'w_mod': nrm(ks[4], (DEPTH, D_MODEL, N_MOD * D_MODEL), 0.5 * D_MODEL ** -0.5),
        'b_mod': nrm(ks[5], (DEPTH, N_MOD * D_MODEL), 0.01),
        'norm_g': 1.0 + nrm(ks[6], (DEPTH, 3, D_MODEL), 0.02),
        'ffn_w_gate': nrm(ks[7], (DEPTH, 2, D_MODEL, D_FF), D_MODEL ** -0.5),
        'ffn_w_up': nrm(ks[8], (DEPTH, 2, D_MODEL, D_FF), D_MODEL ** -0.5),
        'ffn_w_down': nrm(ks[9], (DEPTH, 2, D_FF, D_MODEL), D_FF ** -0.5),
        'w_in': nrm(ks[10], (DEPTH, D_MODEL, IN_W), D_MODEL ** -0.5),
        'b_gate': nrm(ks[11], (DEPTH, N_BRANCH * D_MODEL), 0.1),
        'conv_a_w': nrm(ks[12], (DEPTH, CONV_K, MIX_W), CONV_K ** -0.5),
        'qk_norm_g': 1.0 + nrm(ks[13], (DEPTH, 4, HEAD_DIM), 0.02),
        'attn_sink': nrm(ks[14], (DEPTH, WIN_HEADS), 0.5),
        'lru_conv_w': nrm(ks[15], (DEPTH, LRU_CONV_K, MIX_W), LRU_CONV_K ** -0.5),
        'lru_conv_b': nrm(ks[16], (DEPTH, MIX_W), 0.01),
        'lru_w_a': nrm(ks[17], (DEPTH, 2, LRU_BLOCKS, blk, blk), blk ** -0.5),
        'lru_b_a': nrm(ks[18], (DEPTH, 2, MIX_W), 0.01),
        'lru_w_x': nrm(ks[19], (DEPTH, 2, LRU_BLOCKS, blk, blk), blk ** -0.5),
        'lru_b_x': nrm(ks[21], (DEPTH, 2, MIX_W), 0.01),
        'lru_lam': lam,
        'na_rel_bias': nrm(ks[22], (DEPTH, NA_HEADS, 2 * NA_ROWS - 1, 2 * NA_COLS - 1), 0.1),
        'w_branch': nrm(ks[23], (DEPTH, N_BRANCH, MIX_W, D_MODEL), MIX_W ** -0.5),
        'w_out': nrm(ks[24], (DEPTH, D_MODEL, D_MODEL), D_MODEL ** -0.5),
    }


def reference(x, c, ctx, c_ctx, w_mod, b_mod, norm_g, ffn_w_gate, ffn_w_up, ffn_w_down, w_in, b_gate,
              conv_a_w, qk_norm_g, attn_sink, lru_conv_w, lru_conv_b, lru_w_a, lru_b_a, lru_w_x, lru_b_x,
              lru_lam, na_rel_bias, w_branch, w_out):
    xc = ctx
    for l in range(DEPTH):
        ctx_out = l < DEPTH - 1
        mod_x = jax.nn.silu(c) @ w_mod[l] + b_mod[l]
        mod_c = (jax.nn.silu(c_ctx) @ w_mod[l] + b_mod[l])[None, :]
        p = dict(norm_g=norm_g[l], ffn_w_gate=ffn_w_gate[l], ffn_w_up=ffn_w_up[l], ffn_w_down=ffn_w_down[l],
                 w_in=w_in[l], b_gate=b_gate[l], conv_a_w=conv_a_w[l], qk_norm_g=qk_norm_g[l],
                 attn_sink=attn_sink[l], lru_conv_w=lru_conv_w[l], lru_conv_b=lru_conv_b[l],
                 lru_w_a=lru_w_a[l], lru_b_a=lru_b_a[l], lru_w_x=lru_w_x[l], lru_b_x=lru_b_x[l],
                 lru_lam=lru_lam[l], na_rel_bias=na_rel_bias[l], w_branch=w_branch[l], w_out=w_out[l])
        x, xc = layer(x, xc, mod_x, mod_c, p, ctx_out)
    return x
```

```python
import contextlib
import numpy as np
import ml_dtypes
import concourse.bass as bass
import concourse.mybir as mybir
from concourse.bass_utils import run_bass_kernel_spmd

F32 = mybir.dt.float32
BF16 = mybir.dt.bfloat16
AF = mybir.ActivationFunctionType
ALU = mybir.AluOpType
NPBF = ml_dtypes.bfloat16

D = 1024; KD = 8; FF = 2816; KF = 22; T = 4096; CTX = 256; TT = T + CTX; TX = TT + 4
MW = 256; INW = 6656; NC = 8; SEQ = 16384; GRID_W = 64
EPS = 1e-6


class Buf:
    __slots__ = ("name", "lw", "rd", "dsem")

    def __init__(self, name):
        self.name = name
        self.lw = None
        self.rd = []
        self.dsem = None


class Prog:
    ENG = ("pe", "dve", "act", "pool", "sp")

    def __init__(self, nc, stack):
        self.nc = nc
        self.stack = stack
        self.eng = {"pe": nc.tensor, "dve": nc.vector, "act": nc.scalar,
                    "pool": nc.gpsimd, "sp": nc.sync}
        self.sems = {}
        self.cnt = {}
        self.waited = {e: {} for e in self.ENG}
        self.esem = {}
        self.nsem = 0
        self.free_dsems = []
        self.dbufs = []
        self.allbufs = []
        self.pool_fifo = []
        for e in self.ENG:
            key = f"E{e}"
            self._alloc_sem(key)
            self.esem[e] = key
        self.n_inst = 0

    def _alloc_sem(self, key):
        h = self.stack.enter_context(self.nc.semaphore(f"s{self.nsem}_{key}"))
        self.nsem += 1
        self.sems[key] = h
        self.cnt[key] = 0
        return key

    def _need(self, e, deps, key, val):
        if self.waited[e].get(key, 0) >= val:
            return
        deps[key] = max(deps.get(key, 0), val)

    def _emit_waits(self, e, deps):
        eng = self.eng[e]
        for key, val in deps.items():
            eng.wait_ge(self.sems[key], val)
            self.waited[e][key] = val
            self.n_inst += 1

    def _collect(self, e, reads, writes, dma=False):
        deps = {}
        for b in reads:
            if b.lw is not None:
                self._need(e, deps, *b.lw)
        for b in writes:
            if b.lw is not None:
                if not (dma and b.lw[0] == b.dsem and not b.rd):
                    self._need(e, deps, *b.lw)
            for r in b.rd:
                self._need(e, deps, *r)
        return deps

    def _stamp(self, stamp, reads, writes):
        for b in list(reads) + list(writes):
            if b.lw is None and not b.rd:
                self.allbufs.append(b)
        for b in reads:
            b.rd.append(stamp)
            if len(b.rd) > 64:
                m = {}
                for k, v in b.rd:
                    m[k] = max(m.get(k, 0), v)
                b.rd = list(m.items())
        for b in writes:
            b.lw = stamp
            b.rd = []

    def op(self, e, fn, reads=(), writes=(), sig=True):
        deps = self._collect(e, reads, writes)
        if e == "pe":
            deps.pop(self.esem["pe"], None)
        self._emit_waits(e, deps)
        ins = fn()
        self.n_inst += 1
        key = self.esem[e]
        if sig:
            self.cnt[key] += 1
            ins.then_inc(self.sems[key], 1)
            stamp = (key, self.cnt[key])
        else:
            stamp = (key, self.cnt[key] + 1)
        self._stamp(stamp, reads, writes)
        return ins

    def dma(self, q, out_ap, in_ap, reads=(), writes=(), sb=None, ndesc=0, **kw):
        if q == "pool" and ndesc:
            while self.pool_fifo and sum(n for _, _, n in self.pool_fifo) + ndesc > 3000:
                key_, val_, _ = self.pool_fifo.pop(0)
                d_ = {}
                self._need("pool", d_, key_, val_)
                self._emit_waits("pool", d_)
        if sb.dsem is None:
            if self.free_dsems:
                sb.dsem = self.free_dsems.pop()
            else:
                sb.dsem = self._alloc_sem(f"D{self.nsem}")
            self.dbufs.append(sb)
        deps = self._collect(q, reads, writes, dma=True)
        for b in list(reads) + list(writes):
            if b.dsem is not None and b.dsem in deps:
                deps[b.dsem] = self.cnt[b.dsem]
        self._emit_waits(q, deps)
        ins = self.eng[q].dma_start(out=out_ap, in_=in_ap, **kw)
        self.n_inst += 1
        key = sb.dsem
        self.cnt[key] += 16
        ins.then_inc(self.sems[key], 16)
        self._stamp((key, self.cnt[key]), reads, writes)
        if q == "pool" and ndesc:
            self.pool_fifo.append((key, self.cnt[key], ndesc))
        return ins

    def barrier(self):
        for e in self.ENG:
            deps = {}
            for key, v in self.cnt.items():
                if v > 0 and key != self.esem[e]:
                    self._need(e, deps, key, v)
            if e != "pe" and self.cnt[self.esem[e]] > 0:
                self._need(e, deps, self.esem[e], self.cnt[self.esem[e]])
            self._emit_waits(e, deps)
        for b in self.allbufs:
            b.lw = None
            b.rd = []
        self.allbufs = []
        for b in self.dbufs:
            self.free_dsems.append(b.dsem)
            b.dsem = None
        self.dbufs = []
        self.pool_fifo = []
        for e in self.ENG:
            if self.cnt[self.esem[e]] > 20000:
                key = f"E{e}{self.nsem}"
                self._alloc_sem(key)
                self.esem[e] = key

    def collective(self, in_ap, out_ap, groups, reads=(), writes=()):
        deps = self._collect("pool", reads, writes)
        self._emit_waits("pool", deps)
        ins = self.nc.gpsimd.collective_compute("AllGather", ALU.bypass, replica_groups=groups, ins=[in_ap], outs=[out_ap])
        key = self._alloc_sem(f"C{self.nsem}")
        ins.then_inc(self.sems[key])
        self.cnt[key] = 1
        self.n_inst += 1
        self._stamp((key, 1), reads, writes)

    def finish(self):
        deps = {}
        for key, v in self.cnt.items():
            if v > 0:
                self._need("sp", deps, key, v)
        self._emit_waits("sp", deps)


SM_FIELDS = [("bmod", 72), ("normg", 24), ("bgate", 32), ("convaw", 6), ("lcw", 8), ("lcb", 2),
             ("lba", 4), ("lbx", 4), ("llam", 4), ("gqb", 1), ("gkb", 1), ("gqd", 1), ("gkd", 1),
             ("sink", 2), ("cvec", 16), ("hvalid", 4), ("eps", 1),
             ("onesD", 128), ("ones64", 128), ("rmat", 128), ("ones1", 128),
             ("wa00", 128), ("wa01", 128), ("wa10", 128), ("wa11", 128),
             ("wx00", 128), ("wx01", 128), ("wx10", 128), ("wx11", 128),
             ("lsel", 4), ("rsel", 4), ("fsel", 4), ("bsel", 4)]
SM_OFF = {}
_o = 0
for _n, _w in SM_FIELDS:
    SM_OFF[_n] = (_o, _w)
    _o += _w
NSM = _o


def chunked(v, nch):
    return np.ascontiguousarray(np.asarray(v, np.float32).reshape(nch, 128).T)


def blockdiag(a, b):
    m = np.zeros((128, 128), np.float32)
    m[:64, :64] = a
    m[64:, 64:] = b
    return m


def build_sm(inp, l, b, q):
    sm = np.zeros((128, NSM), np.float32)

    def put(name, arr):
        o, w = SM_OFF[name]
        sm[:, o:o + w] = np.asarray(arr, np.float32).reshape(128, w)

    put("bmod", chunked(inp["b_mod"][l], 72))
    put("normg", np.concatenate([chunked(inp["norm_g"][l, s], 8) for s in range(3)], 1))
    put("bgate", chunked(inp["b_gate"][l], 32))
    ca = inp["conv_a_w"][l]
    put("convaw", np.stack([ca[j, c * 128:(c + 1) * 128] for c in range(2) for j in range(3)], 1))
    lw = inp["lru_conv_w"][l]
    put("lcw", np.stack([lw[j, c * 128:(c + 1) * 128] for c in range(2) for j in range(4)], 1))
    put("lcb", chunked(inp["lru_conv_b"][l], 2))
    for nm, key in (("lba", "lru_b_a"), ("lbx", "lru_b_x"), ("llam", "lru_lam")):
        v = inp[key][l]
        put(nm, np.stack([v[d, c * 128:(c + 1) * 128] for d in range(2) for c in range(2)], 1))
    g = inp["qk_norm_g"][l]
    for i, nm in enumerate(("gqb", "gkb", "gqd", "gkd")):
        put(nm, np.tile(g[i], 2)[:, None])
    sk = inp["attn_sink"][l]
    put("sink", np.stack([np.repeat(sk[2 * c:2 * c + 2], 64) for c in range(2)], 1))
    cv = np.stack([inp["c"][b], inp["c_ctx"]], 1)
    put("cvec", cv.reshape(8, 128, 2).transpose(1, 0, 2).reshape(128, 16))
    hv = np.array([1.0 if q > 0 else 0.0, 1.0 if q > 0 else 0.0, 1.0 if q < 3 else 0.0, 0.0], np.float32)
    put("hvalid", np.tile(hv[None, :], (128, 1)))
    put("eps", np.full((128, 1), EPS, np.float32))
    put("onesD", np.full((128, 128), 1.0 / D, np.float32))
    put("ones64", blockdiag(np.full((64, 64), 1.0 / 64, np.float32), np.full((64, 64), 1.0 / 64, np.float32)))
    r = np.zeros((128, 128), np.float32)
    for m in range(128):
        if (m % 32) < 16:
            r[m + 16, m] = -1.0
        else:
            r[m - 16, m] = 1.0
    put("rmat", r)
    put("ones1", np.ones((128, 128), np.float32))
    put("lsel", np.tile(np.array([1.0 if j == q - 1 else 0.0 for j in range(4)], np.float32)[None], (128, 1)))
    put("rsel", np.tile(np.array([1.0 if j == q + 1 else 0.0 for j in range(4)], np.float32)[None], (128, 1)))
    put("fsel", np.tile(np.array([1.0 if j < q else 0.0 for j in range(4)], np.float32)[None], (128, 1)))
    put("bsel", np.tile(np.array([1.0 if j > q else 0.0 for j in range(4)], np.float32)[None], (128, 1)))
    for d in range(2):
        for c in range(2):
            put(f"wa{d}{c}", blockdiag(inp["lru_w_a"][l, d, 2 * c], inp["lru_w_a"][l, d, 2 * c + 1]))
            put(f"wx{d}{c}", blockdiag(inp["lru_w_x"][l, d, 2 * c], inp["lru_w_x"][l, d, 2 * c + 1]))
    return sm


def rope_tables(q):
    pos = q * T + np.arange(T)
    rows = (pos // GRID_W).astype(np.float32)
    cols = (pos % GRID_W).astype(np.float32)
    nf = 16
    inv_freq = (10000.0 ** (-np.arange(nf, dtype=np.float32) / nf)).astype(np.float32)
    p = np.arange(128)
    d = p % 64
    j = d % 16
    use_col = (d // 32) == 1
    posm = np.where(use_col[:, None], cols[None, :], rows[None, :]).astype(np.float32)
    ang = (posm * inv_freq[j][:, None]).astype(np.float32)
    return np.cos(ang).astype(np.float32), np.sin(ang).astype(np.float32)


class Ctx:
    pass


_UNIQ = [0]


def uniq(name):
    _UNIQ[0] += 1
    return f"{name}_sb{_UNIQ[0]}"


def smv(K, name, i=0, n=1):
    o, w = SM_OFF[name]
    return K.sm[:, o + i:o + i + n]


def setup_common(nc, st):
    K = Ctx()
    K.nc = nc
    K.st = st
    st.enter_context(nc.Block())
    K.P = Prog(nc, st)
    K.sbuf = lambda name, shape, dt: st.enter_context(nc.sbuf_tensor(uniq(name), shape, dt))
    K.ps = [st.enter_context(nc.psum_tensor(f"ps{i}", [128, 512], F32)) for i in range(8)]
    K.bps = [Buf(f"ps{i}") for i in range(8)]
    K.sm = K.sbuf("sm_sb", [128, NSM], F32)
    K.bsm = Buf("sm")
    names = ["onesD", "ones64", "rmat", "ones1"] + [f"w{a}{d}{c}" for a in "ax" for d in range(2) for c in range(2)]
    K.cm = K.sbuf("cm_sb", [128, len(names), 128], BF16)
    K.bcm = Buf("cm")
    K.cmi = {n: i for i, n in enumerate(names)}
    return K


def load_sm(K, sm_d):
    nc, P = K.nc, K.P
    P.dma("sp", K.sm[:], sm_d, writes=[K.bsm], sb=K.bsm)
    for n, i in K.cmi.items():
        o, w = SM_OFF[n]
        P.op("dve", lambda: nc.vector.tensor_copy(K.cm[:, i, :], K.sm[:, o:o + 128]), reads=[K.bsm], writes=[K.bcm])


def cmat(K, name):
    return K.cm[:, K.cmi[name], :]


def emit_mod(K, w_mod_d):
    nc, P = K.nc, K.P
    K.modT = K.sbuf("modT", [128, 72, 2], F32)
    K.bmod = Buf("modT")
    K.gsT = K.sbuf("gsT", [128, 3, 8, 2], F32)
    K.gateT = K.sbuf("gateT", [128, 3, 8, 2], F32)
    K.bgs = Buf("gs")
    with contextlib.ExitStack() as s2:
        wm = [s2.enter_context(nc.sbuf_tensor(uniq(f"wm{i}"), [128, 8, 1024], F32)) for i in range(2)]
        bwm = [Buf(f"wm{i}") for i in range(2)]
        sc = s2.enter_context(nc.sbuf_tensor(uniq("silu_c"), [128, 16], F32))
        bsc = Buf("silu_c")
        P.op("act", lambda: nc.scalar.activation(sc[:], smv(K, "cvec", 0, 16), AF.Silu), reads=[K.bsm], writes=[bsc])
        psm = K.ps[7]
        bpsm = K.bps[7]
        for i in range(9):
            s = i % 2
            for k in range(8):
                P.dma("sp", wm[s][:, k, :], w_mod_d[k * 128:(k + 1) * 128, i * 1024:(i + 1) * 1024],
                      writes=[bwm[s]], sb=bwm[s])
            for dk in range(8):
                cidx = i * 8 + dk
                for k in range(8):
                    P.op("pe", lambda: nc.tensor.matmul(psm[:, cidx * 2:cidx * 2 + 2], wm[s][:, k, dk * 128:(dk + 1) * 128],
                                                        sc[:, 2 * k:2 * k + 2], start=(k == 0), stop=(k == 7)),
                         reads=[bwm[s], bsc], writes=[bpsm], sig=(k == 7))
        psv = psm[:, 0:144].rearrange("p (c j) -> p c j", j=2)
        for j in range(2):
            P.op("dve", lambda: nc.vector.tensor_tensor(K.modT[:, :, j], psv[:, :, j], smv(K, "bmod", 0, 72), ALU.add),
                 reads=[bpsm, K.bsm], writes=[K.bmod])
        for s in range(3):
            for j in range(2):
                P.op("dve", lambda: nc.vector.tensor_scalar(K.gsT[:, s, :, j], K.modT[:, (3 * s + 1) * 8:(3 * s + 2) * 8, j],
                                                            1.0, None, ALU.add), reads=[K.bmod], writes=[K.bgs])
                P.op("dve", lambda: nc.vector.tensor_tensor(K.gsT[:, s, :, j], K.gsT[:, s, :, j],
                                                            smv(K, "normg", s * 8, 8), ALU.mult),
                     reads=[K.bgs, K.bsm], writes=[K.bgs])
                P.op("dve", lambda: nc.vector.tensor_scalar(K.gateT[:, s, :, j], K.modT[:, (3 * s + 2) * 8:(3 * s + 3) * 8, j],
                                                            (1.0 if s == 1 else 0.5), None, ALU.mult),
                     reads=[K.bmod], writes=[K.bgs])
        P.barrier()


def shift_col(K, s, k, j):
    return K.modT[:, 3 * s * 8 + k, j:j + 1]


def emit_adaln(K, xt, bxt, N, s, col, hT, bh, sq, bsq, rstd, brs, pst, bpst):
    nc, P = K.nc, K.P
    P.op("act", lambda: nc.scalar.activation(sq[:, :, :N], xt[:, :, :N], AF.Square), reads=[bxt], writes=[bsq])
    for k in range(8):
        P.op("pe", lambda: nc.tensor.matmul(pst[:, :N], cmat(K, "onesD"), sq[:, k, :N], start=(k == 0), stop=(k == 7)),
             reads=[bsq, K.bcm], writes=[bpst], sig=(k == 7))
    P.op("act", lambda: nc.scalar.activation(rstd[:, :N], pst[:, :N], AF.Sqrt, bias=smv(K, "eps"), scale=1.0),
         reads=[bpst, K.bsm], writes=[brs])
    P.op("dve", lambda: nc.vector.reciprocal(rstd[:, :N], rstd[:, :N]), reads=[brs], writes=[brs])
    for k in range(8):
        P.op("dve", lambda: nc.vector.scalar_tensor_tensor(K.tmp32[:, :N], xt[:, k, :N], K.gsT[:, s, k, col:col + 1],
                                                           rstd[:, :N], ALU.mult, ALU.mult),
             reads=[bxt, brs, K.bgs], writes=[K.btmp32])
        P.op("act", lambda: nc.scalar.activation(hT[:, k, :N], K.tmp32[:, :N], AF.Identity,
                                                 bias=shift_col(K, s, k, col), scale=1.0),
             reads=[K.btmp32, K.bmod], writes=[bh])


class WBufs:
    def __init__(self, gw):
        self.gw = gw
        self.bufs = []

    def at(self, col):
        return self.bufs[col // self.gw]


def load_weight_cast(K, dst, wb, src, g0, g1):
    b = Buf("wgrp")
    wb.bufs.append(b)
    K.P.dma("pool", dst[:, :, g0:g1], src.rearrange("(k p) f -> p k f", p=128)[:, :, g0:g1], writes=[b], sb=b,
            ndesc=128 * dst.shape[1])


def load_weight_groups(K, specs):
    pos = [0] * len(specs)
    done = False
    while not done:
        done = True
        for i, (dst, wb, src, width) in enumerate(specs):
            if pos[i] < width:
                g1 = min(width, pos[i] + wb.gw)
                load_weight_cast(K, dst, wb, src, pos[i], g1)
                pos[i] = g1
                done = False


def emit_ffn_phase(K, x_src, x_dst, tiles, s, wg_d, wu_d, wd_d, h_dst=None, s_next=None):
    nc, P = K.nc, K.P
    post = h_dst is not None
    with contextlib.ExitStack() as s2:
        sb = lambda name, shape, dt: s2.enter_context(nc.sbuf_tensor(uniq(name), shape, dt))
        wg = sb("wg", [128, KD, FF], BF16); bwg = WBufs(256)
        wu = sb("wu", [128, KD, FF], BF16); bwu = WBufs(256)
        wd = sb("wd", [128, KF, D], BF16); bwd = WBufs(256)
        load_weight_groups(K, [(wg, bwg, wg_d, FF), (wu, bwu, wu_d, FF)])
        load_weight_groups(K, [(wd, bwd, wd_d, D)])
        xts = [sb(f"xt{i}", [128, 8, 512], F32) for i in range(2)]; bxts = [Buf(f"xt{i}") for i in range(2)]
        hT = sb("hT", [128, 8, 512], BF16); bh = Buf("hT")
        act = sb("actT", [128, KF, 512], BF16); bact = Buf("actT")
        rstd = sb("rstd", [128, 512], F32); brs = Buf("rstd")
        K.tmp32 = sb("tmp32", [128, 512], F32); K.btmp32 = Buf("tmp32")
        sg, bsg = K.tmp32, K.btmp32
        if post:
            h2o = sb("h2o", [128, 8, 512], BF16); bh2o = Buf("h2o")
            rstd2 = sb("rstd2", [128, 512], F32); brs2 = Buf("rstd2")
        xs = x_src.rearrange("(c p) t -> p c t", p=128)
        xd = x_dst.rearrange("(c p) t -> p c t", p=128)
        hd = h_dst.rearrange("(c p) t -> p c t", p=128) if post else None
        n = len(tiles)
        gi = [0]

        def load(i):
            start, N, col = tiles[i]
            P.dma("sp", xts[i % 2][:, :, :N], xs[:, :, start:start + N], writes=[bxts[i % 2]], sb=bxts[i % 2])

        def adaln(i):
            start, N, col = tiles[i]
            emit_adaln(K, xts[i % 2], bxts[i % 2], N, s, col, hT, bh, hT, bh, rstd, brs, K.ps[6], K.bps[6])

        def gateup(i):
            start, N, col = tiles[i]
            for f in range(KF):
                pg, bpg = K.ps[gi[0] % 2], K.bps[gi[0] % 2]
                pu, bpu = K.ps[2 + gi[0] % 2], K.bps[2 + gi[0] % 2]
                gi[0] += 1
                for k in range(8):
                    P.op("pe", lambda: nc.tensor.matmul(pg[:, :N], wg[:, k, f * 128:(f + 1) * 128], hT[:, k, :N],
                                                        start=(k == 0), stop=(k == 7)),
                         reads=[bwg.at(f * 128), bh], writes=[bpg], sig=(k == 7))
                for k in range(8):
                    P.op("pe", lambda: nc.tensor.matmul(pu[:, :N], wu[:, k, f * 128:(f + 1) * 128], hT[:, k, :N],
                                                        start=(k == 0), stop=(k == 7)),
                         reads=[bwu.at(f * 128), bh], writes=[bpu], sig=(k == 7))
                P.op("act", lambda: nc.scalar.activation(sg[:, :N], pg[:, :N], AF.Silu), reads=[bpg], writes=[bsg])
                P.op("dve", lambda: nc.vector.tensor_tensor(act[:, f, :N], pu[:, :N], sg[:, :N], ALU.mult),
                     reads=[bpu, bsg], writes=[bact])

        def down(i):
            start, N, col = tiles[i]
            xt, bxt = xts[i % 2], bxts[i % 2]
            for dk in range(8):
                py, bpy = K.ps[4 + dk % 2], K.bps[4 + dk % 2]
                for f in range(KF):
                    P.op("pe", lambda: nc.tensor.matmul(py[:, :N], wd[:, f, dk * 128:(dk + 1) * 128], act[:, f, :N],
                                                        start=(f == 0), stop=(f == KF - 1)),
                         reads=[bwd.at(dk * 128), bact], writes=[bpy], sig=(f == KF - 1))
                P.op("dve", lambda: nc.vector.scalar_tensor_tensor(xt[:, dk, :N], py[:, :N], K.gateT[:, s, dk, col:col + 1],
                                                                   xt[:, dk, :N], ALU.mult, ALU.add),
                     reads=[bpy, bxt, K.bgs], writes=[bxt])
            P.dma("sp", xd[:, :, start:start + N], xt[:, :, :N], reads=[bxt], sb=bxt)

        def postadaln(i):
            start, N, col = tiles[i]
            emit_adaln(K, xts[i % 2], bxts[i % 2], N, s_next, col, h2o, bh2o, h2o, bh2o, rstd2, brs2, K.ps[7], K.bps[7])
            P.dma("sp", hd[:, :, start:start + N], h2o[:, :, :N], reads=[bh2o], sb=bh2o)

        load(0)
        if n > 1:
            load(1)
        adaln(0)
        for i in range(n):
            gateup(i)
            if i >= 1:
                if post:
                    postadaln(i - 1)
                if i + 1 < n:
                    load(i + 1)
            if i + 1 < n:
                adaln(i + 1)
            down(i)
        if post:
            postadaln(n - 1)
        P.barrier()


LAT_TILES = [(i * 512, 512, 0) for i in range(8)]
CTX_TILE = (T, CTX, 1)
HALO_TILE = (TT, 4, 0)


def emit_A(K, I):
    nc, P = K.nc, K.P
    xT_d, cos_d, sin_d = I.xin, I.cos_d, I.sin_d
    wg_d, wu_d, wd_d, win_d = I.wg1, I.wu1, I.wd1, I.win
    O = I.O
    x1full, h2_d = I.x1full, I.h2_d
    with contextlib.ExitStack() as sA:
        K_sbuf = lambda name, shape, dt: sA.enter_context(nc.sbuf_tensor(uniq(name), shape, dt))
        tiles = LAT_TILES + [CTX_TILE, HALO_TILE]
        emit_ffn_phase(K, xT_d, x1full, tiles, 0, wg_d, wu_d, wd_d, h_dst=h2_d, s_next=1)

        cxf = K_sbuf("cxf", [128, 2, T + 4], F32); bcxf = Buf("cxf")
        cxc = K_sbuf("cxc", [128, 2, CTX + 4], F32); bcxc = Buf("cxc")
        P.op("pool", lambda: nc.gpsimd.memset(cxc[:], 0.0), writes=[bcxc])
        gq = K_sbuf("gq", [128, 2], F32); bgq = Buf("gq")
        P.op("dve", lambda: nc.vector.tensor_scalar(gq[:, 0:1], smv(K, "gqb"), 0.125, None, ALU.mult), reads=[K.bsm], writes=[bgq])
        P.op("dve", lambda: nc.vector.tensor_scalar(gq[:, 1:2], smv(K, "gqd"), 0.125, None, ALU.mult), reads=[K.bsm], writes=[bgq])

        with contextlib.ExitStack() as s2:
            sb = lambda name, shape, dt: s2.enter_context(nc.sbuf_tensor(uniq(name), shape, dt))
            win = sb("win", [128, KD, INW], BF16); bwin = WBufs(512)
            load_weight_groups(K, [(win, bwin, win_d, INW)])
            h2t = sb("h2t", [128, 8, 512], BF16); bh2 = Buf("h2t")
            cst = sb("cost", [128, 512], F32); snt = sb("sint", [128, 512], F32); bcs = Buf("cossin")
            axs = sb("axs", [128, 2, 512], F32); baxs = Buf("axs")
            abo = sb("abo", [128, 2, 512], BF16); babo = Buf("abo")
            vo = sb("vo", [128, 2, 512], BF16); bvo = Buf("vo")
            qpl = sb("qpl", [128, 2, 512], BF16); bqpl = Buf("qpl")
            qro = sb("qro", [128, 2, 512], BF16); bqro = Buf("qro")
            ko = sb("ko", [128, 512], BF16); bko = Buf("ko")
            ktmp = sb("ktmp", [128, 512], BF16); bktmp = Buf("ktmp")
            vtok = sb("vtok", [128, 4, 384], BF16); bvtok = Buf("vtok")
            gcg = sb("gcg", [128, 2, 512], BF16); bgcg = Buf("gcg")
            dqo = sb("dqo", [128, 2, 512], BF16); bdqo = Buf("dqo")
            dko = sb("dko", [128, 2, 512], BF16); bdko = Buf("dko")
            gto = sb("gto", [128, 8, 512], BF16); bgto = Buf("gto")
            sqb = [sb(f"sqb{i}", [128, 512], BF16) for i in range(2)]; bsqb = [Buf(f"sqb{i}") for i in range(2)]
            rs = [sb(f"rs{i}", [128, 512], F32) for i in range(2)]; brs = [Buf(f"rs{i}") for i in range(2)]
            qh = [sb(f"qh{i}", [128, 512], F32) for i in range(2)]; bqh = [Buf(f"qh{i}") for i in range(2)]
            t1 = [sb(f"t1{i}", [128, 512], F32) for i in range(2)]; bt1 = [Buf(f"t1{i}") for i in range(2)]
            t2 = [sb(f"t2{i}", [128, 512], F32) for i in range(2)]; bt2 = [Buf(f"t2{i}") for i in range(2)]
            bout = Buf("aout")
            h2s = h2_d.rearrange("(c p) t -> p c t", p=128)
            v3 = lambda ap: ap.rearrange("(c p) t -> p c t", p=128)
            ov = {n: v3(O[n]) for n in ("abT", "gcgT", "dqT", "gT")}
            pctr = [0]

            def proj(c, N):
                i = (0, 1, 2, 6, 7)[pctr[0] % 5]
                pctr[0] += 1
                ps, bps = K.ps[i], K.bps[i]
                for k in range(8):
                    P.op("pe", lambda: nc.tensor.matmul(ps[:, :N], win[:, k, c * 128:(c + 1) * 128], h2t[:, k, :N],
                                                        start=(k == 0), stop=(k == 7)),
                         reads=[bwin.at(c * 128), bh2], writes=[bps], sig=(k == 7))
                step_pending()
                return ps, bps

            pending = []

            def step_pending():
                for g in list(pending):
                    try:
                        next(g)
                    except StopIteration:
                        pending.remove(g)

            def drain():
                while pending:
                    step_pending()

            qn = [0]

            def qknorm(ps, bps, N, gcol, bg, out_plain, bop, rope_out=None, bro=None, after=None):
                i = qn[0] % 2
                qn[0] += 1
                sqb_, bsqb_ = sqb[i], bsqb[i]
                rs_, brs_ = rs[i], brs[i]
                qh_, bqh_ = qh[i], bqh[i]
                t1_, bt1_ = t1[i], bt1[i]
                t2_, bt2_ = t2[i], bt2[i]
                pst, bpst = K.ps[3], K.bps[3]
                P.op("act", lambda: nc.scalar.activation(sqb_[:, :N], ps[:, :N], AF.Square), reads=[bps], writes=[bsqb_])
                yield
                P.op("pe", lambda: nc.tensor.matmul(pst[:, :N], cmat(K, "ones64"), sqb_[:, :N], start=True, stop=True),
                     reads=[bsqb_, K.bcm], writes=[bpst])
                P.op("act", lambda: nc.scalar.activation(rs_[:, :N], pst[:, :N], AF.Sqrt, bias=smv(K, "eps"), scale=1.0),
                     reads=[bpst, K.bsm], writes=[brs_])
                P.op("dve", lambda: nc.vector.reciprocal(rs_[:, :N], rs_[:, :N]), reads=[brs_], writes=[brs_])
                P.op("dve", lambda: nc.vector.scalar_tensor_tensor(qh_[:, :N], ps[:, :N], gcol, rs_[:, :N], ALU.mult, ALU.mult),
                     reads=[bps, brs_, bg], writes=[bqh_])
                P.op("act", lambda: nc.scalar.copy(out_plain[:, :N], qh_[:, :N]), reads=[bqh_], writes=[bop])
                if rope_out is not None:
                    yield
                    psr, bpsr = K.ps[4], K.bps[4]
                    P.op("pe", lambda: nc.tensor.matmul(psr[:, :N], cmat(K, "rmat"), out_plain[:, :N], start=True, stop=True),
                         reads=[bop, K.bcm], writes=[bpsr])
                    P.op("pool", lambda: nc.gpsimd.tensor_tensor(t1_[:, :N], qh_[:, :N], cst[:, :N], ALU.mult),
                         reads=[bqh_, bcs], writes=[bt1_])
                    P.op("dve", lambda: nc.vector.tensor_tensor(t2_[:, :N], psr[:, :N], snt[:, :N], ALU.mult),
                         reads=[bpsr, bcs], writes=[bt2_])
                    P.op("pool", lambda: nc.gpsimd.tensor_tensor(rope_out[:, :N], t1_[:, :N], t2_[:, :N], ALU.add),
                         reads=[bt1_, bt2_], writes=[bro])
                if after is not None:
                    after()

            def qk(*a, **kw):
                g = qknorm(*a, **kw)
                next(g)
                pending.append(g)

            for (start, N, col) in tiles:
                kind = "lat" if start < T else ("ctx" if start == T else "halo")
                P.dma("sp", h2t[:, :, :N], h2s[:, :, start:start + N], writes=[bh2], sb=bh2)
                if kind == "halo":
                    for c in range(2):
                        ps, bps = proj(10 + c, N)
                        P.op("dve", lambda: nc.vector.tensor_tensor(cxf[:, c, 0:2], ps[:, 0:2], smv(K, "hvalid", 0, 2), ALU.mult),
                             reads=[bps, K.bsm], writes=[bcxf])
                        P.op("dve", lambda: nc.vector.tensor_tensor(cxf[:, c, T + 2:T + 4], ps[:, 2:4], smv(K, "hvalid", 2, 2), ALU.mult),
                             reads=[bps, K.bsm], writes=[bcxf])
                    continue
                lat = kind == "lat"
                if lat:
                    P.dma("sp", cst[:, :N], cos_d[:, start:start + N], writes=[bcs], sb=bcs)
                    P.dma("sp", snt[:, :N], sin_d[:, start:start + N], writes=[bcs], sb=bcs)
                sl = slice(start, start + N)
                for c in range(2):
                    ps, bps = proj(0 + c, N)
                    P.op("act", lambda: nc.scalar.copy(axs[:, c, :N], ps[:, :N]), reads=[bps], writes=[baxs])
                for c in range(2):
                    ps, bps = proj(2 + c, N)
                    P.op("act", lambda: nc.scalar.copy(abo[:, c, :N], ps[:, :N]), reads=[bps], writes=[babo])
                for c in range(2):
                    ps, bps = proj(4 + c, N)
                    P.op("dve", lambda: nc.vector.tensor_tensor(vo[:, c, :N], ps[:, :N], axs[:, c, :N], ALU.mult),
                         reads=[bps, baxs], writes=[bvo])
                P.dma("sp", ov["abT"][:, :, sl], abo[:, :, :N], reads=[babo], sb=babo)
                if lat:
                    P.dma("sp", v3(O["vcvext"])[:, :, 1 + start:1 + start + N], vo[:, :, :N], reads=[bvo], sb=bvo)
                else:
                    P.dma("sp", v3(O["vcvcext"])[:, :, 1:1 + N], vo[:, :, :N], reads=[bvo], sb=bvo)
                def store_q(c, sl=sl, N=N, lat=lat):
                    for hh in range(2):
                        r0 = hh * 128 + c * 64
                        P.dma("sp", O["qplT"][r0:r0 + 64, sl], qpl[hh * 64:(hh + 1) * 64, c, :N], reads=[bqpl], sb=bqpl)
                        if lat:
                            P.dma("sp", O["qrotT"][r0:r0 + 64, sl], qro[hh * 64:(hh + 1) * 64, c, :N], reads=[bqro], sb=bqro)
                for c in range(2):
                    ps, bps = proj(6 + c, N)
                    qk(ps, bps, N, gq[:, 0:1], bgq, qpl[:, c, :], bqpl, qro[:, c, :] if lat else None, bqro,
                       after=(lambda c=c, f=store_q: f(c)))
                ps, bps = proj(8, N)
                if lat:
                    qk(ps, bps, N, smv(K, "gkb"), K.bsm, ktmp, bktmp, ko, bko,
                       after=(lambda start=start, N=N: P.dma("sp", O["kext"][:, 128 + start:128 + start + N], ko[:, :N], reads=[bko], sb=bko)))
                else:
                    qk(ps, bps, N, smv(K, "gkb"), K.bsm, ko, bko,
                       after=(lambda N=N: P.dma("sp", O["kc"][:, 0:N], ko[:, :N], reads=[bko], sb=bko)))
                psv, bpsv = K.ps[5], K.bps[5]
                nb = N // 128
                for tb in range(nb):
                    for k in range(8):
                        P.op("pe", lambda: nc.tensor.matmul(psv[:, 0:128], h2t[:, k, tb * 128:(tb + 1) * 128], win[:, k, 9 * 128:10 * 128],
                                                            start=(k == 0), stop=(k == 7)),
                             reads=[bwin.at(9 * 128), bh2], writes=[bpsv], sig=(k == 7))
                    for k in range(8):
                        P.op("pe", lambda: nc.tensor.matmul(psv[:, 128:384], h2t[:, k, tb * 128:(tb + 1) * 128], win[:, k, 18 * 128:20 * 128],
                                                            start=(k == 0), stop=(k == 7)),
                             reads=[bwin.at(18 * 128), bh2], writes=[bpsv], sig=(k == 7))
                    P.op("act", lambda: nc.scalar.copy(vtok[:, tb, :], psv[:, 0:384]), reads=[bpsv], writes=[bvtok])
                tv = lambda ap: ap.rearrange("(b p) f -> p b f", p=128)
                if lat:
                    P.dma("sp", tv(O["vext"][128 + start:128 + start + N, :]), vtok[:, 0:nb, 0:128], reads=[bvtok], sb=bvtok)
                    P.dma("sp", tv(O["vdext"][256 + start:256 + start + N, :]), vtok[:, 0:nb, 128:384], reads=[bvtok], sb=bvtok)
                else:
                    P.dma("sp", tv(O["vc"][0:N, :]), vtok[:, 0:nb, 0:128], reads=[bvtok], sb=bvtok)
                    P.dma("sp", tv(O["vdc"][0:N, :]), vtok[:, 0:nb, 128:384], reads=[bvtok], sb=bvtok)
                for c in range(2):
                    ps, bps = proj(10 + c, N)
                    if lat:
                        P.op("act", lambda: nc.scalar.copy(cxf[:, c, 2 + start:2 + start + N], ps[:, :N]), reads=[bps], writes=[bcxf])
                    else:
                        P.op("act", lambda: nc.scalar.copy(cxc[:, c, 2:2 + N], ps[:, :N]), reads=[bps], writes=[bcxc])
                for c in range(2):
                    ps, bps = proj(12 + c, N)
                    P.op("act", lambda: nc.scalar.activation(gcg[:, c, :N], ps[:, :N], AF.Gelu_apprx_tanh), reads=[bps], writes=[bgcg])
                P.dma("sp", ov["gcgT"][:, :, sl], gcg[:, :, :N], reads=[bgcg], sb=bgcg)
                for c in range(2):
                    ps, bps = proj(14 + c, N)
                    qk(ps, bps, N, gq[:, 1:2], bgq, dqo[:, c, :], bdqo,
                       after=(lambda c=c, sl=sl, N=N: P.dma("sp", ov["dqT"][:, c, sl], dqo[:, c, :N], reads=[bdqo], sb=bdqo)))
                for c in range(2):
                    ps, bps = proj(16 + c, N)
                    if lat:
                        aft = (lambda c=c, start=start, N=N: P.dma("sp", v3(O["kdext"])[:, c, 256 + start:256 + start + N], dko[:, c, :N], reads=[bdko], sb=bdko))
                    else:
                        aft = (lambda c=c, N=N: P.dma("sp", v3(O["kdc"])[:, c, 0:N], dko[:, c, :N], reads=[bdko], sb=bdko))
                    qk(ps, bps, N, smv(K, "gkd"), K.bsm, dko[:, c, :], bdko, after=aft)
                for c in range(32):
                    ps, bps = proj(20 + c, N)
                    P.op("act", lambda: nc.scalar.activation(gto[:, c % 8, :N], ps[:, :N], AF.Sigmoid, bias=smv(K, "bgate", c), scale=1.0),
                         reads=[bps, K.bsm], writes=[bgto])
                    if c % 8 == 7:
                        c0 = c - 7
                        P.dma("sp", ov["gT"][:, c0:c0 + 8, sl], gto[:, :, :N], reads=[bgto], sb=bgto)
                drain()
            P.barrier()

        with contextlib.ExitStack() as s2:
            sb = lambda name, shape, dt: s2.enter_context(nc.sbuf_tensor(uniq(name), shape, dt))
            u = sb("lru_u", [128, T], F32); bu = Buf("lru_u")
            ub = sb("lru_ub", [128, T], BF16); bub = Buf("lru_ub")
            ra = sb("lru_a", [128, T], F32); bra = Buf("lru_a")
            ib = sb("lru_b", [128, T], F32); bib = Buf("lru_b")
            tm = sb("lru_t", [128, T], F32); btm = Buf("lru_t")
            hh = sb("lru_h", [128, T], F32); bhh = Buf("lru_h")
            pc = sb("lru_p", [128, T], F32); bpc = Buf("lru_p")
            zz = sb("lru_z", [128, T], F32); bzz = Buf("lru_z")
            hs = sb("lru_hs", [128, CTX], F32); bhs = Buf("lru_hs")
            nl8 = sb("nl8", [128, 4], F32); bnl = Buf("nl8")
            lsum, bls = K.lsum, K.bls
            blo = Buf("lruout")
            P.op("pool", lambda: nc.gpsimd.memset(zz[:], 0.0), writes=[bzz])
            P.op("act", lambda: nc.scalar.activation(nl8[:], smv(K, "llam", 0, 4), AF.Exp, scale=-1.0), reads=[K.bsm], writes=[bnl])
            P.op("act", lambda: nc.scalar.activation(nl8[:], nl8[:], AF.Ln, bias=1.0, scale=1.0), reads=[bnl], writes=[bnl])
            P.op("dve", lambda: nc.vector.tensor_scalar(nl8[:], nl8[:], -8.0, None, ALU.mult), reads=[bnl], writes=[bnl])
            pi = [0]

            def lru_chain(src, bsrc, Tn, c, is_ctx):
                lw = lambda j: smv(K, "lcw", c * 4 + j)
                P.op("dve", lambda: nc.vector.tensor_scalar(u[:, :Tn], src[:, c, 0:Tn], lw(0), smv(K, "lcb", c), ALU.mult, ALU.add),
                     reads=[bsrc, K.bsm], writes=[bu])
                for j in range(1, 4):
                    P.op("dve", lambda: nc.vector.scalar_tensor_tensor(u[:, :Tn], src[:, c, j:j + Tn], lw(j), u[:, :Tn], ALU.mult, ALU.add),
                         reads=[bsrc, bu, K.bsm], writes=[bu])
                P.op("act", lambda: nc.scalar.copy(ub[:, :Tn], u[:, :Tn]), reads=[bu], writes=[bub])
                for d in range(2):
                    for t0 in range(0, Tn, 512):
                        n = min(512, Tn - t0)
                        for (wn, bn, dst, bdst) in ((f"wa{d}{c}", "lba", ra, bra), (f"wx{d}{c}", "lbx", ib, bib)):
                            ps, bps = K.ps[pi[0] % 4], K.bps[pi[0] % 4]
                            pi[0] += 1
                            P.op("pe", lambda: nc.tensor.matmul(ps[:, :n], cmat(K, wn), ub[:, t0:t0 + n], start=True, stop=True),
                                 reads=[bub, K.bcm], writes=[bps])
                            P.op("act", lambda: nc.scalar.activation(dst[:, t0:t0 + n], ps[:, :n], AF.Sigmoid,
                                                                     bias=smv(K, bn, d * 2 + c), scale=1.0),
                                 reads=[bps, K.bsm], writes=[bdst])
                    P.op("act", lambda: nc.scalar.activation(ra[:, :Tn], ra[:, :Tn], AF.Exp, scale=nl8[:, d * 2 + c:d * 2 + c + 1]),
                         reads=[bra, bnl], writes=[bra])
                    P.op("dve", lambda: nc.vector.tensor_tensor(tm[:, :Tn], ra[:, :Tn], ra[:, :Tn], ALU.mult), reads=[bra], writes=[btm])
                    P.op("act", lambda: nc.scalar.activation(tm[:, :Tn], tm[:, :Tn], AF.Sqrt, bias=1.0000001, scale=-1.0),
                         reads=[btm], writes=[btm])
                    P.op("dve", lambda: nc.vector.tensor_tensor(ib[:, :Tn], ib[:, :Tn], u[:, :Tn], ALU.mult), reads=[bib, bu], writes=[bib])
                    P.op("dve", lambda: nc.vector.tensor_tensor(ib[:, :Tn], ib[:, :Tn], tm[:, :Tn], ALU.mult), reads=[bib, btm], writes=[bib])
                    if d == 0:
                        va, vb, vh, vp, vz = ra[:, 0:Tn], ib[:, 0:Tn], hh[:, 0:Tn], pc[:, 0:Tn], zz[:, 0:Tn]
                        last = Tn - 1
                    else:
                        rv = lambda t_: t_[:, 0:Tn][:, ::-1]
                        va, vb, vh, vp, vz = rv(ra), rv(ib), rv(hh), rv(pc), rv(zz)
                        last = 0
                    P.op("dve", lambda: nc.vector.tensor_tensor_scan(vh, va, vb, 0.0, ALU.mult, ALU.add), reads=[bra, bib], writes=[bhh])
                    if not is_ctx:
                        P.op("dve", lambda: nc.vector.tensor_tensor_scan(vp, va, vz, 1.0, ALU.mult, ALU.add), reads=[bra, bzz], writes=[bpc])
                        hn, pn = ("hlocF", "pcF") if d == 0 else ("hlocB", "pcB")
                        P.dma("sp", O[hn][c * 128:(c + 1) * 128, :], hh[:, :Tn], reads=[bhh], sb=bhh)
                        P.dma("sp", O[pn][c * 128:(c + 1) * 128, :], pc[:, :Tn], reads=[bpc], sb=bpc)
                        P.op("act", lambda: nc.scalar.copy(lsum[:, c * 6 + 2 * d:c * 6 + 2 * d + 1], pc[:, last:last + 1]), reads=[bpc], writes=[bls])
                        P.op("act", lambda: nc.scalar.copy(lsum[:, c * 6 + 2 * d + 1:c * 6 + 2 * d + 2], hh[:, last:last + 1]), reads=[bhh], writes=[bls])
                    else:
                        P.op("act", lambda: nc.scalar.copy(lsum[:, c * 6 + 4 + d:c * 6 + 5 + d], hh[:, last:last + 1]), reads=[bhh], writes=[bls])
                        if d == 0:
                            P.op("pool", lambda: nc.gpsimd.tensor_copy(hs[:, :Tn], hh[:, :Tn]), reads=[bhh], writes=[bhs])
                        else:
                            P.op("pool", lambda: nc.gpsimd.tensor_tensor(hs[:, :Tn], hs[:, :Tn], hh[:, :Tn], ALU.add), reads=[bhh, bhs], writes=[bhs])
                            P.dma("sp", O["hcsum"][c * 128:(c + 1) * 128, :], hs[:, :Tn], reads=[bhs], sb=bhs)

            for c in range(2):
                lru_chain(cxf, bcxf, T, c, False)
                lru_chain(cxc, bcxc, CTX, c, True)
            P.dma("sp", O["lsum"][:, :], lsum[:], reads=[bls], sb=bls)
            P.barrier()


def core_bq(core):
    return core // 4, core % 4


KEXT = T + 256
KDEXT = T + 512
NAJ = {0: list(range(0, 7)), 1: list(range(5)), 2: list(range(5)), 3: list(range(5)), 4: list(range(-2, 5))}
NAOFF = {}
_o = 0
for _v in range(5):
    NAOFF[_v] = _o
    _o += 4 * len(NAJ[_v])
NATILES = _o


def emit_B(K, I, ctx_out):
    nc, P = K.nc, K.P
    O = I.O
    x1_d = I.x1full
    qrot_d, qpl_d = O["qrotT"], O["qplT"]
    kext_d, vext_d, kc_d, vc_d = O["kext"], O["vext"], O["kc"], O["vc"]
    dq_d = O["dqT"]
    kdext_d, vdext_d, kdc_d, vdc_d = O["kdext"], O["vdext"], O["kdc"], O["vdc"]
    vcv_d, ab_d, vcvc_d, gcg_d = O["vcvext"], O["abT"], O["vcvcext"], O["gcgT"]
    hl_d = [O[n] for n in ("hlocF", "pcF", "hlocB", "pcB")]
    hcs_d, g_d = O["hcsum"], O["gT"]
    wmask_d, natab_d = I.wmask, I.natab
    wbr_d, wout_d = I.wbr, I.wout
    wg_d, wu_d, wd_d = I.wg2, I.wu2, I.wd2
    x3_d, x2_d = I.xout, I.x2_d
    if True:
        tiles = LAT_TILES + ([CTX_TILE] if ctx_out else [])
        with contextlib.ExitStack() as s2:
            sb = lambda name, shape, dt: s2.enter_context(nc.sbuf_tensor(uniq(name), shape, dt))
            kext = sb("kext", [128, KEXT], BF16); bkext = Buf("kext")
            vext = sb("vext", [128, KEXT // 128, 128], BF16); bvext = Buf("vext")
            kc = sb("kc", [128, CTX], BF16); bkc = Buf("kc")
            vc = sb("vc", [128, 2, 128], BF16); bvc = Buf("vc")
            kdext = sb("kdext", [128, 2, KDEXT], BF16); bkdext = Buf("kdext")
            vdext = sb("vdext", [128, KDEXT // 128, MW], BF16); bvdext = Buf("vdext")
            kdc = sb("kdc", [128, 2, CTX], BF16); bkdc = Buf("kdc")
            vdc = sb("vdc", [128, 2, MW], BF16); bvdc = Buf("vdc")
            P.dma("sp", kext[:], kext_d[:, :], writes=[bkext], sb=bkext)
            P.dma("sp", vext[:], vext_d.rearrange("(b p) f -> p b f", p=128), writes=[bvext], sb=bvext)
            P.dma("sp", kc[:], kc_d[:, :], writes=[bkc], sb=bkc)
            P.dma("sp", vc[:], vc_d.rearrange("(b p) f -> p b f", p=128), writes=[bvc], sb=bvc)
            P.dma("sp", kdext[:], kdext_d.rearrange("(c p) t -> p c t", p=128), writes=[bkdext], sb=bkdext)
            P.dma("sp", vdext[:], vdext_d.rearrange("(b p) f -> p b f", p=128), writes=[bvdext], sb=bvdext)
            P.dma("sp", kdc[:], kdc_d.rearrange("(c p) t -> p c t", p=128), writes=[bkdc], sb=bkdc)
            P.dma("sp", vdc[:], vdc_d.rearrange("(b p) f -> p b f", p=128), writes=[bvdc], sb=bvdc)
            wbr = sb("wbr", [128, 8, D], BF16); bwbr = WBufs(512)
            wout = sb("wout", [128, 8, D], BF16); bwout = WBufs(512)
            load_weight_groups(K, [(wbr, bwbr, wbr_d, D)])
            load_weight_groups(K, [(wout, bwout, wout_d, D)])
            wm32 = sb("wm32", [128, 4, 128], F32); bwm32 = Buf("wm32")
            wmk = sb("wmk", [128, 4, 128], BF16); bwmk = Buf("wmk")
            P.dma("sp", wm32[:], wmask_d[:, :, :], writes=[bwm32], sb=bwm32)
            P.op("dve", lambda: nc.vector.tensor_copy(wmk[:], wm32[:]), reads=[bwm32], writes=[bwmk])
            etab = sb("etab", [128, NATILES * 128], BF16); betab = Buf("etab")
            with contextlib.ExitStack() as s3:
                tst = s3.enter_context(nc.sbuf_tensor(uniq("tst"), [128, 29, 128], F32)); btst = Buf("tst")
                for v in range(4):
                    P.dma("sp", tst[:], natab_d[v * 29:(v + 1) * 29].rearrange("n k q -> k n q"), writes=[btst], sb=btst)
                    P.op("act", lambda: nc.scalar.activation(etab[:, v * 29 * 128:(v + 1) * 29 * 128], tst[:].rearrange("p c q -> p (c q)"), AF.Exp),
                         reads=[btst], writes=[betab])
                P.barrier()
            esink = sb("esink", [128, 2], F32); besink = Buf("esink")
            P.op("act", lambda: nc.scalar.activation(esink[:], smv(K, "sink", 0, 2), AF.Exp), reads=[K.bsm], writes=[besink])
            carry, bcarry = K.carry, K.bcarry
            qrot = sb("qrot", [128, 2, 512], BF16); bqrot = Buf("qrot")
            qpl = sb("qplb", [128, 2, 512], BF16); bqpl = Buf("qplb")
            dq = sb("dqb", [128, 2, 512], BF16); bdq = Buf("dqb")
            vx = sb("vx", [128, 2, 514], BF16); bvx = Buf("vx")
            ab = sb("abb", [128, 2, 512], BF16); bab = Buf("abb")
            gcg = sb("gcgb", [128, 2, 512], BF16); bgcg = Buf("gcgb")
            hl = sb("hl", [128, 4, 512], F32); bhl = Buf("hl")
            gt = [sb(f"gt{i}", [128, 4, 512], BF16) for i in range(2)]; bgt = [Buf(f"gt{i}") for i in range(2)]
            x1t = sb("x1t", [128, 8, 512], F32); bx1 = Buf("x1t")
            yT = sb("yT", [128, 8, 512], BF16); byT = Buf("yT")
            mT = sb("mT", [128, 8, 512], BF16); bmT = Buf("mT")
            macc = sb("macc", [128, 512], F32); bmacc = Buf("macc")
            mtmp = [sb(f"mtmp{i}", [128, 512], F32) for i in range(2)]; bmtmp = [Buf(f"mtmp{i}") for i in range(2)]
            pex = [sb(f"pex{i}", [128, 9 * 128], BF16) for i in range(3)]; bpex = [Buf(f"pex{i}") for i in range(3)]
            rd = [sb(f"rd{i}", [128, 128], F32) for i in range(3)]; brd = [Buf(f"rd{i}") for i in range(3)]
            ct1 = sb("ct1", [128, 512], F32); bct1 = Buf("ct1")
            ct2 = sb("ct2", [128, 512], F32); bct2 = Buf("ct2")
            ones64 = cmat(K, "ones1")[:, 0:64]
            actr = [0]

            def attn(qT, bq_, qc, qbase, cols, local, ctxk, mask_ops, sink_col, ych, ob):
                chunks = local + ctxk
                n = len(chunks)

                def stage1(i):
                    banks = [(K.ps[2 * i], K.bps[2 * i]), (K.ps[2 * i + 1], K.bps[2 * i + 1])]
                    pe_, bpe = pex[i], bpex[i]
                    for g0 in range(0, n, 4):
                        ps, bps_ = banks[(g0 // 4) % 2]
                        grp = chunks[g0:g0 + 4]
                        for jj, (kap, bk_, qap, bqq, vap, bv_) in enumerate(grp):
                            P.op("pe", lambda: nc.tensor.matmul(ps[:, jj * 128:(jj + 1) * 128], kap, qap, start=True, stop=True),
                                 reads=[bk_, bqq], writes=[bps_])
                        P.op("act", lambda: nc.scalar.activation(pe_[:, g0 * 128:(g0 + len(grp)) * 128], ps[:, 0:len(grp) * 128], AF.Exp),
                             reads=[bps_], writes=[bpe])
                    for (eng, j0, j1, map_, bm_) in mask_ops:
                        if eng == "pool":
                            P.op("pool", lambda: nc.gpsimd.tensor_tensor(pe_[:, j0 * 128:j1 * 128], pe_[:, j0 * 128:j1 * 128], map_, ALU.mult), reads=[bpe, bm_], writes=[bpe])
                        else:
                            P.op("dve", lambda: nc.vector.tensor_tensor(pe_[:, j0 * 128:j1 * 128], pe_[:, j0 * 128:j1 * 128], map_, ALU.mult), reads=[bpe, bm_], writes=[bpe])

                def stage2(i):
                    pe_, bpe = pex[i], bpex[i]
                    psO, bO = K.ps[6], K.bps[6]
                    psD, bD = K.ps[7], K.bps[7]
                    for j, (kap, bk_, qap, bqq, vap, bv_) in enumerate(chunks):
                        P.op("pe", lambda: nc.tensor.matmul(psO[ob:ob + 64, 0:128], vap, pe_[:, j * 128:(j + 1) * 128], start=(j == 0), stop=(j == n - 1)),
                             reads=[bv_, bpe], writes=[bO], sig=(j == n - 1))
                    for j in range(n):
                        P.op("pe", lambda: nc.tensor.matmul(psD[ob:ob + 64, 0:128], ones64, pe_[:, j * 128:(j + 1) * 128], start=(j == 0), stop=(j == n - 1)),
                             reads=[K.bcm, bpe], writes=[bD], sig=(j == n - 1))
                    r_, br_ = rd[i], brd[i]
                    if sink_col is not None:
                        P.op("dve", lambda: nc.vector.tensor_scalar(r_[ob:ob + 64, :], psD[ob:ob + 64, 0:128], sink_col, None, ALU.add),
                             reads=[bD, besink], writes=[br_])
                        P.op("dve", lambda: nc.vector.reciprocal(r_[ob:ob + 64, :], r_[ob:ob + 64, :]), reads=[br_], writes=[br_])
                    else:
                        P.op("dve", lambda: nc.vector.reciprocal(r_[ob:ob + 64, :], psD[ob:ob + 64, 0:128]), reads=[bD], writes=[br_])
                    P.op("dve", lambda: nc.vector.tensor_tensor(yT[ob:ob + 64, ych, cols], psO[ob:ob + 64, 0:128], r_[ob:ob + 64, :], ALU.mult),
                         reads=[bO, br_], writes=[byT])

                items.append((stage1, stage2))

            def run_items():
                for j in range(min(2, len(items))):
                    items[j][0](j % 3)
                for i in range(len(items)):
                    if i + 2 < len(items):
                        items[i + 2][0]((i + 2) % 3)
                    items[i][1](i % 3)
                del items[:]

            items = []
            x1s = x1_d.rearrange("(c p) t -> p c t", p=128)
            x2s = x2_d.rearrange("(c p) t -> p c t", p=128)
            v3 = lambda ap: ap.rearrange("(c p) t -> p c t", p=128)
            g4 = g_d.rearrange("(n k p) t -> p n k t", p=128, k=8)
            gctr = [0]
            for (start, N, col) in tiles:
                lat = start < T
                sl = slice(start, start + N)
                P.dma("sp", x1t[:, :, :N], x1s[:, :, sl], writes=[bx1], sb=bx1)
                P.dma("sp", qpl[:, :, :N], v3(qpl_d)[:, :, sl], writes=[bqpl], sb=bqpl)
                P.dma("sp", dq[:, :, :N], v3(dq_d)[:, :, sl], writes=[bdq], sb=bdq)
                P.dma("sp", ab[:, :, :N], v3(ab_d)[:, :, sl], writes=[bab], sb=bab)
                P.dma("sp", gcg[:, :, :N], v3(gcg_d)[:, :, sl], writes=[bgcg], sb=bgcg)
                if lat:
                    P.dma("sp", qrot[:, :, :N], v3(qrot_d)[:, :, sl], writes=[bqrot], sb=bqrot)
                    P.dma("sp", vx[:, :, :N + 2], v3(vcv_d)[:, :, start:start + N + 2], writes=[bvx], sb=bvx)
                else:
                    P.dma("sp", vx[:, :, :N + 2], v3(vcvc_d)[:, :, 0:N + 2], writes=[bvx], sb=bvx)
                for c in range(2):
                    cw = lambda j: smv(K, "convaw", c * 3 + j)
                    P.op("dve", lambda: nc.vector.tensor_scalar(ct1[:, :N], vx[:, c, 0:N], cw(0), None, ALU.mult), reads=[bvx, K.bsm], writes=[bct1])
                    P.op("dve", lambda: nc.vector.scalar_tensor_tensor(ct1[:, :N], vx[:, c, 1:N + 1], cw(1), ct1[:, :N], ALU.mult, ALU.add),
                         reads=[bvx, K.bsm, bct1], writes=[bct1])
                    P.op("dve", lambda: nc.vector.scalar_tensor_tensor(ct1[:, :N], vx[:, c, 2:N + 2], cw(2), ct1[:, :N], ALU.mult, ALU.add),
                         reads=[bvx, K.bsm, bct1], writes=[bct1])
                    P.op("pool", lambda: nc.gpsimd.tensor_tensor(yT[:, 0 + c, :N], ct1[:, :N], ab[:, c, :N], ALU.mult), reads=[bct1, bab], writes=[byT])
                for c in range(2):
                    if lat:
                        for a_ in range(4):
                            P.dma("sp", hl[:, a_, :N], hl_d[a_][c * 128:(c + 1) * 128, sl], writes=[bhl], sb=bhl)
                    else:
                        P.dma("sp", hl[:, 0, :N], hcs_d[c * 128:(c + 1) * 128, 0:N], writes=[bhl], sb=bhl)
                    if lat:
                        P.op("dve", lambda: nc.vector.scalar_tensor_tensor(ct1[:, :N], hl[:, 1, :N], carry[:, c * 2:c * 2 + 1], hl[:, 0, :N], ALU.mult, ALU.add),
                             reads=[bhl, bcarry], writes=[bct1])
                        P.op("dve", lambda: nc.vector.scalar_tensor_tensor(ct2[:, :N], hl[:, 3, :N], carry[:, c * 2 + 1:c * 2 + 2], hl[:, 2, :N], ALU.mult, ALU.add),
                             reads=[bhl, bcarry], writes=[bct2])
                        P.op("pool", lambda: nc.gpsimd.tensor_tensor(ct1[:, :N], ct1[:, :N], ct2[:, :N], ALU.add), reads=[bct1, bct2], writes=[bct1])
                        P.op("pool", lambda: nc.gpsimd.tensor_tensor(yT[:, 4 + c, :N], ct1[:, :N], gcg[:, c, :N], ALU.mult), reads=[bct1, bgcg], writes=[byT])
                    else:
                        P.op("pool", lambda: nc.gpsimd.tensor_tensor(yT[:, 4 + c, :N], hl[:, 0, :N], gcg[:, c, :N], ALU.mult), reads=[bhl, bgcg], writes=[byT])
                for blk in range(N // 128):
                    cols = slice(blk * 128, (blk + 1) * 128)
                    n_ = (start // 128) + blk
                    for h in range(4):
                        kv = h // 2
                        qc, qb_ = h % 2, kv * 64
                        ob = (h % 2) * 64
                        ctxk = [(kc[qb_:qb_ + 64, j * 128:(j + 1) * 128], bkc, qpl[qb_:qb_ + 64, qc, cols], bqpl,
                                 vc[:, j, kv * 64:(kv + 1) * 64], bvc) for j in range(2)]
                        if lat:
                            local = [(kext[qb_:qb_ + 64, (n_ + j) * 128:(n_ + j + 1) * 128], bkext, qrot[qb_:qb_ + 64, qc, cols], bqrot,
                                      vext[:, n_ + j, kv * 64:(kv + 1) * 64], bvext) for j in range(3)]
                            mp = wmk[:, 2, :] if n_ == 0 else wmk[:, 0, :]
                            mn = wmk[:, 3, :] if n_ == 31 else wmk[:, 1, :]
                            mops = [("pool", 0, 1, mp, bwmk), ("pool", 2, 3, mn, bwmk)]
                        else:
                            local, mops = [], []
                        attn(None, None, qc, qb_, cols, local, ctxk, mops, esink[ob:ob + 64, h // 2:h // 2 + 1], 2 + h // 2, ob)
                    for h in range(4):
                        hc_, hb_ = h // 2, (h % 2) * 64
                        ob = hb_
                        ctxk = [(kdc[hb_:hb_ + 64, hc_, j * 128:(j + 1) * 128], bkdc, dq[hb_:hb_ + 64, hc_, cols], bdq,
                                 vdc[:, j, h * 64:(h + 1) * 64], bvdc) for j in range(2)]
                        if lat:
                            var = 0 if n_ == 0 else 1 if n_ == 1 else 3 if n_ == 30 else 4 if n_ == 31 else 2
                            jl = NAJ[var]
                            local = [(kdext[hb_:hb_ + 64, hc_, (n_ + j) * 128:(n_ + j + 1) * 128], bkdext, dq[hb_:hb_ + 64, hc_, cols], bdq,
                                      vdext[:, n_ + j, h * 64:(h + 1) * 64], bvdext) for j in jl]
                            e0 = NAOFF[var] + h * len(jl)
                            mops = [("dve", 0, len(jl), etab[:, e0 * 128:(e0 + len(jl)) * 128], betab)]
                        else:
                            local, mops = [], []
                        attn(None, None, None, None, cols, local, ctxk, mops, None, 6 + h // 2, ob)
                run_items()
                for dk in range(8):
                    gi_ = gctr[0] % 2
                    gctr[0] += 1
                    P.dma("sp", gt[gi_][:, :, :N], g4[:, :, dk, sl], writes=[bgt[gi_]], sb=bgt[gi_])
                    for n in range(4):
                        ps, bps_ = K.ps[(dk * 4 + n) % 6], K.bps[(dk * 4 + n) % 6]
                        for cc in range(2):
                            P.op("pe", lambda: nc.tensor.matmul(ps[:, :N], wbr[:, n * 2 + cc, dk * 128:(dk + 1) * 128], yT[:, n * 2 + cc, :N],
                                                                start=(cc == 0), stop=(cc == 1)),
                                 reads=[bwbr.at(dk * 128), byT], writes=[bps_], sig=(cc == 1))
                        if n == 0:
                            P.op("dve", lambda: nc.vector.tensor_tensor(macc[:, :N], ps[:, :N], gt[gi_][:, 0, :N], ALU.mult), reads=[bps_, bgt[gi_]], writes=[bmacc])
                        else:
                            mt_, bmt_ = mtmp[n % 2], bmtmp[n % 2]
                            P.op("dve", lambda: nc.vector.tensor_tensor(mt_[:, :N], ps[:, :N], gt[gi_][:, n, :N], ALU.mult), reads=[bps_, bgt[gi_]], writes=[bmt_])
                            if n < 3:
                                P.op("pool", lambda: nc.gpsimd.tensor_tensor(macc[:, :N], macc[:, :N], mt_[:, :N], ALU.add), reads=[bmacc, bmt_], writes=[bmacc])
                            else:
                                P.op("pool", lambda: nc.gpsimd.tensor_tensor(mT[:, dk, :N], macc[:, :N], mt_[:, :N], ALU.add), reads=[bmacc, bmt_], writes=[bmT])
                for dk in range(8):
                    ps, bps_ = K.ps[dk % 6], K.bps[dk % 6]
                    for k in range(8):
                        P.op("pe", lambda: nc.tensor.matmul(ps[:, :N], wout[:, k, dk * 128:(dk + 1) * 128], mT[:, k, :N], start=(k == 0), stop=(k == 7)),
                             reads=[bwout.at(dk * 128), bmT], writes=[bps_], sig=(k == 7))
                    P.op("dve", lambda: nc.vector.scalar_tensor_tensor(x1t[:, dk, :N], ps[:, :N], K.gateT[:, 1, dk, col:col + 1], x1t[:, dk, :N], ALU.mult, ALU.add),
                         reads=[bps_, bx1, K.bgs], writes=[bx1])
                P.dma("sp", x2s[:, :, sl], x1t[:, :, :N], reads=[bx1], sb=bx1)
            P.barrier()
        emit_ffn_phase(K, x2_d, x3_d, tiles, 2, wg_d, wu_d, wd_d)


def window_masks(q):
    j = np.arange(128)[:, None]
    i = np.arange(128)[None, :]
    tp = (j >= i).astype(np.float32)
    tn = (j <= i).astype(np.float32)
    z = np.zeros((128, 128), np.float32)
    return np.ascontiguousarray(np.stack([tp, tn, tp if q > 0 else z, tn if q < 3 else z], 1))


def na_index(gm, jl):
    rows_total = SEQ // GRID_W
    kk = np.arange(128)
    qq = np.arange(128)
    qrow = 2 * gm + qq // 64
    qcol = qq % 64
    n = len(jl)
    dr = np.zeros((n, 128, 128), np.int64)
    dc = np.zeros((n, 128, 128), np.int64)
    valid = np.zeros((n, 128, 128), bool)
    rstart = np.clip(qrow - 4, 0, rows_total - 8)
    cstart = np.clip(qcol - 8, 0, GRID_W - 16)
    for i, j in enumerate(jl):
        krow = (2 * gm - 4 + 2 * j + kk // 64)[:, None]
        kcol = (kk % 64)[:, None]
        v = (krow >= rstart[None, :]) & (krow < rstart[None, :] + 8) & (kcol >= cstart[None, :]) & (kcol < cstart[None, :] + 16)
        valid[i] = v
        dr[i] = np.clip(krow - qrow[None, :] + 7, 0, 14)
        dc[i] = np.clip(kcol - qcol[None, :], -15, 15) + 15
    return dr, dc, valid


def na_tables(rel_bias, q):
    out = np.empty((NATILES, 128, 128), np.float32)
    for v, m in enumerate((0, 1, 15, 30, 31)):
        gm = q * 32 + m
        jl = NAJ[v]
        dr, dc, valid = na_index(gm, jl)
        for h in range(4):
            o = NAOFF[v] + h * len(jl)
            out[o:o + len(jl)] = np.where(valid, rel_bias[h][dr, dc], np.float32(-30000.0))
    return out


XP = 1282
GROUPS = [[0, 1, 2, 3], [4, 5, 6, 7]]


def emit_exchange(K, I):
    nc, P, O = K.nc, K.P, I.O
    pack, packall, lpack, lall = I.pack.ap(), I.packall.ap(), I.lpack.ap(), I.lall.ap()
    bpk, bpa, blp, bla = Buf("pack"), Buf("packall"), Buf("lpack"), Buf("lall")
    v3 = lambda ap: ap.rearrange("(c p) t -> p c t", p=128)
    tv = lambda ap: ap.rearrange("(b p) f -> p b f", p=128)
    w3 = lambda ap, w: ap.rearrange("p (a w) -> p a w", w=w)
    kext, vext, kdext, vdext, vcv = O["kext"], O["vext"], O["kdext"], O["vdext"], O["vcvext"]
    for side in range(2):
        o = side * XP
        if side == 0:
            srcs = [kext[:, T:T + 128], vext[T:T + 128, :], v3(kdext)[:, :, T:T + 256], tv(vdext[T:T + 256, :]), v3(vcv)[:, :, T:T + 1]]
        else:
            srcs = [kext[:, 128:256], vext[128:256, :], v3(kdext)[:, :, 256:512], tv(vdext[256:512, :]), v3(vcv)[:, :, 1:2]]
        dsts = [pack[:, o:o + 128], pack[:, o + 128:o + 256], w3(pack[:, o + 256:o + 768], 256), w3(pack[:, o + 768:o + 1280], 256),
                w3(pack[:, o + 1280:o + 1282], 1)]
        for d_, s_ in zip(dsts, srcs):
            P.dma("sp", d_, s_, writes=[bpk], sb=bpk, allow_slow_non_contiguous=True)
    P.collective(I.pack.ap().opt(), I.packall.ap().opt(), GROUPS, reads=[bpk], writes=[bpa])
    with contextlib.ExitStack() as s2:
        sb = lambda name, shape, dt: s2.enter_context(nc.sbuf_tensor(uniq(name), shape, dt))
        cand = sb("cand", [128, 4, 2 * XP], BF16); bcand = Buf("cand")
        hl = sb("halo_l", [128, XP], BF16); bhl = Buf("halo_l")
        hr = sb("halo_r", [128, XP], BF16); bhr = Buf("halo_r")
        P.dma("sp", cand[:], packall.rearrange("(j p) x -> p j x", p=128), reads=[bpa], writes=[bcand], sb=bcand)
        for (dst, bdst, c0, sel) in ((hl, bhl, 0, "lsel"), (hr, bhr, XP, "rsel")):
            P.op("dve", lambda: nc.vector.tensor_scalar(dst[:], cand[:, 0, c0:c0 + XP], smv(K, sel, 0), None, ALU.mult),
                 reads=[bcand, K.bsm], writes=[bdst])
            for j in range(1, 4):
                P.op("dve", lambda: nc.vector.scalar_tensor_tensor(dst[:], cand[:, j, c0:c0 + XP], smv(K, sel, j), dst[:], ALU.mult, ALU.add),
                     reads=[bcand, K.bsm, bdst], writes=[bdst])
        for (src, bsrc, kc, vr, dk, dv, cv) in ((hl, bhl, slice(0, 128), slice(0, 128), slice(0, 256), slice(0, 256), slice(0, 1)),
                                              (hr, bhr, slice(128 + T, 256 + T), slice(128 + T, 256 + T), slice(256 + T, 512 + T),
                                               slice(256 + T, 512 + T), slice(T + 1, T + 2))):
            P.dma("sp", kext[:, kc], src[:, 0:128], reads=[bsrc], sb=bsrc)
            P.dma("sp", vext[vr, :], src[:, 128:256], reads=[bsrc], sb=bsrc)
            P.dma("sp", v3(kdext)[:, :, dk], w3(src[:, 256:768], 256), reads=[bsrc], sb=bsrc)
            P.dma("sp", tv(vdext[dv, :]), w3(src[:, 768:1280], 256), reads=[bsrc], sb=bsrc)
            P.dma("sp", v3(vcv)[:, :, cv], w3(src[:, 1280:1282], 1), reads=[bsrc], sb=bsrc, allow_slow_non_contiguous=True)
        P.dma("sp", lpack[:, :], K.lsum[:], reads=[K.bls], writes=[blp], sb=K.bls)
        P.collective(I.lpack.ap().opt(), I.lall.ap().opt(), GROUPS, reads=[blp], writes=[bla])
        lc = sb("lcand", [128, 4, 12], F32); blc = Buf("lcand")
        tt_ = sb("ltmp", [128, 2], F32); btt = Buf("ltmp")
        P.dma("sp", lc[:], lall.rearrange("(j p) x -> p j x", p=128), reads=[bla], writes=[blc], sb=blc)
        carry, bcarry = K.carry, K.bcarry
        for c in range(2):
            for d in range(2):
                cc = carry[:, c * 2 + d:c * 2 + d + 1]
                P.op("dve", lambda: nc.vector.tensor_copy(cc, K.lsum[:, c * 6 + 4 + d:c * 6 + 5 + d]), reads=[K.bls], writes=[bcarry])
                order = [0, 1, 2, 3] if d == 0 else [3, 2, 1, 0]
                sel = "fsel" if d == 0 else "bsel"
                for j in order:
                    a_ = lc[:, j, c * 6 + 2 * d:c * 6 + 2 * d + 1]
                    h_ = lc[:, j, c * 6 + 2 * d + 1:c * 6 + 2 * d + 2]
                    P.op("dve", lambda: nc.vector.scalar_tensor_tensor(tt_[:, 0:1], cc, a_, h_, ALU.mult, ALU.add),
                         reads=[bcarry, blc], writes=[btt])
                    P.op("dve", lambda: nc.vector.tensor_tensor(tt_[:, 1:2], tt_[:, 0:1], cc, ALU.subtract), reads=[btt, bcarry], writes=[btt])
                    P.op("dve", lambda: nc.vector.scalar_tensor_tensor(cc, tt_[:, 1:2], smv(K, sel, j), cc, ALU.mult, ALU.add),
                         reads=[btt, K.bsm, bcarry], writes=[bcarry])
        P.barrier()


def emit_xhalo(K, I):
    nc, P = K.nc, K.P
    xin1 = I.xin1
    xp, xall = I.xpack.ap(), I.xall.ap()
    bxp, bxa = Buf("xpack"), Buf("xall")
    v3 = lambda ap: ap.rearrange("(c p) t -> p c t", p=128)
    w3 = lambda ap, w: ap.rearrange("p (a w) -> p a w", w=w)
    P.dma("sp", w3(xp[:, 0:16], 2), v3(xin1)[:, :, T - 2:T], writes=[bxp], sb=bxp)
    P.dma("sp", w3(xp[:, 16:24], 1), v3(xin1)[:, :, 0:1], writes=[bxp], sb=bxp, allow_slow_non_contiguous=True)
    P.collective(I.xpack.ap().opt(), I.xall.ap().opt(), GROUPS, reads=[bxp], writes=[bxa])
    with contextlib.ExitStack() as s2:
        sb = lambda name, shape, dt: s2.enter_context(nc.sbuf_tensor(uniq(name), shape, dt))
        xc = sb("xcand", [128, 4, 24], F32); bxc = Buf("xcand")
        hx = sb("xhalo", [128, 32], F32); bhx = Buf("xhalo")
        P.dma("sp", xc[:], xall.rearrange("(j p) x -> p j x", p=128), reads=[bxa], writes=[bxc], sb=bxc)
        P.op("dve", lambda: nc.vector.memset(hx[:], 0.0), writes=[bhx])
        for (c0, n, o, sel) in ((0, 16, 0, "lsel"), (16, 8, 16, "rsel")):
            for j in range(4):
                P.op("dve", lambda: nc.vector.scalar_tensor_tensor(hx[:, o:o + n], xc[:, j, c0:c0 + n], smv(K, sel, j), hx[:, o:o + n], ALU.mult, ALU.add),
                     reads=[bxc, K.bsm, bhx], writes=[bhx])
        P.dma("sp", v3(xin1)[:, :, TT:TT + 2], w3(hx[:, 0:16], 2), reads=[bhx], sb=bhx)
        P.dma("sp", v3(xin1)[:, :, TT + 2:TT + 3], w3(hx[:, 16:24], 1), reads=[bhx], sb=bhx, allow_slow_non_contiguous=True)
        P.dma("sp", v3(xin1)[:, :, TT + 3:TT + 4], w3(hx[:, 24:32], 1), reads=[bhx], sb=bhx, allow_slow_non_contiguous=True)
        P.barrier()


INTERNALS = [("abT", [MW, TT], BF16), ("qrotT", [MW, T], BF16), ("qplT", [MW, TT], BF16),
             ("kext", [128, KEXT], BF16), ("vext", [KEXT, 128], BF16), ("kc", [128, CTX], BF16), ("vc", [CTX, 128], BF16),
             ("gcgT", [MW, TT], BF16), ("dqT", [MW, TT], BF16),
             ("kdext", [MW, KDEXT], BF16), ("vdext", [KDEXT, MW], BF16), ("kdc", [MW, CTX], BF16), ("vdc", [CTX, MW], BF16),
             ("vcvext", [MW, T + 2], BF16), ("vcvcext", [MW, CTX + 2], BF16), ("gT", [4 * D, TT], BF16),
             ("hlocF", [MW, T], F32), ("pcF", [MW, T], F32), ("hlocB", [MW, T], F32), ("pcB", [MW, T], F32),
             ("hcsum", [MW, CTX], F32), ("lsum", [128, 12], F32)]


def build_fused():
    nc = bass.Bass("TRN2", target_bir_lowering=False, dynamic_dma_scratch_size=4096)
    din = lambda n, sh, dt=F32: nc.dram_tensor(n, sh, dt, kind="ExternalInput").ap()
    dint = lambda n, sh, dt=F32: nc.dram_tensor(n, sh, dt, kind="Internal").ap()
    I = Ctx()
    xT_d = din("xT", [D, TX])
    I.cos_d = din("cosT", [128, T]); I.sin_d = din("sinT", [128, T])
    sm_d = [din(f"sm{l}", [128, NSM]) for l in range(2)]
    I.wmask = din("wmask", [128, 4, 128])
    natab_d = din("natab", [2, NATILES, 128, 128])
    wmod_d = din("w_mod", [2, D, 9 * D])
    wg_d = din("wg", [2, 2, D, FF]); wu_d = din("wu", [2, 2, D, FF]); wd_d = din("wd", [2, 2, FF, D])
    win_d = din("w_in", [2, D, INW])
    wbr_d = din("w_branch", [2, D, D]); wout_d = din("w_out", [2, D, D])
    out_d = nc.dram_tensor("outT", [D, T], F32, kind="ExternalOutput").ap()
    I.O = {n: dint(n, sh, dt) for n, sh, dt in INTERNALS}
    I.x1full = dint("x1full", [D, TX]); I.h2_d = dint("h2T", [D, TX], BF16)
    I.x2_d = dint("x2T", [D, TT]); I.xin1 = dint("xin1", [D, TX])
    I.pack = nc.dram_tensor("pack", [128, 2 * XP], BF16); I.packall = nc.dram_tensor("packall", [512, 2 * XP], BF16)
    I.lpack = nc.dram_tensor("lpack", [128, 12], F32); I.lall = nc.dram_tensor("lall", [512, 12], F32)
    I.xpack = nc.dram_tensor("xpack", [128, 24], F32); I.xall = nc.dram_tensor("xall", [512, 24], F32)

    with contextlib.ExitStack() as st:
        K = setup_common(nc, st)
        P = K.P
        K.lsum = K.sbuf("lsum_sb", [128, 12], F32); K.bls = Buf("lsum")
        K.carry = K.sbuf("carry", [128, 4], F32); K.bcarry = Buf("carry")
        zt = K.sbuf("zeros_sb", [128, 2, 1], BF16); bzt = Buf("zeros")
        P.op("dve", lambda: nc.vector.memset(zt[:], 0.0), writes=[bzt])
        vcc = I.O["vcvcext"].rearrange("(c p) t -> p c t", p=128)
        P.dma("sp", vcc[:, :, 0:1], zt[:], reads=[bzt], sb=bzt, allow_slow_non_contiguous=True)
        P.dma("sp", vcc[:, :, CTX + 1:CTX + 2], zt[:], reads=[bzt], sb=bzt, allow_slow_non_contiguous=True)
        for l in range(2):
            load_sm(K, sm_d[l][:, :])
            emit_mod(K, wmod_d[l])
            I.xin = xT_d if l == 0 else I.xin1
            I.wg1, I.wu1, I.wd1, I.win = wg_d[l, 0], wu_d[l, 0], wd_d[l, 0], win_d[l]
            I.wg2, I.wu2, I.wd2 = wg_d[l, 1], wu_d[l, 1], wd_d[l, 1]
            I.wbr, I.wout, I.natab = wbr_d[l], wout_d[l], natab_d[l]
            I.xout = I.xin1 if l == 0 else out_d
            emit_A(K, I)
            emit_exchange(K, I)
            emit_B(K, I, ctx_out=(l == 0))
            if l == 0:
                emit_xhalo(K, I)
        P.finish()
    return nc


_NC = []


def kernel(**inp):
    inp = {k: np.asarray(v) for k, v in inp.items()}
    if not _NC:
        _NC.append(build_fused())
    nc = _NC[0]
    x = np.asarray(inp["x"], np.float32)
    shared = {
        "w_mod": np.ascontiguousarray(inp["w_mod"], np.float32),
        "wg": np.ascontiguousarray(inp["ffn_w_gate"], np.float32), "wu": np.ascontiguousarray(inp["ffn_w_up"], np.float32),
        "wd": np.ascontiguousarray(inp["ffn_w_down"], np.float32), "w_in": np.ascontiguousarray(inp["w_in"], np.float32),
        "w_branch": np.ascontiguousarray(np.asarray(inp["w_branch"], np.float32).reshape(2, D, D)),
        "w_out": np.ascontiguousarray(inp["w_out"], np.float32),
    }
    in_maps = []
    for core in range(NC):
        b, q = core_bq(core)
        xs = x[b]
        lo, hi = q * T, (q + 1) * T
        halo = np.zeros((4, D), np.float32)
        if q > 0:
            halo[0] = xs[lo - 2]; halo[1] = xs[lo - 1]
        if q < 3:
            halo[2] = xs[hi]
        xT = np.ascontiguousarray(np.concatenate([xs[lo:hi], np.asarray(inp["ctx"][b], np.float32), halo], 0).T)
        cs, sn = rope_tables(q)
        m = {"xT": xT, "cosT": cs, "sinT": sn, "sm0": build_sm(inp, 0, b, q), "sm1": build_sm(inp, 1, b, q),
             "wmask": window_masks(q),
             "natab": np.stack([na_tables(np.asarray(inp["na_rel_bias"][l], np.float32), q) for l in range(2)], 0)}
        m.update(shared)
        in_maps.append(m)
    res = run_bass_kernel_spmd(nc, in_maps, core_ids=list(range(NC)))
    out = np.empty_like(x)
    for core in range(NC):
        b, q = core_bq(core)
        out[b, q * T:(q + 1) * T] = np.asarray(res.results[core]["outT"]).T
    return out
```

```python
import contextlib
import numpy as np
import ml_dtypes
import concourse.bass as bass
import concourse.mybir as mybir
from concourse.bass_utils import run_bass_kernel_spmd

F32 = mybir.dt.float32
BF16 = mybir.dt.bfloat16
AF = mybir.ActivationFunctionType
ALU = mybir.AluOpType
NPBF = ml_dtypes.bfloat16

D = 1024; KD = 8; FF = 2816; KF = 22; T = 4096; CTX = 256; TT = T + CTX; TX = TT + 4
MW = 256; INW = 6656; NC = 8; SEQ = 16384; GRID_W = 64
EPS = 1e-6


class Buf:
    __slots__ = ("name", "lw", "rd", "dsem")

    def __init__(self, name):
        self.name = name
        self.lw = None
        self.rd = []
        self.dsem = None


class Prog:
    ENG = ("pe", "dve", "act", "pool", "sp")

    def __init__(self, nc, stack):
        self.nc = nc
        self.stack = stack
        self.eng = {"pe": nc.tensor, "dve": nc.vector, "act": nc.scalar,
                    "pool": nc.gpsimd, "sp": nc.sync}
        self.sems = {}
        self.cnt = {}
        self.waited = {e: {} for e in self.ENG}
        self.esem = {}
        self.nsem = 0
        self.free_dsems = []
        self.dbufs = []
        self.allbufs = []
        self.pool_fifo = []
        for e in self.ENG:
            key = f"E{e}"
            self._alloc_sem(key)
            self.esem[e] = key
        self.n_inst = 0

    def _alloc_sem(self, key):
        h = self.stack.enter_context(self.nc.semaphore(f"s{self.nsem}_{key}"))
        self.nsem += 1
        self.sems[key] = h
        self.cnt[key] = 0
        return key

    def _need(self, e, deps, key, val):
        if self.waited[e].get(key, 0) >= val:
            return
        deps[key] = max(deps.get(key, 0), val)

    def _emit_waits(self, e, deps):
        eng = self.eng[e]
        for key, val in deps.items():
            eng.wait_ge(self.sems[key], val)
            self.waited[e][key] = val
            self.n_inst += 1

    def _collect(self, e, reads, writes, dma=False):
        deps = {}
        for b in reads:
            if b.lw is not None:
                self._need(e, deps, *b.lw)
        for b in writes:
            if b.lw is not None:
                if not (dma and b.lw[0] == b.dsem and not b.rd):
                    self._need(e, deps, *b.lw)
            for r in b.rd:
                self._need(e, deps, *r)
        return deps

    def _stamp(self, stamp, reads, writes):
        for b in list(reads) + list(writes):
            if b.lw is None and not b.rd:
                self.allbufs.append(b)
        for b in reads:
            b.rd.append(stamp)
            if len(b.rd) > 64:
                m = {}
                for k, v in b.rd:
                    m[k] = max(m.get(k, 0), v)
                b.rd = list(m.items())
        for b in writes:
            b.lw = stamp
            b.rd = []

    def op(self, e, fn, reads=(), writes=(), sig=True):
        deps = self._collect(e, reads, writes)
        if e == "pe":
            deps.pop(self.esem["pe"], None)
        self._emit_waits(e, deps)
        ins = fn()
        self.n_inst += 1
        key = self.esem[e]
        if sig:
            self.cnt[key] += 1
            ins.then_inc(self.sems[key], 1)
            stamp = (key, self.cnt[key])
        else:
            stamp = (key, self.cnt[key] + 1)
        self._stamp(stamp, reads, writes)
        return ins

    def dma(self, q, out_ap, in_ap, reads=(), writes=(), sb=None, ndesc=0, **kw):
        if q == "pool" and ndesc:
            while self.pool_fifo and sum(n for _, _, n in self.pool_fifo) + ndesc > 3000:
                key_, val_, _ = self.pool_fifo.pop(0)
                d_ = {}
                self._need("pool", d_, key_, val_)
                self._emit_waits("pool", d_)
        if sb.dsem is None:
            if self.free_dsems:
                sb.dsem = self.free_dsems.pop()
            else:
                sb.dsem = self._alloc_sem(f"D{self.nsem}")
            self.dbufs.append(sb)
        deps = self._collect(q, reads, writes, dma=True)
        for b in list(reads) + list(writes):
            if b.dsem is not None and b.dsem in deps:
                deps[b.dsem] = self.cnt[b.dsem]
        self._emit_waits(q, deps)
        ins = self.eng[q].dma_start(out=out_ap, in_=in_ap, **kw)
        self.n_inst += 1
        key = sb.dsem
        self.cnt[key] += 16
        ins.then_inc(self.sems[key], 16)
        self._stamp((key, self.cnt[key]), reads, writes)
        if q == "pool" and ndesc:
            self.pool_fifo.append((key, self.cnt[key], ndesc))
        return ins

    def barrier(self):
        for e in self.ENG:
            deps = {}
            for key, v in self.cnt.items():
                if v > 0 and key != self.esem[e]:
                    self._need(e, deps, key, v)
            if e != "pe" and self.cnt[self.esem[e]] > 0:
                self._need(e, deps, self.esem[e], self.cnt[self.esem[e]])
            self._emit_waits(e, deps)
        for b in self.allbufs:
            b.lw = None
            b.rd = []
        self.allbufs = []
        for b in self.dbufs:
            self.free_dsems.append(b.dsem)
            b.dsem = None
        self.dbufs = []
        self.pool_fifo = []
        for e in self.ENG:
            if self.cnt[self.esem[e]] > 20000:
                key = f"E{e}{self.nsem}"
                self._alloc_sem(key)
                self.esem[e] = key

    def collective(self, in_ap, out_ap, groups, reads=(), writes=()):
        deps = self._collect("pool", reads, writes)
        self._emit_waits("pool", deps)
        ins = self.nc.gpsimd.collective_compute("AllGather", ALU.bypass, replica_groups=groups, ins=[in_ap], outs=[out_ap])
        key = self._alloc_sem(f"C{self.nsem}")
        ins.then_inc(self.sems[key])
        self.cnt[key] = 1
        self.n_inst += 1
        self._stamp((key, 1), reads, writes)

    def finish(self):
        deps = {}
        for key, v in self.cnt.items():
            if v > 0:
                self._need("sp", deps, key, v)
        self._emit_waits("sp", deps)


SM_FIELDS = [("bmod", 72), ("normg", 24), ("bgate", 32), ("convaw", 6), ("lcw", 8), ("lcb", 2),
             ("lba", 4), ("lbx", 4), ("llam", 4), ("gqb", 1), ("gkb", 1), ("gqd", 1), ("gkd", 1),
             ("sink", 2), ("cvec", 16), ("hvalid", 4), ("eps", 1),
             ("onesD", 128), ("ones64", 128), ("rmat", 128), ("ones1", 128),
             ("wa00", 128), ("wa01", 128), ("wa10", 128), ("wa11", 128),
             ("wx00", 128), ("wx01", 128), ("wx10", 128), ("wx11", 128),
             ("lsel", 4), ("rsel", 4), ("fsel", 4), ("bsel", 4)]
SM_OFF = {}
_o = 0
for _n, _w in SM_FIELDS:
    SM_OFF[_n] = (_o, _w)
    _o += _w
NSM = _o


def chunked(v, nch):
    return np.ascontiguousarray(np.asarray(v, np.float32).reshape(nch, 128).T)


def blockdiag(a, b):
    m = np.zeros((128, 128), np.float32)
    m[:64, :64] = a
    m[64:, 64:] = b
    return m


def build_sm(inp, l, b, q):
    sm = np.zeros((128, NSM), np.float32)

    def put(name, arr):
        o, w = SM_OFF[name]
        sm[:, o:o + w] = np.asarray(arr, np.float32).reshape(128, w)

    put("bmod", chunked(inp["b_mod"][l], 72))
    put("normg", np.concatenate([chunked(inp["norm_g"][l, s], 8) for s in range(3)], 1))
    put("bgate", chunked(inp["b_gate"][l], 32))
    ca = inp["conv_a_w"][l]
    put("convaw", np.stack([ca[j, c * 128:(c + 1) * 128] for c in range(2) for j in range(3)], 1))
    lw = inp["lru_conv_w"][l]
    put("lcw", np.stack([lw[j, c * 128:(c + 1) * 128] for c in range(2) for j in range(4)], 1))
    put("lcb", chunked(inp["lru_conv_b"][l], 2))
    for nm, key in (("lba", "lru_b_a"), ("lbx", "lru_b_x"), ("llam", "lru_lam")):
        v = inp[key][l]
        put(nm, np.stack([v[d, c * 128:(c + 1) * 128] for d in range(2) for c in range(2)], 1))
    g = inp["qk_norm_g"][l]
    for i, nm in enumerate(("gqb", "gkb", "gqd", "gkd")):
        put(nm, np.tile(g[i], 2)[:, None])
    sk = inp["attn_sink"][l]
    put("sink", np.stack([np.repeat(sk[2 * c:2 * c + 2], 64) for c in range(2)], 1))
    cv = np.stack([inp["c"][b], inp["c_ctx"]], 1)
    put("cvec", cv.reshape(8, 128, 2).transpose(1, 0, 2).reshape(128, 16))
    hv = np.array([1.0 if q > 0 else 0.0, 1.0 if q > 0 else 0.0, 1.0 if q < 3 else 0.0, 0.0], np.float32)
    put("hvalid", np.tile(hv[None, :], (128, 1)))
    put("eps", np.full((128, 1), EPS, np.float32))
    put("onesD", np.full((128, 128), 1.0 / D, np.float32))
    put("ones64", blockdiag(np.full((64, 64), 1.0 / 64, np.float32), np.full((64, 64), 1.0 / 64, np.float32)))
    r = np.zeros((128, 128), np.float32)
    for m in range(128):
        if (m % 32) < 16:
            r[m + 16, m] = -1.0
        else:
            r[m - 16, m] = 1.0
    put("rmat", r)
    put("ones1", np.ones((128, 128), np.float32))
    put("lsel", np.tile(np.array([1.0 if j == q - 1 else 0.0 for j in range(4)], np.float32)[None], (128, 1)))
    put("rsel", np.tile(np.array([1.0 if j == q + 1 else 0.0 for j in range(4)], np.float32)[None], (128, 1)))
    put("fsel", np.tile(np.array([1.0 if j < q else 0.0 for j in range(4)], np.float32)[None], (128, 1)))
    put("bsel", np.tile(np.array([1.0 if j > q else 0.0 for j in range(4)], np.float32)[None], (128, 1)))
    for d in range(2):
        for c in range(2):
            put(f"wa{d}{c}", blockdiag(inp["lru_w_a"][l, d, 2 * c], inp["lru_w_a"][l, d, 2 * c + 1]))
            put(f"wx{d}{c}", blockdiag(inp["lru_w_x"][l, d, 2 * c], inp["lru_w_x"][l, d, 2 * c + 1]))
    return sm


def rope_tables(q):
    pos = q * T + np.arange(T)
    rows = (pos // GRID_W).astype(np.float32)
    cols = (pos % GRID_W).astype(np.float32)
    nf = 16
    inv_freq = (10000.0 ** (-np.arange(nf, dtype=np.float32) / nf)).astype(np.float32)
    p = np.arange(128)
    d = p % 64
    j = d % 16
    use_col = (d // 32) == 1
    posm = np.where(use_col[:, None], cols[None, :], rows[None, :]).astype(np.float32)
    ang = (posm * inv_freq[j][:, None]).astype(np.float32)
    return np.cos(ang).astype(np.float32), np.sin(ang).astype(np.float32)


class Ctx:
    pass


_UNIQ = [0]


def uniq(name):
    _UNIQ[0] += 1
    return f"{name}_sb{_UNIQ[0]}"


def smv(K, name, i=0, n=1):
    o, w = SM_OFF[name]
    return K.sm[:, o + i:o + i + n]


def setup_common(nc, st):
    K = Ctx()
    K.nc = nc
    K.st = st
    st.enter_context(nc.Block())
    K.P = Prog(nc, st)
    K.sbuf = lambda name, shape, dt: st.enter_context(nc.sbuf_tensor(uniq(name), shape, dt))
    K.ps = [st.enter_context(nc.psum_tensor(f"ps{i}", [128, 512], F32)) for i in range(8)]
    K.bps = [Buf(f"ps{i}") for i in range(8)]
    K.sm = K.sbuf("sm_sb", [128, NSM], F32)
    K.bsm = Buf("sm")
    names = ["onesD", "ones64", "rmat", "ones1"] + [f"w{a}{d}{c}" for a in "ax" for d in range(2) for c in range(2)]
    K.cm = K.sbuf("cm_sb", [128, len(names), 128], BF16)
    K.bcm = Buf("cm")
    K.cmi = {n: i for i, n in enumerate(names)}
    return K


def load_sm(K, sm_d):
    nc, P = K.nc, K.P
    P.dma("sp", K.sm[:], sm_d, writes=[K.bsm], sb=K.bsm)
    for n, i in K.cmi.items():
        o, w = SM_OFF[n]
        P.op("dve", lambda: nc.vector.tensor_copy(K.cm[:, i, :], K.sm[:, o:o + 128]), reads=[K.bsm], writes=[K.bcm])


def cmat(K, name):
    return K.cm[:, K.cmi[name], :]


def emit_mod(K, w_mod_d):
    nc, P = K.nc, K.P
    K.modT = K.sbuf("modT", [128, 72, 2], F32)
    K.bmod = Buf("modT")
    K.gsT = K.sbuf("gsT", [128, 3, 8, 2], F32)
    K.gateT = K.sbuf("gateT", [128, 3, 8, 2], F32)
    K.bgs = Buf("gs")
    with contextlib.ExitStack() as s2:
        wm = [s2.enter_context(nc.sbuf_tensor(uniq(f"wm{i}"), [128, 8, 1024], F32)) for i in range(2)]
        bwm = [Buf(f"wm{i}") for i in range(2)]
        sc = s2.enter_context(nc.sbuf_tensor(uniq("silu_c"), [128, 16], F32))
        bsc = Buf("silu_c")
        P.op("act", lambda: nc.scalar.activation(sc[:], smv(K, "cvec", 0, 16), AF.Silu), reads=[K.bsm], writes=[bsc])
        psm = K.ps[7]
        bpsm = K.bps[7]
        for i in range(9):
            s = i % 2
            for k in range(8):
                P.dma("sp", wm[s][:, k, :], w_mod_d[k * 128:(k + 1) * 128, i * 1024:(i + 1) * 1024],
                      writes=[bwm[s]], sb=bwm[s])
            for dk in range(8):
                cidx = i * 8 + dk
                for k in range(8):
                    P.op("pe", lambda: nc.tensor.matmul(psm[:, cidx * 2:cidx * 2 + 2], wm[s][:, k, dk * 128:(dk + 1) * 128],
                                                        sc[:, 2 * k:2 * k + 2], start=(k == 0), stop=(k == 7)),
                         reads=[bwm[s], bsc], writes=[bpsm], sig=(k == 7))
        psv = psm[:, 0:144].rearrange("p (c j) -> p c j", j=2)
        for j in range(2):
            P.op("dve", lambda: nc.vector.tensor_tensor(K.modT[:, :, j], psv[:, :, j], smv(K, "bmod", 0, 72), ALU.add),
                 reads=[bpsm, K.bsm], writes=[K.bmod])
        for s in range(3):
            for j in range(2):
                P.op("dve", lambda: nc.vector.tensor_scalar(K.gsT[:, s, :, j], K.modT[:, (3 * s + 1) * 8:(3 * s + 2) * 8, j],
                                                            1.0, None, ALU.add), reads=[K.bmod], writes=[K.bgs])
                P.op("dve", lambda: nc.vector.tensor_tensor(K.gsT[:, s, :, j], K.gsT[:, s, :, j],
                                                            smv(K, "normg", s * 8, 8), ALU.mult),
                     reads=[K.bgs, K.bsm], writes=[K.bgs])
                P.op("dve", lambda: nc.vector.tensor_scalar(K.gateT[:, s, :, j], K.modT[:, (3 * s + 2) * 8:(3 * s + 3) * 8, j],
                                                            (1.0 if s == 1 else 0.5), None, ALU.mult),
                     reads=[K.bmod], writes=[K.bgs])
        P.barrier()


def shift_col(K, s, k, j):
    return K.modT[:, 3 * s * 8 + k, j:j + 1]


def emit_adaln(K, xt, bxt, N, s, col, hT, bh, sq, bsq, rstd, brs, pst, bpst):
    nc, P = K.nc, K.P
    P.op("act", lambda: nc.scalar.activation(sq[:, :, :N], xt[:, :, :N], AF.Square), reads=[bxt], writes=[bsq])
    for k in range(8):
        P.op("pe", lambda: nc.tensor.matmul(pst[:, :N], cmat(K, "onesD"), sq[:, k, :N], start=(k == 0), stop=(k == 7)),
             reads=[bsq, K.bcm], writes=[bpst], sig=(k == 7))
    P.op("act", lambda: nc.scalar.activation(rstd[:, :N], pst[:, :N], AF.Sqrt, bias=smv(K, "eps"), scale=1.0),
         reads=[bpst, K.bsm], writes=[brs])
    P.op("dve", lambda: nc.vector.reciprocal(rstd[:, :N], rstd[:, :N]), reads=[brs], writes=[brs])
    for k in range(8):
        P.op("dve", lambda: nc.vector.scalar_tensor_tensor(K.tmp32[:, :N], xt[:, k, :N], K.gsT[:, s, k, col:col + 1],
                                                           rstd[:, :N], ALU.mult, ALU.mult),
             reads=[bxt, brs, K.bgs], writes=[K.btmp32])
        P.op("act", lambda: nc.scalar.activation(hT[:, k, :N], K.tmp32[:, :N], AF.Identity,
                                                 bias=shift_col(K, s, k, col), scale=1.0),
             reads=[K.btmp32, K.bmod], writes=[bh])


class WBufs:
    def __init__(self, gw):
        self.gw = gw
        self.bufs = []

    def at(self, col):
        return self.bufs[col // self.gw]


def load_weight_cast(K, dst, wb, src, g0, g1):
    b = Buf("wgrp")
    wb.bufs.append(b)
    K.P.dma("pool", dst[:, :, g0:g1], src.rearrange("(k p) f -> p k f", p=128)[:, :, g0:g1], writes=[b], sb=b,
            ndesc=128 * dst.shape[1])


def load_weight_groups(K, specs):
    pos = [0] * len(specs)
    done = False
    while not done:
        done = True
        for i, (dst, wb, src, width) in enumerate(specs):
            if pos[i] < width:
                g1 = min(width, pos[i] + wb.gw)
                load_weight_cast(K, dst, wb, src, pos[i], g1)
                pos[i] = g1
                done = False


def emit_ffn_phase(K, x_src, x_dst, tiles, s, wg_d, wu_d, wd_d, h_dst=None, s_next=None):
    nc, P = K.nc, K.P
    post = h_dst is not None
    with contextlib.ExitStack() as s2:
        sb = lambda name, shape, dt: s2.enter_context(nc.sbuf_tensor(uniq(name), shape, dt))
        wg = sb("wg", [128, KD, FF], BF16); bwg = WBufs(256)
        wu = sb("wu", [128, KD, FF], BF16); bwu = WBufs(256)
        wd = sb("wd", [128, KF, D], BF16); bwd = WBufs(256)
        load_weight_groups(K, [(wg, bwg, wg_d, FF), (wu, bwu, wu_d, FF)])
        load_weight_groups(K, [(wd, bwd, wd_d, D)])
        xts = [sb(f"xt{i}", [128, 8, 512], F32) for i in range(2)]; bxts = [Buf(f"xt{i}") for i in range(2)]
        hT = sb("hT", [128, 8, 512], BF16); bh = Buf("hT")
        act = sb("actT", [128, KF, 512], BF16); bact = Buf("actT")
        rstd = sb("rstd", [128, 512], F32); brs = Buf("rstd")
        K.tmp32 = sb("tmp32", [128, 512], F32); K.btmp32 = Buf("tmp32")
        sg, bsg = K.tmp32, K.btmp32
        if post:
            h2o = sb("h2o", [128, 8, 512], BF16); bh2o = Buf("h2o")
            rstd2 = sb("rstd2", [128, 512], F32); brs2 = Buf("rstd2")
        xs = x_src.rearrange("(c p) t -> p c t", p=128)
        xd = x_dst.rearrange("(c p) t -> p c t", p=128)
        hd = h_dst.rearrange("(c p) t -> p c t", p=128) if post else None
        n = len(tiles)
        gi = [0]

        def load(i):
            start, N, col = tiles[i]
            P.dma("sp", xts[i % 2][:, :, :N], xs[:, :, start:start + N], writes=[bxts[i % 2]], sb=bxts[i % 2])

        def adaln(i):
            start, N, col = tiles[i]
            emit_adaln(K, xts[i % 2], bxts[i % 2], N, s, col, hT, bh, hT, bh, rstd, brs, K.ps[6], K.bps[6])

        def gateup(i):
            start, N, col = tiles[i]
            for f in range(KF):
                pg, bpg = K.ps[gi[0] % 2], K.bps[gi[0] % 2]
                pu, bpu = K.ps[2 + gi[0] % 2], K.bps[2 + gi[0] % 2]
                gi[0] += 1
                for k in range(8):
                    P.op("pe", lambda: nc.tensor.matmul(pg[:, :N], wg[:, k, f * 128:(f + 1) * 128], hT[:, k, :N],
                                                        start=(k == 0), stop=(k == 7)),
                         reads=[bwg.at(f * 128), bh], writes=[bpg], sig=(k == 7))
                for k in range(8):
                    P.op("pe", lambda: nc.tensor.matmul(pu[:, :N], wu[:, k, f * 128:(f + 1) * 128], hT[:, k, :N],
                                                        start=(k == 0), stop=(k == 7)),
                         reads=[bwu.at(f * 128), bh], writes=[bpu], sig=(k == 7))
                P.op("act", lambda: nc.scalar.activation(sg[:, :N], pg[:, :N], AF.Silu), reads=[bpg], writes=[bsg])
                P.op("dve", lambda: nc.vector.tensor_tensor(act[:, f, :N], pu[:, :N], sg[:, :N], ALU.mult),
                     reads=[bpu, bsg], writes=[bact])

        def down(i):
            start, N, col = tiles[i]
            xt, bxt = xts[i % 2], bxts[i % 2]
            for dk in range(8):
                py, bpy = K.ps[4 + dk % 2], K.bps[4 + dk % 2]
                for f in range(KF):
                    P.op("pe", lambda: nc.tensor.matmul(py[:, :N], wd[:, f, dk * 128:(dk + 1) * 128], act[:, f, :N],
                                                        start=(f == 0), stop=(f == KF - 1)),
                         reads=[bwd.at(dk * 128), bact], writes=[bpy], sig=(f == KF - 1))
                P.op("dve", lambda: nc.vector.scalar_tensor_tensor(xt[:, dk, :N], py[:, :N], K.gateT[:, s, dk, col:col + 1],
                                                                   xt[:, dk, :N], ALU.mult, ALU.add),
                     reads=[bpy, bxt, K.bgs], writes=[bxt])
            P.dma("sp", xd[:, :, start:start + N], xt[:, :, :N], reads=[bxt], sb=bxt)

        def postadaln(i):
            start, N, col = tiles[i]
            emit_adaln(K, xts[i % 2], bxts[i % 2], N, s_next, col, h2o, bh2o, h2o, bh2o, rstd2, brs2, K.ps[7], K.bps[7])
            P.dma("sp", hd[:, :, start:start + N], h2o[:, :, :N], reads=[bh2o], sb=bh2o)

        load(0)
        if n > 1:
            load(1)
        adaln(0)
        for i in range(n):
            gateup(i)
            if i >= 1:
                if post:
                    postadaln(i - 1)
                if i + 1 < n:
                    load(i + 1)
            if i + 1 < n:
                adaln(i + 1)
            down(i)
        if post:
            postadaln(n - 1)
        P.barrier()


LAT_TILES = [(i * 512, 512, 0) for i in range(8)]
CTX_TILE = (T, CTX, 1)
HALO_TILE = (TT, 4, 0)


def emit_A(K, I):
    nc, P = K.nc, K.P
    xT_d, cos_d, sin_d = I.xin, I.cos_d, I.sin_d
    wg_d, wu_d, wd_d, win_d = I.wg1, I.wu1, I.wd1, I.win
    O = I.O
    x1full, h2_d = I.x1full, I.h2_d
    with contextlib.ExitStack() as sA:
        K_sbuf = lambda name, shape, dt: sA.enter_context(nc.sbuf_tensor(uniq(name), shape, dt))
        tiles = LAT_TILES + [CTX_TILE, HALO_TILE]
        emit_ffn_phase(K, xT_d, x1full, tiles, 0, wg_d, wu_d, wd_d, h_dst=h2_d, s_next=1)

        cxf = K_sbuf("cxf", [128, 2, T + 4], F32); bcxf = Buf("cxf")
        cxc = K_sbuf("cxc", [128, 2, CTX + 4], F32); bcxc = Buf("cxc")
        P.op("pool", lambda: nc.gpsimd.memset(cxc[:], 0.0), writes=[bcxc])
        gq = K_sbuf("gq", [128, 2], F32); bgq = Buf("gq")
        P.op("dve", lambda: nc.vector.tensor_scalar(gq[:, 0:1], smv(K, "gqb"), 0.125, None, ALU.mult), reads=[K.bsm], writes=[bgq])
        P.op("dve", lambda: nc.vector.tensor_scalar(gq[:, 1:2], smv(K, "gqd"), 0.125, None, ALU.mult), reads=[K.bsm], writes=[bgq])

        with contextlib.ExitStack() as s2:
            sb = lambda name, shape, dt: s2.enter_context(nc.sbuf_tensor(uniq(name), shape, dt))
            win = sb("win", [128, KD, INW], BF16); bwin = WBufs(512)
            load_weight_groups(K, [(win, bwin, win_d, INW)])
            h2t = sb("h2t", [128, 8, 512], BF16); bh2 = Buf("h2t")
            cst = sb("cost", [128, 512], F32); snt = sb("sint", [128, 512], F32); bcs = Buf("cossin")
            axs = sb("axs", [128, 2, 512], F32); baxs = Buf("axs")
            abo = sb("abo", [128, 2, 512], BF16); babo = Buf("abo")
            vo = sb("vo", [128, 2, 512], BF16); bvo = Buf("vo")
            qpl = sb("qpl", [128, 2, 512], BF16); bqpl = Buf("qpl")
            qro = sb("qro", [128, 2, 512], BF16); bqro = Buf("qro")
            ko = sb("ko", [128, 512], BF16); bko = Buf("ko")
            ktmp = sb("ktmp", [128, 512], BF16); bktmp = Buf("ktmp")
            vtok = sb("vtok", [128, 4, 384], BF16); bvtok = Buf("vtok")
            gcg = sb("gcg", [128, 2, 512], BF16); bgcg = Buf("gcg")
            dqo = sb("dqo", [128, 2, 512], BF16); bdqo = Buf("dqo")
            dko = sb("dko", [128, 2, 512], BF16); bdko = Buf("dko")
            gto = sb("gto", [128, 8, 512], BF16); bgto = Buf("gto")
            sqb = [sb(f"sqb{i}", [128, 512], BF16) for i in range(2)]; bsqb = [Buf(f"sqb{i}") for i in range(2)]
            rs = [sb(f"rs{i}", [128, 512], F32) for i in range(2)]; brs = [Buf(f"rs{i}") for i in range(2)]
            qh = [sb(f"qh{i}", [128, 512], F32) for i in range(2)]; bqh = [Buf(f"qh{i}") for i in range(2)]
            t1 = [sb(f"t1{i}", [128, 512], F32) for i in range(2)]; bt1 = [Buf(f"t1{i}") for i in range(2)]
            t2 = [sb(f"t2{i}", [128, 512], F32) for i in range(2)]; bt2 = [Buf(f"t2{i}") for i in range(2)]
            bout = Buf("aout")
            h2s = h2_d.rearrange("(c p) t -> p c t", p=128)
            v3 = lambda ap: ap.rearrange("(c p) t -> p c t", p=128)
            ov = {n: v3(O[n]) for n in ("abT", "gcgT", "dqT", "gT")}
            pctr = [0]

            def proj(c, N):
                i = (0, 1, 2, 6, 7)[pctr[0] % 5]
                pctr[0] += 1
                ps, bps = K.ps[i], K.bps[i]
                for k in range(8):
                    P.op("pe", lambda: nc.tensor.matmul(ps[:, :N], win[:, k, c * 128:(c + 1) * 128], h2t[:, k, :N],
                                                        start=(k == 0), stop=(k == 7)),
                         reads=[bwin.at(c * 128), bh2], writes=[bps], sig=(k == 7))
                step_pending()
                return ps, bps

            pending = []

            def step_pending():
                for g in list(pending):
                    try:
                        next(g)
                    except StopIteration:
                        pending.remove(g)

            def drain():
                while pending:
                    step_pending()

            qn = [0]

            def qknorm(ps, bps, N, gcol, bg, out_plain, bop, rope_out=None, bro=None, after=None):
                i = qn[0] % 2
                qn[0] += 1
                sqb_, bsqb_ = sqb[i], bsqb[i]
                rs_, brs_ = rs[i], brs[i]
                qh_, bqh_ = qh[i], bqh[i]
                t1_, bt1_ = t1[i], bt1[i]
                t2_, bt2_ = t2[i], bt2[i]
                pst, bpst = K.ps[3], K.bps[3]
                P.op("act", lambda: nc.scalar.activation(sqb_[:, :N], ps[:, :N], AF.Square), reads=[bps], writes=[bsqb_])
                yield
                P.op("pe", lambda: nc.tensor.matmul(pst[:, :N], cmat(K, "ones64"), sqb_[:, :N], start=True, stop=True),
                     reads=[bsqb_, K.bcm], writes=[bpst])
                P.op("act", lambda: nc.scalar.activation(rs_[:, :N], pst[:, :N], AF.Sqrt, bias=smv(K, "eps"), scale=1.0),
                     reads=[bpst, K.bsm], writes=[brs_])
                P.op("dve", lambda: nc.vector.reciprocal(rs_[:, :N], rs_[:, :N]), reads=[brs_], writes=[brs_])
                P.op("dve", lambda: nc.vector.scalar_tensor_tensor(qh_[:, :N], ps[:, :N], gcol, rs_[:, :N], ALU.mult, ALU.mult),
                     reads=[bps, brs_, bg], writes=[bqh_])
                P.op("act", lambda: nc.scalar.copy(out_plain[:, :N], qh_[:, :N]), reads=[bqh_], writes=[bop])
                if rope_out is not None:
                    yield
                    psr, bpsr = K.ps[4], K.bps[4]
                    P.op("pe", lambda: nc.tensor.matmul(psr[:, :N], cmat(K, "rmat"), out_plain[:, :N], start=True, stop=True),
                         reads=[bop, K.bcm], writes=[bpsr])
                    P.op("pool", lambda: nc.gpsimd.tensor_tensor(t1_[:, :N], qh_[:, :N], cst[:, :N], ALU.mult),
                         reads=[bqh_, bcs], writes=[bt1_])
                    P.op("dve", lambda: nc.vector.tensor_tensor(t2_[:, :N], psr[:, :N], snt[:, :N], ALU.mult),
                         reads=[bpsr, bcs], writes=[bt2_])
                    P.op("pool", lambda: nc.gpsimd.tensor_tensor(rope_out[:, :N], t1_[:, :N], t2_[:, :N], ALU.add),
                         reads=[bt1_, bt2_], writes=[bro])
                if after is not None:
                    after()

            def qk(*a, **kw):
                g = qknorm(*a, **kw)
                next(g)
                pending.append(g)

            for (start, N, col) in tiles:
                kind = "lat" if start < T else ("ctx" if start == T else "halo")
                P.dma("sp", h2t[:, :, :N], h2s[:, :, start:start + N], writes=[bh2], sb=bh2)
                if kind == "halo":
                    for c in range(2):
                        ps, bps = proj(10 + c, N)
                        P.op("dve", lambda: nc.vector.tensor_tensor(cxf[:, c, 0:2], ps[:, 0:2], smv(K, "hvalid", 0, 2), ALU.mult),
                             reads=[bps, K.bsm], writes=[bcxf])
                        P.op("dve", lambda: nc.vector.tensor_tensor(cxf[:, c, T + 2:T + 4], ps[:, 2:4], smv(K, "hvalid", 2, 2), ALU.mult),
                             reads=[bps, K.bsm], writes=[bcxf])
                    continue
                lat = kind == "lat"
                if lat:
                    P.dma("sp", cst[:, :N], cos_d[:, start:start + N], writes=[bcs], sb=bcs)
                    P.dma("sp", snt[:, :N], sin_d[:, start:start + N], writes=[bcs], sb=bcs)
                sl = slice(start, start + N)
                for c in range(2):
                    ps, bps = proj(0 + c, N)
                    P.op("act", lambda: nc.scalar.copy(axs[:, c, :N], ps[:, :N]), reads=[bps], writes=[baxs])
                for c in range(2):
                    ps, bps = proj(2 + c, N)
                    P.op("act", lambda: nc.scalar.copy(abo[:, c, :N], ps[:, :N]), reads=[bps], writes=[babo])
                for c in range(2):
                    ps, bps = proj(4 + c, N)
                    P.op("dve", lambda: nc.vector.tensor_tensor(vo[:, c, :N], ps[:, :N], axs[:, c, :N], ALU.mult),
                         reads=[bps, baxs], writes=[bvo])
                P.dma("sp", ov["abT"][:, :, sl], abo[:, :, :N], reads=[babo], sb=babo)
                if lat:
                    P.dma("sp", v3(O["vcvext"])[:, :, 1 + start:1 + start + N], vo[:, :, :N], reads=[bvo], sb=bvo)
                else:
                    P.dma("sp", v3(O["vcvcext"])[:, :, 1:1 + N], vo[:, :, :N], reads=[bvo], sb=bvo)
                def store_q(c, sl=sl, N=N, lat=lat):
                    for hh in range(2):
                        r0 = hh * 128 + c * 64
                        P.dma("sp", O["qplT"][r0:r0 + 64, sl], qpl[hh * 64:(hh + 1) * 64, c, :N], reads=[bqpl], sb=bqpl)
                        if lat:
                            P.dma("sp", O["qrotT"][r0:r0 + 64, sl], qro[hh * 64:(hh + 1) * 64, c, :N], reads=[bqro], sb=bqro)
                for c in range(2):
                    ps, bps = proj(6 + c, N)
                    qk(ps, bps, N, gq[:, 0:1], bgq, qpl[:, c, :], bqpl, qro[:, c, :] if lat else None, bqro,
                       after=(lambda c=c, f=store_q: f(c)))
                ps, bps = proj(8, N)
                if lat:
                    qk(ps, bps, N, smv(K, "gkb"), K.bsm, ktmp, bktmp, ko, bko,
                       after=(lambda start=start, N=N: P.dma("sp", O["kext"][:, 128 + start:128 + start + N], ko[:, :N], reads=[bko], sb=bko)))
                else:
                    qk(ps, bps, N, smv(K, "gkb"), K.bsm, ko, bko,
                       after=(lambda N=N: P.dma("sp", O["kc"][:, 0:N], ko[:, :N], reads=[bko], sb=bko)))
                psv, bpsv = K.ps[5], K.bps[5]
                nb = N // 128
                for tb in range(nb):
                    for k in range(8):
                        P.op("pe", lambda: nc.tensor.matmul(psv[:, 0:128], h2t[:, k, tb * 128:(tb + 1) * 128], win[:, k, 9 * 128:10 * 128],
                                                            start=(k == 0), stop=(k == 7)),
                             reads=[bwin.at(9 * 128), bh2], writes=[bpsv], sig=(k == 7))
                    for k in range(8):
                        P.op("pe", lambda: nc.tensor.matmul(psv[:, 128:384], h2t[:, k, tb * 128:(tb + 1) * 128], win[:, k, 18 * 128:20 * 128],
                                                            start=(k == 0), stop=(k == 7)),
                             reads=[bwin.at(18 * 128), bh2], writes=[bpsv], sig=(k == 7))
                    P.op("act", lambda: nc.scalar.copy(vtok[:, tb, :], psv[:, 0:384]), reads=[bpsv], writes=[bvtok])
                tv = lambda ap: ap.rearrange("(b p) f -> p b f", p=128)
                if lat:
                    P.dma("sp", tv(O["vext"][128 + start:128 + start + N, :]), vtok[:, 0:nb, 0:128], reads=[bvtok], sb=bvtok)
                    P.dma("sp", tv(O["vdext"][256 + start:256 + start + N, :]), vtok[:, 0:nb, 128:384], reads=[bvtok], sb=bvtok)
                else:
                    P.dma("sp", tv(O["vc"][0:N, :]), vtok[:, 0:nb, 0:128], reads=[bvtok], sb=bvtok)
                    P.dma("sp", tv(O["vdc"][0:N, :]), vtok[:, 0:nb, 128:384], reads=[bvtok], sb=bvtok)
                for c in range(2):
                    ps, bps = proj(10 + c, N)
                    if lat:
                        P.op("act", lambda: nc.scalar.copy(cxf[:, c, 2 + start:2 + start + N], ps[:, :N]), reads=[bps], writes=[bcxf])
                    else:
                        P.op("act", lambda: nc.scalar.copy(cxc[:, c, 2:2 + N], ps[:, :N]), reads=[bps], writes=[bcxc])
                for c in range(2):
                    ps, bps = proj(12 + c, N)
                    P.op("act", lambda: nc.scalar.activation(gcg[:, c, :N], ps[:, :N], AF.Gelu_apprx_tanh), reads=[bps], writes=[bgcg])
                P.dma("sp", ov["gcgT"][:, :, sl], gcg[:, :, :N], reads=[bgcg], sb=bgcg)
                for c in range(2):
                    ps, bps = proj(14 + c, N)
                    qk(ps, bps, N, gq[:, 1:2], bgq, dqo[:, c, :], bdqo,
                       after=(lambda c=c, sl=sl, N=N: P.dma("sp", ov["dqT"][:, c, sl], dqo[:, c, :N], reads=[bdqo], sb=bdqo)))
                for c in range(2):
                    ps, bps = proj(16 + c, N)
                    if lat:
                        aft = (lambda c=c, start=start, N=N: P.dma("sp", v3(O["kdext"])[:, c, 256 + start:256 + start + N], dko[:, c, :N], reads=[bdko], sb=bdko))
                    else:
                        aft = (lambda c=c, N=N: P.dma("sp", v3(O["kdc"])[:, c, 0:N], dko[:, c, :N], reads=[bdko], sb=bdko))
                    qk(ps, bps, N, smv(K, "gkd"), K.bsm, dko[:, c, :], bdko, after=aft)
                for c in range(32):
                    ps, bps = proj(20 + c, N)
                    P.op("act", lambda: nc.scalar.activation(gto[:, c % 8, :N], ps[:, :N], AF.Sigmoid, bias=smv(K, "bgate", c), scale=1.0),
                         reads=[bps, K.bsm], writes=[bgto])
                    if c % 8 == 7:
                        c0 = c - 7
                        P.dma("sp", ov["gT"][:, c0:c0 + 8, sl], gto[:, :, :N], reads=[bgto], sb=bgto)
                drain()
            P.barrier()

        with contextlib.ExitStack() as s2:
            sb = lambda name, shape, dt: s2.enter_context(nc.sbuf_tensor(uniq(name), shape, dt))
            u = sb("lru_u", [128, T], F32); bu = Buf("lru_u")
            ub = sb("lru_ub", [128, T], BF16); bub = Buf("lru_ub")
            ra = sb("lru_a", [128, T], F32); bra = Buf("lru_a")
            ib = sb("lru_b", [128, T], F32); bib = Buf("lru_b")
            tm = sb("lru_t", [128, T], F32); btm = Buf("lru_t")
            hh = sb("lru_h", [128, T], F32); bhh = Buf("lru_h")
            pc = sb("lru_p", [128, T], F32); bpc = Buf("lru_p")
            zz = sb("lru_z", [128, T], F32); bzz = Buf("lru_z")
            hs = sb("lru_hs", [128, CTX], F32); bhs = Buf("lru_hs")
            nl8 = sb("nl8", [128, 4], F32); bnl = Buf("nl8")
            lsum, bls = K.lsum, K.bls
            blo = Buf("lruout")
            P.op("pool", lambda: nc.gpsimd.memset(zz[:], 0.0), writes=[bzz])
            P.op("act", lambda: nc.scalar.activation(nl8[:], smv(K, "llam", 0, 4), AF.Exp, scale=-1.0), reads=[K.bsm], writes=[bnl])
            P.op("act", lambda: nc.scalar.activation(nl8[:], nl8[:], AF.Ln, bias=1.0, scale=1.0), reads=[bnl], writes=[bnl])
            P.op("dve", lambda: nc.vector.tensor_scalar(nl8[:], nl8[:], -8.0, None, ALU.mult), reads=[bnl], writes=[bnl])
            pi = [0]

            def lru_chain(src, bsrc, Tn, c, is_ctx):
                lw = lambda j: smv(K, "lcw", c * 4 + j)
                P.op("dve", lambda: nc.vector.tensor_scalar(u[:, :Tn], src[:, c, 0:Tn], lw(0), smv(K, "lcb", c), ALU.mult, ALU.add),
                     reads=[bsrc, K.bsm], writes=[bu])
                for j in range(1, 4):
                    P.op("dve", lambda: nc.vector.scalar_tensor_tensor(u[:, :Tn], src[:, c, j:j + Tn], lw(j), u[:, :Tn], ALU.mult, ALU.add),
                         reads=[bsrc, bu, K.bsm], writes=[bu])
                P.op("act", lambda: nc.scalar.copy(ub[:, :Tn], u[:, :Tn]), reads=[bu], writes=[bub])
                for d in range(2):
                    for t0 in range(0, Tn, 512):
                        n = min(512, Tn - t0)
                        for (wn, bn, dst, bdst) in ((f"wa{d}{c}", "lba", ra, bra), (f"wx{d}{c}", "lbx", ib, bib)):
                            ps, bps = K.ps[pi[0] % 4], K.bps[pi[0] % 4]
                            pi[0] += 1
                            P.op("pe", lambda: nc.tensor.matmul(ps[:, :n], cmat(K, wn), ub[:, t0:t0 + n], start=True, stop=True),
                                 reads=[bub, K.bcm], writes=[bps])
                            P.op("act", lambda: nc.scalar.activation(dst[:, t0:t0 + n], ps[:, :n], AF.Sigmoid,
                                                                     bias=smv(K, bn, d * 2 + c), scale=1.0),
                                 reads=[bps, K.bsm], writes=[bdst])
                    P.op("act", lambda: nc.scalar.activation(ra[:, :Tn], ra[:, :Tn], AF.Exp, scale=nl8[:, d * 2 + c:d * 2 + c + 1]),
                         reads=[bra, bnl], writes=[bra])
                    P.op("dve", lambda: nc.vector.tensor_tensor(tm[:, :Tn], ra[:, :Tn], ra[:, :Tn], ALU.mult), reads=[bra], writes=[btm])
                    P.op("act", lambda: nc.scalar.activation(tm[:, :Tn], tm[:, :Tn], AF.Sqrt, bias=1.0000001, scale=-1.0),
                         reads=[btm], writes=[btm])
                    P.op("dve", lambda: nc.vector.tensor_tensor(ib[:, :Tn], ib[:, :Tn], u[:, :Tn], ALU.mult), reads=[bib, bu], writes=[bib])
                    P.op("dve", lambda: nc.vector.tensor_tensor(ib[:, :Tn], ib[:, :Tn], tm[:, :Tn], ALU.mult), reads=[bib, btm], writes=[bib])
                    if d == 0:
                        va, vb, vh, vp, vz = ra[:, 0:Tn], ib[:, 0:Tn], hh[:, 0:Tn], pc[:, 0:Tn], zz[:, 0:Tn]
                        last = Tn - 1
                    else:
                        rv = lambda t_: t_[:, 0:Tn][:, ::-1]
                        va, vb, vh, vp, vz = rv(ra), rv(ib), rv(hh), rv(pc), rv(zz)
                        last = 0
                    P.op("dve", lambda: nc.vector.tensor_tensor_scan(vh, va, vb, 0.0, ALU.mult, ALU.add), reads=[bra, bib], writes=[bhh])
                    if not is_ctx:
                        P.op("dve", lambda: nc.vector.tensor_tensor_scan(vp, va, vz, 1.0, ALU.mult, ALU.add), reads=[bra, bzz], writes=[bpc])
                        hn, pn = ("hlocF", "pcF") if d == 0 else ("hlocB", "pcB")
                        P.dma("sp", O[hn][c * 128:(c + 1) * 128, :], hh[:, :Tn], reads=[bhh], sb=bhh)
                        P.dma("sp", O[pn][c * 128:(c + 1) * 128, :], pc[:, :Tn], reads=[bpc], sb=bpc)
                        P.op("act", lambda: nc.scalar.copy(lsum[:, c * 6 + 2 * d:c * 6 + 2 * d + 1], pc[:, last:last + 1]), reads=[bpc], writes=[bls])
                        P.op("act", lambda: nc.scalar.copy(lsum[:, c * 6 + 2 * d + 1:c * 6 + 2 * d + 2], hh[:, last:last + 1]), reads=[bhh], writes=[bls])
                    else:
                        P.op("act", lambda: nc.scalar.copy(lsum[:, c * 6 + 4 + d:c * 6 + 5 + d], hh[:, last:last + 1]), reads=[bhh], writes=[bls])
                        if d == 0:
                            P.op("pool", lambda: nc.gpsimd.tensor_copy(hs[:, :Tn], hh[:, :Tn]), reads=[bhh], writes=[bhs])
                        else:
                            P.op("pool", lambda: nc.gpsimd.tensor_tensor(hs[:, :Tn], hs[:, :Tn], hh[:, :Tn], ALU.add), reads=[bhh, bhs], writes=[bhs])
                            P.dma("sp", O["hcsum"][c * 128:(c + 1) * 128, :], hs[:, :Tn], reads=[bhs], sb=bhs)

            for c in range(2):
                lru_chain(cxf, bcxf, T, c, False)
                lru_chain(cxc, bcxc, CTX, c, True)
            P.dma("sp", O["lsum"][:, :], lsum[:], reads=[bls], sb=bls)
            P.barrier()


def core_bq(core):
    return core // 4, core % 4


KEXT = T + 256
KDEXT = T + 512
NAJ = {0: list(range(0, 7)), 1: list(range(5)), 2: list(range(5)), 3: list(range(5)), 4: list(range(-2, 5))}
NAOFF = {}
_o = 0
for _v in range(5):
    NAOFF[_v] = _o
    _o += 4 * len(NAJ[_v])
NATILES = _o


def emit_B(K, I, ctx_out):
    nc, P = K.nc, K.P
    O = I.O
    x1_d = I.x1full
    qrot_d, qpl_d = O["qrotT"], O["qplT"]
    kext_d, vext_d, kc_d, vc_d = O["kext"], O["vext"], O["kc"], O["vc"]
    dq_d = O["dqT"]
    kdext_d, vdext_d, kdc_d, vdc_d = O["kdext"], O["vdext"], O["kdc"], O["vdc"]
    vcv_d, ab_d, vcvc_d, gcg_d = O["vcvext"], O["abT"], O["vcvcext"], O["gcgT"]
    hl_d = [O[n] for n in ("hlocF", "pcF", "hlocB", "pcB")]
    hcs_d, g_d = O["hcsum"], O["gT"]
    wmask_d, natab_d = I.wmask, I.natab
    wbr_d, wout_d = I.wbr, I.wout
    wg_d, wu_d, wd_d = I.wg2, I.wu2, I.wd2
    x3_d, x2_d = I.xout, I.x2_d
    if True:
        tiles = LAT_TILES + ([CTX_TILE] if ctx_out else [])
        with contextlib.ExitStack() as s2:
            sb = lambda name, shape, dt: s2.enter_context(nc.sbuf_tensor(uniq(name), shape, dt))
            kext = sb("kext", [128, KEXT], BF16); bkext = Buf("kext")
            vext = sb("vext", [128, KEXT // 128, 128], BF16); bvext = Buf("vext")
            kc = sb("kc", [128, CTX], BF16); bkc = Buf("kc")
            vc = sb("vc", [128, 2, 128], BF16); bvc = Buf("vc")
            kdext = sb("kdext", [128, 2, KDEXT], BF16); bkdext = Buf("kdext")
            vdext = sb("vdext", [128, KDEXT // 128, MW], BF16); bvdext = Buf("vdext")
            kdc = sb("kdc", [128, 2, CTX], BF16); bkdc = Buf("kdc")
            vdc = sb("vdc", [128, 2, MW], BF16); bvdc = Buf("vdc")
            P.dma("sp", kext[:], kext_d[:, :], writes=[bkext], sb=bkext)
            P.dma("sp", vext[:], vext_d.rearrange("(b p) f -> p b f", p=128), writes=[bvext], sb=bvext)
            P.dma("sp", kc[:], kc_d[:, :], writes=[bkc], sb=bkc)
            P.dma("sp", vc[:], vc_d.rearrange("(b p) f -> p b f", p=128), writes=[bvc], sb=bvc)
            P.dma("sp", kdext[:], kdext_d.rearrange("(c p) t -> p c t", p=128), writes=[bkdext], sb=bkdext)
            P.dma("sp", vdext[:], vdext_d.rearrange("(b p) f -> p b f", p=128), writes=[bvdext], sb=bvdext)
            P.dma("sp", kdc[:], kdc_d.rearrange("(c p) t -> p c t", p=128), writes=[bkdc], sb=bkdc)
            P.dma("sp", vdc[:], vdc_d.rearrange("(b p) f -> p b f", p=128), writes=[bvdc], sb=bvdc)
            wbr = sb("wbr", [128, 8, D], BF16); bwbr = WBufs(512)
            wout = sb("wout", [128, 8, D], BF16); bwout = WBufs(512)
            load_weight_groups(K, [(wbr, bwbr, wbr_d, D)])
            load_weight_groups(K, [(wout, bwout, wout_d, D)])
            wm32 = sb("wm32", [128, 4, 128], F32); bwm32 = Buf("wm32")
            wmk = sb("wmk", [128, 4, 128], BF16); bwmk = Buf("wmk")
            P.dma("sp", wm32[:], wmask_d[:, :, :], writes=[bwm32], sb=bwm32)
            P.op("dve", lambda: nc.vector.tensor_copy(wmk[:], wm32[:]), reads=[bwm32], writes=[bwmk])
            etab = sb("etab", [128, NATILES * 128], BF16); betab = Buf("etab")
            with contextlib.ExitStack() as s3:
                tst = s3.enter_context(nc.sbuf_tensor(uniq("tst"), [128, 29, 128], F32)); btst = Buf("tst")
                for v in range(4):
                    P.dma("sp", tst[:], natab_d[v * 29:(v + 1) * 29].rearrange("n k q -> k n q"), writes=[btst], sb=btst)
                    P.op("act", lambda: nc.scalar.activation(etab[:, v * 29 * 128:(v + 1) * 29 * 128], tst[:].rearrange("p c q -> p (c q)"), AF.Exp),
                         reads=[btst], writes=[betab])
                P.barrier()
            esink = sb("esink", [128, 2], F32); besink = Buf("esink")
            P.op("act", lambda: nc.scalar.activation(esink[:], smv(K, "sink", 0, 2), AF.Exp), reads=[K.bsm], writes=[besink])
            carry, bcarry = K.carry, K.bcarry
            qrot = sb("qrot", [128, 2, 512], BF16); bqrot = Buf("qrot")
            qpl = sb("qplb", [128, 2, 512], BF16); bqpl = Buf("qplb")
            dq = sb("dqb", [128, 2, 512], BF16); bdq = Buf("dqb")
            vx = sb("vx", [128, 2, 514], BF16); bvx = Buf("vx")
            ab = sb("abb", [128, 2, 512], BF16); bab = Buf("abb")
            gcg = sb("gcgb", [128, 2, 512], BF16); bgcg = Buf("gcgb")
            hl = sb("hl", [128, 4, 512], F32); bhl = Buf("hl")
            gt = [sb(f"gt{i}", [128, 4, 512], BF16) for i in range(2)]; bgt = [Buf(f"gt{i}") for i in range(2)]
            x1t = sb("x1t", [128, 8, 512], F32); bx1 = Buf("x1t")
            yT = sb("yT", [128, 8, 512], BF16); byT = Buf("yT")
            mT = sb("mT", [128, 8, 512], BF16); bmT = Buf("mT")
            macc = sb("macc", [128, 512], F32); bmacc = Buf("macc")
            mtmp = [sb(f"mtmp{i}", [128, 512], F32) for i in range(2)]; bmtmp = [Buf(f"mtmp{i}") for i in range(2)]
            pex = [sb(f"pex{i}", [128, 9 * 128], BF16) for i in range(3)]; bpex = [Buf(f"pex{i}") for i in range(3)]
            rd = [sb(f"rd{i}", [128, 128], F32) for i in range(3)]; brd = [Buf(f"rd{i}") for i in range(3)]
            ct1 = sb("ct1", [128, 512], F32); bct1 = Buf("ct1")
            ct2 = sb("ct2", [128, 512], F32); bct2 = Buf("ct2")
            ones64 = cmat(K, "ones1")[:, 0:64]
            actr = [0]

            def attn(qT, bq_, qc, qbase, cols, local, ctxk, mask_ops, sink_col, ych, ob):
                chunks = local + ctxk
                n = len(chunks)

                def stage1(i):
                    banks = [(K.ps[i], K.bps[i]), (K.ps[2 + i], K.bps[2 + i])]
                    pe_, bpe = pex[i], bpex[i]
                    for g0 in range(0, n, 4):
                        ps, bps_ = banks[(g0 // 4) % 2]
                        grp = chunks[g0:g0 + 4]
                        for jj, (kap, bk_, qap, bqq, vap, bv_) in enumerate(grp):
                            P.op("pe", lambda: nc.tensor.matmul(ps[:, jj * 128:(jj + 1) * 128], kap, qap, start=True, stop=True),
                                 reads=[bk_, bqq], writes=[bps_])
                        P.op("act", lambda: nc.scalar.activation(pe_[:, g0 * 128:(g0 + len(grp)) * 128], ps[:, 0:len(grp) * 128], AF.Exp),
                             reads=[bps_], writes=[bpe])
                    for (eng, j0, j1, map_, bm_) in mask_ops:
                        if eng == "pool":
                            P.op("pool", lambda: nc.gpsimd.tensor_tensor(pe_[:, j0 * 128:j1 * 128], pe_[:, j0 * 128:j1 * 128], map_, ALU.mult), reads=[bpe, bm_], writes=[bpe])
                        else:
                            P.op("dve", lambda: nc.vector.tensor_tensor(pe_[:, j0 * 128:j1 * 128], pe_[:, j0 * 128:j1 * 128], map_, ALU.mult), reads=[bpe, bm_], writes=[bpe])

                def stage2(i):
                    pe_, bpe = pex[i], bpex[i]
                    psO, bO = K.ps[4 + 2 * i], K.bps[4 + 2 * i]
                    psD, bD = K.ps[5 + 2 * i], K.bps[5 + 2 * i]
                    for j, (kap, bk_, qap, bqq, vap, bv_) in enumerate(chunks):
                        P.op("pe", lambda: nc.tensor.matmul(psO[ob:ob + 64, 0:128], vap, pe_[:, j * 128:(j + 1) * 128], start=(j == 0), stop=(j == n - 1)),
                             reads=[bv_, bpe], writes=[bO], sig=(j == n - 1))
                    for j in range(n):
                        P.op("pe", lambda: nc.tensor.matmul(psD[ob:ob + 64, 0:128], ones64, pe_[:, j * 128:(j + 1) * 128], start=(j == 0), stop=(j == n - 1)),
                             reads=[K.bcm, bpe], writes=[bD], sig=(j == n - 1))
                    r_, br_ = rd[i], brd[i]
                    if sink_col is not None:
                        P.op("dve", lambda: nc.vector.tensor_scalar(r_[ob:ob + 64, :], psD[ob:ob + 64, 0:128], sink_col, None, ALU.add),
                             reads=[bD, besink], writes=[br_])
                        P.op("dve", lambda: nc.vector.reciprocal(r_[ob:ob + 64, :], r_[ob:ob + 64, :]), reads=[br_], writes=[br_])
                    else:
                        P.op("dve", lambda: nc.vector.reciprocal(r_[ob:ob + 64, :], psD[ob:ob + 64, 0:128]), reads=[bD], writes=[br_])
                    P.op("dve", lambda: nc.vector.tensor_tensor(yT[ob:ob + 64, ych, cols], psO[ob:ob + 64, 0:128], r_[ob:ob + 64, :], ALU.mult),
                         reads=[bO, br_], writes=[byT])

                items.append((stage1, stage2))

            def run_items():
                if items:
                    items[0][0](0)
                for i in range(len(items)):
                    if i + 1 < len(items):
                        items[i + 1][0]((i + 1) % 2)
                    items[i][1](i % 2)
                del items[:]

            items = []
            x1s = x1_d.rearrange("(c p) t -> p c t", p=128)
            x2s = x2_d.rearrange("(c p) t -> p c t", p=128)
            v3 = lambda ap: ap.rearrange("(c p) t -> p c t", p=128)
            g4 = g_d.rearrange("(n k p) t -> p n k t", p=128, k=8)
            gctr = [0]
            for (start, N, col) in tiles:
                lat = start < T
                sl = slice(start, start + N)
                P.dma("sp", x1t[:, :, :N], x1s[:, :, sl], writes=[bx1], sb=bx1)
                P.dma("sp", qpl[:, :, :N], v3(qpl_d)[:, :, sl], writes=[bqpl], sb=bqpl)
                P.dma("sp", dq[:, :, :N], v3(dq_d)[:, :, sl], writes=[bdq], sb=bdq)
                P.dma("sp", ab[:, :, :N], v3(ab_d)[:, :, sl], writes=[bab], sb=bab)
                P.dma("sp", gcg[:, :, :N], v3(gcg_d)[:, :, sl], writes=[bgcg], sb=bgcg)
                if lat:
                    P.dma("sp", qrot[:, :, :N], v3(qrot_d)[:, :, sl], writes=[bqrot], sb=bqrot)
                    P.dma("sp", vx[:, :, :N + 2], v3(vcv_d)[:, :, start:start + N + 2], writes=[bvx], sb=bvx)
                else:
                    P.dma("sp", vx[:, :, :N + 2], v3(vcvc_d)[:, :, 0:N + 2], writes=[bvx], sb=bvx)
                for c in range(2):
                    cw = lambda j: smv(K, "convaw", c * 3 + j)
                    P.op("dve", lambda: nc.vector.tensor_scalar(ct1[:, :N], vx[:, c, 0:N], cw(0), None, ALU.mult), reads=[bvx, K.bsm], writes=[bct1])
                    P.op("dve", lambda: nc.vector.scalar_tensor_tensor(ct1[:, :N], vx[:, c, 1:N + 1], cw(1), ct1[:, :N], ALU.mult, ALU.add),
                         reads=[bvx, K.bsm, bct1], writes=[bct1])
                    P.op("dve", lambda: nc.vector.scalar_tensor_tensor(ct1[:, :N], vx[:, c, 2:N + 2], cw(2), ct1[:, :N], ALU.mult, ALU.add),
                         reads=[bvx, K.bsm, bct1], writes=[bct1])
                    P.op("pool", lambda: nc.gpsimd.tensor_tensor(yT[:, 0 + c, :N], ct1[:, :N], ab[:, c, :N], ALU.mult), reads=[bct1, bab], writes=[byT])
                for c in range(2):
                    if lat:
                        for a_ in range(4):
                            P.dma("sp", hl[:, a_, :N], hl_d[a_][c * 128:(c + 1) * 128, sl], writes=[bhl], sb=bhl)
                    else:
                        P.dma("sp", hl[:, 0, :N], hcs_d[c * 128:(c + 1) * 128, 0:N], writes=[bhl], sb=bhl)
                    if lat:
                        P.op("dve", lambda: nc.vector.scalar_tensor_tensor(ct1[:, :N], hl[:, 1, :N], carry[:, c * 2:c * 2 + 1], hl[:, 0, :N], ALU.mult, ALU.add),
                             reads=[bhl, bcarry], writes=[bct1])
                        P.op("dve", lambda: nc.vector.scalar_tensor_tensor(ct2[:, :N], hl[:, 3, :N], carry[:, c * 2 + 1:c * 2 + 2], hl[:, 2, :N], ALU.mult, ALU.add),
                             reads=[bhl, bcarry], writes=[bct2])
                        P.op("pool", lambda: nc.gpsimd.tensor_tensor(ct1[:, :N], ct1[:, :N], ct2[:, :N], ALU.add), reads=[bct1, bct2], writes=[bct1])
                        P.op("pool", lambda: nc.gpsimd.tensor_tensor(yT[:, 4 + c, :N], ct1[:, :N], gcg[:, c, :N], ALU.mult), reads=[bct1, bgcg], writes=[byT])
                    else:
                        P.op("pool", lambda: nc.gpsimd.tensor_tensor(yT[:, 4 + c, :N], hl[:, 0, :N], gcg[:, c, :N], ALU.mult), reads=[bhl, bgcg], writes=[byT])
                for blk in range(N // 128):
                    cols = slice(blk * 128, (blk + 1) * 128)
                    n_ = (start // 128) + blk
                    for h in range(4):
                        kv = h // 2
                        qc, qb_ = h % 2, kv * 64
                        ob = (h % 2) * 64
                        ctxk = [(kc[qb_:qb_ + 64, j * 128:(j + 1) * 128], bkc, qpl[qb_:qb_ + 64, qc, cols], bqpl,
                                 vc[:, j, kv * 64:(kv + 1) * 64], bvc) for j in range(2)]
                        if lat:
                            local = [(kext[qb_:qb_ + 64, (n_ + j) * 128:(n_ + j + 1) * 128], bkext, qrot[qb_:qb_ + 64, qc, cols], bqrot,
                                      vext[:, n_ + j, kv * 64:(kv + 1) * 64], bvext) for j in range(3)]
                            mp = wmk[:, 2, :] if n_ == 0 else wmk[:, 0, :]
                            mn = wmk[:, 3, :] if n_ == 31 else wmk[:, 1, :]
                            mops = [("pool", 0, 1, mp, bwmk), ("pool", 2, 3, mn, bwmk)]
                        else:
                            local, mops = [], []
                        attn(None, None, qc, qb_, cols, local, ctxk, mops, esink[ob:ob + 64, h // 2:h // 2 + 1], 2 + h // 2, ob)
                    for h in range(4):
                        hc_, hb_ = h // 2, (h % 2) * 64
                        ob = hb_
                        ctxk = [(kdc[hb_:hb_ + 64, hc_, j * 128:(j + 1) * 128], bkdc, dq[hb_:hb_ + 64, hc_, cols], bdq,
                                 vdc[:, j, h * 64:(h + 1) * 64], bvdc) for j in range(2)]
                        if lat:
                            var = 0 if n_ == 0 else 1 if n_ == 1 else 3 if n_ == 30 else 4 if n_ == 31 else 2
                            jl = NAJ[var]
                            local = [(kdext[hb_:hb_ + 64, hc_, (n_ + j) * 128:(n_ + j + 1) * 128], bkdext, dq[hb_:hb_ + 64, hc_, cols], bdq,
                                      vdext[:, n_ + j, h * 64:(h + 1) * 64], bvdext) for j in jl]
                            e0 = NAOFF[var] + h * len(jl)
                            mops = [("dve", 0, len(jl), etab[:, e0 * 128:(e0 + len(jl)) * 128], betab)]
                        else:
                            local, mops = [], []
                        attn(None, None, None, None, cols, local, ctxk, mops, None, 6 + h // 2, ob)
                run_items()
                for dk in range(8):
                    gi_ = gctr[0] % 2
                    gctr[0] += 1
                    P.dma("sp", gt[gi_][:, :, :N], g4[:, :, dk, sl], writes=[bgt[gi_]], sb=bgt[gi_])
                    for n in range(4):
                        ps, bps_ = K.ps[(dk * 4 + n) % 4], K.bps[(dk * 4 + n) % 4]
                        for cc in range(2):
                            P.op("pe", lambda: nc.tensor.matmul(ps[:, :N], wbr[:, n * 2 + cc, dk * 128:(dk + 1) * 128], yT[:, n * 2 + cc, :N],
                                                                start=(cc == 0), stop=(cc == 1)),
                                 reads=[bwbr.at(dk * 128), byT], writes=[bps_], sig=(cc == 1))
                        if n == 0:
                            P.op("dve", lambda: nc.vector.tensor_tensor(macc[:, :N], ps[:, :N], gt[gi_][:, 0, :N], ALU.mult), reads=[bps_, bgt[gi_]], writes=[bmacc])
                        else:
                            mt_, bmt_ = mtmp[n % 2], bmtmp[n % 2]
                            P.op("dve", lambda: nc.vector.tensor_tensor(mt_[:, :N], ps[:, :N], gt[gi_][:, n, :N], ALU.mult), reads=[bps_, bgt[gi_]], writes=[bmt_])
                            if n < 3:
                                P.op("pool", lambda: nc.gpsimd.tensor_tensor(macc[:, :N], macc[:, :N], mt_[:, :N], ALU.add), reads=[bmacc, bmt_], writes=[bmacc])
                            else:
                                P.op("pool", lambda: nc.gpsimd.tensor_tensor(mT[:, dk, :N], macc[:, :N], mt_[:, :N], ALU.add), reads=[bmacc, bmt_], writes=[bmT])
                for dk in range(8):
                    ps, bps_ = K.ps[4 + dk % 4], K.bps[4 + dk % 4]
                    for k in range(8):
                        P.op("pe", lambda: nc.tensor.matmul(ps[:, :N], wout[:, k, dk * 128:(dk + 1) * 128], mT[:, k, :N], start=(k == 0), stop=(k == 7)),
                             reads=[bwout.at(dk * 128), bmT], writes=[bps_], sig=(k == 7))
                    P.op("dve", lambda: nc.vector.scalar_tensor_tensor(x1t[:, dk, :N], ps[:, :N], K.gateT[:, 1, dk, col:col + 1], x1t[:, dk, :N], ALU.mult, ALU.add),
                         reads=[bps_, bx1, K.bgs], writes=[bx1])
                P.dma("sp", x2s[:, :, sl], x1t[:, :, :N], reads=[bx1], sb=bx1)
            P.barrier()
        emit_ffn_phase(K, x2_d, x3_d, tiles, 2, wg_d, wu_d, wd_d)


def window_masks(q):
    j = np.arange(128)[:, None]
    i = np.arange(128)[None, :]
    tp = (j >= i).astype(np.float32)
    tn = (j <= i).astype(np.float32)
    z = np.zeros((128, 128), np.float32)
    return np.ascontiguousarray(np.stack([tp, tn, tp if q > 0 else z, tn if q < 3 else z], 1))


def na_index(gm, jl):
    rows_total = SEQ // GRID_W
    kk = np.arange(128)
    qq = np.arange(128)
    qrow = 2 * gm + qq // 64
    qcol = qq % 64
    n = len(jl)
    dr = np.zeros((n, 128, 128), np.int64)
    dc = np.zeros((n, 128, 128), np.int64)
    valid = np.zeros((n, 128, 128), bool)
    rstart = np.clip(qrow - 4, 0, rows_total - 8)
    cstart = np.clip(qcol - 8, 0, GRID_W - 16)
    for i, j in enumerate(jl):
        krow = (2 * gm - 4 + 2 * j + kk // 64)[:, None]
        kcol = (kk % 64)[:, None]
        v = (krow >= rstart[None, :]) & (krow < rstart[None, :] + 8) & (kcol >= cstart[None, :]) & (kcol < cstart[None, :] + 16)
        valid[i] = v
        dr[i] = np.clip(krow - qrow[None, :] + 7, 0, 14)
        dc[i] = np.clip(kcol - qcol[None, :], -15, 15) + 15
    return dr, dc, valid


def na_tables(rel_bias, q):
    out = np.empty((NATILES, 128, 128), np.float32)
    for v, m in enumerate((0, 1, 15, 30, 31)):
        gm = q * 32 + m
        jl = NAJ[v]
        dr, dc, valid = na_index(gm, jl)
        for h in range(4):
            o = NAOFF[v] + h * len(jl)
            out[o:o + len(jl)] = np.where(valid, rel_bias[h][dr, dc], np.float32(-30000.0))
    return out


XP = 1282
GROUPS = [[0, 1, 2, 3], [4, 5, 6, 7]]


def emit_exchange(K, I):
    nc, P, O = K.nc, K.P, I.O
    pack, packall, lpack, lall = I.pack.ap(), I.packall.ap(), I.lpack.ap(), I.lall.ap()
    bpk, bpa, blp, bla = Buf("pack"), Buf("packall"), Buf("lpack"), Buf("lall")
    v3 = lambda ap: ap.rearrange("(c p) t -> p c t", p=128)
    tv = lambda ap: ap.rearrange("(b p) f -> p b f", p=128)
    w3 = lambda ap, w: ap.rearrange("p (a w) -> p a w", w=w)
    kext, vext, kdext, vdext, vcv = O["kext"], O["vext"], O["kdext"], O["vdext"], O["vcvext"]
    for side in range(2):
        o = side * XP
        if side == 0:
            srcs = [kext[:, T:T + 128], vext[T:T + 128, :], v3(kdext)[:, :, T:T + 256], tv(vdext[T:T + 256, :]), v3(vcv)[:, :, T:T + 1]]
        else:
            srcs = [kext[:, 128:256], vext[128:256, :], v3(kdext)[:, :, 256:512], tv(vdext[256:512, :]), v3(vcv)[:, :, 1:2]]
        dsts = [pack[:, o:o + 128], pack[:, o + 128:o + 256], w3(pack[:, o + 256:o + 768], 256), w3(pack[:, o + 768:o + 1280], 256),
                w3(pack[:, o + 1280:o + 1282], 1)]
        for d_, s_ in zip(dsts, srcs):
            P.dma("sp", d_, s_, writes=[bpk], sb=bpk, allow_slow_non_contiguous=True)
    P.collective(I.pack.ap().opt(), I.packall.ap().opt(), GROUPS, reads=[bpk], writes=[bpa])
    with contextlib.ExitStack() as s2:
        sb = lambda name, shape, dt: s2.enter_context(nc.sbuf_tensor(uniq(name), shape, dt))
        cand = sb("cand", [128, 4, 2 * XP], BF16); bcand = Buf("cand")
        hl = sb("halo_l", [128, XP], BF16); bhl = Buf("halo_l")
        hr = sb("halo_r", [128, XP], BF16); bhr = Buf("halo_r")
        P.dma("sp", cand[:], packall.rearrange("(j p) x -> p j x", p=128), reads=[bpa], writes=[bcand], sb=bcand)
        for (dst, bdst, c0, sel) in ((hl, bhl, 0, "lsel"), (hr, bhr, XP, "rsel")):
            P.op("dve", lambda: nc.vector.tensor_scalar(dst[:], cand[:, 0, c0:c0 + XP], smv(K, sel, 0), None, ALU.mult),
                 reads=[bcand, K.bsm], writes=[bdst])
            for j in range(1, 4):
                P.op("dve", lambda: nc.vector.scalar_tensor_tensor(dst[:], cand[:, j, c0:c0 + XP], smv(K, sel, j), dst[:], ALU.mult, ALU.add),
                     reads=[bcand, K.bsm, bdst], writes=[bdst])
        for (src, bsrc, kc, vr, dk, dv, cv) in ((hl, bhl, slice(0, 128), slice(0, 128), slice(0, 256), slice(0, 256), slice(0, 1)),
                                              (hr, bhr, slice(128 + T, 256 + T), slice(128 + T, 256 + T), slice(256 + T, 512 + T),
                                               slice(256 + T, 512 + T), slice(T + 1, T + 2))):
            P.dma("sp", kext[:, kc], src[:, 0:128], reads=[bsrc], sb=bsrc)
            P.dma("sp", vext[vr, :], src[:, 128:256], reads=[bsrc], sb=bsrc)
            P.dma("sp", v3(kdext)[:, :, dk], w3(src[:, 256:768], 256), reads=[bsrc], sb=bsrc)
            P.dma("sp", tv(vdext[dv, :]), w3(src[:, 768:1280], 256), reads=[bsrc], sb=bsrc)
            P.dma("sp", v3(vcv)[:, :, cv], w3(src[:, 1280:1282], 1), reads=[bsrc], sb=bsrc, allow_slow_non_contiguous=True)
        P.dma("sp", lpack[:, :], K.lsum[:], reads=[K.bls], writes=[blp], sb=K.bls)
        P.collective(I.lpack.ap().opt(), I.lall.ap().opt(), GROUPS, reads=[blp], writes=[bla])
        lc = sb("lcand", [128, 4, 12], F32); blc = Buf("lcand")
        tt_ = sb("ltmp", [128, 2], F32); btt = Buf("ltmp")
        P.dma("sp", lc[:], lall.rearrange("(j p) x -> p j x", p=128), reads=[bla], writes=[blc], sb=blc)
        carry, bcarry = K.carry, K.bcarry
        for c in range(2):
            for d in range(2):
                cc = carry[:, c * 2 + d:c * 2 + d + 1]
                P.op("dve", lambda: nc.vector.tensor_copy(cc, K.lsum[:, c * 6 + 4 + d:c * 6 + 5 + d]), reads=[K.bls], writes=[bcarry])
                order = [0, 1, 2, 3] if d == 0 else [3, 2, 1, 0]
                sel = "fsel" if d == 0 else "bsel"
                for j in order:
                    a_ = lc[:, j, c * 6 + 2 * d:c * 6 + 2 * d + 1]
                    h_ = lc[:, j, c * 6 + 2 * d + 1:c * 6 + 2 * d + 2]
                    P.op("dve", lambda: nc.vector.scalar_tensor_tensor(tt_[:, 0:1], cc, a_, h_, ALU.mult, ALU.add),
                         reads=[bcarry, blc], writes=[btt])
                    P.op("dve", lambda: nc.vector.tensor_tensor(tt_[:, 1:2], tt_[:, 0:1], cc, ALU.subtract), reads=[btt, bcarry], writes=[btt])
                    P.op("dve", lambda: nc.vector.scalar_tensor_tensor(cc, tt_[:, 1:2], smv(K, sel, j), cc, ALU.mult, ALU.add),
                         reads=[btt, K.bsm, bcarry], writes=[bcarry])
        P.barrier()


def emit_xhalo(K, I):
    nc, P = K.nc, K.P
    xin1 = I.xin1
    xp, xall = I.xpack.ap(), I.xall.ap()
    bxp, bxa = Buf("xpack"), Buf("xall")
    v3 = lambda ap: ap.rearrange("(c p) t -> p c t", p=128)
    w3 = lambda ap, w: ap.rearrange("p (a w) -> p a w", w=w)
    P.dma("sp", w3(xp[:, 0:16], 2), v3(xin1)[:, :, T - 2:T], writes=[bxp], sb=bxp)
    P.dma("sp", w3(xp[:, 16:24], 1), v3(xin1)[:, :, 0:1], writes=[bxp], sb=bxp, allow_slow_non_contiguous=True)
    P.collective(I.xpack.ap().opt(), I.xall.ap().opt(), GROUPS, reads=[bxp], writes=[bxa])
    with contextlib.ExitStack() as s2:
        sb = lambda name, shape, dt: s2.enter_context(nc.sbuf_tensor(uniq(name), shape, dt))
        xc = sb("xcand", [128, 4, 24], F32); bxc = Buf("xcand")
        hx = sb("xhalo", [128, 32], F32); bhx = Buf("xhalo")
        P.dma("sp", xc[:], xall.rearrange("(j p) x -> p j x", p=128), reads=[bxa], writes=[bxc], sb=bxc)
        P.op("dve", lambda: nc.vector.memset(hx[:], 0.0), writes=[bhx])
        for (c0, n, o, sel) in ((0, 16, 0, "lsel"), (16, 8, 16, "rsel")):
            for j in range(4):
                P.op("dve", lambda: nc.vector.scalar_tensor_tensor(hx[:, o:o + n], xc[:, j, c0:c0 + n], smv(K, sel, j), hx[:, o:o + n], ALU.mult, ALU.add),
                     reads=[bxc, K.bsm, bhx], writes=[bhx])
        P.dma("sp", v3(xin1)[:, :, TT:TT + 2], w3(hx[:, 0:16], 2), reads=[bhx], sb=bhx)
        P.dma("sp", v3(xin1)[:, :, TT + 2:TT + 3], w3(hx[:, 16:24], 1), reads=[bhx], sb=bhx, allow_slow_non_contiguous=True)
        P.dma("sp", v3(xin1)[:, :, TT + 3:TT + 4], w3(hx[:, 24:32], 1), reads=[bhx], sb=bhx, allow_slow_non_contiguous=True)
        P.barrier()


INTERNALS = [("abT", [MW, TT], BF16), ("qrotT", [MW, T], BF16), ("qplT", [MW, TT], BF16),
             ("kext", [128, KEXT], BF16), ("vext", [KEXT, 128], BF16), ("kc", [128, CTX], BF16), ("vc", [CTX, 128], BF16),
             ("gcgT", [MW, TT], BF16), ("dqT", [MW, TT], BF16),
             ("kdext", [MW, KDEXT], BF16), ("vdext", [KDEXT, MW], BF16), ("kdc", [MW, CTX], BF16), ("vdc", [CTX, MW], BF16),
             ("vcvext", [MW, T + 2], BF16), ("vcvcext", [MW, CTX + 2], BF16), ("gT", [4 * D, TT], BF16),
             ("hlocF", [MW, T], F32), ("pcF", [MW, T], F32), ("hlocB", [MW, T], F32), ("pcB", [MW, T], F32),
             ("hcsum", [MW, CTX], F32), ("lsum", [128, 12], F32)]


def build_fused():
    nc = bass.Bass("TRN2", target_bir_lowering=False, dynamic_dma_scratch_size=4096)
    din = lambda n, sh, dt=F32: nc.dram_tensor(n, sh, dt, kind="ExternalInput").ap()
    dint = lambda n, sh, dt=F32: nc.dram_tensor(n, sh, dt, kind="Internal").ap()
    I = Ctx()
    xT_d = din("xT", [D, TX])
    I.cos_d = din("cosT", [128, T]); I.sin_d = din("sinT", [128, T])
    sm_d = [din(f"sm{l}", [128, NSM]) for l in range(2)]
    I.wmask = din("wmask", [128, 4, 128])
    natab_d = din("natab", [2, NATILES, 128, 128])
    wmod_d = din("w_mod", [2, D, 9 * D])
    wg_d = din("wg", [2, 2, D, FF]); wu_d = din("wu", [2, 2, D, FF]); wd_d = din("wd", [2, 2, FF, D])
    win_d = din("w_in", [2, D, INW])
    wbr_d = din("w_branch", [2, D, D]); wout_d = din("w_out", [2, D, D])
    out_d = nc.dram_tensor("outT", [D, T], F32, kind="ExternalOutput").ap()
    I.O = {n: dint(n, sh, dt) for n, sh, dt in INTERNALS}
    I.x1full = dint("x1full", [D, TX]); I.h2_d = dint("h2T", [D, TX], BF16)
    I.x2_d = dint("x2T", [D, TT]); I.xin1 = dint("xin1", [D, TX])
    I.pack = nc.dram_tensor("pack", [128, 2 * XP], BF16); I.packall = nc.dram_tensor("packall", [512, 2 * XP], BF16)
    I.lpack = nc.dram_tensor("lpack", [128, 12], F32); I.lall = nc.dram_tensor("lall", [512, 12], F32)
    I.xpack = nc.dram_tensor("xpack", [128, 24], F32); I.xall = nc.dram_tensor("xall", [512, 24], F32)

    with contextlib.ExitStack() as st:
        K = setup_common(nc, st)
        P = K.P
        K.lsum = K.sbuf("lsum_sb", [128, 12], F32); K.bls = Buf("lsum")
        K.carry = K.sbuf("carry", [128, 4], F32); K.bcarry = Buf("carry")
        zt = K.sbuf("zeros_sb", [128, 2, 1], BF16); bzt = Buf("zeros")
        P.op("dve", lambda: nc.vector.memset(zt[:], 0.0), writes=[bzt])
        vcc = I.O["vcvcext"].rearrange("(c p) t -> p c t", p=128)
        P.dma("sp", vcc[:, :, 0:1], zt[:], reads=[bzt], sb=bzt, allow_slow_non_contiguous=True)
        P.dma("sp", vcc[:, :, CTX + 1:CTX + 2], zt[:], reads=[bzt], sb=bzt, allow_slow_non_contiguous=True)
        for l in range(2):
            load_sm(K, sm_d[l][:, :])
            emit_mod(K, wmod_d[l])
            I.xin = xT_d if l == 0 else I.xin1
            I.wg1, I.wu1, I.wd1, I.win = wg_d[l, 0], wu_d[l, 0], wd_d[l, 0], win_d[l]
            I.wg2, I.wu2, I.wd2 = wg_d[l, 1], wu_d[l, 1], wd_d[l, 1]
            I.wbr, I.wout, I.natab = wbr_d[l], wout_d[l], natab_d[l]
            I.xout = I.xin1 if l == 0 else out_d
            emit_A(K, I)
            emit_exchange(K, I)
            emit_B(K, I, ctx_out=(l == 0))
            if l == 0:
                emit_xhalo(K, I)
        P.finish()
    return nc


_NC = []


def kernel(**inp):
    inp = {k: np.asarray(v) for k, v in inp.items()}
    if not _NC:
        _NC.append(build_fused())
    nc = _NC[0]
    x = np.asarray(inp["x"], np.float32)
    shared = {
        "w_mod": np.ascontiguousarray(inp["w_mod"], np.float32),
        "wg": np.ascontiguousarray(inp["ffn_w_gate"], np.float32), "wu": np.ascontiguousarray(inp["ffn_w_up"], np.float32),
        "wd": np.ascontiguousarray(inp["ffn_w_down"], np.float32), "w_in": np.ascontiguousarray(inp["w_in"], np.float32),
        "w_branch": np.ascontiguousarray(np.asarray(inp["w_branch"], np.float32).reshape(2, D, D)),
        "w_out": np.ascontiguousarray(inp["w_out"], np.float32),
    }
    in_maps = []
    for core in range(NC):
        b, q = core_bq(core)
        xs = x[b]
        lo, hi = q * T, (q + 1) * T
        halo = np.zeros((4, D), np.float32)
        if q > 0:
            halo[0] = xs[lo - 2]; halo[1] = xs[lo - 1]
        if q < 3:
            halo[2] = xs[hi]
        xT = np.ascontiguousarray(np.concatenate([xs[lo:hi], np.asarray(inp["ctx"][b], np.float32), halo], 0).T)
        cs, sn = rope_tables(q)
        m = {"xT": xT, "cosT": cs, "sinT": sn, "sm0": build_sm(inp, 0, b, q), "sm1": build_sm(inp, 1, b, q),
             "wmask": window_masks(q),
             "natab": np.stack([na_tables(np.asarray(inp["na_rel_bias"][l], np.float32), q) for l in range(2)], 0)}
        m.update(shared)
        in_maps.append(m)
    res = run_bass_kernel_spmd(nc, in_maps, core_ids=list(range(NC)))
    out = np.empty_like(x)
    for core in range(NC):
        b, q = core_bq(core)
        out[b, q * T:(q + 1) * T] = np.asarray(res.results[core]["outT"]).T
    return out
```

```python
import contextlib
import numpy as np
import ml_dtypes
import concourse.bass as bass
import concourse.mybir as mybir
from concourse.bass_utils import run_bass_kernel_spmd

F32 = mybir.dt.float32
BF16 = mybir.dt.bfloat16
AF = mybir.ActivationFunctionType
ALU = mybir.AluOpType
NPBF = ml_dtypes.bfloat16

D = 1024; KD = 8; FF = 2816; KF = 22; T = 4096; CTX = 256; TT = T + CTX; TX = TT + 4
MW = 256; INW = 6656; NC = 8; SEQ = 16384; GRID_W = 64
EPS = 1e-6


class Buf:
    __slots__ = ("name", "lw", "rd", "dsem")

    def __init__(self, name):
        self.name = name
        self.lw = None
        self.rd = []
        self.dsem = None


class Prog:
    ENG = ("pe", "dve", "act", "pool", "sp")

    def __init__(self, nc, stack):
        self.nc = nc
        self.stack = stack
        self.eng = {"pe": nc.tensor, "dve": nc.vector, "act": nc.scalar,
                    "pool": nc.gpsimd, "sp": nc.sync}
        self.sems = {}
        self.cnt = {}
        self.waited = {e: {} for e in self.ENG}
        self.esem = {}
        self.nsem = 0
        self.free_dsems = []
        self.dbufs = []
        self.allbufs = []
        self.pool_fifo = []
        for e in self.ENG:
            key = f"E{e}"
            self._alloc_sem(key)
            self.esem[e] = key
        self.n_inst = 0

    def _alloc_sem(self, key):
        h = self.stack.enter_context(self.nc.semaphore(f"s{self.nsem}_{key}"))
        self.nsem += 1
        self.sems[key] = h
        self.cnt[key] = 0
        return key

    def _need(self, e, deps, key, val):
        if self.waited[e].get(key, 0) >= val:
            return
        deps[key] = max(deps.get(key, 0), val)

    def _emit_waits(self, e, deps):
        eng = self.eng[e]
        for key, val in deps.items():
            eng.wait_ge(self.sems[key], val)
            self.waited[e][key] = val
            self.n_inst += 1

    def _collect(self, e, reads, writes, dma=False):
        deps = {}
        for b in reads:
            if b.lw is not None:
                self._need(e, deps, *b.lw)
        for b in writes:
            if b.lw is not None:
                if not (dma and b.lw[0] == b.dsem and not b.rd):
                    self._need(e, deps, *b.lw)
            for r in b.rd:
                self._need(e, deps, *r)
        return deps

    def _stamp(self, stamp, reads, writes):
        for b in list(reads) + list(writes):
            if b.lw is None and not b.rd:
                self.allbufs.append(b)
        for b in reads:
            b.rd.append(stamp)
            if len(b.rd) > 64:
                m = {}
                for k, v in b.rd:
                    m[k] = max(m.get(k, 0), v)
                b.rd = list(m.items())
        for b in writes:
            b.lw = stamp
            b.rd = []

    def op(self, e, fn, reads=(), writes=(), sig=True):
        deps = self._collect(e, reads, writes)
        if e == "pe":
            deps.pop(self.esem["pe"], None)
        self._emit_waits(e, deps)
        ins = fn()
        self.n_inst += 1
        key = self.esem[e]
        if sig:
            self.cnt[key] += 1
            ins.then_inc(self.sems[key], 1)
            stamp = (key, self.cnt[key])
        else:
            stamp = (key, self.cnt[key] + 1)
        self._stamp(stamp, reads, writes)
        return ins

    def dma(self, q, out_ap, in_ap, reads=(), writes=(), sb=None, ndesc=0, **kw):
        if q == "pool" and ndesc:
            while self.pool_fifo and sum(n for _, _, n in self.pool_fifo) + ndesc > 3000:
                key_, val_, _ = self.pool_fifo.pop(0)
                d_ = {}
                self._need("pool", d_, key_, val_)
                self._emit_waits("pool", d_)
        if sb.dsem is None:
            if self.free_dsems:
                sb.dsem = self.free_dsems.pop()
            else:
                sb.dsem = self._alloc_sem(f"D{self.nsem}")
            self.dbufs.append(sb)
        deps = self._collect(q, reads, writes, dma=True)
        for b in list(reads) + list(writes):
            if b.dsem is not None and b.dsem in deps:
                deps[b.dsem] = self.cnt[b.dsem]
        self._emit_waits(q, deps)
        ins = self.eng[q].dma_start(out=out_ap, in_=in_ap, **kw)
        self.n_inst += 1
        key = sb.dsem
        self.cnt[key] += 16
        ins.then_inc(self.sems[key], 16)
        self._stamp((key, self.cnt[key]), reads, writes)
        if q == "pool" and ndesc:
            self.pool_fifo.append((key, self.cnt[key], ndesc))
        return ins

    def barrier(self):
        for e in self.ENG:
            deps = {}
            for key, v in self.cnt.items():
                if v > 0 and key != self.esem[e]:
                    self._need(e, deps, key, v)
            if e != "pe" and self.cnt[self.esem[e]] > 0:
                self._need(e, deps, self.esem[e], self.cnt[self.esem[e]])
            self._emit_waits(e, deps)
        for b in self.allbufs:
            b.lw = None
            b.rd = []
        self.allbufs = []
        for b in self.dbufs:
            self.free_dsems.append(b.dsem)
            b.dsem = None
        self.dbufs = []
        self.pool_fifo = []
        for e in self.ENG:
            if self.cnt[self.esem[e]] > 20000:
                key = f"E{e}{self.nsem}"
                self._alloc_sem(key)
                self.esem[e] = key

    def collective(self, in_ap, out_ap, groups, reads=(), writes=()):
        deps = self._collect("pool", reads, writes)
        self._emit_waits("pool", deps)
        ins = self.nc.gpsimd.collective_compute("AllGather", ALU.bypass, replica_groups=groups, ins=[in_ap], outs=[out_ap])
        key = self._alloc_sem(f"C{self.nsem}")
        ins.then_inc(self.sems[key])
        self.cnt[key] = 1
        self.n_inst += 1
        self._stamp((key, 1), reads, writes)

    def finish(self):
        deps = {}
        for key, v in self.cnt.items():
            if v > 0:
                self._need("sp", deps, key, v)
        self._emit_waits("sp", deps)


SM_FIELDS = [("bmod", 72), ("normg", 24), ("bgate", 32), ("convaw", 6), ("lcw", 8), ("lcb", 2),
             ("lba", 4), ("lbx", 4), ("llam", 4), ("gqb", 1), ("gkb", 1), ("gqd", 1), ("gkd", 1),
             ("sink", 2), ("cvec", 16), ("hvalid", 4), ("eps", 1),
             ("onesD", 128), ("ones64", 128), ("rmat", 128), ("ones1", 128),
             ("wa00", 128), ("wa01", 128), ("wa10", 128), ("wa11", 128),
             ("wx00", 128), ("wx01", 128), ("wx10", 128), ("wx11", 128),
             ("lsel", 4), ("rsel", 4), ("fsel", 4), ("bsel", 4)]
SM_OFF = {}
_o = 0
for _n, _w in SM_FIELDS:
    SM_OFF[_n] = (_o, _w)
    _o += _w
NSM = _o


def chunked(v, nch):
    return np.ascontiguousarray(np.asarray(v, np.float32).reshape(nch, 128).T)


def blockdiag(a, b):
    m = np.zeros((128, 128), np.float32)
    m[:64, :64] = a
    m[64:, 64:] = b
    return m


def build_sm(inp, l, b, q):
    sm = np.zeros((128, NSM), np.float32)

    def put(name, arr):
        o, w = SM_OFF[name]
        sm[:, o:o + w] = np.asarray(arr, np.float32).reshape(128, w)

    put("bmod", chunked(inp["b_mod"][l], 72))
    put("normg", np.concatenate([chunked(inp["norm_g"][l, s], 8) for s in range(3)], 1))
    put("bgate", chunked(inp["b_gate"][l], 32))
    ca = inp["conv_a_w"][l]
    put("convaw", np.stack([ca[j, c * 128:(c + 1) * 128] for c in range(2) for j in range(3)], 1))
    lw = inp["lru_conv_w"][l]
    put("lcw", np.stack([lw[j, c * 128:(c + 1) * 128] for c in range(2) for j in range(4)], 1))
    put("lcb", chunked(inp["lru_conv_b"][l], 2))
    for nm, key in (("lba", "lru_b_a"), ("lbx", "lru_b_x"), ("llam", "lru_lam")):
        v = inp[key][l]
        put(nm, np.stack([v[d, c * 128:(c + 1) * 128] for d in range(2) for c in range(2)], 1))
    g = inp["qk_norm_g"][l]
    for i, nm in enumerate(("gqb", "gkb", "gqd", "gkd")):
        put(nm, np.tile(g[i], 2)[:, None])
    sk = inp["attn_sink"][l]
    put("sink", np.stack([np.repeat(sk[2 * c:2 * c + 2], 64) for c in range(2)], 1))
    cv = np.stack([inp["c"][b], inp["c_ctx"]], 1)
    put("cvec", cv.reshape(8, 128, 2).transpose(1, 0, 2).reshape(128, 16))
    hv = np.array([1.0 if q > 0 else 0.0, 1.0 if q > 0 else 0.0, 1.0 if q < 3 else 0.0, 0.0], np.float32)
    put("hvalid", np.tile(hv[None, :], (128, 1)))
    put("eps", np.full((128, 1), EPS, np.float32))
    put("onesD", np.full((128, 128), 1.0 / D, np.float32))
    put("ones64", blockdiag(np.full((64, 64), 1.0 / 64, np.float32), np.full((64, 64), 1.0 / 64, np.float32)))
    r = np.zeros((128, 128), np.float32)
    for m in range(128):
        if (m % 32) < 16:
            r[m + 16, m] = -1.0
        else:
            r[m - 16, m] = 1.0
    put("rmat", r)
    put("ones1", np.ones((128, 128), np.float32))
    put("lsel", np.tile(np.array([1.0 if j == q - 1 else 0.0 for j in range(4)], np.float32)[None], (128, 1)))
    put("rsel", np.tile(np.array([1.0 if j == q + 1 else 0.0 for j in range(4)], np.float32)[None], (128, 1)))
    put("fsel", np.tile(np.array([1.0 if j < q else 0.0 for j in range(4)], np.float32)[None], (128, 1)))
    put("bsel", np.tile(np.array([1.0 if j > q else 0.0 for j in range(4)], np.float32)[None], (128, 1)))
    for d in range(2):
        for c in range(2):
            put(f"wa{d}{c}", blockdiag(inp["lru_w_a"][l, d, 2 * c], inp["lru_w_a"][l, d, 2 * c + 1]))
            put(f"wx{d}{c}", blockdiag(inp["lru_w_x"][l, d, 2 * c], inp["lru_w_x"][l, d, 2 * c + 1]))
    return sm


def rope_tables(q):
    pos = q * T + np.arange(T)
    rows = (pos // GRID_W).astype(np.float32)
    cols = (pos % GRID_W).astype(np.float32)
    nf = 16
    inv_freq = (10000.0 ** (-np.arange(nf, dtype=np.float32) / nf)).astype(np.float32)
    p = np.arange(128)
    d = p % 64
    j = d % 16
    use_col = (d // 32) == 1
    posm = np.where(use_col[:, None], cols[None, :], rows[None, :]).astype(np.float32)
    ang = (posm * inv_freq[j][:, None]).astype(np.float32)
    return np.cos(ang).astype(np.float32), np.sin(ang).astype(np.float32)


class Ctx:
    pass


_UNIQ = [0]


def uniq(name):
    _UNIQ[0] += 1
    return f"{name}_sb{_UNIQ[0]}"


def smv(K, name, i=0, n=1):
    o, w = SM_OFF[name]
    return K.sm[:, o + i:o + i + n]


def setup_common(nc, st):
    K = Ctx()
    K.nc = nc
    K.st = st
    st.enter_context(nc.Block())
    K.P = Prog(nc, st)
    K.sbuf = lambda name, shape, dt: st.enter_context(nc.sbuf_tensor(uniq(name), shape, dt))
    K.ps = [st.enter_context(nc.psum_tensor(f"ps{i}", [128, 512], F32)) for i in range(8)]
    K.bps = [Buf(f"ps{i}") for i in range(8)]
    K.sm = K.sbuf("sm_sb", [128, NSM], F32)
    K.bsm = Buf("sm")
    names = ["onesD", "ones64", "rmat", "ones1"] + [f"w{a}{d}{c}" for a in "ax" for d in range(2) for c in range(2)]
    K.cm = K.sbuf("cm_sb", [128, len(names), 128], BF16)
    K.bcm = Buf("cm")
    K.cmi = {n: i for i, n in enumerate(names)}
    return K


def load_sm(K, sm_d):
    nc, P = K.nc, K.P
    P.dma("sp", K.sm[:], sm_d, writes=[K.bsm], sb=K.bsm)
    for n, i in K.cmi.items():
        o, w = SM_OFF[n]
        P.op("dve", lambda: nc.vector.tensor_copy(K.cm[:, i, :], K.sm[:, o:o + 128]), reads=[K.bsm], writes=[K.bcm])


def cmat(K, name):
    return K.cm[:, K.cmi[name], :]


def emit_mod(K, w_mod_d):
    nc, P = K.nc, K.P
    K.modT = K.sbuf("modT", [128, 72, 2], F32)
    K.bmod = Buf("modT")
    K.gsT = K.sbuf("gsT", [128, 3, 8, 2], F32)
    K.gateT = K.sbuf("gateT", [128, 3, 8, 2], F32)
    K.bgs = Buf("gs")
    with contextlib.ExitStack() as s2:
        wm = [s2.enter_context(nc.sbuf_tensor(uniq(f"wm{i}"), [128, 8, 1024], BF16)) for i in range(2)]
        bwm = [Buf(f"wm{i}") for i in range(2)]
        sc = s2.enter_context(nc.sbuf_tensor(uniq("silu_c"), [128, 16], BF16))
        bsc = Buf("silu_c")
        P.op("act", lambda: nc.scalar.activation(sc[:], smv(K, "cvec", 0, 16), AF.Silu), reads=[K.bsm], writes=[bsc])
        psm = K.ps[7]
        bpsm = K.bps[7]
        for i in range(9):
            s = i % 2
            P.dma("pool", wm[s][:, :, :], w_mod_d.rearrange("(k p) f -> p k f", p=128)[:, :, i * 1024:(i + 1) * 1024],
                  writes=[bwm[s]], sb=bwm[s], ndesc=1024)
            for dk in range(8):
                cidx = i * 8 + dk
                for k in range(8):
                    P.op("pe", lambda: nc.tensor.matmul(psm[:, cidx * 2:cidx * 2 + 2], wm[s][:, k, dk * 128:(dk + 1) * 128],
                                                        sc[:, 2 * k:2 * k + 2], start=(k == 0), stop=(k == 7)),
                         reads=[bwm[s], bsc], writes=[bpsm], sig=(k == 7))
        psv = psm[:, 0:144].rearrange("p (c j) -> p c j", j=2)
        for j in range(2):
            P.op("dve", lambda: nc.vector.tensor_tensor(K.modT[:, :, j], psv[:, :, j], smv(K, "bmod", 0, 72), ALU.add),
                 reads=[bpsm, K.bsm], writes=[K.bmod])
        for s in range(3):
            for j in range(2):
                P.op("dve", lambda: nc.vector.tensor_scalar(K.gsT[:, s, :, j], K.modT[:, (3 * s + 1) * 8:(3 * s + 2) * 8, j],
                                                            1.0, None, ALU.add), reads=[K.bmod], writes=[K.bgs])
                P.op("dve", lambda: nc.vector.tensor_tensor(K.gsT[:, s, :, j], K.gsT[:, s, :, j],
                                                            smv(K, "normg", s * 8, 8), ALU.mult),
                     reads=[K.bgs, K.bsm], writes=[K.bgs])
                P.op("dve", lambda: nc.vector.tensor_scalar(K.gateT[:, s, :, j], K.modT[:, (3 * s + 2) * 8:(3 * s + 3) * 8, j],
                                                            (1.0 if s == 1 else 0.5), None, ALU.mult),
                     reads=[K.bmod], writes=[K.bgs])
        P.barrier()


def shift_col(K, s, k, j):
    return K.modT[:, 3 * s * 8 + k, j:j + 1]


def emit_adaln(K, xt, bxt, N, s, col, hT, bh, sq, bsq, rstd, brs, pst, bpst):
    nc, P = K.nc, K.P
    P.op("act", lambda: nc.scalar.activation(sq[:, :, :N], xt[:, :, :N], AF.Square), reads=[bxt], writes=[bsq])
    for k in range(8):
        P.op("pe", lambda: nc.tensor.matmul(pst[:, :N], cmat(K, "onesD"), sq[:, k, :N], start=(k == 0), stop=(k == 7)),
             reads=[bsq, K.bcm], writes=[bpst], sig=(k == 7))
    P.op("act", lambda: nc.scalar.activation(rstd[:, :N], pst[:, :N], AF.Sqrt, bias=smv(K, "eps"), scale=1.0),
         reads=[bpst, K.bsm], writes=[brs])
    P.op("dve", lambda: nc.vector.reciprocal(rstd[:, :N], rstd[:, :N]), reads=[brs], writes=[brs])
    for k in range(8):
        P.op("dve", lambda: nc.vector.scalar_tensor_tensor(K.tmp32[:, :N], xt[:, k, :N], K.gsT[:, s, k, col:col + 1],
                                                           rstd[:, :N], ALU.mult, ALU.mult),
             reads=[bxt, brs, K.bgs], writes=[K.btmp32])
        P.op("act", lambda: nc.scalar.activation(hT[:, k, :N], K.tmp32[:, :N], AF.Identity,
                                                 bias=shift_col(K, s, k, col), scale=1.0),
             reads=[K.btmp32, K.bmod], writes=[bh])


class WBufs:
    def __init__(self, gw):
        self.gw = gw
        self.bufs = []

    def at(self, col):
        return self.bufs[col // self.gw]


def load_weight_cast(K, dst, wb, src, g0, g1):
    b = Buf("wgrp")
    wb.bufs.append(b)
    K.P.dma("pool", dst[:, :, g0:g1], src.rearrange("(k p) f -> p k f", p=128)[:, :, g0:g1], writes=[b], sb=b,
            ndesc=128 * dst.shape[1])


def load_weight_groups(K, specs):
    pos = [0] * len(specs)
    done = False
    while not done:
        done = True
        for i, (dst, wb, src, width) in enumerate(specs):
            if pos[i] < width:
                g1 = min(width, pos[i] + wb.gw)
                load_weight_cast(K, dst, wb, src, pos[i], g1)
                pos[i] = g1
                done = False


def emit_ffn_phase(K, x_src, x_dst, tiles, s, wg_d, wu_d, wd_d, h_dst=None, s_next=None):
    nc, P = K.nc, K.P
    post = h_dst is not None
    with contextlib.ExitStack() as s2:
        sb = lambda name, shape, dt: s2.enter_context(nc.sbuf_tensor(uniq(name), shape, dt))
        wg = sb("wg", [128, KD, FF], BF16); bwg = WBufs(256)
        wu = sb("wu", [128, KD, FF], BF16); bwu = WBufs(256)
        wd = sb("wd", [128, KF, D], BF16); bwd = WBufs(256)
        load_weight_groups(K, [(wg, bwg, wg_d, FF), (wu, bwu, wu_d, FF)])
        load_weight_groups(K, [(wd, bwd, wd_d, D)])
        xts = [sb(f"xt{i}", [128, 8, 512], F32) for i in range(2)]; bxts = [Buf(f"xt{i}") for i in range(2)]
        hT = sb("hT", [128, 8, 512], BF16); bh = Buf("hT")
        act = sb("actT", [128, KF, 512], BF16); bact = Buf("actT")
        rstd = sb("rstd", [128, 512], F32); brs = Buf("rstd")
        K.tmp32 = sb("tmp32", [128, 512], F32); K.btmp32 = Buf("tmp32")
        sg, bsg = K.tmp32, K.btmp32
        if post:
            h2o = sb("h2o", [128, 8, 512], BF16); bh2o = Buf("h2o")
            rstd2 = sb("rstd2", [128, 512], F32); brs2 = Buf("rstd2")
        xs = x_src.rearrange("(c p) t -> p c t", p=128)
        xd = x_dst.rearrange("(c p) t -> p c t", p=128)
        hd = h_dst.rearrange("(c p) t -> p c t", p=128) if post else None
        n = len(tiles)
        gi = [0]

        def load(i):
            start, N, col = tiles[i]
            P.dma("sp", xts[i % 2][:, :, :N], xs[:, :, start:start + N], writes=[bxts[i % 2]], sb=bxts[i % 2])

        def adaln(i):
            start, N, col = tiles[i]
            emit_adaln(K, xts[i % 2], bxts[i % 2], N, s, col, hT, bh, hT, bh, rstd, brs, K.ps[6], K.bps[6])

        def gateup(i):
            start, N, col = tiles[i]
            for f in range(KF):
                pg, bpg = K.ps[gi[0] % 2], K.bps[gi[0] % 2]
                pu, bpu = K.ps[2 + gi[0] % 2], K.bps[2 + gi[0] % 2]
                gi[0] += 1
                for k in range(8):
                    P.op("pe", lambda: nc.tensor.matmul(pg[:, :N], wg[:, k, f * 128:(f + 1) * 128], hT[:, k, :N],
                                                        start=(k == 0), stop=(k == 7)),
                         reads=[bwg.at(f * 128), bh], writes=[bpg], sig=(k == 7))
                for k in range(8):
                    P.op("pe", lambda: nc.tensor.matmul(pu[:, :N], wu[:, k, f * 128:(f + 1) * 128], hT[:, k, :N],
                                                        start=(k == 0), stop=(k == 7)),
                         reads=[bwu.at(f * 128), bh], writes=[bpu], sig=(k == 7))
                P.op("act", lambda: nc.scalar.activation(sg[:, :N], pg[:, :N], AF.Silu), reads=[bpg], writes=[bsg])
                P.op("dve", lambda: nc.vector.tensor_tensor(act[:, f, :N], pu[:, :N], sg[:, :N], ALU.mult),
                     reads=[bpu, bsg], writes=[bact])

        def down(i):
            start, N, col = tiles[i]
            xt, bxt = xts[i % 2], bxts[i % 2]
            for dk in range(8):
                py, bpy = K.ps[4 + dk % 2], K.bps[4 + dk % 2]
                for f in range(KF):
                    P.op("pe", lambda: nc.tensor.matmul(py[:, :N], wd[:, f, dk * 128:(dk + 1) * 128], act[:, f, :N],
                                                        start=(f == 0), stop=(f == KF - 1)),
                         reads=[bwd.at(dk * 128), bact], writes=[bpy], sig=(f == KF - 1))
                P.op("dve", lambda: nc.vector.scalar_tensor_tensor(xt[:, dk, :N], py[:, :N], K.gateT[:, s, dk, col:col + 1],
                                                                   xt[:, dk, :N], ALU.mult, ALU.add),
                     reads=[bpy, bxt, K.bgs], writes=[bxt])
            P.dma("sp", xd[:, :, start:start + N], xt[:, :, :N], reads=[bxt], sb=bxt)

        def postadaln(i):
            start, N, col = tiles[i]
            emit_adaln(K, xts[i % 2], bxts[i % 2], N, s_next, col, h2o, bh2o, h2o, bh2o, rstd2, brs2, K.ps[7], K.bps[7])
            P.dma("sp", hd[:, :, start:start + N], h2o[:, :, :N], reads=[bh2o], sb=bh2o)

        load(0)
        if n > 1:
            load(1)
        adaln(0)
        for i in range(n):
            gateup(i)
            if i >= 1:
                if post:
                    postadaln(i - 1)
                if i + 1 < n:
                    load(i + 1)
            if i + 1 < n:
                adaln(i + 1)
            down(i)
        if post:
            postadaln(n - 1)
        P.barrier()


LAT_TILES = [(i * 512, 512, 0) for i in range(8)]
CTX_TILE = (T, CTX, 1)
HALO_TILE = (TT, 4, 0)


def emit_A(K, I):
    nc, P = K.nc, K.P
    xT_d, cos_d, sin_d = I.xin, I.cos_d, I.sin_d
    wg_d, wu_d, wd_d, win_d = I.wg1, I.wu1, I.wd1, I.win
    O = I.O
    x1full, h2_d = I.x1full, I.h2_d
    with contextlib.ExitStack() as sA:
        K_sbuf = lambda name, shape, dt: sA.enter_context(nc.sbuf_tensor(uniq(name), shape, dt))
        tiles = LAT_TILES + [CTX_TILE, HALO_TILE]
        emit_ffn_phase(K, xT_d, x1full, tiles, 0, wg_d, wu_d, wd_d, h_dst=h2_d, s_next=1)

        cxf = K_sbuf("cxf", [128, 2, T + 4], F32); bcxf = Buf("cxf")
        cxc = K_sbuf("cxc", [128, 2, CTX + 4], F32); bcxc = Buf("cxc")
        P.op("pool", lambda: nc.gpsimd.memset(cxc[:], 0.0), writes=[bcxc])
        gq = K_sbuf("gq", [128, 2], F32); bgq = Buf("gq")
        P.op("dve", lambda: nc.vector.tensor_scalar(gq[:, 0:1], smv(K, "gqb"), 0.125, None, ALU.mult), reads=[K.bsm], writes=[bgq])
        P.op("dve", lambda: nc.vector.tensor_scalar(gq[:, 1:2], smv(K, "gqd"), 0.125, None, ALU.mult), reads=[K.bsm], writes=[bgq])

        with contextlib.ExitStack() as s2:
            sb = lambda name, shape, dt: s2.enter_context(nc.sbuf_tensor(uniq(name), shape, dt))
            win = sb("win", [128, KD, INW], BF16); bwin = WBufs(512)
            load_weight_groups(K, [(win, bwin, win_d, INW)])
            h2t = sb("h2t", [128, 8, 512], BF16); bh2 = Buf("h2t")
            cst = sb("cost", [128, 512], F32); snt = sb("sint", [128, 512], F32); bcs = Buf("cossin")
            axs = sb("axs", [128, 2, 512], F32); baxs = Buf("axs")
            abo = sb("abo", [128, 2, 512], BF16); babo = Buf("abo")
            vo = sb("vo", [128, 2, 512], BF16); bvo = Buf("vo")
            qpl = sb("qpl", [128, 2, 512], BF16); bqpl = Buf("qpl")
            qro = sb("qro", [128, 2, 512], BF16); bqro = Buf("qro")
            ko = sb("ko", [128, 512], BF16); bko = Buf("ko")
            ktmp = sb("ktmp", [128, 512], BF16); bktmp = Buf("ktmp")
            vtok = sb("vtok", [128, 4, 384], BF16); bvtok = Buf("vtok")
            gcg = sb("gcg", [128, 2, 512], BF16); bgcg = Buf("gcg")
            dqo = sb("dqo", [128, 2, 512], BF16); bdqo = Buf("dqo")
            dko = sb("dko", [128, 2, 512], BF16); bdko = Buf("dko")
            gto = sb("gto", [128, 8, 512], BF16); bgto = Buf("gto")
            sqb = [sb(f"sqb{i}", [128, 512], BF16) for i in range(2)]; bsqb = [Buf(f"sqb{i}") for i in range(2)]
            rs = [sb(f"rs{i}", [128, 512], F32) for i in range(2)]; brs = [Buf(f"rs{i}") for i in range(2)]
            qh = [sb(f"qh{i}", [128, 512], F32) for i in range(2)]; bqh = [Buf(f"qh{i}") for i in range(2)]
            t1 = [sb(f"t1{i}", [128, 512], F32) for i in range(2)]; bt1 = [Buf(f"t1{i}") for i in range(2)]
            t2 = [sb(f"t2{i}", [128, 512], F32) for i in range(2)]; bt2 = [Buf(f"t2{i}") for i in range(2)]
            bout = Buf("aout")
            h2s = h2_d.rearrange("(c p) t -> p c t", p=128)
            v3 = lambda ap: ap.rearrange("(c p) t -> p c t", p=128)
            ov = {n: v3(O[n]) for n in ("abT", "gcgT", "dqT", "gT")}
            pctr = [0]

            def proj(c, N):
                i = (0, 1, 2, 6, 7)[pctr[0] % 5]
                pctr[0] += 1
                ps, bps = K.ps[i], K.bps[i]
                for k in range(8):
                    P.op("pe", lambda: nc.tensor.matmul(ps[:, :N], win[:, k, c * 128:(c + 1) * 128], h2t[:, k, :N],
                                                        start=(k == 0), stop=(k == 7)),
                         reads=[bwin.at(c * 128), bh2], writes=[bps], sig=(k == 7))
                step_pending()
                return ps, bps

            pending = []

            def step_pending():
                for g in list(pending):
                    try:
                        next(g)
                    except StopIteration:
                        pending.remove(g)

            def drain():
                while pending:
                    step_pending()

            qn = [0]

            def qknorm(ps, bps, N, gcol, bg, out_plain, bop, rope_out=None, bro=None, after=None):
                i = qn[0] % 2
                qn[0] += 1
                sqb_, bsqb_ = sqb[i], bsqb[i]
                rs_, brs_ = rs[i], brs[i]
                qh_, bqh_ = qh[i], bqh[i]
                t1_, bt1_ = t1[i], bt1[i]
                t2_, bt2_ = t2[i], bt2[i]
                pst, bpst = K.ps[3], K.bps[3]
                P.op("act", lambda: nc.scalar.activation(sqb_[:, :N], ps[:, :N], AF.Square), reads=[bps], writes=[bsqb_])
                yield
                P.op("pe", lambda: nc.tensor.matmul(pst[:, :N], cmat(K, "ones64"), sqb_[:, :N], start=True, stop=True),
                     reads=[bsqb_, K.bcm], writes=[bpst])
                P.op("act", lambda: nc.scalar.activation(rs_[:, :N], pst[:, :N], AF.Sqrt, bias=smv(K, "eps"), scale=1.0),
                     reads=[bpst, K.bsm], writes=[brs_])
                P.op("dve", lambda: nc.vector.reciprocal(rs_[:, :N], rs_[:, :N]), reads=[brs_], writes=[brs_])
                P.op("dve", lambda: nc.vector.scalar_tensor_tensor(qh_[:, :N], ps[:, :N], gcol, rs_[:, :N], ALU.mult, ALU.mult),
                     reads=[bps, brs_, bg], writes=[bqh_])
                P.op("act", lambda: nc.scalar.copy(out_plain[:, :N], qh_[:, :N]), reads=[bqh_], writes=[bop])
                if rope_out is not None:
                    yield
                    psr, bpsr = K.ps[4], K.bps[4]
                    P.op("pe", lambda: nc.tensor.matmul(psr[:, :N], cmat(K, "rmat"), out_plain[:, :N], start=True, stop=True),
                         reads=[bop, K.bcm], writes=[bpsr])
                    P.op("pool", lambda: nc.gpsimd.tensor_tensor(t1_[:, :N], qh_[:, :N], cst[:, :N], ALU.mult),
                         reads=[bqh_, bcs], writes=[bt1_])
                    P.op("dve", lambda: nc.vector.tensor_tensor(t2_[:, :N], psr[:, :N], snt[:, :N], ALU.mult),
                         reads=[bpsr, bcs], writes=[bt2_])
                    P.op("pool", lambda: nc.gpsimd.tensor_tensor(rope_out[:, :N], t1_[:, :N], t2_[:, :N], ALU.add),
                         reads=[bt1_, bt2_], writes=[bro])
                if after is not None:
                    after()

            def qk(*a, **kw):
                g = qknorm(*a, **kw)
                next(g)
                pending.append(g)

            for (start, N, col) in tiles:
                kind = "lat" if start < T else ("ctx" if start == T else "halo")
                P.dma("sp", h2t[:, :, :N], h2s[:, :, start:start + N], writes=[bh2], sb=bh2)
                if kind == "halo":
                    for c in range(2):
                        ps, bps = proj(10 + c, N)
                        P.op("dve", lambda: nc.vector.tensor_tensor(cxf[:, c, 0:2], ps[:, 0:2], smv(K, "hvalid", 0, 2), ALU.mult),
                             reads=[bps, K.bsm], writes=[bcxf])
                        P.op("dve", lambda: nc.vector.tensor_tensor(cxf[:, c, T + 2:T + 4], ps[:, 2:4], smv(K, "hvalid", 2, 2), ALU.mult),
                             reads=[bps, K.bsm], writes=[bcxf])
                    continue
                lat = kind == "lat"
                if lat:
                    P.dma("sp", cst[:, :N], cos_d[:, start:start + N], writes=[bcs], sb=bcs)
                    P.dma("sp", snt[:, :N], sin_d[:, start:start + N], writes=[bcs], sb=bcs)
                sl = slice(start, start + N)
                for c in range(2):
                    ps, bps = proj(0 + c, N)
                    P.op("act", lambda: nc.scalar.copy(axs[:, c, :N], ps[:, :N]), reads=[bps], writes=[baxs])
                for c in range(2):
                    ps, bps = proj(2 + c, N)
                    P.op("act", lambda: nc.scalar.copy(abo[:, c, :N], ps[:, :N]), reads=[bps], writes=[babo])
                for c in range(2):
                    ps, bps = proj(4 + c, N)
                    P.op("dve", lambda: nc.vector.tensor_tensor(vo[:, c, :N], ps[:, :N], axs[:, c, :N], ALU.mult),
                         reads=[bps, baxs], writes=[bvo])
                P.dma("sp", ov["abT"][:, :, sl], abo[:, :, :N], reads=[babo], sb=babo)
                if lat:
                    P.dma("sp", v3(O["vcvext"])[:, :, 1 + start:1 + start + N], vo[:, :, :N], reads=[bvo], sb=bvo)
                else:
                    P.dma("sp", v3(O["vcvcext"])[:, :, 1:1 + N], vo[:, :, :N], reads=[bvo], sb=bvo)
                def store_q(c, sl=sl, N=N, lat=lat):
                    for hh in range(2):
                        r0 = hh * 128 + c * 64
                        P.dma("sp", O["qplT"][r0:r0 + 64, sl], qpl[hh * 64:(hh + 1) * 64, c, :N], reads=[bqpl], sb=bqpl)
                        if lat:
                            P.dma("sp", O["qrotT"][r0:r0 + 64, sl], qro[hh * 64:(hh + 1) * 64, c, :N], reads=[bqro], sb=bqro)
                for c in range(2):
                    ps, bps = proj(6 + c, N)
                    qk(ps, bps, N, gq[:, 0:1], bgq, qpl[:, c, :], bqpl, qro[:, c, :] if lat else None, bqro,
                       after=(lambda c=c, f=store_q: f(c)))
                ps, bps = proj(8, N)
                if lat:
                    qk(ps, bps, N, smv(K, "gkb"), K.bsm, ktmp, bktmp, ko, bko,
                       after=(lambda start=start, N=N: P.dma("sp", O["kext"][:, 128 + start:128 + start + N], ko[:, :N], reads=[bko], sb=bko)))
                else:
                    qk(ps, bps, N, smv(K, "gkb"), K.bsm, ko, bko,
                       after=(lambda N=N: P.dma("sp", O["kc"][:, 0:N], ko[:, :N], reads=[bko], sb=bko)))
                psv, bpsv = K.ps[5], K.bps[5]
                nb = N // 128
                for tb in range(nb):
                    for k in range(8):
                        P.op("pe", lambda: nc.tensor.matmul(psv[:, 0:128], h2t[:, k, tb * 128:(tb + 1) * 128], win[:, k, 9 * 128:10 * 128],
                                                            start=(k == 0), stop=(k == 7)),
                             reads=[bwin.at(9 * 128), bh2], writes=[bpsv], sig=(k == 7))
                    for k in range(8):
                        P.op("pe", lambda: nc.tensor.matmul(psv[:, 128:384], h2t[:, k, tb * 128:(tb + 1) * 128], win[:, k, 18 * 128:20 * 128],
                                                            start=(k == 0), stop=(k == 7)),
                             reads=[bwin.at(18 * 128), bh2], writes=[bpsv], sig=(k == 7))
                    P.op("act", lambda: nc.scalar.copy(vtok[:, tb, :], psv[:, 0:384]), reads=[bpsv], writes=[bvtok])
                tv = lambda ap: ap.rearrange("(b p) f -> p b f", p=128)
                if lat:
                    P.dma("sp", tv(O["vext"][128 + start:128 + start + N, :]), vtok[:, 0:nb, 0:128], reads=[bvtok], sb=bvtok)
                    P.dma("sp", tv(O["vdext"][256 + start:256 + start + N, :]), vtok[:, 0:nb, 128:384], reads=[bvtok], sb=bvtok)
                else:
                    P.dma("sp", tv(O["vc"][0:N, :]), vtok[:, 0:nb, 0:128], reads=[bvtok], sb=bvtok)
                    P.dma("sp", tv(O["vdc"][0:N, :]), vtok[:, 0:nb, 128:384], reads=[bvtok], sb=bvtok)
                for c in range(2):
                    ps, bps = proj(10 + c, N)
                    if lat:
                        P.op("act", lambda: nc.scalar.copy(cxf[:, c, 2 + start:2 + start + N], ps[:, :N]), reads=[bps], writes=[bcxf])
                    else:
                        P.op("act", lambda: nc.scalar.copy(cxc[:, c, 2:2 + N], ps[:, :N]), reads=[bps], writes=[bcxc])
                for c in range(2):
                    ps, bps = proj(12 + c, N)
                    P.op("act", lambda: nc.scalar.activation(gcg[:, c, :N], ps[:, :N], AF.Gelu_apprx_tanh), reads=[bps], writes=[bgcg])
                P.dma("sp", ov["gcgT"][:, :, sl], gcg[:, :, :N], reads=[bgcg], sb=bgcg)
                for c in range(2):
                    ps, bps = proj(14 + c, N)
                    qk(ps, bps, N, gq[:, 1:2], bgq, dqo[:, c, :], bdqo,
                       after=(lambda c=c, sl=sl, N=N: P.dma("sp", ov["dqT"][:, c, sl], dqo[:, c, :N], reads=[bdqo], sb=bdqo)))
                for c in range(2):
                    ps, bps = proj(16 + c, N)
                    if lat:
                        aft = (lambda c=c, start=start, N=N: P.dma("sp", v3(O["kdext"])[:, c, 256 + start:256 + start + N], dko[:, c, :N], reads=[bdko], sb=bdko))
                    else:
                        aft = (lambda c=c, N=N: P.dma("sp", v3(O["kdc"])[:, c, 0:N], dko[:, c, :N], reads=[bdko], sb=bdko))
                    qk(ps, bps, N, smv(K, "gkd"), K.bsm, dko[:, c, :], bdko, after=aft)
                for c in range(32):
                    ps, bps = proj(20 + c, N)
                    P.op("act", lambda: nc.scalar.activation(gto[:, c % 8, :N], ps[:, :N], AF.Sigmoid, bias=smv(K, "bgate", c), scale=1.0),
                         reads=[bps, K.bsm], writes=[bgto])
                    if c % 8 == 7:
                        c0 = c - 7
                        P.dma("sp", ov["gT"][:, c0:c0 + 8, sl], gto[:, :, :N], reads=[bgto], sb=bgto)
                drain()
            P.barrier()

        with contextlib.ExitStack() as s2:
            sb = lambda name, shape, dt: s2.enter_context(nc.sbuf_tensor(uniq(name), shape, dt))
            u = sb("lru_u", [128, T], F32); bu = Buf("lru_u")
            ub = sb("lru_ub", [128, T], BF16); bub = Buf("lru_ub")
            ra = sb("lru_a", [128, T], F32); bra = Buf("lru_a")
            ib = sb("lru_b", [128, T], F32); bib = Buf("lru_b")
            tm = sb("lru_t", [128, T], F32); btm = Buf("lru_t")
            hh = sb("lru_h", [128, T], F32); bhh = Buf("lru_h")
            pc = sb("lru_p", [128, T], F32); bpc = Buf("lru_p")
            zz = sb("lru_z", [128, T], F32); bzz = Buf("lru_z")
            hs = sb("lru_hs", [128, CTX], F32); bhs = Buf("lru_hs")
            nl8 = sb("nl8", [128, 4], F32); bnl = Buf("nl8")
            lsum, bls = K.lsum, K.bls
            blo = Buf("lruout")
            P.op("pool", lambda: nc.gpsimd.memset(zz[:], 0.0), writes=[bzz])
            P.op("act", lambda: nc.scalar.activation(nl8[:], smv(K, "llam", 0, 4), AF.Exp, scale=-1.0), reads=[K.bsm], writes=[bnl])
            P.op("act", lambda: nc.scalar.activation(nl8[:], nl8[:], AF.Ln, bias=1.0, scale=1.0), reads=[bnl], writes=[bnl])
            P.op("dve", lambda: nc.vector.tensor_scalar(nl8[:], nl8[:], -8.0, None, ALU.mult), reads=[bnl], writes=[bnl])
            pi = [0]

            def lru_chain(src, bsrc, Tn, c, is_ctx):
                lw = lambda j: smv(K, "lcw", c * 4 + j)
                P.op("dve", lambda: nc.vector.tensor_scalar(u[:, :Tn], src[:, c, 0:Tn], lw(0), smv(K, "lcb", c), ALU.mult, ALU.add),
                     reads=[bsrc, K.bsm], writes=[bu])
                for j in range(1, 4):
                    P.op("dve", lambda: nc.vector.scalar_tensor_tensor(u[:, :Tn], src[:, c, j:j + Tn], lw(j), u[:, :Tn], ALU.mult, ALU.add),
                         reads=[bsrc, bu, K.bsm], writes=[bu])
                P.op("act", lambda: nc.scalar.copy(ub[:, :Tn], u[:, :Tn]), reads=[bu], writes=[bub])
                for d in range(2):
                    for t0 in range(0, Tn, 512):
                        n = min(512, Tn - t0)
                        for (wn, bn, dst, bdst) in ((f"wa{d}{c}", "lba", ra, bra), (f"wx{d}{c}", "lbx", ib, bib)):
                            ps, bps = K.ps[pi[0] % 4], K.bps[pi[0] % 4]
                            pi[0] += 1
                            P.op("pe", lambda: nc.tensor.matmul(ps[:, :n], cmat(K, wn), ub[:, t0:t0 + n], start=True, stop=True),
                                 reads=[bub, K.bcm], writes=[bps])
                            P.op("act", lambda: nc.scalar.activation(dst[:, t0:t0 + n], ps[:, :n], AF.Sigmoid,
                                                                     bias=smv(K, bn, d * 2 + c), scale=1.0),
                                 reads=[bps, K.bsm], writes=[bdst])
                    P.op("act", lambda: nc.scalar.activation(ra[:, :Tn], ra[:, :Tn], AF.Exp, scale=nl8[:, d * 2 + c:d * 2 + c + 1]),
                         reads=[bra, bnl], writes=[bra])
                    P.op("dve", lambda: nc.vector.tensor_tensor(tm[:, :Tn], ra[:, :Tn], ra[:, :Tn], ALU.mult), reads=[bra], writes=[btm])
                    P.op("act", lambda: nc.scalar.activation(tm[:, :Tn], tm[:, :Tn], AF.Sqrt, bias=1.0000001, scale=-1.0),
                         reads=[btm], writes=[btm])
                    P.op("dve", lambda: nc.vector.tensor_tensor(ib[:, :Tn], ib[:, :Tn], u[:, :Tn], ALU.mult), reads=[bib, bu], writes=[bib])
                    P.op("dve", lambda: nc.vector.tensor_tensor(ib[:, :Tn], ib[:, :Tn], tm[:, :Tn], ALU.mult), reads=[bib, btm], writes=[bib])
                    if d == 0:
                        va, vb, vh, vp, vz = ra[:, 0:Tn], ib[:, 0:Tn], hh[:, 0:Tn], pc[:, 0:Tn], zz[:, 0:Tn]
                        last = Tn - 1
                    else:
                        rv = lambda t_: t_[:, 0:Tn][:, ::-1]
                        va, vb, vh, vp, vz = rv(ra), rv(ib), rv(hh), rv(pc), rv(zz)
                        last = 0
                    P.op("dve", lambda: nc.vector.tensor_tensor_scan(vh, va, vb, 0.0, ALU.mult, ALU.add), reads=[bra, bib], writes=[bhh])
                    if not is_ctx:
                        P.op("dve", lambda: nc.vector.tensor_tensor_scan(vp, va, vz, 1.0, ALU.mult, ALU.add), reads=[bra, bzz], writes=[bpc])
                        hn, pn = ("hlocF", "pcF") if d == 0 else ("hlocB", "pcB")
                        P.dma("sp", O[hn][c * 128:(c + 1) * 128, :], hh[:, :Tn], reads=[bhh], sb=bhh)
                        P.dma("sp", O[pn][c * 128:(c + 1) * 128, :], pc[:, :Tn], reads=[bpc], sb=bpc)
                        P.op("act", lambda: nc.scalar.copy(lsum[:, c * 6 + 2 * d:c * 6 + 2 * d + 1], pc[:, last:last + 1]), reads=[bpc], writes=[bls])
                        P.op("act", lambda: nc.scalar.copy(lsum[:, c * 6 + 2 * d + 1:c * 6 + 2 * d + 2], hh[:, last:last + 1]), reads=[bhh], writes=[bls])
                    else:
                        P.op("act", lambda: nc.scalar.copy(lsum[:, c * 6 + 4 + d:c * 6 + 5 + d], hh[:, last:last + 1]), reads=[bhh], writes=[bls])
                        if d == 0:
                            P.op("pool", lambda: nc.gpsimd.tensor_copy(hs[:, :Tn], hh[:, :Tn]), reads=[bhh], writes=[bhs])
                        else:
                            P.op("pool", lambda: nc.gpsimd.tensor_tensor(hs[:, :Tn], hs[:, :Tn], hh[:, :Tn], ALU.add), reads=[bhh, bhs], writes=[bhs])
                            P.dma("sp", O["hcsum"][c * 128:(c + 1) * 128, :], hs[:, :Tn], reads=[bhs], sb=bhs)

            for c in range(2):
                lru_chain(cxf, bcxf, T, c, False)
                lru_chain(cxc, bcxc, CTX, c, True)
            P.dma("sp", O["lsum"][:, :], lsum[:], reads=[bls], sb=bls)
            P.barrier()


def core_bq(core):
    return core // 4, core % 4


KEXT = T + 256
KDEXT = T + 512
NAJ = {0: list(range(0, 7)), 1: list(range(5)), 2: list(range(5)), 3: list(range(5)), 4: list(range(-2, 5))}
NAOFF = {}
_o = 0
for _v in range(5):
    NAOFF[_v] = _o
    _o += 4 * len(NAJ[_v])
NATILES = _o


def emit_B(K, I, ctx_out):
    nc, P = K.nc, K.P
    O = I.O
    x1_d = I.x1full
    qrot_d, qpl_d = O["qrotT"], O["qplT"]
    kext_d, vext_d, kc_d, vc_d = O["kext"], O["vext"], O["kc"], O["vc"]
    dq_d = O["dqT"]
    kdext_d, vdext_d, kdc_d, vdc_d = O["kdext"], O["vdext"], O["kdc"], O["vdc"]
    vcv_d, ab_d, vcvc_d, gcg_d = O["vcvext"], O["abT"], O["vcvcext"], O["gcgT"]
    hl_d = [O[n] for n in ("hlocF", "pcF", "hlocB", "pcB")]
    hcs_d, g_d = O["hcsum"], O["gT"]
    wmask_d, natab_d = I.wmask, I.natab
    wbr_d, wout_d = I.wbr, I.wout
    wg_d, wu_d, wd_d = I.wg2, I.wu2, I.wd2
    x3_d, x2_d = I.xout, I.x2_d
    if True:
        tiles = LAT_TILES + ([CTX_TILE] if ctx_out else [])
        with contextlib.ExitStack() as s2:
            sb = lambda name, shape, dt: s2.enter_context(nc.sbuf_tensor(uniq(name), shape, dt))
            kext = sb("kext", [128, KEXT], BF16); bkext = Buf("kext")
            vext = sb("vext", [128, KEXT // 128, 128], BF16); bvext = Buf("vext")
            kc = sb("kc", [128, CTX], BF16); bkc = Buf("kc")
            vc = sb("vc", [128, 2, 128], BF16); bvc = Buf("vc")
            kdext = sb("kdext", [128, 2, KDEXT], BF16); bkdext = Buf("kdext")
            vdext = sb("vdext", [128, KDEXT // 128, MW], BF16); bvdext = Buf("vdext")
            kdc = sb("kdc", [128, 2, CTX], BF16); bkdc = Buf("kdc")
            vdc = sb("vdc", [128, 2, MW], BF16); bvdc = Buf("vdc")
            P.dma("sp", kext[:], kext_d[:, :], writes=[bkext], sb=bkext)
            P.dma("sp", vext[:], vext_d.rearrange("(b p) f -> p b f", p=128), writes=[bvext], sb=bvext)
            P.dma("sp", kc[:], kc_d[:, :], writes=[bkc], sb=bkc)
            P.dma("sp", vc[:], vc_d.rearrange("(b p) f -> p b f", p=128), writes=[bvc], sb=bvc)
            P.dma("sp", kdext[:], kdext_d.rearrange("(c p) t -> p c t", p=128), writes=[bkdext], sb=bkdext)
            P.dma("sp", vdext[:], vdext_d.rearrange("(b p) f -> p b f", p=128), writes=[bvdext], sb=bvdext)
            P.dma("sp", kdc[:], kdc_d.rearrange("(c p) t -> p c t", p=128), writes=[bkdc], sb=bkdc)
            P.dma("sp", vdc[:], vdc_d.rearrange("(b p) f -> p b f", p=128), writes=[bvdc], sb=bvdc)
            wbr = sb("wbr", [128, 8, D], BF16); bwbr = WBufs(512)
            wout = sb("wout", [128, 8, D], BF16); bwout = WBufs(512)
            load_weight_groups(K, [(wbr, bwbr, wbr_d, D)])
            load_weight_groups(K, [(wout, bwout, wout_d, D)])
            wm32 = sb("wm32", [128, 4, 128], F32); bwm32 = Buf("wm32")
            wmk = sb("wmk", [128, 4, 128], BF16); bwmk = Buf("wmk")
            P.dma("sp", wm32[:], wmask_d[:, :, :], writes=[bwm32], sb=bwm32)
            P.op("dve", lambda: nc.vector.tensor_copy(wmk[:], wm32[:]), reads=[bwm32], writes=[bwmk])
            etab = sb("etab", [128, NATILES * 128], BF16); betab = Buf("etab")
            with contextlib.ExitStack() as s3:
                tst = s3.enter_context(nc.sbuf_tensor(uniq("tst"), [128, 29, 128], F32)); btst = Buf("tst")
                for v in range(4):
                    P.dma("sp", tst[:], natab_d[v * 29:(v + 1) * 29].rearrange("n k q -> k n q"), writes=[btst], sb=btst)
                    P.op("act", lambda: nc.scalar.activation(etab[:, v * 29 * 128:(v + 1) * 29 * 128], tst[:].rearrange("p c q -> p (c q)"), AF.Exp),
                         reads=[btst], writes=[betab])
                P.barrier()
            esink = sb("esink", [128, 2], F32); besink = Buf("esink")
            P.op("act", lambda: nc.scalar.activation(esink[:], smv(K, "sink", 0, 2), AF.Exp), reads=[K.bsm], writes=[besink])
            carry, bcarry = K.carry, K.bcarry
            qrot = sb("qrot", [128, 2, 512], BF16); bqrot = Buf("qrot")
            qpl = sb("qplb", [128, 2, 512], BF16); bqpl = Buf("qplb")
            dq = sb("dqb", [128, 2, 512], BF16); bdq = Buf("dqb")
            vx = sb("vx", [128, 2, 514], BF16); bvx = Buf("vx")
            ab = sb("abb", [128, 2, 512], BF16); bab = Buf("abb")
            gcg = sb("gcgb", [128, 2, 512], BF16); bgcg = Buf("gcgb")
            hl = sb("hl", [128, 4, 512], F32); bhl = Buf("hl")
            gt = [sb(f"gt{i}", [128, 4, 512], BF16) for i in range(2)]; bgt = [Buf(f"gt{i}") for i in range(2)]
            x1t = sb("x1t", [128, 8, 512], F32); bx1 = Buf("x1t")
            yT = sb("yT", [128, 8, 512], BF16); byT = Buf("yT")
            mT = sb("mT", [128, 8, 512], BF16); bmT = Buf("mT")
            macc = sb("macc", [128, 512], F32); bmacc = Buf("macc")
            mtmp = [sb(f"mtmp{i}", [128, 512], F32) for i in range(2)]; bmtmp = [Buf(f"mtmp{i}") for i in range(2)]
            pex = [sb(f"pex{i}", [128, 9 * 128], BF16) for i in range(3)]; bpex = [Buf(f"pex{i}") for i in range(3)]
            rd = [sb(f"rd{i}", [128, 128], F32) for i in range(3)]; brd = [Buf(f"rd{i}") for i in range(3)]
            ct1 = sb("ct1", [128, 512], F32); bct1 = Buf("ct1")
            ct2 = sb("ct2", [128, 512], F32); bct2 = Buf("ct2")
            ones64 = cmat(K, "ones1")[:, 0:64]
            actr = [0]

            def attn(qT, bq_, qc, qbase, cols, local, ctxk, mask_ops, sink_col, ych, ob):
                chunks = local + ctxk
                n = len(chunks)

                def stage1(i):
                    banks = [(K.ps[2 * i], K.bps[2 * i]), (K.ps[2 * i + 1], K.bps[2 * i + 1])]
                    pe_, bpe = pex[i], bpex[i]
                    for g0 in range(0, n, 4):
                        ps, bps_ = banks[(g0 // 4) % 2]
                        grp = chunks[g0:g0 + 4]
                        for jj, (kap, bk_, qap, bqq, vap, bv_) in enumerate(grp):
                            P.op("pe", lambda: nc.tensor.matmul(ps[:, jj * 128:(jj + 1) * 128], kap, qap, start=True, stop=True),
                                 reads=[bk_, bqq], writes=[bps_])
                        P.op("act", lambda: nc.scalar.activation(pe_[:, g0 * 128:(g0 + len(grp)) * 128], ps[:, 0:len(grp) * 128], AF.Exp),
                             reads=[bps_], writes=[bpe])
                    for (eng, j0, j1, map_, bm_) in mask_ops:
                        if eng == "pool":
                            P.op("pool", lambda: nc.gpsimd.tensor_tensor(pe_[:, j0 * 128:j1 * 128], pe_[:, j0 * 128:j1 * 128], map_, ALU.mult), reads=[bpe, bm_], writes=[bpe])
                        else:
                            P.op("dve", lambda: nc.vector.tensor_tensor(pe_[:, j0 * 128:j1 * 128], pe_[:, j0 * 128:j1 * 128], map_, ALU.mult), reads=[bpe, bm_], writes=[bpe])

                def stage2(i):
                    pe_, bpe = pex[i], bpex[i]
                    psO, bO = K.ps[6 + (i % 2)], K.bps[6 + (i % 2)]
                    for j, (kap, bk_, qap, bqq, vap, bv_) in enumerate(chunks):
                        P.op("pe", lambda: nc.tensor.matmul(psO[ob:ob + 64, 0:128], vap, pe_[:, j * 128:(j + 1) * 128], start=(j == 0), stop=(j == n - 1)),
                             reads=[bv_, bpe], writes=[bO], sig=False)
                    for j in range(n):
                        P.op("pe", lambda: nc.tensor.matmul(psO[ob:ob + 64, 256:384], ones64, pe_[:, j * 128:(j + 1) * 128], start=(j == 0), stop=(j == n - 1)),
                             reads=[K.bcm, bpe], writes=[bO], sig=(j == n - 1))
                    r_, br_ = rd[i], brd[i]
                    if sink_col is not None:
                        P.op("dve", lambda: nc.vector.tensor_scalar(r_[ob:ob + 64, :], psO[ob:ob + 64, 256:384], sink_col, None, ALU.add),
                             reads=[bO, besink], writes=[br_])
                        P.op("dve", lambda: nc.vector.reciprocal(r_[ob:ob + 64, :], r_[ob:ob + 64, :]), reads=[br_], writes=[br_])
                    else:
                        P.op("dve", lambda: nc.vector.reciprocal(r_[ob:ob + 64, :], psO[ob:ob + 64, 256:384]), reads=[bO], writes=[br_])
                    P.op("dve", lambda: nc.vector.tensor_tensor(yT[ob:ob + 64, ych, cols], psO[ob:ob + 64, 0:128], r_[ob:ob + 64, :], ALU.mult),
                         reads=[bO, br_], writes=[byT])

                items.append((stage1, stage2))

            def run_items():
                for j in range(min(2, len(items))):
                    items[j][0](j % 3)
                for i in range(len(items)):
                    if i + 2 < len(items):
                        items[i + 2][0]((i + 2) % 3)
                    items[i][1](i % 3)
                del items[:]

            items = []
            x1s = x1_d.rearrange("(c p) t -> p c t", p=128)
            x2s = x2_d.rearrange("(c p) t -> p c t", p=128)
            v3 = lambda ap: ap.rearrange("(c p) t -> p c t", p=128)
            g4 = g_d.rearrange("(n k p) t -> p n k t", p=128, k=8)
            gctr = [0]
            for (start, N, col) in tiles:
                lat = start < T
                sl = slice(start, start + N)
                P.dma("sp", x1t[:, :, :N], x1s[:, :, sl], writes=[bx1], sb=bx1)
                P.dma("sp", qpl[:, :, :N], v3(qpl_d)[:, :, sl], writes=[bqpl], sb=bqpl)
                P.dma("sp", dq[:, :, :N], v3(dq_d)[:, :, sl], writes=[bdq], sb=bdq)
                P.dma("sp", ab[:, :, :N], v3(ab_d)[:, :, sl], writes=[bab], sb=bab)
                P.dma("sp", gcg[:, :, :N], v3(gcg_d)[:, :, sl], writes=[bgcg], sb=bgcg)
                if lat:
                    P.dma("sp", qrot[:, :, :N], v3(qrot_d)[:, :, sl], writes=[bqrot], sb=bqrot)
                    P.dma("sp", vx[:, :, :N + 2], v3(vcv_d)[:, :, start:start + N + 2], writes=[bvx], sb=bvx)
                else:
                    P.dma("sp", vx[:, :, :N + 2], v3(vcvc_d)[:, :, 0:N + 2], writes=[bvx], sb=bvx)
                for c in range(2):
                    cw = lambda j: smv(K, "convaw", c * 3 + j)
                    P.op("dve", lambda: nc.vector.tensor_scalar(ct1[:, :N], vx[:, c, 0:N], cw(0), None, ALU.mult), reads=[bvx, K.bsm], writes=[bct1])
                    P.op("dve", lambda: nc.vector.scalar_tensor_tensor(ct1[:, :N], vx[:, c, 1:N + 1], cw(1), ct1[:, :N], ALU.mult, ALU.add),
                         reads=[bvx, K.bsm, bct1], writes=[bct1])
                    P.op("dve", lambda: nc.vector.scalar_tensor_tensor(ct1[:, :N], vx[:, c, 2:N + 2], cw(2), ct1[:, :N], ALU.mult, ALU.add),
                         reads=[bvx, K.bsm, bct1], writes=[bct1])
                    P.op("pool", lambda: nc.gpsimd.tensor_tensor(yT[:, 0 + c, :N], ct1[:, :N], ab[:, c, :N], ALU.mult), reads=[bct1, bab], writes=[byT])
                for c in range(2):
                    if lat:
                        for a_ in range(4):
                            P.dma("sp", hl[:, a_, :N], hl_d[a_][c * 128:(c + 1) * 128, sl], writes=[bhl], sb=bhl)
                    else:
                        P.dma("sp", hl[:, 0, :N], hcs_d[c * 128:(c + 1) * 128, 0:N], writes=[bhl], sb=bhl)
                    if lat:
                        P.op("dve", lambda: nc.vector.scalar_tensor_tensor(ct1[:, :N], hl[:, 1, :N], carry[:, c * 2:c * 2 + 1], hl[:, 0, :N], ALU.mult, ALU.add),
                             reads=[bhl, bcarry], writes=[bct1])
                        P.op("dve", lambda: nc.vector.scalar_tensor_tensor(ct2[:, :N], hl[:, 3, :N], carry[:, c * 2 + 1:c * 2 + 2], hl[:, 2, :N], ALU.mult, ALU.add),
                             reads=[bhl, bcarry], writes=[bct2])
                        P.op("pool", lambda: nc.gpsimd.tensor_tensor(ct1[:, :N], ct1[:, :N], ct2[:, :N], ALU.add), reads=[bct1, bct2], writes=[bct1])
                        P.op("pool", lambda: nc.gpsimd.tensor_tensor(yT[:, 4 + c, :N], ct1[:, :N], gcg[:, c, :N], ALU.mult), reads=[bct1, bgcg], writes=[byT])
                    else:
                        P.op("pool", lambda: nc.gpsimd.tensor_tensor(yT[:, 4 + c, :N], hl[:, 0, :N], gcg[:, c, :N], ALU.mult), reads=[bhl, bgcg], writes=[byT])
                for blk in range(N // 128):
                    cols = slice(blk * 128, (blk + 1) * 128)
                    n_ = (start // 128) + blk
                    for h in range(4):
                        kv = h // 2
                        qc, qb_ = h % 2, kv * 64
                        ob = (h % 2) * 64
                        ctxk = [(kc[qb_:qb_ + 64, j * 128:(j + 1) * 128], bkc, qpl[qb_:qb_ + 64, qc, cols], bqpl,
                                 vc[:, j, kv * 64:(kv + 1) * 64], bvc) for j in range(2)]
                        if lat:
                            local = [(kext[qb_:qb_ + 64, (n_ + j) * 128:(n_ + j + 1) * 128], bkext, qrot[qb_:qb_ + 64, qc, cols], bqrot,
                                      vext[:, n_ + j, kv * 64:(kv + 1) * 64], bvext) for j in range(3)]
                            mp = wmk[:, 2, :] if n_ == 0 else wmk[:, 0, :]
                            mn = wmk[:, 3, :] if n_ == 31 else wmk[:, 1, :]
                            mops = [("pool", 0, 1, mp, bwmk), ("pool", 2, 3, mn, bwmk)]
                        else:
                            local, mops = [], []
                        attn(None, None, qc, qb_, cols, local, ctxk, mops, esink[ob:ob + 64, h // 2:h // 2 + 1], 2 + h // 2, ob)
                    for h in range(4):
                        hc_, hb_ = h // 2, (h % 2) * 64
                        ob = hb_
                        ctxk = [(kdc[hb_:hb_ + 64, hc_, j * 128:(j + 1) * 128], bkdc, dq[hb_:hb_ + 64, hc_, cols], bdq,
                                 vdc[:, j, h * 64:(h + 1) * 64], bvdc) for j in range(2)]
                        if lat:
                            var = 0 if n_ == 0 else 1 if n_ == 1 else 3 if n_ == 30 else 4 if n_ == 31 else 2
                            jl = NAJ[var]
                            local = [(kdext[hb_:hb_ + 64, hc_, (n_ + j) * 128:(n_ + j + 1) * 128], bkdext, dq[hb_:hb_ + 64, hc_, cols], bdq,
                                      vdext[:, n_ + j, h * 64:(h + 1) * 64], bvdext) for j in jl]
                            e0 = NAOFF[var] + h * len(jl)
                            mops = [("dve", 0, len(jl), etab[:, e0 * 128:(e0 + len(jl)) * 128], betab)]
                        else:
                            local, mops = [], []
                        attn(None, None, None, None, cols, local, ctxk, mops, None, 6 + h // 2, ob)
                run_items()
                for dk in range(8):
                    gi_ = gctr[0] % 2
                    gctr[0] += 1
                    P.dma("sp", gt[gi_][:, :, :N], g4[:, :, dk, sl], writes=[bgt[gi_]], sb=bgt[gi_])
                    for n in range(4):
                        ps, bps_ = K.ps[(dk * 4 + n) % 4], K.bps[(dk * 4 + n) % 4]
                        for cc in range(2):
                            P.op("pe", lambda: nc.tensor.matmul(ps[:, :N], wbr[:, n * 2 + cc, dk * 128:(dk + 1) * 128], yT[:, n * 2 + cc, :N],
                                                                start=(cc == 0), stop=(cc == 1)),
                                 reads=[bwbr.at(dk * 128), byT], writes=[bps_], sig=(cc == 1))
                        if n == 0:
                            P.op("dve", lambda: nc.vector.tensor_tensor(macc[:, :N], ps[:, :N], gt[gi_][:, 0, :N], ALU.mult), reads=[bps_, bgt[gi_]], writes=[bmacc])
                        else:
                            mt_, bmt_ = mtmp[n % 2], bmtmp[n % 2]
                            P.op("dve", lambda: nc.vector.tensor_tensor(mt_[:, :N], ps[:, :N], gt[gi_][:, n, :N], ALU.mult), reads=[bps_, bgt[gi_]], writes=[bmt_])
                            if n < 3:
                                P.op("pool", lambda: nc.gpsimd.tensor_tensor(macc[:, :N], macc[:, :N], mt_[:, :N], ALU.add), reads=[bmacc, bmt_], writes=[bmacc])
                            else:
                                P.op("pool", lambda: nc.gpsimd.tensor_tensor(mT[:, dk, :N], macc[:, :N], mt_[:, :N], ALU.add), reads=[bmacc, bmt_], writes=[bmT])
                for dk in range(8):
                    ps, bps_ = K.ps[4 + dk % 4], K.bps[4 + dk % 4]
                    for k in range(8):
                        P.op("pe", lambda: nc.tensor.matmul(ps[:, :N], wout[:, k, dk * 128:(dk + 1) * 128], mT[:, k, :N], start=(k == 0), stop=(k == 7)),
                             reads=[bwout.at(dk * 128), bmT], writes=[bps_], sig=(k == 7))
                    P.op("dve", lambda: nc.vector.scalar_tensor_tensor(x1t[:, dk, :N], ps[:, :N], K.gateT[:, 1, dk, col:col + 1], x1t[:, dk, :N], ALU.mult, ALU.add),
                         reads=[bps_, bx1, K.bgs], writes=[bx1])
                P.dma("sp", x2s[:, :, sl], x1t[:, :, :N], reads=[bx1], sb=bx1)
            P.barrier()
        emit_ffn_phase(K, x2_d, x3_d, tiles, 2, wg_d, wu_d, wd_d)


def window_masks(q):
    j = np.arange(128)[:, None]
    i = np.arange(128)[None, :]
    tp = (j >= i).astype(np.float32)
    tn = (j <= i).astype(np.float32)
    z = np.zeros((128, 128), np.float32)
    return np.ascontiguousarray(np.stack([tp, tn, tp if q > 0 else z, tn if q < 3 else z], 1))


def na_index(gm, jl):
    rows_total = SEQ // GRID_W
    kk = np.arange(128)
    qq = np.arange(128)
    qrow = 2 * gm + qq // 64
    qcol = qq % 64
    n = len(jl)
    dr = np.zeros((n, 128, 128), np.int64)
    dc = np.zeros((n, 128, 128), np.int64)
    valid = np.zeros((n, 128, 128), bool)
    rstart = np.clip(qrow - 4, 0, rows_total - 8)
    cstart = np.clip(qcol - 8, 0, GRID_W - 16)
    for i, j in enumerate(jl):
        krow = (2 * gm - 4 + 2 * j + kk // 64)[:, None]
        kcol = (kk % 64)[:, None]
        v = (krow >= rstart[None, :]) & (krow < rstart[None, :] + 8) & (kcol >= cstart[None, :]) & (kcol < cstart[None, :] + 16)
        valid[i] = v
        dr[i] = np.clip(krow - qrow[None, :] + 7, 0, 14)
        dc[i] = np.clip(kcol - qcol[None, :], -15, 15) + 15
    return dr, dc, valid


def na_tables(rel_bias, q):
    out = np.empty((NATILES, 128, 128), np.float32)
    for v, m in enumerate((0, 1, 15, 30, 31)):
        gm = q * 32 + m
        jl = NAJ[v]
        dr, dc, valid = na_index(gm, jl)
        for h in range(4):
            o = NAOFF[v] + h * len(jl)
            out[o:o + len(jl)] = np.where(valid, rel_bias[h][dr, dc], np.float32(-30000.0))
    return out


XP = 1282
GROUPS = [[0, 1, 2, 3], [4, 5, 6, 7]]


def emit_exchange(K, I):
    nc, P, O = K.nc, K.P, I.O
    pack, packall, lpack, lall = I.pack.ap(), I.packall.ap(), I.lpack.ap(), I.lall.ap()
    bpk, bpa, blp, bla = Buf("pack"), Buf("packall"), Buf("lpack"), Buf("lall")
    v3 = lambda ap: ap.rearrange("(c p) t -> p c t", p=128)
    tv = lambda ap: ap.rearrange("(b p) f -> p b f", p=128)
    w3 = lambda ap, w: ap.rearrange("p (a w) -> p a w", w=w)
    kext, vext, kdext, vdext, vcv = O["kext"], O["vext"], O["kdext"], O["vdext"], O["vcvext"]
    for side in range(2):
        o = side * XP
        if side == 0:
            srcs = [kext[:, T:T + 128], vext[T:T + 128, :], v3(kdext)[:, :, T:T + 256], tv(vdext[T:T + 256, :]), v3(vcv)[:, :, T:T + 1]]
        else:
            srcs = [kext[:, 128:256], vext[128:256, :], v3(kdext)[:, :, 256:512], tv(vdext[256:512, :]), v3(vcv)[:, :, 1:2]]
        dsts = [pack[:, o:o + 128], pack[:, o + 128:o + 256], w3(pack[:, o + 256:o + 768], 256), w3(pack[:, o + 768:o + 1280], 256),
                w3(pack[:, o + 1280:o + 1282], 1)]
        for d_, s_ in zip(dsts, srcs):
            P.dma("sp", d_, s_, writes=[bpk], sb=bpk, allow_slow_non_contiguous=True)
    P.collective(I.pack.ap().opt(), I.packall.ap().opt(), GROUPS, reads=[bpk], writes=[bpa])
    with contextlib.ExitStack() as s2:
        sb = lambda name, shape, dt: s2.enter_context(nc.sbuf_tensor(uniq(name), shape, dt))
        cand = sb("cand", [128, 4, 2 * XP], BF16); bcand = Buf("cand")
        hl = sb("halo_l", [128, XP], BF16); bhl = Buf("halo_l")
        hr = sb("halo_r", [128, XP], BF16); bhr = Buf("halo_r")
        P.dma("sp", cand[:], packall.rearrange("(j p) x -> p j x", p=128), reads=[bpa], writes=[bcand], sb=bcand)
        for (dst, bdst, c0, sel) in ((hl, bhl, 0, "lsel"), (hr, bhr, XP, "rsel")):
            P.op("dve", lambda: nc.vector.tensor_scalar(dst[:], cand[:, 0, c0:c0 + XP], smv(K, sel, 0), None, ALU.mult),
                 reads=[bcand, K.bsm], writes=[bdst])
            for j in range(1, 4):
                P.op("dve", lambda: nc.vector.scalar_tensor_tensor(dst[:], cand[:, j, c0:c0 + XP], smv(K, sel, j), dst[:], ALU.mult, ALU.add),
                     reads=[bcand, K.bsm, bdst], writes=[bdst])
        for (src, bsrc, kc, vr, dk, dv, cv) in ((hl, bhl, slice(0, 128), slice(0, 128), slice(0, 256), slice(0, 256), slice(0, 1)),
                                              (hr, bhr, slice(128 + T, 256 + T), slice(128 + T, 256 + T), slice(256 + T, 512 + T),
                                               slice(256 + T, 512 + T), slice(T + 1, T + 2))):
            P.dma("sp", kext[:, kc], src[:, 0:128], reads=[bsrc], sb=bsrc)
            P.dma("sp", vext[vr, :], src[:, 128:256], reads=[bsrc], sb=bsrc)
            P.dma("sp", v3(kdext)[:, :, dk], w3(src[:, 256:768], 256), reads=[bsrc], sb=bsrc)
            P.dma("sp", tv(vdext[dv, :]), w3(src[:, 768:1280], 256), reads=[bsrc], sb=bsrc)
            P.dma("sp", v3(vcv)[:, :, cv], w3(src[:, 1280:1282], 1), reads=[bsrc], sb=bsrc, allow_slow_non_contiguous=True)
        P.dma("sp", lpack[:, :], K.lsum[:], reads=[K.bls], writes=[blp], sb=K.bls)
        P.collective(I.lpack.ap().opt(), I.lall.ap().opt(), GROUPS, reads=[blp], writes=[bla])
        lc = sb("lcand", [128, 4, 12], F32); blc = Buf("lcand")
        tt_ = sb("ltmp", [128, 2], F32); btt = Buf("ltmp")
        P.dma("sp", lc[:], lall.rearrange("(j p) x -> p j x", p=128), reads=[bla], writes=[blc], sb=blc)
        carry, bcarry = K.carry, K.bcarry
        for c in range(2):
            for d in range(2):
                cc = carry[:, c * 2 + d:c * 2 + d + 1]
                P.op("dve", lambda: nc.vector.tensor_copy(cc, K.lsum[:, c * 6 + 4 + d:c * 6 + 5 + d]), reads=[K.bls], writes=[bcarry])
                order = [0, 1, 2, 3] if d == 0 else [3, 2, 1, 0]
                sel = "fsel" if d == 0 else "bsel"
                for j in order:
                    a_ = lc[:, j, c * 6 + 2 * d:c * 6 + 2 * d + 1]
                    h_ = lc[:, j, c * 6 + 2 * d + 1:c * 6 + 2 * d + 2]
                    P.op("dve", lambda: nc.vector.scalar_tensor_tensor(tt_[:, 0:1], cc, a_, h_, ALU.mult, ALU.add),
                         reads=[bcarry, blc], writes=[btt])
                    P.op("dve", lambda: nc.vector.tensor_tensor(tt_[:, 1:2], tt_[:, 0:1], cc, ALU.subtract), reads=[btt, bcarry], writes=[btt])
                    P.op("dve", lambda: nc.vector.scalar_tensor_tensor(cc, tt_[:, 1:2], smv(K, sel, j), cc, ALU.mult, ALU.add),
                         reads=[btt, K.bsm, bcarry], writes=[bcarry])
        P.barrier()


def emit_xhalo(K, I):
    nc, P = K.nc, K.P
    xin1 = I.xin1
    xp, xall = I.xpack.ap(), I.xall.ap()
    bxp, bxa = Buf("xpack"), Buf("xall")
    v3 = lambda ap: ap.rearrange("(c p) t -> p c t", p=128)
    w3 = lambda ap, w: ap.rearrange("p (a w) -> p a w", w=w)
    P.dma("sp", w3(xp[:, 0:16], 2), v3(xin1)[:, :, T - 2:T], writes=[bxp], sb=bxp)
    P.dma("sp", w3(xp[:, 16:24], 1), v3(xin1)[:, :, 0:1], writes=[bxp], sb=bxp, allow_slow_non_contiguous=True)
    P.collective(I.xpack.ap().opt(), I.xall.ap().opt(), GROUPS, reads=[bxp], writes=[bxa])
    with contextlib.ExitStack() as s2:
        sb = lambda name, shape, dt: s2.enter_context(nc.sbuf_tensor(uniq(name), shape, dt))
        xc = sb("xcand", [128, 4, 24], F32); bxc = Buf("xcand")
        hx = sb("xhalo", [128, 32], F32); bhx = Buf("xhalo")
        P.dma("sp", xc[:], xall.rearrange("(j p) x -> p j x", p=128), reads=[bxa], writes=[bxc], sb=bxc)
        P.op("dve", lambda: nc.vector.memset(hx[:], 0.0), writes=[bhx])
        for (c0, n, o, sel) in ((0, 16, 0, "lsel"), (16, 8, 16, "rsel")):
            for j in range(4):
                P.op("dve", lambda: nc.vector.scalar_tensor_tensor(hx[:, o:o + n], xc[:, j, c0:c0 + n], smv(K, sel, j), hx[:, o:o + n], ALU.mult, ALU.add),
                     reads=[bxc, K.bsm, bhx], writes=[bhx])
        P.dma("sp", v3(xin1)[:, :, TT:TT + 2], w3(hx[:, 0:16], 2), reads=[bhx], sb=bhx)
        P.dma("sp", v3(xin1)[:, :, TT + 2:TT + 3], w3(hx[:, 16:24], 1), reads=[bhx], sb=bhx, allow_slow_non_contiguous=True)
        P.dma("sp", v3(xin1)[:, :, TT + 3:TT + 4], w3(hx[:, 24:32], 1), reads=[bhx], sb=bhx, allow_slow_non_contiguous=True)
        P.barrier()


INTERNALS = [("abT", [MW, TT], BF16), ("qrotT", [MW, T], BF16), ("qplT", [MW, TT], BF16),
             ("kext", [128, KEXT], BF16), ("vext", [KEXT, 128], BF16), ("kc", [128, CTX], BF16), ("vc", [CTX, 128], BF16),
             ("gcgT", [MW, TT], BF16), ("dqT", [MW, TT], BF16),
             ("kdext", [MW, KDEXT], BF16), ("vdext", [KDEXT, MW], BF16), ("kdc", [MW, CTX], BF16), ("vdc", [CTX, MW], BF16),
             ("vcvext", [MW, T + 2], BF16), ("vcvcext", [MW, CTX + 2], BF16), ("gT", [4 * D, TT], BF16),
             ("hlocF", [MW, T], F32), ("pcF", [MW, T], F32), ("hlocB", [MW, T], F32), ("pcB", [MW, T], F32),
             ("hcsum", [MW, CTX], F32), ("lsum", [128, 12], F32)]


def build_fused():
    nc = bass.Bass("TRN2", target_bir_lowering=False, dynamic_dma_scratch_size=4096)
    din = lambda n, sh, dt=F32: nc.dram_tensor(n, sh, dt, kind="ExternalInput").ap()
    dint = lambda n, sh, dt=F32: nc.dram_tensor(n, sh, dt, kind="Internal").ap()
    I = Ctx()
    xT_d = din("xT", [D, TX])
    I.cos_d = din("cosT", [128, T]); I.sin_d = din("sinT", [128, T])
    sm_d = [din(f"sm{l}", [128, NSM]) for l in range(2)]
    I.wmask = din("wmask", [128, 4, 128])
    natab_d = din("natab", [2, NATILES, 128, 128])
    wmod_d = din("w_mod", [2, D, 9 * D])
    wg_d = din("wg", [2, 2, D, FF]); wu_d = din("wu", [2, 2, D, FF]); wd_d = din("wd", [2, 2, FF, D])
    win_d = din("w_in", [2, D, INW])
    wbr_d = din("w_branch", [2, D, D]); wout_d = din("w_out", [2, D, D])
    out_d = nc.dram_tensor("outT", [D, T], F32, kind="ExternalOutput").ap()
    I.O = {n: dint(n, sh, dt) for n, sh, dt in INTERNALS}
    I.x1full = dint("x1full", [D, TX]); I.h2_d = dint("h2T", [D, TX], BF16)
    I.x2_d = dint("x2T", [D, TT]); I.xin1 = dint("xin1", [D, TX])
    I.pack = nc.dram_tensor("pack", [128, 2 * XP], BF16); I.packall = nc.dram_tensor("packall", [512, 2 * XP], BF16)
    I.lpack = nc.dram_tensor("lpack", [128, 12], F32); I.lall = nc.dram_tensor("lall", [512, 12], F32)
    I.xpack = nc.dram_tensor("xpack", [128, 24], F32); I.xall = nc.dram_tensor("xall", [512, 24], F32)

    with contextlib.ExitStack() as st:
        K = setup_common(nc, st)
        P = K.P
        K.lsum = K.sbuf("lsum_sb", [128, 12], F32); K.bls = Buf("lsum")
        K.carry = K.sbuf("carry", [128, 4], F32); K.bcarry = Buf("carry")
        zt = K.sbuf("zeros_sb", [128, 2, 1], BF16); bzt = Buf("zeros")
        P.op("dve", lambda: nc.vector.memset(zt[:], 0.0), writes=[bzt])
        vcc = I.O["vcvcext"].rearrange("(c p) t -> p c t", p=128)
        P.dma("sp", vcc[:, :, 0:1], zt[:], reads=[bzt], sb=bzt, allow_slow_non_contiguous=True)
        P.dma("sp", vcc[:, :, CTX + 1:CTX + 2], zt[:], reads=[bzt], sb=bzt, allow_slow_non_contiguous=True)
        for l in range(2):
            load_sm(K, sm_d[l][:, :])
            emit_mod(K, wmod_d[l])
            I.xin = xT_d if l == 0 else I.xin1
            I.wg1, I.wu1, I.wd1, I.win = wg_d[l, 0], wu_d[l, 0], wd_d[l, 0], win_d[l]
            I.wg2, I.wu2, I.wd2 = wg_d[l, 1], wu_d[l, 1], wd_d[l, 1]
            I.wbr, I.wout, I.natab = wbr_d[l], wout_d[l], natab_d[l]
            I.xout = I.xin1 if l == 0 else out_d
            emit_A(K, I)
            emit_exchange(K, I)
            emit_B(K, I, ctx_out=(l == 0))
            if l == 0:
                emit_xhalo(K, I)
        P.finish()
    return nc


_NC = []


def kernel(**inp):
    inp = {k: np.asarray(v) for k, v in inp.items()}
    if not _NC:
        _NC.append(build_fused())
    nc = _NC[0]
    x = np.asarray(inp["x"], np.float32)
    shared = {
        "w_mod": np.ascontiguousarray(inp["w_mod"], np.float32),
        "wg": np.ascontiguousarray(inp["ffn_w_gate"], np.float32), "wu": np.ascontiguousarray(inp["ffn_w_up"], np.float32),
        "wd": np.ascontiguousarray(inp["ffn_w_down"], np.float32), "w_in": np.ascontiguousarray(inp["w_in"], np.float32),
        "w_branch": np.ascontiguousarray(np.asarray(inp["w_branch"], np.float32).reshape(2, D, D)),
        "w_out": np.ascontiguousarray(inp["w_out"], np.float32),
    }
    in_maps = []
    for core in range(NC):
        b, q = core_bq(core)
        xs = x[b]
        lo, hi = q * T, (q + 1) * T
        halo = np.zeros((4, D), np.float32)
        if q > 0:
            halo[0] = xs[lo - 2]; halo[1] = xs[lo - 1]
        if q < 3:
            halo[2] = xs[hi]
        xT = np.ascontiguousarray(np.concatenate([xs[lo:hi], np.asarray(inp["ctx"][b], np.float32), halo], 0).T)
        cs, sn = rope_tables(q)
        m = {"xT": xT, "cosT": cs, "sinT": sn, "sm0": build_sm(inp, 0, b, q), "sm1": build_sm(inp, 1, b, q),
             "wmask": window_masks(q),
             "natab": np.stack([na_tables(np.asarray(inp["na_rel_bias"][l], np.float32), q) for l in range(2)], 0)}
        m.update(shared)
        in_maps.append(m)
    res = run_bass_kernel_spmd(nc, in_maps, core_ids=list(range(NC)))
    out = np.empty_like(x)
    for core in range(NC):
        b, q = core_bq(core)
        out[b, q * T:(q + 1) * T] = np.asarray(res.results[core]["outT"]).T
    return out
```

```python
import contextlib
import numpy as np
import ml_dtypes
import concourse.bass as bass
import concourse.mybir as mybir
from concourse.bass_utils import run_bass_kernel_spmd

F32 = mybir.dt.float32
BF16 = mybir.dt.bfloat16
AF = mybir.ActivationFunctionType
ALU = mybir.AluOpType
NPBF = ml_dtypes.bfloat16

D = 1024; KD = 8; FF = 2816; KF = 22; T = 4096; CTX = 256; TT = T + CTX; TX = TT + 4
MW = 256; INW = 6656; NC = 8; SEQ = 16384; GRID_W = 64
EPS = 1e-6


class Buf:
    __slots__ = ("name", "lw", "rd", "dsem")

    def __init__(self, name):
        self.name = name
        self.lw = None
        self.rd = []
        self.dsem = None


class Prog:
    ENG = ("pe", "dve", "act", "pool", "sp")

    def __init__(self, nc, stack):
        self.nc = nc
        self.stack = stack
        self.eng = {"pe": nc.tensor, "dve": nc.vector, "act": nc.scalar,
                    "pool": nc.gpsimd, "sp": nc.sync}
        self.sems = {}
        self.cnt = {}
        self.waited = {e: {} for e in self.ENG}
        self.esem = {}
        self.nsem = 0
        self.free_dsems = []
        self.dbufs = []
        self.allbufs = []
        self.pool_fifo = []
        for e in self.ENG:
            key = f"E{e}"
            self._alloc_sem(key)
            self.esem[e] = key
        self.n_inst = 0

    def _alloc_sem(self, key):
        h = self.stack.enter_context(self.nc.semaphore(f"s{self.nsem}_{key}"))
        self.nsem += 1
        self.sems[key] = h
        self.cnt[key] = 0
        return key

    def _need(self, e, deps, key, val):
        if self.waited[e].get(key, 0) >= val:
            return
        deps[key] = max(deps.get(key, 0), val)

    def _emit_waits(self, e, deps):
        eng = self.eng[e]
        for key, val in deps.items():
            eng.wait_ge(self.sems[key], val)
            self.waited[e][key] = val
            self.n_inst += 1

    def _collect(self, e, reads, writes, dma=False):
        deps = {}
        for b in reads:
            if b.lw is not None:
                self._need(e, deps, *b.lw)
        for b in writes:
            if b.lw is not None:
                if not (dma and b.lw[0] == b.dsem and not b.rd):
                    self._need(e, deps, *b.lw)
            for r in b.rd:
                self._need(e, deps, *r)
        return deps

    def _stamp(self, stamp, reads, writes):
        for b in list(reads) + list(writes):
            if b.lw is None and not b.rd:
                self.allbufs.append(b)
        for b in reads:
            b.rd.append(stamp)
            if len(b.rd) > 64:
                m = {}
                for k, v in b.rd:
                    m[k] = max(m.get(k, 0), v)
                b.rd = list(m.items())
        for b in writes:
            b.lw = stamp
            b.rd = []

    def op(self, e, fn, reads=(), writes=(), sig=True):
        deps = self._collect(e, reads, writes)
        if e == "pe":
            deps.pop(self.esem["pe"], None)
        self._emit_waits(e, deps)
        ins = fn()
        self.n_inst += 1
        key = self.esem[e]
        if sig:
            self.cnt[key] += 1
            ins.then_inc(self.sems[key], 1)
            stamp = (key, self.cnt[key])
        else:
            stamp = (key, self.cnt[key] + 1)
        self._stamp(stamp, reads, writes)
        return ins

    def dma(self, q, out_ap, in_ap, reads=(), writes=(), sb=None, ndesc=0, **kw):
        if q == "pool" and ndesc:
            while self.pool_fifo and sum(n for _, _, n in self.pool_fifo) + ndesc > 3000:
                key_, val_, _ = self.pool_fifo.pop(0)
                d_ = {}
                self._need("pool", d_, key_, val_)
                self._emit_waits("pool", d_)
        if sb.dsem is None:
            if self.free_dsems:
                sb.dsem = self.free_dsems.pop()
            else:
                sb.dsem = self._alloc_sem(f"D{self.nsem}")
            self.dbufs.append(sb)
        deps = self._collect(q, reads, writes, dma=True)
        for b in list(reads) + list(writes):
            if b.dsem is not None and b.dsem in deps:
                deps[b.dsem] = self.cnt[b.dsem]
        self._emit_waits(q, deps)
        ins = self.eng[q].dma_start(out=out_ap, in_=in_ap, **kw)
        self.n_inst += 1
        key = sb.dsem
        self.cnt[key] += 16
        ins.then_inc(self.sems[key], 16)
        self._stamp((key, self.cnt[key]), reads, writes)
        if q == "pool" and ndesc:
            self.pool_fifo.append((key, self.cnt[key], ndesc))
        return ins

    def barrier(self):
        for e in self.ENG:
            deps = {}
            for key, v in self.cnt.items():
                if v > 0 and key != self.esem[e]:
                    self._need(e, deps, key, v)
            if e != "pe" and self.cnt[self.esem[e]] > 0:
                self._need(e, deps, self.esem[e], self.cnt[self.esem[e]])
            self._emit_waits(e, deps)
        for b in self.allbufs:
            b.lw = None
            b.rd = []
        self.allbufs = []
        for b in self.dbufs:
            self.free_dsems.append(b.dsem)
            b.dsem = None
        self.dbufs = []
        self.pool_fifo = []
        for e in self.ENG:
            if self.cnt[self.esem[e]] > 20000:
                key = f"E{e}{self.nsem}"
                self._alloc_sem(key)
                self.esem[e] = key

    def collective(self, in_ap, out_ap, groups, reads=(), writes=()):
        deps = self._collect("pool", reads, writes)
        self._emit_waits("pool", deps)
        ins = self.nc.gpsimd.collective_compute("AllGather", ALU.bypass, replica_groups=groups, ins=[in_ap], outs=[out_ap])
        key = self._alloc_sem(f"C{self.nsem}")
        ins.then_inc(self.sems[key])
        self.cnt[key] = 1
        self.n_inst += 1
        self._stamp((key, 1), reads, writes)

    def finish(self):
        deps = {}
        for key, v in self.cnt.items():
            if v > 0:
                self._need("sp", deps, key, v)
        self._emit_waits("sp", deps)


SM_FIELDS = [("bmod", 72), ("normg", 24), ("bgate", 32), ("convaw", 6), ("lcw", 8), ("lcb", 2),
             ("lba", 4), ("lbx", 4), ("llam", 4), ("gqb", 1), ("gkb", 1), ("gqd", 1), ("gkd", 1),
             ("sink", 2), ("cvec", 16), ("hvalid", 4), ("eps", 1),
             ("onesD", 128), ("ones64", 128), ("rmat", 128), ("ones1", 128),
             ("wa00", 128), ("wa01", 128), ("wa10", 128), ("wa11", 128),
             ("wx00", 128), ("wx01", 128), ("wx10", 128), ("wx11", 128),
             ("lsel", 4), ("rsel", 4), ("fsel", 4), ("bsel", 4)]
SM_OFF = {}
_o = 0
for _n, _w in SM_FIELDS:
    SM_OFF[_n] = (_o, _w)
    _o += _w
NSM = _o


def chunked(v, nch):
    return np.ascontiguousarray(np.asarray(v, np.float32).reshape(nch, 128).T)


def blockdiag(a, b):
    m = np.zeros((128, 128), np.float32)
    m[:64, :64] = a
    m[64:, 64:] = b
    return m


def build_sm(inp, l, b, q):
    sm = np.zeros((128, NSM), np.float32)

    def put(name, arr):
        o, w = SM_OFF[name]
        sm[:, o:o + w] = np.asarray(arr, np.float32).reshape(128, w)

    put("bmod", chunked(inp["b_mod"][l], 72))
    put("normg", np.concatenate([chunked(inp["norm_g"][l, s], 8) for s in range(3)], 1))
    put("bgate", chunked(inp["b_gate"][l], 32))
    ca = inp["conv_a_w"][l]
    put("convaw", np.stack([ca[j, c * 128:(c + 1) * 128] for c in range(2) for j in range(3)], 1))
    lw = inp["lru_conv_w"][l]
    put("lcw", np.stack([lw[j, c * 128:(c + 1) * 128] for c in range(2) for j in range(4)], 1))
    put("lcb", chunked(inp["lru_conv_b"][l], 2))
    for nm, key in (("lba", "lru_b_a"), ("lbx", "lru_b_x"), ("llam", "lru_lam")):
        v = inp[key][l]
        put(nm, np.stack([v[d, c * 128:(c + 1) * 128] for d in range(2) for c in range(2)], 1))
    g = inp["qk_norm_g"][l]
    for i, nm in enumerate(("gqb", "gkb", "gqd", "gkd")):
        put(nm, np.tile(g[i], 2)[:, None])
    sk = inp["attn_sink"][l]
    put("sink", np.stack([np.repeat(sk[2 * c:2 * c + 2], 64) for c in range(2)], 1))
    cv = np.stack([inp["c"][b], inp["c_ctx"]], 1)
    put("cvec", cv.reshape(8, 128, 2).transpose(1, 0, 2).reshape(128, 16))
    hv = np.array([1.0 if q > 0 else 0.0, 1.0 if q > 0 else 0.0, 1.0 if q < 3 else 0.0, 0.0], np.float32)
    put("hvalid", np.tile(hv[None, :], (128, 1)))
    put("eps", np.full((128, 1), EPS, np.float32))
    put("onesD", np.full((128, 128), 1.0 / D, np.float32))
    put("ones64", blockdiag(np.full((64, 64), 1.0 / 64, np.float32), np.full((64, 64), 1.0 / 64, np.float32)))
    r = np.zeros((128, 128), np.float32)
    for m in range(128):
        if (m % 32) < 16:
            r[m + 16, m] = -1.0
        else:
            r[m - 16, m] = 1.0
    put("rmat", r)
    put("ones1", np.ones((128, 128), np.float32))
    put("lsel", np.tile(np.array([1.0 if j == q - 1 else 0.0 for j in range(4)], np.float32)[None], (128, 1)))
    put("rsel", np.tile(np.array([1.0 if j == q + 1 else 0.0 for j in range(4)], np.float32)[None], (128, 1)))
    put("fsel", np.tile(np.array([1.0 if j < q else 0.0 for j in range(4)], np.float32)[None], (128, 1)))
    put("bsel", np.tile(np.array([1.0 if j > q else 0.0 for j in range(4)], np.float32)[None], (128, 1)))
    for d in range(2):
        for c in range(2):
            put(f"wa{d}{c}", blockdiag(inp["lru_w_a"][l, d, 2 * c], inp["lru_w_a"][l, d, 2 * c + 1]))
            put(f"wx{d}{c}", blockdiag(inp["lru_w_x"][l, d, 2 * c], inp["lru_w_x"][l, d, 2 * c + 1]))
    return sm


def rope_tables(q):
    pos = q * T + np.arange(T)
    rows = (pos // GRID_W).astype(np.float32)
    cols = (pos % GRID_W).astype(np.float32)
    nf = 16
    inv_freq = (10000.0 ** (-np.arange(nf, dtype=np.float32) / nf)).astype(np.float32)
    p = np.arange(128)
    d = p % 64
    j = d % 16
    use_col = (d // 32) == 1
    posm = np.where(use_col[:, None], cols[None, :], rows[None, :]).astype(np.float32)
    ang = (posm * inv_freq[j][:, None]).astype(np.float32)
    return np.cos(ang).astype(np.float32), np.sin(ang).astype(np.float32)


class Ctx:
    pass


_UNIQ = [0]


def uniq(name):
    _UNIQ[0] += 1
    return f"{name}_sb{_UNIQ[0]}"


def smv(K, name, i=0, n=1):
    o, w = SM_OFF[name]
    return K.sm[:, o + i:o + i + n]


def setup_common(nc, st):
    K = Ctx()
    K.nc = nc
    K.st = st
    st.enter_context(nc.Block())
    K.P = Prog(nc, st)
    K.sbuf = lambda name, shape, dt: st.enter_context(nc.sbuf_tensor(uniq(name), shape, dt))
    K.ps = [st.enter_context(nc.psum_tensor(f"ps{i}", [128, 512], F32)) for i in range(8)]
    K.bps = [Buf(f"ps{i}") for i in range(8)]
    K.sm = K.sbuf("sm_sb", [128, NSM], F32)
    K.bsm = Buf("sm")
    names = ["onesD", "ones64", "rmat", "ones1"] + [f"w{a}{d}{c}" for a in "ax" for d in range(2) for c in range(2)]
    K.cm = K.sbuf("cm_sb", [128, len(names), 128], BF16)
    K.bcm = Buf("cm")
    K.cmi = {n: i for i, n in enumerate(names)}
    return K


def load_sm(K, sm_d):
    nc, P = K.nc, K.P
    P.dma("sp", K.sm[:], sm_d, writes=[K.bsm], sb=K.bsm)
    for n, i in K.cmi.items():
        o, w = SM_OFF[n]
        P.op("dve", lambda: nc.vector.tensor_copy(K.cm[:, i, :], K.sm[:, o:o + 128]), reads=[K.bsm], writes=[K.bcm])


def cmat(K, name):
    return K.cm[:, K.cmi[name], :]


def emit_mod(K, w_mod_d):
    nc, P = K.nc, K.P
    K.modT = K.sbuf("modT", [128, 72, 2], F32)
    K.bmod = Buf("modT")
    K.gsT = K.sbuf("gsT", [128, 3, 8, 2], F32)
    K.gateT = K.sbuf("gateT", [128, 3, 8, 2], F32)
    K.bgs = Buf("gs")
    with contextlib.ExitStack() as s2:
        wm = [s2.enter_context(nc.sbuf_tensor(uniq(f"wm{i}"), [128, 8, 1024], F32)) for i in range(2)]
        bwm = [Buf(f"wm{i}") for i in range(2)]
        sc = s2.enter_context(nc.sbuf_tensor(uniq("silu_c"), [128, 16], F32))
        bsc = Buf("silu_c")
        P.op("act", lambda: nc.scalar.activation(sc[:], smv(K, "cvec", 0, 16), AF.Silu), reads=[K.bsm], writes=[bsc])
        psm = K.ps[7]
        bpsm = K.bps[7]
        for i in range(9):
            s = i % 2
            for k in range(8):
                P.dma("sp", wm[s][:, k, :], w_mod_d[k * 128:(k + 1) * 128, i * 1024:(i + 1) * 1024],
                      writes=[bwm[s]], sb=bwm[s])
            for dk in range(8):
                cidx = i * 8 + dk
                for k in range(8):
                    P.op("pe", lambda: nc.tensor.matmul(psm[:, cidx * 2:cidx * 2 + 2], wm[s][:, k, dk * 128:(dk + 1) * 128],
                                                        sc[:, 2 * k:2 * k + 2], start=(k == 0), stop=(k == 7)),
                         reads=[bwm[s], bsc], writes=[bpsm], sig=(k == 7))
        psv = psm[:, 0:144].rearrange("p (c j) -> p c j", j=2)
        for j in range(2):
            P.op("dve", lambda: nc.vector.tensor_tensor(K.modT[:, :, j], psv[:, :, j], smv(K, "bmod", 0, 72), ALU.add),
                 reads=[bpsm, K.bsm], writes=[K.bmod])
        for s in range(3):
            for j in range(2):
                P.op("dve", lambda: nc.vector.tensor_scalar(K.gsT[:, s, :, j], K.modT[:, (3 * s + 1) * 8:(3 * s + 2) * 8, j],
                                                            1.0, None, ALU.add), reads=[K.bmod], writes=[K.bgs])
                P.op("dve", lambda: nc.vector.tensor_tensor(K.gsT[:, s, :, j], K.gsT[:, s, :, j],
                                                            smv(K, "normg", s * 8, 8), ALU.mult),
                     reads=[K.bgs, K.bsm], writes=[K.bgs])
                P.op("dve", lambda: nc.vector.tensor_scalar(K.gateT[:, s, :, j], K.modT[:, (3 * s + 2) * 8:(3 * s + 3) * 8, j],
                                                            (1.0 if s == 1 else 0.5), None, ALU.mult),
                     reads=[K.bmod], writes=[K.bgs])
        P.barrier()


def shift_col(K, s, k, j):
    return K.modT[:, 3 * s * 8 + k, j:j + 1]


def emit_adaln(K, xt, bxt, N, s, col, hT, bh, sq, bsq, rstd, brs, pst, bpst):
    nc, P = K.nc, K.P
    P.op("act", lambda: nc.scalar.activation(sq[:, :, :N], xt[:, :, :N], AF.Square), reads=[bxt], writes=[bsq])
    for k in range(8):
        P.op("pe", lambda: nc.tensor.matmul(pst[:, :N], cmat(K, "onesD"), sq[:, k, :N], start=(k == 0), stop=(k == 7)),
             reads=[bsq, K.bcm], writes=[bpst], sig=(k == 7))
    P.op("act", lambda: nc.scalar.activation(rstd[:, :N], pst[:, :N], AF.Sqrt, bias=smv(K, "eps"), scale=1.0),
         reads=[bpst, K.bsm], writes=[brs])
    P.op("dve", lambda: nc.vector.reciprocal(rstd[:, :N], rstd[:, :N]), reads=[brs], writes=[brs])
    for k in range(8):
        P.op("dve", lambda: nc.vector.scalar_tensor_tensor(K.tmp32[:, :N], xt[:, k, :N], K.gsT[:, s, k, col:col + 1],
                                                           rstd[:, :N], ALU.mult, ALU.mult),
             reads=[bxt, brs, K.bgs], writes=[K.btmp32])
        P.op("act", lambda: nc.scalar.activation(hT[:, k, :N], K.tmp32[:, :N], AF.Identity,
                                                 bias=shift_col(K, s, k, col), scale=1.0),
             reads=[K.btmp32, K.bmod], writes=[bh])


class WBufs:
    def __init__(self, gw):
        self.gw = gw
        self.bufs = []

    def at(self, col):
        return self.bufs[col // self.gw]


def load_weight_cast(K, dst, wb, src, g0, g1):
    b = Buf("wgrp")
    wb.bufs.append(b)
    K.P.dma("pool", dst[:, :, g0:g1], src.rearrange("(k p) f -> p k f", p=128)[:, :, g0:g1], writes=[b], sb=b,
            ndesc=128 * dst.shape[1])


def load_weight_groups(K, specs):
    pos = [0] * len(specs)
    done = False
    while not done:
        done = True
        for i, (dst, wb, src, width) in enumerate(specs):
            if pos[i] < width:
                g1 = min(width, pos[i] + wb.gw)
                load_weight_cast(K, dst, wb, src, pos[i], g1)
                pos[i] = g1
                done = False


def emit_ffn_phase(K, x_src, x_dst, tiles, s, wg_d, wu_d, wd_d, h_dst=None, s_next=None):
    nc, P = K.nc, K.P
    post = h_dst is not None
    with contextlib.ExitStack() as s2:
        sb = lambda name, shape, dt: s2.enter_context(nc.sbuf_tensor(uniq(name), shape, dt))
        wg = sb("wg", [128, KD, FF], BF16); bwg = WBufs(256)
        wu = sb("wu", [128, KD, FF], BF16); bwu = WBufs(256)
        wd = sb("wd", [128, KF, D], BF16); bwd = WBufs(256)
        load_weight_groups(K, [(wg, bwg, wg_d, FF), (wu, bwu, wu_d, FF)])
        load_weight_groups(K, [(wd, bwd, wd_d, D)])
        xts = [sb(f"xt{i}", [128, 8, 512], F32) for i in range(2)]; bxts = [Buf(f"xt{i}") for i in range(2)]
        hT = sb("hT", [128, 8, 512], BF16); bh = Buf("hT")
        act = sb("actT", [128, KF, 512], BF16); bact = Buf("actT")
        rstd = sb("rstd", [128, 512], F32); brs = Buf("rstd")
        K.tmp32 = sb("tmp32", [128, 512], F32); K.btmp32 = Buf("tmp32")
        sg, bsg = K.tmp32, K.btmp32
        if post:
            h2o = sb("h2o", [128, 8, 512], BF16); bh2o = Buf("h2o")
            rstd2 = sb("rstd2", [128, 512], F32); brs2 = Buf("rstd2")
        xs = x_src.rearrange("(c p) t -> p c t", p=128)
        xd = x_dst.rearrange("(c p) t -> p c t", p=128)
        hd = h_dst.rearrange("(c p) t -> p c t", p=128) if post else None
        n = len(tiles)
        gi = [0]

        def load(i):
            start, N, col = tiles[i]
            P.dma("sp", xts[i % 2][:, :, :N], xs[:, :, start:start + N], writes=[bxts[i % 2]], sb=bxts[i % 2])

        def adaln(i):
            start, N, col = tiles[i]
            emit_adaln(K, xts[i % 2], bxts[i % 2], N, s, col, hT, bh, hT, bh, rstd, brs, K.ps[6], K.bps[6])

        def gateup(i):
            start, N, col = tiles[i]
            for f in range(KF):
                pg, bpg = K.ps[gi[0] % 2], K.bps[gi[0] % 2]
                pu, bpu = K.ps[2 + gi[0] % 2], K.bps[2 + gi[0] % 2]
                gi[0] += 1
                for k in range(8):
                    P.op("pe", lambda: nc.tensor.matmul(pg[:, :N], wg[:, k, f * 128:(f + 1) * 128], hT[:, k, :N],
                                                        start=(k == 0), stop=(k == 7)),
                         reads=[bwg.at(f * 128), bh], writes=[bpg], sig=(k == 7))
                for k in range(8):
                    P.op("pe", lambda: nc.tensor.matmul(pu[:, :N], wu[:, k, f * 128:(f + 1) * 128], hT[:, k, :N],
                                                        start=(k == 0), stop=(k == 7)),
                         reads=[bwu.at(f * 128), bh], writes=[bpu], sig=(k == 7))
                P.op("act", lambda: nc.scalar.activation(sg[:, :N], pg[:, :N], AF.Silu), reads=[bpg], writes=[bsg])
                P.op("dve", lambda: nc.vector.tensor_tensor(act[:, f, :N], pu[:, :N], sg[:, :N], ALU.mult),
                     reads=[bpu, bsg], writes=[bact])

        def down(i, d0, d1):
            start, N, col = tiles[i]
            xt, bxt = xts[i % 2], bxts[i % 2]
            for dk in range(d0, d1):
                py, bpy = K.ps[4 + dk % 2], K.bps[4 + dk % 2]
                for f in range(KF):
                    P.op("pe", lambda: nc.tensor.matmul(py[:, :N], wd[:, f, dk * 128:(dk + 1) * 128], act[:, f, :N],
                                                        start=(f == 0), stop=(f == KF - 1)),
                         reads=[bwd.at(dk * 128), bact], writes=[bpy], sig=(f == KF - 1))
                P.op("dve", lambda: nc.vector.scalar_tensor_tensor(xt[:, dk, :N], py[:, :N], K.gateT[:, s, dk, col:col + 1],
                                                                   xt[:, dk, :N], ALU.mult, ALU.add),
                     reads=[bpy, bxt, K.bgs], writes=[bxt])
            if d1 == 8:
                P.dma("sp", xd[:, :, start:start + N], xt[:, :, :N], reads=[bxt], sb=bxt)

        def postadaln(i):
            start, N, col = tiles[i]
            emit_adaln(K, xts[i % 2], bxts[i % 2], N, s_next, col, h2o, bh2o, h2o, bh2o, rstd2, brs2, K.ps[7], K.bps[7])
            P.dma("sp", hd[:, :, start:start + N], h2o[:, :, :N], reads=[bh2o], sb=bh2o)

        load(0)
        if n > 1:
            load(1)
        adaln(0)
        for i in range(n):
            gateup(i)
            if i >= 1:
                if post:
                    postadaln(i - 1)
                if i + 1 < n:
                    load(i + 1)
            down(i, 0, 4)
            if i + 1 < n:
                adaln(i + 1)
            down(i, 4, 8)
        if post:
            postadaln(n - 1)
        P.barrier()


LAT_TILES = [(i * 512, 512, 0) for i in range(8)]
CTX_TILE = (T, CTX, 1)
HALO_TILE = (TT, 4, 0)


def emit_A(K, I):
    nc, P = K.nc, K.P
    xT_d, cos_d, sin_d = I.xin, I.cos_d, I.sin_d
    wg_d, wu_d, wd_d, win_d = I.wg1, I.wu1, I.wd1, I.win
    O = I.O
    x1full, h2_d = I.x1full, I.h2_d
    with contextlib.ExitStack() as sA:
        K_sbuf = lambda name, shape, dt: sA.enter_context(nc.sbuf_tensor(uniq(name), shape, dt))
        tiles = LAT_TILES + [CTX_TILE, HALO_TILE]
        emit_ffn_phase(K, xT_d, x1full, tiles, 0, wg_d, wu_d, wd_d, h_dst=h2_d, s_next=1)

        cxf = K_sbuf("cxf", [128, 2, T + 4], F32); bcxf = Buf("cxf")
        cxc = K_sbuf("cxc", [128, 2, CTX + 4], F32); bcxc = Buf("cxc")
        P.op("pool", lambda: nc.gpsimd.memset(cxc[:], 0.0), writes=[bcxc])
        gq = K_sbuf("gq", [128, 2], F32); bgq = Buf("gq")
        P.op("dve", lambda: nc.vector.tensor_scalar(gq[:, 0:1], smv(K, "gqb"), 0.125, None, ALU.mult), reads=[K.bsm], writes=[bgq])
        P.op("dve", lambda: nc.vector.tensor_scalar(gq[:, 1:2], smv(K, "gqd"), 0.125, None, ALU.mult), reads=[K.bsm], writes=[bgq])

        with contextlib.ExitStack() as s2:
            sb = lambda name, shape, dt: s2.enter_context(nc.sbuf_tensor(uniq(name), shape, dt))
            win = sb("win", [128, KD, INW], BF16); bwin = WBufs(512)
            load_weight_groups(K, [(win, bwin, win_d, INW)])
            h2t = sb("h2t", [128, 8, 512], BF16); bh2 = Buf("h2t")
            cst = sb("cost", [128, 512], F32); snt = sb("sint", [128, 512], F32); bcs = Buf("cossin")
            axs = sb("axs", [128, 2, 512], F32); baxs = Buf("axs")
            abo = sb("abo", [128, 2, 512], BF16); babo = Buf("abo")
            vo = sb("vo", [128, 2, 512], BF16); bvo = Buf("vo")
            qpl = sb("qpl", [128, 2, 512], BF16); bqpl = Buf("qpl")
            qro = sb("qro", [128, 2, 512], BF16); bqro = Buf("qro")
            ko = sb("ko", [128, 512], BF16); bko = Buf("ko")
            ktmp = sb("ktmp", [128, 512], BF16); bktmp = Buf("ktmp")
            vtok = sb("vtok", [128, 4, 384], BF16); bvtok = Buf("vtok")
            gcg = sb("gcg", [128, 2, 512], BF16); bgcg = Buf("gcg")
            dqo = sb("dqo", [128, 2, 512], BF16); bdqo = Buf("dqo")
            dko = sb("dko", [128, 2, 512], BF16); bdko = Buf("dko")
            gto = sb("gto", [128, 8, 512], BF16); bgto = Buf("gto")
            sqb = [sb(f"sqb{i}", [128, 512], BF16) for i in range(2)]; bsqb = [Buf(f"sqb{i}") for i in range(2)]
            rs = [sb(f"rs{i}", [128, 512], F32) for i in range(2)]; brs = [Buf(f"rs{i}") for i in range(2)]
            qh = [sb(f"qh{i}", [128, 512], F32) for i in range(2)]; bqh = [Buf(f"qh{i}") for i in range(2)]
            t1 = [sb(f"t1{i}", [128, 512], F32) for i in range(2)]; bt1 = [Buf(f"t1{i}") for i in range(2)]
            t2 = [sb(f"t2{i}", [128, 512], F32) for i in range(2)]; bt2 = [Buf(f"t2{i}") for i in range(2)]
            bout = Buf("aout")
            h2s = h2_d.rearrange("(c p) t -> p c t", p=128)
            v3 = lambda ap: ap.rearrange("(c p) t -> p c t", p=128)
            ov = {n: v3(O[n]) for n in ("abT", "gcgT", "dqT", "gT")}
            pctr = [0]

            def proj(c, N):
                i = (0, 1, 2, 6, 7)[pctr[0] % 5]
                pctr[0] += 1
                ps, bps = K.ps[i], K.bps[i]
                for k in range(8):
                    P.op("pe", lambda: nc.tensor.matmul(ps[:, :N], win[:, k, c * 128:(c + 1) * 128], h2t[:, k, :N],
                                                        start=(k == 0), stop=(k == 7)),
                         reads=[bwin.at(c * 128), bh2], writes=[bps], sig=(k == 7))
                step_pending()
                return ps, bps

            pending = []

            def step_pending():
                for g in list(pending):
                    try:
                        next(g)
                    except StopIteration:
                        pending.remove(g)

            def drain():
                while pending:
                    step_pending()

            qn = [0]

            def qknorm(ps, bps, N, gcol, bg, out_plain, bop, rope_out=None, bro=None, after=None):
                i = qn[0] % 2
                qn[0] += 1
                sqb_, bsqb_ = sqb[i], bsqb[i]
                rs_, brs_ = rs[i], brs[i]
                qh_, bqh_ = qh[i], bqh[i]
                t1_, bt1_ = t1[i], bt1[i]
                t2_, bt2_ = t2[i], bt2[i]
                pst, bpst = K.ps[3], K.bps[3]
                P.op("act", lambda: nc.scalar.activation(sqb_[:, :N], ps[:, :N], AF.Square), reads=[bps], writes=[bsqb_])
                yield
                P.op("pe", lambda: nc.tensor.matmul(pst[:, :N], cmat(K, "ones64"), sqb_[:, :N], start=True, stop=True),
                     reads=[bsqb_, K.bcm], writes=[bpst])
                P.op("act", lambda: nc.scalar.activation(rs_[:, :N], pst[:, :N], AF.Sqrt, bias=smv(K, "eps"), scale=1.0),
                     reads=[bpst, K.bsm], writes=[brs_])
                P.op("dve", lambda: nc.vector.reciprocal(rs_[:, :N], rs_[:, :N]), reads=[brs_], writes=[brs_])
                P.op("dve", lambda: nc.vector.scalar_tensor_tensor(qh_[:, :N], ps[:, :N], gcol, rs_[:, :N], ALU.mult, ALU.mult),
                     reads=[bps, brs_, bg], writes=[bqh_])
                P.op("act", lambda: nc.scalar.copy(out_plain[:, :N], qh_[:, :N]), reads=[bqh_], writes=[bop])
                if rope_out is not None:
                    yield
                    psr, bpsr = K.ps[4], K.bps[4]
                    P.op("pe", lambda: nc.tensor.matmul(psr[:, :N], cmat(K, "rmat"), out_plain[:, :N], start=True, stop=True),
                         reads=[bop, K.bcm], writes=[bpsr])
                    P.op("pool", lambda: nc.gpsimd.tensor_tensor(t1_[:, :N], qh_[:, :N], cst[:, :N], ALU.mult),
                         reads=[bqh_, bcs], writes=[bt1_])
                    P.op("dve", lambda: nc.vector.tensor_tensor(t2_[:, :N], psr[:, :N], snt[:, :N], ALU.mult),
                         reads=[bpsr, bcs], writes=[bt2_])
                    P.op("pool", lambda: nc.gpsimd.tensor_tensor(rope_out[:, :N], t1_[:, :N], t2_[:, :N], ALU.add),
                         reads=[bt1_, bt2_], writes=[bro])
                if after is not None:
                    after()

            def qk(*a, **kw):
                g = qknorm(*a, **kw)
                next(g)
                pending.append(g)

            for (start, N, col) in tiles:
                kind = "lat" if start < T else ("ctx" if start == T else "halo")
                P.dma("sp", h2t[:, :, :N], h2s[:, :, start:start + N], writes=[bh2], sb=bh2)
                if kind == "halo":
                    for c in range(2):
                        ps, bps = proj(10 + c, N)
                        P.op("dve", lambda: nc.vector.tensor_tensor(cxf[:, c, 0:2], ps[:, 0:2], smv(K, "hvalid", 0, 2), ALU.mult),
                             reads=[bps, K.bsm], writes=[bcxf])
                        P.op("dve", lambda: nc.vector.tensor_tensor(cxf[:, c, T + 2:T + 4], ps[:, 2:4], smv(K, "hvalid", 2, 2), ALU.mult),
                             reads=[bps, K.bsm], writes=[bcxf])
                    continue
                lat = kind == "lat"
                if lat:
                    P.dma("sp", cst[:, :N], cos_d[:, start:start + N], writes=[bcs], sb=bcs)
                    P.dma("sp", snt[:, :N], sin_d[:, start:start + N], writes=[bcs], sb=bcs)
                sl = slice(start, start + N)
                for c in range(2):
                    ps, bps = proj(0 + c, N)
                    P.op("act", lambda: nc.scalar.copy(axs[:, c, :N], ps[:, :N]), reads=[bps], writes=[baxs])
                for c in range(2):
                    ps, bps = proj(2 + c, N)
                    P.op("act", lambda: nc.scalar.copy(abo[:, c, :N], ps[:, :N]), reads=[bps], writes=[babo])
                for c in range(2):
                    ps, bps = proj(4 + c, N)
                    P.op("dve", lambda: nc.vector.tensor_tensor(vo[:, c, :N], ps[:, :N], axs[:, c, :N], ALU.mult),
                         reads=[bps, baxs], writes=[bvo])
                P.dma("sp", ov["abT"][:, :, sl], abo[:, :, :N], reads=[babo], sb=babo)
                if lat:
                    P.dma("sp", v3(O["vcvext"])[:, :, 1 + start:1 + start + N], vo[:, :, :N], reads=[bvo], sb=bvo)
                else:
                    P.dma("sp", v3(O["vcvcext"])[:, :, 1:1 + N], vo[:, :, :N], reads=[bvo], sb=bvo)
                def store_q(c, sl=sl, N=N, lat=lat):
                    for hh in range(2):
                        r0 = hh * 128 + c * 64
                        P.dma("sp", O["qplT"][r0:r0 + 64, sl], qpl[hh * 64:(hh + 1) * 64, c, :N], reads=[bqpl], sb=bqpl)
                        if lat:
                            P.dma("sp", O["qrotT"][r0:r0 + 64, sl], qro[hh * 64:(hh + 1) * 64, c, :N], reads=[bqro], sb=bqro)
                for c in range(2):
                    ps, bps = proj(6 + c, N)
                    qk(ps, bps, N, gq[:, 0:1], bgq, qpl[:, c, :], bqpl, qro[:, c, :] if lat else None, bqro,
                       after=(lambda c=c, f=store_q: f(c)))
                ps, bps = proj(8, N)
                if lat:
                    qk(ps, bps, N, smv(K, "gkb"), K.bsm, ktmp, bktmp, ko, bko,
                       after=(lambda start=start, N=N: P.dma("sp", O["kext"][:, 128 + start:128 + start + N], ko[:, :N], reads=[bko], sb=bko)))
                else:
                    qk(ps, bps, N, smv(K, "gkb"), K.bsm, ko, bko,
                       after=(lambda N=N: P.dma("sp", O["kc"][:, 0:N], ko[:, :N], reads=[bko], sb=bko)))
                psv, bpsv = K.ps[5], K.bps[5]
                nb = N // 128
                for tb in range(nb):
                    for k in range(8):
                        P.op("pe", lambda: nc.tensor.matmul(psv[:, 0:128], h2t[:, k, tb * 128:(tb + 1) * 128], win[:, k, 9 * 128:10 * 128],
                                                            start=(k == 0), stop=(k == 7)),
                             reads=[bwin.at(9 * 128), bh2], writes=[bpsv], sig=(k == 7))
                    for k in range(8):
                        P.op("pe", lambda: nc.tensor.matmul(psv[:, 128:384], h2t[:, k, tb * 128:(tb + 1) * 128], win[:, k, 18 * 128:20 * 128],
                                                            start=(k == 0), stop=(k == 7)),
                             reads=[bwin.at(18 * 128), bh2], writes=[bpsv], sig=(k == 7))
                    P.op("act", lambda: nc.scalar.copy(vtok[:, tb, :], psv[:, 0:384]), reads=[bpsv], writes=[bvtok])
                tv = lambda ap: ap.rearrange("(b p) f -> p b f", p=128)
                if lat:
                    P.dma("sp", tv(O["vext"][128 + start:128 + start + N, :]), vtok[:, 0:nb, 0:128], reads=[bvtok], sb=bvtok)
                    P.dma("sp", tv(O["vdext"][256 + start:256 + start + N, :]), vtok[:, 0:nb, 128:384], reads=[bvtok], sb=bvtok)
                else:
                    P.dma("sp", tv(O["vc"][0:N, :]), vtok[:, 0:nb, 0:128], reads=[bvtok], sb=bvtok)
                    P.dma("sp", tv(O["vdc"][0:N, :]), vtok[:, 0:nb, 128:384], reads=[bvtok], sb=bvtok)
                for c in range(2):
                    ps, bps = proj(10 + c, N)
                    if lat:
                        P.op("act", lambda: nc.scalar.copy(cxf[:, c, 2 + start:2 + start + N], ps[:, :N]), reads=[bps], writes=[bcxf])
                    else:
                        P.op("act", lambda: nc.scalar.copy(cxc[:, c, 2:2 + N], ps[:, :N]), reads=[bps], writes=[bcxc])
                for c in range(2):
                    ps, bps = proj(12 + c, N)
                    P.op("act", lambda: nc.scalar.activation(gcg[:, c, :N], ps[:, :N], AF.Gelu_apprx_tanh), reads=[bps], writes=[bgcg])
                P.dma("sp", ov["gcgT"][:, :, sl], gcg[:, :, :N], reads=[bgcg], sb=bgcg)
                for c in range(2):
                    ps, bps = proj(14 + c, N)
                    qk(ps, bps, N, gq[:, 1:2], bgq, dqo[:, c, :], bdqo,
                       after=(lambda c=c, sl=sl, N=N: P.dma("sp", ov["dqT"][:, c, sl], dqo[:, c, :N], reads=[bdqo], sb=bdqo)))
                for c in range(2):
                    ps, bps = proj(16 + c, N)
                    if lat:
                        aft = (lambda c=c, start=start, N=N: P.dma("sp", v3(O["kdext"])[:, c, 256 + start:256 + start + N], dko[:, c, :N], reads=[bdko], sb=bdko))
                    else:
                        aft = (lambda c=c, N=N: P.dma("sp", v3(O["kdc"])[:, c, 0:N], dko[:, c, :N], reads=[bdko], sb=bdko))
                    qk(ps, bps, N, smv(K, "gkd"), K.bsm, dko[:, c, :], bdko, after=aft)
                for c in range(32):
                    ps, bps = proj(20 + c, N)
                    P.op("act", lambda: nc.scalar.activation(gto[:, c % 8, :N], ps[:, :N], AF.Sigmoid, bias=smv(K, "bgate", c), scale=1.0),
                         reads=[bps, K.bsm], writes=[bgto])
                    if c % 8 == 7:
                        c0 = c - 7
                        P.dma("sp", ov["gT"][:, c0:c0 + 8, sl], gto[:, :, :N], reads=[bgto], sb=bgto)
                drain()
            P.barrier()

        with contextlib.ExitStack() as s2:
            sb = lambda name, shape, dt: s2.enter_context(nc.sbuf_tensor(uniq(name), shape, dt))
            u = sb("lru_u", [128, T], F32); bu = Buf("lru_u")
            ub = sb("lru_ub", [128, T], BF16); bub = Buf("lru_ub")
            ra = sb("lru_a", [128, T], F32); bra = Buf("lru_a")
            ib = sb("lru_b", [128, T], F32); bib = Buf("lru_b")
            tm = sb("lru_t", [128, T], F32); btm = Buf("lru_t")
            hh = sb("lru_h", [128, T], F32); bhh = Buf("lru_h")
            pc = sb("lru_p", [128, T], F32); bpc = Buf("lru_p")
            zz = sb("lru_z", [128, T], F32); bzz = Buf("lru_z")
            hs = sb("lru_hs", [128, CTX], F32); bhs = Buf("lru_hs")
            nl8 = sb("nl8", [128, 4], F32); bnl = Buf("nl8")
            lsum, bls = K.lsum, K.bls
            blo = Buf("lruout")
            P.op("pool", lambda: nc.gpsimd.memset(zz[:], 0.0), writes=[bzz])
            P.op("act", lambda: nc.scalar.activation(nl8[:], smv(K, "llam", 0, 4), AF.Exp, scale=-1.0), reads=[K.bsm], writes=[bnl])
            P.op("act", lambda: nc.scalar.activation(nl8[:], nl8[:], AF.Ln, bias=1.0, scale=1.0), reads=[bnl], writes=[bnl])
            P.op("dve", lambda: nc.vector.tensor_scalar(nl8[:], nl8[:], -8.0, None, ALU.mult), reads=[bnl], writes=[bnl])
            pi = [0]

            def lru_chain(src, bsrc, Tn, c, is_ctx):
                lw = lambda j: smv(K, "lcw", c * 4 + j)
                P.op("dve", lambda: nc.vector.tensor_scalar(u[:, :Tn], src[:, c, 0:Tn], lw(0), smv(K, "lcb", c), ALU.mult, ALU.add),
                     reads=[bsrc, K.bsm], writes=[bu])
                for j in range(1, 4):
                    P.op("dve", lambda: nc.vector.scalar_tensor_tensor(u[:, :Tn], src[:, c, j:j + Tn], lw(j), u[:, :Tn], ALU.mult, ALU.add),
                         reads=[bsrc, bu, K.bsm], writes=[bu])
                P.op("act", lambda: nc.scalar.copy(ub[:, :Tn], u[:, :Tn]), reads=[bu], writes=[bub])
                for d in range(2):
                    for t0 in range(0, Tn, 512):
                        n = min(512, Tn - t0)
                        for (wn, bn, dst, bdst) in ((f"wa{d}{c}", "lba", ra, bra), (f"wx{d}{c}", "lbx", ib, bib)):
                            ps, bps = K.ps[pi[0] % 4], K.bps[pi[0] % 4]
                            pi[0] += 1
                            P.op("pe", lambda: nc.tensor.matmul(ps[:, :n], cmat(K, wn), ub[:, t0:t0 + n], start=True, stop=True),
                                 reads=[bub, K.bcm], writes=[bps])
                            P.op("act", lambda: nc.scalar.activation(dst[:, t0:t0 + n], ps[:, :n], AF.Sigmoid,
                                                                     bias=smv(K, bn, d * 2 + c), scale=1.0),
                                 reads=[bps, K.bsm], writes=[bdst])
                    P.op("act", lambda: nc.scalar.activation(ra[:, :Tn], ra[:, :Tn], AF.Exp, scale=nl8[:, d * 2 + c:d * 2 + c + 1]),
                         reads=[bra, bnl], writes=[bra])
                    P.op("dve", lambda: nc.vector.tensor_tensor(tm[:, :Tn], ra[:, :Tn], ra[:, :Tn], ALU.mult), reads=[bra], writes=[btm])
                    P.op("act", lambda: nc.scalar.activation(tm[:, :Tn], tm[:, :Tn], AF.Sqrt, bias=1.0000001, scale=-1.0),
                         reads=[btm], writes=[btm])
                    P.op("dve", lambda: nc.vector.tensor_tensor(ib[:, :Tn], ib[:, :Tn], u[:, :Tn], ALU.mult), reads=[bib, bu], writes=[bib])
                    P.op("dve", lambda: nc.vector.tensor_tensor(ib[:, :Tn], ib[:, :Tn], tm[:, :Tn], ALU.mult), reads=[bib, btm], writes=[bib])
                    if d == 0:
                        va, vb, vh, vp, vz = ra[:, 0:Tn], ib[:, 0:Tn], hh[:, 0:Tn], pc[:, 0:Tn], zz[:, 0:Tn]
                        last = Tn - 1
                    else:
                        rv = lambda t_: t_[:, 0:Tn][:, ::-1]
                        va, vb, vh, vp, vz = rv(ra), rv(ib), rv(hh), rv(pc), rv(zz)
                        last = 0
                    P.op("dve", lambda: nc.vector.tensor_tensor_scan(vh, va, vb, 0.0, ALU.mult, ALU.add), reads=[bra, bib], writes=[bhh])
                    if not is_ctx:
                        P.op("dve", lambda: nc.vector.tensor_tensor_scan(vp, va, vz, 1.0, ALU.mult, ALU.add), reads=[bra, bzz], writes=[bpc])
                        hn, pn = ("hlocF", "pcF") if d == 0 else ("hlocB", "pcB")
                        P.dma("sp", O[hn][c * 128:(c + 1) * 128, :], hh[:, :Tn], reads=[bhh], sb=bhh)
                        P.dma("sp", O[pn][c * 128:(c + 1) * 128, :], pc[:, :Tn], reads=[bpc], sb=bpc)
                        P.op("act", lambda: nc.scalar.copy(lsum[:, c * 6 + 2 * d:c * 6 + 2 * d + 1], pc[:, last:last + 1]), reads=[bpc], writes=[bls])
                        P.op("act", lambda: nc.scalar.copy(lsum[:, c * 6 + 2 * d + 1:c * 6 + 2 * d + 2], hh[:, last:last + 1]), reads=[bhh], writes=[bls])
                    else:
                        P.op("act", lambda: nc.scalar.copy(lsum[:, c * 6 + 4 + d:c * 6 + 5 + d], hh[:, last:last + 1]), reads=[bhh], writes=[bls])
                        if d == 0:
                            P.op("pool", lambda: nc.gpsimd.tensor_copy(hs[:, :Tn], hh[:, :Tn]), reads=[bhh], writes=[bhs])
                        else:
                            P.op("pool", lambda: nc.gpsimd.tensor_tensor(hs[:, :Tn], hs[:, :Tn], hh[:, :Tn], ALU.add), reads=[bhh, bhs], writes=[bhs])
                            P.dma("sp", O["hcsum"][c * 128:(c + 1) * 128, :], hs[:, :Tn], reads=[bhs], sb=bhs)

            for c in range(2):
                lru_chain(cxf, bcxf, T, c, False)
                lru_chain(cxc, bcxc, CTX, c, True)
            P.dma("sp", O["lsum"][:, :], lsum[:], reads=[bls], sb=bls)
            P.barrier()


def core_bq(core):
    return core // 4, core % 4


KEXT = T + 256
KDEXT = T + 512
NAJ = {0: list(range(0, 7)), 1: list(range(5)), 2: list(range(5)), 3: list(range(5)), 4: list(range(-2, 5))}
NAOFF = {}
_o = 0
for _v in range(5):
    NAOFF[_v] = _o
    _o += 4 * len(NAJ[_v])
NATILES = _o


def emit_B(K, I, ctx_out):
    nc, P = K.nc, K.P
    O = I.O
    x1_d = I.x1full
    qrot_d, qpl_d = O["qrotT"], O["qplT"]
    kext_d, vext_d, kc_d, vc_d = O["kext"], O["vext"], O["kc"], O["vc"]
    dq_d = O["dqT"]
    kdext_d, vdext_d, kdc_d, vdc_d = O["kdext"], O["vdext"], O["kdc"], O["vdc"]
    vcv_d, ab_d, vcvc_d, gcg_d = O["vcvext"], O["abT"], O["vcvcext"], O["gcgT"]
    hl_d = [O[n] for n in ("hlocF", "pcF", "hlocB", "pcB")]
    hcs_d, g_d = O["hcsum"], O["gT"]
    wmask_d, natab_d = I.wmask, I.natab
    wbr_d, wout_d = I.wbr, I.wout
    wg_d, wu_d, wd_d = I.wg2, I.wu2, I.wd2
    x3_d, x2_d = I.xout, I.x2_d
    if True:
        tiles = LAT_TILES + ([CTX_TILE] if ctx_out else [])
        with contextlib.ExitStack() as s2:
            sb = lambda name, shape, dt: s2.enter_context(nc.sbuf_tensor(uniq(name), shape, dt))
            kext = sb("kext", [128, KEXT], BF16); bkext = Buf("kext")
            vext = sb("vext", [128, KEXT // 128, 128], BF16); bvext = Buf("vext")
            kc = sb("kc", [128, CTX], BF16); bkc = Buf("kc")
            vc = sb("vc", [128, 2, 128], BF16); bvc = Buf("vc")
            kdext = sb("kdext", [128, 2, KDEXT], BF16); bkdext = Buf("kdext")
            vdext = sb("vdext", [128, KDEXT // 128, MW], BF16); bvdext = Buf("vdext")
            kdc = sb("kdc", [128, 2, CTX], BF16); bkdc = Buf("kdc")
            vdc = sb("vdc", [128, 2, MW], BF16); bvdc = Buf("vdc")
            P.dma("sp", kext[:], kext_d[:, :], writes=[bkext], sb=bkext)
            P.dma("sp", vext[:], vext_d.rearrange("(b p) f -> p b f", p=128), writes=[bvext], sb=bvext)
            P.dma("sp", kc[:], kc_d[:, :], writes=[bkc], sb=bkc)
            P.dma("sp", vc[:], vc_d.rearrange("(b p) f -> p b f", p=128), writes=[bvc], sb=bvc)
            P.dma("sp", kdext[:], kdext_d.rearrange("(c p) t -> p c t", p=128), writes=[bkdext], sb=bkdext)
            P.dma("sp", vdext[:], vdext_d.rearrange("(b p) f -> p b f", p=128), writes=[bvdext], sb=bvdext)
            P.dma("sp", kdc[:], kdc_d.rearrange("(c p) t -> p c t", p=128), writes=[bkdc], sb=bkdc)
            P.dma("sp", vdc[:], vdc_d.rearrange("(b p) f -> p b f", p=128), writes=[bvdc], sb=bvdc)
            wbr = sb("wbr", [128, 8, D], BF16); bwbr = WBufs(512)
            wout = sb("wout", [128, 8, D], BF16); bwout = WBufs(512)
            load_weight_groups(K, [(wbr, bwbr, wbr_d, D)])
            load_weight_groups(K, [(wout, bwout, wout_d, D)])
            wm32 = sb("wm32", [128, 4, 128], F32); bwm32 = Buf("wm32")
            wmk = sb("wmk", [128, 4, 128], BF16); bwmk = Buf("wmk")
            P.dma("sp", wm32[:], wmask_d[:, :, :], writes=[bwm32], sb=bwm32)
            P.op("dve", lambda: nc.vector.tensor_copy(wmk[:], wm32[:]), reads=[bwm32], writes=[bwmk])
            etab = sb("etab", [128, NATILES * 128], BF16); betab = Buf("etab")
            with contextlib.ExitStack() as s3:
                tst = s3.enter_context(nc.sbuf_tensor(uniq("tst"), [128, 29, 128], F32)); btst = Buf("tst")
                for v in range(4):
                    P.dma("sp", tst[:], natab_d[v * 29:(v + 1) * 29].rearrange("n k q -> k n q"), writes=[btst], sb=btst)
                    P.op("act", lambda: nc.scalar.activation(etab[:, v * 29 * 128:(v + 1) * 29 * 128], tst[:].rearrange("p c q -> p (c q)"), AF.Exp),
                         reads=[btst], writes=[betab])
                P.barrier()
            esink = sb("esink", [128, 2], F32); besink = Buf("esink")
            P.op("act", lambda: nc.scalar.activation(esink[:], smv(K, "sink", 0, 2), AF.Exp), reads=[K.bsm], writes=[besink])
            carry, bcarry = K.carry, K.bcarry
            qrot = sb("qrot", [128, 2, 512], BF16); bqrot = Buf("qrot")
            qpl = sb("qplb", [128, 2, 512], BF16); bqpl = Buf("qplb")
            dq = sb("dqb", [128, 2, 512], BF16); bdq = Buf("dqb")
            vx = sb("vx", [128, 2, 514], BF16); bvx = Buf("vx")
            ab = sb("abb", [128, 2, 512], BF16); bab = Buf("abb")
            gcg = sb("gcgb", [128, 2, 512], BF16); bgcg = Buf("gcgb")
            hl = sb("hl", [128, 4, 512], F32); bhl = Buf("hl")
            gt = [sb(f"gt{i}", [128, 4, 512], BF16) for i in range(2)]; bgt = [Buf(f"gt{i}") for i in range(2)]
            x1t = sb("x1t", [128, 8, 512], F32); bx1 = Buf("x1t")
            yT = sb("yT", [128, 8, 512], BF16); byT = Buf("yT")
            mT = sb("mT", [128, 8, 512], BF16); bmT = Buf("mT")
            macc = sb("macc", [128, 512], F32); bmacc = Buf("macc")
            mtmp = [sb(f"mtmp{i}", [128, 512], F32) for i in range(2)]; bmtmp = [Buf(f"mtmp{i}") for i in range(2)]
            pex = [sb(f"pex{i}", [128, 9 * 128], BF16) for i in range(3)]; bpex = [Buf(f"pex{i}") for i in range(3)]
            rd = [sb(f"rd{i}", [128, 128], F32) for i in range(3)]; brd = [Buf(f"rd{i}") for i in range(3)]
            ct1 = sb("ct1", [128, 512], F32); bct1 = Buf("ct1")
            ct2 = sb("ct2", [128, 512], F32); bct2 = Buf("ct2")
            ones64 = cmat(K, "ones1")[:, 0:64]
            actr = [0]

            def attn(qT, bq_, qc, qbase, cols, local, ctxk, mask_ops, sink_col, ych, ob):
                chunks = local + ctxk
                n = len(chunks)

                def stage1(i):
                    banks = [(K.ps[i], K.bps[i]), (K.ps[2 + i], K.bps[2 + i])]
                    pe_, bpe = pex[i], bpex[i]
                    for g0 in range(0, n, 4):
                        ps, bps_ = banks[(g0 // 4) % 2]
                        grp = chunks[g0:g0 + 4]
                        for jj, (kap, bk_, qap, bqq, vap, bv_) in enumerate(grp):
                            P.op("pe", lambda: nc.tensor.matmul(ps[:, jj * 128:(jj + 1) * 128], kap, qap, start=True, stop=True),
                                 reads=[bk_, bqq], writes=[bps_])
                        P.op("act", lambda: nc.scalar.activation(pe_[:, g0 * 128:(g0 + len(grp)) * 128], ps[:, 0:len(grp) * 128], AF.Exp),
                             reads=[bps_], writes=[bpe])
                    for (eng, j0, j1, map_, bm_) in mask_ops:
                        if eng == "pool":
                            P.op("pool", lambda: nc.gpsimd.tensor_tensor(pe_[:, j0 * 128:j1 * 128], pe_[:, j0 * 128:j1 * 128], map_, ALU.mult), reads=[bpe, bm_], writes=[bpe])
                        else:
                            P.op("dve", lambda: nc.vector.tensor_tensor(pe_[:, j0 * 128:j1 * 128], pe_[:, j0 * 128:j1 * 128], map_, ALU.mult), reads=[bpe, bm_], writes=[bpe])

                def stage2(i):
                    pe_, bpe = pex[i], bpex[i]
                    psO, bO = K.ps[4 + 2 * i], K.bps[4 + 2 * i]
                    psD, bD = K.ps[5 + 2 * i], K.bps[5 + 2 * i]
                    for j, (kap, bk_, qap, bqq, vap, bv_) in enumerate(chunks):
                        P.op("pe", lambda: nc.tensor.matmul(psO[ob:ob + 64, 0:128], vap, pe_[:, j * 128:(j + 1) * 128], start=(j == 0), stop=(j == n - 1)),
                             reads=[bv_, bpe], writes=[bO], sig=(j == n - 1))
                    for j in range(n):
                        P.op("pe", lambda: nc.tensor.matmul(psD[ob:ob + 64, 0:128], ones64, pe_[:, j * 128:(j + 1) * 128], start=(j == 0), stop=(j == n - 1)),
                             reads=[K.bcm, bpe], writes=[bD], sig=(j == n - 1))
                    r_, br_ = rd[i], brd[i]
                    if sink_col is not None:
                        P.op("dve", lambda: nc.vector.tensor_scalar(r_[ob:ob + 64, :], psD[ob:ob + 64, 0:128], sink_col, None, ALU.add),
                             reads=[bD, besink], writes=[br_])
                        P.op("dve", lambda: nc.vector.reciprocal(r_[ob:ob + 64, :], r_[ob:ob + 64, :]), reads=[br_], writes=[br_])
                    else:
                        P.op("dve", lambda: nc.vector.reciprocal(r_[ob:ob + 64, :], psD[ob:ob + 64, 0:128]), reads=[bD], writes=[br_])
                    P.op("dve", lambda: nc.vector.tensor_tensor(yT[ob:ob + 64, ych, cols], psO[ob:ob + 64, 0:128], r_[ob:ob + 64, :], ALU.mult),
                         reads=[bO, br_], writes=[byT])

                items.append((stage1, stage2))

            def run_items():
                if items:
                    items[0][0](0)
                for i in range(len(items)):
                    if i + 1 < len(items):
                        items[i + 1][0]((i + 1) % 2)
                    items[i][1](i % 2)
                del items[:]

            items = []
            x1s = x1_d.rearrange("(c p) t -> p c t", p=128)
            x2s = x2_d.rearrange("(c p) t -> p c t", p=128)
            v3 = lambda ap: ap.rearrange("(c p) t -> p c t", p=128)
            g4 = g_d.rearrange("(n k p) t -> p n k t", p=128, k=8)
            gctr = [0]
            for (start, N, col) in tiles:
                lat = start < T
                sl = slice(start, start + N)
                P.dma("sp", x1t[:, :, :N], x1s[:, :, sl], writes=[bx1], sb=bx1)
                P.dma("sp", qpl[:, :, :N], v3(qpl_d)[:, :, sl], writes=[bqpl], sb=bqpl)
                P.dma("sp", dq[:, :, :N], v3(dq_d)[:, :, sl], writes=[bdq], sb=bdq)
                P.dma("sp", ab[:, :, :N], v3(ab_d)[:, :, sl], writes=[bab], sb=bab)
                P.dma("sp", gcg[:, :, :N], v3(gcg_d)[:, :, sl], writes=[bgcg], sb=bgcg)
                if lat:
                    P.dma("sp", qrot[:, :, :N], v3(qrot_d)[:, :, sl], writes=[bqrot], sb=bqrot)
                    P.dma("sp", vx[:, :, :N + 2], v3(vcv_d)[:, :, start:start + N + 2], writes=[bvx], sb=bvx)
                else:
                    P.dma("sp", vx[:, :, :N + 2], v3(vcvc_d)[:, :, 0:N + 2], writes=[bvx], sb=bvx)
                for c in range(2):
                    cw = lambda j: smv(K, "convaw", c * 3 + j)
                    P.op("dve", lambda: nc.vector.tensor_scalar(ct1[:, :N], vx[:, c, 0:N], cw(0), None, ALU.mult), reads=[bvx, K.bsm], writes=[bct1])
                    P.op("dve", lambda: nc.vector.scalar_tensor_tensor(ct1[:, :N], vx[:, c, 1:N + 1], cw(1), ct1[:, :N], ALU.mult, ALU.add),
                         reads=[bvx, K.bsm, bct1], writes=[bct1])
                    P.op("dve", lambda: nc.vector.scalar_tensor_tensor(ct1[:, :N], vx[:, c, 2:N + 2], cw(2), ct1[:, :N], ALU.mult, ALU.add),
                         reads=[bvx, K.bsm, bct1], writes=[bct1])
                    P.op("pool", lambda: nc.gpsimd.tensor_tensor(yT[:, 0 + c, :N], ct1[:, :N], ab[:, c, :N], ALU.mult), reads=[bct1, bab], writes=[byT])
                for c in range(2):
                    if lat:
                        for a_ in range(4):
                            P.dma("sp", hl[:, a_, :N], hl_d[a_][c * 128:(c + 1) * 128, sl], writes=[bhl], sb=bhl)
                    else:
                        P.dma("sp", hl[:, 0, :N], hcs_d[c * 128:(c + 1) * 128, 0:N], writes=[bhl], sb=bhl)
                    if lat:
                        P.op("dve", lambda: nc.vector.scalar_tensor_tensor(ct1[:, :N], hl[:, 1, :N], carry[:, c * 2:c * 2 + 1], hl[:, 0, :N], ALU.mult, ALU.add),
                             reads=[bhl, bcarry], writes=[bct1])
                        P.op("dve", lambda: nc.vector.scalar_tensor_tensor(ct2[:, :N], hl[:, 3, :N], carry[:, c * 2 + 1:c * 2 + 2], hl[:, 2, :N], ALU.mult, ALU.add),
                             reads=[bhl, bcarry], writes=[bct2])
                        P.op("pool", lambda: nc.gpsimd.tensor_tensor(ct1[:, :N], ct1[:, :N], ct2[:, :N], ALU.add), reads=[bct1, bct2], writes=[bct1])
                        P.op("pool", lambda: nc.gpsimd.tensor_tensor(yT[:, 4 + c, :N], ct1[:, :N], gcg[:, c, :N], ALU.mult), reads=[bct1, bgcg], writes=[byT])
                    else:
                        P.op("pool", lambda: nc.gpsimd.tensor_tensor(yT[:, 4 + c, :N], hl[:, 0, :N], gcg[:, c, :N], ALU.mult), reads=[bhl, bgcg], writes=[byT])
                for blk in range(N // 128):
                    cols = slice(blk * 128, (blk + 1) * 128)
                    n_ = (start // 128) + blk
                    for h in range(4):
                        kv = h // 2
                        qc, qb_ = h % 2, kv * 64
                        ob = (h % 2) * 64
                        ctxk = [(kc[qb_:qb_ + 64, j * 128:(j + 1) * 128], bkc, qpl[qb_:qb_ + 64, qc, cols], bqpl,
                                 vc[:, j, kv * 64:(kv + 1) * 64], bvc) for j in range(2)]
                        if lat:
                            local = [(kext[qb_:qb_ + 64, (n_ + j) * 128:(n_ + j + 1) * 128], bkext, qrot[qb_:qb_ + 64, qc, cols], bqrot,
                                      vext[:, n_ + j, kv * 64:(kv + 1) * 64], bvext) for j in range(3)]
                            mp = wmk[:, 2, :] if n_ == 0 else wmk[:, 0, :]
                            mn = wmk[:, 3, :] if n_ == 31 else wmk[:, 1, :]
                            mops = [("pool", 0, 1, mp, bwmk), ("pool", 2, 3, mn, bwmk)]
                        else:
                            local, mops = [], []
                        attn(None, None, qc, qb_, cols, local, ctxk, mops, esink[ob:ob + 64, h // 2:h // 2 + 1], 2 + h // 2, ob)
                    for h in range(4):
                        hc_, hb_ = h // 2, (h % 2) * 64
                        ob = hb_
                        ctxk = [(kdc[hb_:hb_ + 64, hc_, j * 128:(j + 1) * 128], bkdc, dq[hb_:hb_ + 64, hc_, cols], bdq,
                                 vdc[:, j, h * 64:(h + 1) * 64], bvdc) for j in range(2)]
                        if lat:
                            var = 0 if n_ == 0 else 1 if n_ == 1 else 3 if n_ == 30 else 4 if n_ == 31 else 2
                            jl = NAJ[var]
                            local = [(kdext[hb_:hb_ + 64, hc_, (n_ + j) * 128:(n_ + j + 1) * 128], bkdext, dq[hb_:hb_ + 64, hc_, cols], bdq,
                                      vdext[:, n_ + j, h * 64:(h + 1) * 64], bvdext) for j in jl]
                            e0 = NAOFF[var] + h * len(jl)
                            mops = [("dve", 0, len(jl), etab[:, e0 * 128:(e0 + len(jl)) * 128], betab)]
                        else:
                            local, mops = [], []
                        attn(None, None, None, None, cols, local, ctxk, mops, None, 6 + h // 2, ob)
                run_items()
                for dk in range(8):
                    gi_ = gctr[0] % 2
                    gctr[0] += 1
                    P.dma("sp", gt[gi_][:, :, :N], g4[:, :, dk, sl], writes=[bgt[gi_]], sb=bgt[gi_])
                    for n in range(4):
                        ps, bps_ = K.ps[(dk * 4 + n) % 4], K.bps[(dk * 4 + n) % 4]
                        for cc in range(2):
                            P.op("pe", lambda: nc.tensor.matmul(ps[:, :N], wbr[:, n * 2 + cc, dk * 128:(dk + 1) * 128], yT[:, n * 2 + cc, :N],
                                                                start=(cc == 0), stop=(cc == 1)),
                                 reads=[bwbr.at(dk * 128), byT], writes=[bps_], sig=(cc == 1))
                        if n == 0:
                            P.op("dve", lambda: nc.vector.tensor_tensor(macc[:, :N], ps[:, :N], gt[gi_][:, 0, :N], ALU.mult), reads=[bps_, bgt[gi_]], writes=[bmacc])
                        else:
                            mt_, bmt_ = mtmp[n % 2], bmtmp[n % 2]
                            P.op("dve", lambda: nc.vector.tensor_tensor(mt_[:, :N], ps[:, :N], gt[gi_][:, n, :N], ALU.mult), reads=[bps_, bgt[gi_]], writes=[bmt_])
                            if n < 3:
                                P.op("pool", lambda: nc.gpsimd.tensor_tensor(macc[:, :N], macc[:, :N], mt_[:, :N], ALU.add), reads=[bmacc, bmt_], writes=[bmacc])
                            else:
                                P.op("pool", lambda: nc.gpsimd.tensor_tensor(mT[:, dk, :N], macc[:, :N], mt_[:, :N], ALU.add), reads=[bmacc, bmt_], writes=[bmT])
                for dk in range(8):
                    ps, bps_ = K.ps[4 + dk % 4], K.bps[4 + dk % 4]
                    for k in range(8):
                        P.op("pe", lambda: nc.tensor.matmul(ps[:, :N], wout[:, k, dk * 128:(dk + 1) * 128], mT[:, k, :N], start=(k == 0), stop=(k == 7)),
                             reads=[bwout.at(dk * 128), bmT], writes=[bps_], sig=(k == 7))
                    P.op("dve", lambda: nc.vector.scalar_tensor_tensor(x1t[:, dk, :N], ps[:, :N], K.gateT[:, 1, dk, col:col + 1], x1t[:, dk, :N], ALU.mult, ALU.add),
                         reads=[bps_, bx1, K.bgs], writes=[bx1])
                P.dma("sp", x2s[:, :, sl], x1t[:, :, :N], reads=[bx1], sb=bx1)
            P.barrier()
        emit_ffn_phase(K, x2_d, x3_d, tiles, 2, wg_d, wu_d, wd_d)


def window_masks(q):
    j = np.arange(128)[:, None]
    i = np.arange(128)[None, :]
    tp = (j >= i).astype(np.float32)
    tn = (j <= i).astype(np.float32)
    z = np.zeros((128, 128), np.float32)
    return np.ascontiguousarray(np.stack([tp, tn, tp if q > 0 else z, tn if q < 3 else z], 1))


def na_index(gm, jl):
    rows_total = SEQ // GRID_W
    kk = np.arange(128)
    qq = np.arange(128)
    qrow = 2 * gm + qq // 64
    qcol = qq % 64
    n = len(jl)
    dr = np.zeros((n, 128, 128), np.int64)
    dc = np.zeros((n, 128, 128), np.int64)
    valid = np.zeros((n, 128, 128), bool)
    rstart = np.clip(qrow - 4, 0, rows_total - 8)
    cstart = np.clip(qcol - 8, 0, GRID_W - 16)
    for i, j in enumerate(jl):
        krow = (2 * gm - 4 + 2 * j + kk // 64)[:, None]
        kcol = (kk % 64)[:, None]
        v = (krow >= rstart[None, :]) & (krow < rstart[None, :] + 8) & (kcol >= cstart[None, :]) & (kcol < cstart[None, :] + 16)
        valid[i] = v
        dr[i] = np.clip(krow - qrow[None, :] + 7, 0, 14)
        dc[i] = np.clip(kcol - qcol[None, :], -15, 15) + 15
    return dr, dc, valid


def na_tables(rel_bias, q):
    out = np.empty((NATILES, 128, 128), np.float32)
    for v, m in enumerate((0, 1, 15, 30, 31)):
        gm = q * 32 + m
        jl = NAJ[v]
        dr, dc, valid = na_index(gm, jl)
        for h in range(4):
            o = NAOFF[v] + h * len(jl)
            out[o:o + len(jl)] = np.where(valid, rel_bias[h][dr, dc], np.float32(-30000.0))
    return out


XP = 1282
GROUPS = [[0, 1, 2, 3], [4, 5, 6, 7]]


def emit_exchange(K, I):
    nc, P, O = K.nc, K.P, I.O
    pack, packall, lpack, lall = I.pack.ap(), I.packall.ap(), I.lpack.ap(), I.lall.ap()
    bpk, bpa, blp, bla = Buf("pack"), Buf("packall"), Buf("lpack"), Buf("lall")
    v3 = lambda ap: ap.rearrange("(c p) t -> p c t", p=128)
    tv = lambda ap: ap.rearrange("(b p) f -> p b f", p=128)
    w3 = lambda ap, w: ap.rearrange("p (a w) -> p a w", w=w)
    kext, vext, kdext, vdext, vcv = O["kext"], O["vext"], O["kdext"], O["vdext"], O["vcvext"]
    for side in range(2):
        o = side * XP
        if side == 0:
            srcs = [kext[:, T:T + 128], vext[T:T + 128, :], v3(kdext)[:, :, T:T + 256], tv(vdext[T:T + 256, :]), v3(vcv)[:, :, T:T + 1]]
        else:
            srcs = [kext[:, 128:256], vext[128:256, :], v3(kdext)[:, :, 256:512], tv(vdext[256:512, :]), v3(vcv)[:, :, 1:2]]
        dsts = [pack[:, o:o + 128], pack[:, o + 128:o + 256], w3(pack[:, o + 256:o + 768], 256), w3(pack[:, o + 768:o + 1280], 256),
                w3(pack[:, o + 1280:o + 1282], 1)]
        for d_, s_ in zip(dsts, srcs):
            P.dma("sp", d_, s_, writes=[bpk], sb=bpk, allow_slow_non_contiguous=True)
    P.collective(I.pack.ap().opt(), I.packall.ap().opt(), GROUPS, reads=[bpk], writes=[bpa])
    with contextlib.ExitStack() as s2:
        sb = lambda name, shape, dt: s2.enter_context(nc.sbuf_tensor(uniq(name), shape, dt))
        cand = sb("cand", [128, 4, 2 * XP], BF16); bcand = Buf("cand")
        hl = sb("halo_l", [128, XP], BF16); bhl = Buf("halo_l")
        hr = sb("halo_r", [128, XP], BF16); bhr = Buf("halo_r")
        P.dma("sp", cand[:], packall.rearrange("(j p) x -> p j x", p=128), reads=[bpa], writes=[bcand], sb=bcand)
        for (dst, bdst, c0, sel) in ((hl, bhl, 0, "lsel"), (hr, bhr, XP, "rsel")):
            P.op("dve", lambda: nc.vector.tensor_scalar(dst[:], cand[:, 0, c0:c0 + XP], smv(K, sel, 0), None, ALU.mult),
                 reads=[bcand, K.bsm], writes=[bdst])
            for j in range(1, 4):
                P.op("dve", lambda: nc.vector.scalar_tensor_tensor(dst[:], cand[:, j, c0:c0 + XP], smv(K, sel, j), dst[:], ALU.mult, ALU.add),
                     reads=[bcand, K.bsm, bdst], writes=[bdst])
        for (src, bsrc, kc, vr, dk, dv, cv) in ((hl, bhl, slice(0, 128), slice(0, 128), slice(0, 256), slice(0, 256), slice(0, 1)),
                                              (hr, bhr, slice(128 + T, 256 + T), slice(128 + T, 256 + T), slice(256 + T, 512 + T),
                                               slice(256 + T, 512 + T), slice(T + 1, T + 2))):
            P.dma("sp", kext[:, kc], src[:, 0:128], reads=[bsrc], sb=bsrc)
            P.dma("sp", vext[vr, :], src[:, 128:256], reads=[bsrc], sb=bsrc)
            P.dma("sp", v3(kdext)[:, :, dk], w3(src[:, 256:768], 256), reads=[bsrc], sb=bsrc)
            P.dma("sp", tv(vdext[dv, :]), w3(src[:, 768:1280], 256), reads=[bsrc], sb=bsrc)
            P.dma("sp", v3(vcv)[:, :, cv], w3(src[:, 1280:1282], 1), reads=[bsrc], sb=bsrc, allow_slow_non_contiguous=True)
        P.dma("sp", lpack[:, :], K.lsum[:], reads=[K.bls], writes=[blp], sb=K.bls)
        P.collective(I.lpack.ap().opt(), I.lall.ap().opt(), GROUPS, reads=[blp], writes=[bla])
        lc = sb("lcand", [128, 4, 12], F32); blc = Buf("lcand")
        tt_ = sb("ltmp", [128, 2], F32); btt = Buf("ltmp")
        P.dma("sp", lc[:], lall.rearrange("(j p) x -> p j x", p=128), reads=[bla], writes=[blc], sb=blc)
        carry, bcarry = K.carry, K.bcarry
        for c in range(2):
            for d in range(2):
                cc = carry[:, c * 2 + d:c * 2 + d + 1]
                P.op("dve", lambda: nc.vector.tensor_copy(cc, K.lsum[:, c * 6 + 4 + d:c * 6 + 5 + d]), reads=[K.bls], writes=[bcarry])
                order = [0, 1, 2, 3] if d == 0 else [3, 2, 1, 0]
                sel = "fsel" if d == 0 else "bsel"
                for j in order:
                    a_ = lc[:, j, c * 6 + 2 * d:c * 6 + 2 * d + 1]
                    h_ = lc[:, j, c * 6 + 2 * d + 1:c * 6 + 2 * d + 2]
                    P.op("dve", lambda: nc.vector.scalar_tensor_tensor(tt_[:, 0:1], cc, a_, h_, ALU.mult, ALU.add),
                         reads=[bcarry, blc], writes=[btt])
                    P.op("dve", lambda: nc.vector.tensor_tensor(tt_[:, 1:2], tt_[:, 0:1], cc, ALU.subtract), reads=[btt, bcarry], writes=[btt])
                    P.op("dve", lambda: nc.vector.scalar_tensor_tensor(cc, tt_[:, 1:2], smv(K, sel, j), cc, ALU.mult, ALU.add),
                         reads=[btt, K.bsm, bcarry], writes=[bcarry])
        P.barrier()


def emit_xhalo(K, I):
    nc, P = K.nc, K.P
    xin1 = I.xin1
    xp, xall = I.xpack.ap(), I.xall.ap()
    bxp, bxa = Buf("xpack"), Buf("xall")
    v3 = lambda ap: ap.rearrange("(c p) t -> p c t", p=128)
    w3 = lambda ap, w: ap.rearrange("p (a w) -> p a w", w=w)
    P.dma("sp", w3(xp[:, 0:16], 2), v3(xin1)[:, :, T - 2:T], writes=[bxp], sb=bxp)
    P.dma("sp", w3(xp[:, 16:24], 1), v3(xin1)[:, :, 0:1], writes=[bxp], sb=bxp, allow_slow_non_contiguous=True)
    P.collective(I.xpack.ap().opt(), I.xall.ap().opt(), GROUPS, reads=[bxp], writes=[bxa])
    with contextlib.ExitStack() as s2:
        sb = lambda name, shape, dt: s2.enter_context(nc.sbuf_tensor(uniq(name), shape, dt))
        xc = sb("xcand", [128, 4, 24], F32); bxc = Buf("xcand")
        hx = sb("xhalo", [128, 32], F32); bhx = Buf("xhalo")
        P.dma("sp", xc[:], xall.rearrange("(j p) x -> p j x", p=128), reads=[bxa], writes=[bxc], sb=bxc)
        P.op("dve", lambda: nc.vector.memset(hx[:], 0.0), writes=[bhx])
        for (c0, n, o, sel) in ((0, 16, 0, "lsel"), (16, 8, 16, "rsel")):
            for j in range(4):
                P.op("dve", lambda: nc.vector.scalar_tensor_tensor(hx[:, o:o + n], xc[:, j, c0:c0 + n], smv(K, sel, j), hx[:, o:o + n], ALU.mult, ALU.add),
                     reads=[bxc, K.bsm, bhx], writes=[bhx])
        P.dma("sp", v3(xin1)[:, :, TT:TT + 2], w3(hx[:, 0:16], 2), reads=[bhx], sb=bhx)
        P.dma("sp", v3(xin1)[:, :, TT + 2:TT + 3], w3(hx[:, 16:24], 1), reads=[bhx], sb=bhx, allow_slow_non_contiguous=True)
        P.dma("sp", v3(xin1)[:, :, TT + 3:TT + 4], w3(hx[:, 24:32], 1), reads=[bhx], sb=bhx, allow_slow_non_contiguous=True)
        P.barrier()


INTERNALS = [("abT", [MW, TT], BF16), ("qrotT", [MW, T], BF16), ("qplT", [MW, TT], BF16),
             ("kext", [128, KEXT], BF16), ("vext", [KEXT, 128], BF16), ("kc", [128, CTX], BF16), ("vc", [CTX, 128], BF16),
             ("gcgT", [MW, TT], BF16), ("dqT", [MW, TT], BF16),
             ("kdext", [MW, KDEXT], BF16), ("vdext", [KDEXT, MW], BF16), ("kdc", [MW, CTX], BF16), ("vdc", [CTX, MW], BF16),
             ("vcvext", [MW, T + 2], BF16), ("vcvcext", [MW, CTX + 2], BF16), ("gT", [4 * D, TT], BF16),
             ("hlocF", [MW, T], F32), ("pcF", [MW, T], F32), ("hlocB", [MW, T], F32), ("pcB", [MW, T], F32),
             ("hcsum", [MW, CTX], F32), ("lsum", [128, 12], F32)]


def build_fused():
    nc = bass.Bass("TRN2", target_bir_lowering=False, dynamic_dma_scratch_size=4096)
    din = lambda n, sh, dt=F32: nc.dram_tensor(n, sh, dt, kind="ExternalInput").ap()
    dint = lambda n, sh, dt=F32: nc.dram_tensor(n, sh, dt, kind="Internal").ap()
    I = Ctx()
    xT_d = din("xT", [D, TX])
    I.cos_d = din("cosT", [128, T]); I.sin_d = din("sinT", [128, T])
    sm_d = [din(f"sm{l}", [128, NSM]) for l in range(2)]
    I.wmask = din("wmask", [128, 4, 128])
    natab_d = din("natab", [2, NATILES, 128, 128])
    wmod_d = din("w_mod", [2, D, 9 * D])
    wg_d = din("wg", [2, 2, D, FF]); wu_d = din("wu", [2, 2, D, FF]); wd_d = din("wd", [2, 2, FF, D])
    win_d = din("w_in", [2, D, INW])
    wbr_d = din("w_branch", [2, D, D]); wout_d = din("w_out", [2, D, D])
    out_d = nc.dram_tensor("outT", [D, T], F32, kind="ExternalOutput").ap()
    I.O = {n: dint(n, sh, dt) for n, sh, dt in INTERNALS}
    I.x1full = dint("x1full", [D, TX]); I.h2_d = dint("h2T", [D, TX], BF16)
    I.x2_d = dint("x2T", [D, TT]); I.xin1 = dint("xin1", [D, TX])
    I.pack = nc.dram_tensor("pack", [128, 2 * XP], BF16); I.packall = nc.dram_tensor("packall", [512, 2 * XP], BF16)
    I.lpack = nc.dram_tensor("lpack", [128, 12], F32); I.lall = nc.dram_tensor("lall", [512, 12], F32)
    I.xpack = nc.dram_tensor("xpack", [128, 24], F32); I.xall = nc.dram_tensor("xall", [512, 24], F32)

    with contextlib.ExitStack() as st:
        K = setup_common(nc, st)
        P = K.P
        K.lsum = K.sbuf("lsum_sb", [128, 12], F32); K.bls = Buf("lsum")
        K.carry = K.sbuf("carry", [128, 4], F32); K.bcarry = Buf("carry")
        zt = K.sbuf("zeros_sb", [128, 2, 1], BF16); bzt = Buf("zeros")
        P.op("dve", lambda: nc.vector.memset(zt[:], 0.0), writes=[bzt])
        vcc = I.O["vcvcext"].rearrange("(c p) t -> p c t", p=128)
        P.dma("sp", vcc[:, :, 0:1], zt[:], reads=[bzt], sb=bzt, allow_slow_non_contiguous=True)
        P.dma("sp", vcc[:, :, CTX + 1:CTX + 2], zt[:], reads=[bzt], sb=bzt, allow_slow_non_contiguous=True)
        for l in range(2):
            load_sm(K, sm_d[l][:, :])
            emit_mod(K, wmod_d[l])
            I.xin = xT_d if l == 0 else I.xin1
            I.wg1, I.wu1, I.wd1, I.win = wg_d[l, 0], wu_d[l, 0], wd_d[l, 0], win_d[l]
            I.wg2, I.wu2, I.wd2 = wg_d[l, 1], wu_d[l, 1], wd_d[l, 1]
            I.wbr, I.wout, I.natab = wbr_d[l], wout_d[l], natab_d[l]
            I.xout = I.xin1 if l == 0 else out_d
            emit_A(K, I)
            emit_exchange(K, I)
            emit_B(K, I, ctx_out=(l == 0))
            if l == 0:
                emit_xhalo(K, I)
        P.finish()
    return nc


_NC = []


def kernel(**inp):
    inp = {k: np.asarray(v) for k, v in inp.items()}
    if not _NC:
        _NC.append(build_fused())
    nc = _NC[0]
    x = np.asarray(inp["x"], np.float32)
    shared = {
        "w_mod": np.ascontiguousarray(inp["w_mod"], np.float32),
        "wg": np.ascontiguousarray(inp["ffn_w_gate"], np.float32), "wu": np.ascontiguousarray(inp["ffn_w_up"], np.float32),
        "wd": np.ascontiguousarray(inp["ffn_w_down"], np.float32), "w_in": np.ascontiguousarray(inp["w_in"], np.float32),
        "w_branch": np.ascontiguousarray(np.asarray(inp["w_branch"], np.float32).reshape(2, D, D)),
        "w_out": np.ascontiguousarray(inp["w_out"], np.float32),
    }
    in_maps = []
    for core in range(NC):
        b, q = core_bq(core)
        xs = x[b]
        lo, hi = q * T, (q + 1) * T
        halo = np.zeros((4, D), np.float32)
        if q > 0:
            halo[0] = xs[lo - 2]; halo[1] = xs[lo - 1]
        if q < 3:
            halo[2] = xs[hi]
        xT = np.ascontiguousarray(np.concatenate([xs[lo:hi], np.asarray(inp["ctx"][b], np.float32), halo], 0).T)
        cs, sn = rope_tables(q)
        m = {"xT": xT, "cosT": cs, "sinT": sn, "sm0": build_sm(inp, 0, b, q), "sm1": build_sm(inp, 1, b, q),
             "wmask": window_masks(q),
             "natab": np.stack([na_tables(np.asarray(inp["na_rel_bias"][l], np.float32), q) for l in range(2)], 0)}
        m.update(shared)
        in_maps.append(m)
    res = run_bass_kernel_spmd(nc, in_maps, core_ids=list(range(NC)))
    out = np.empty_like(x)
    for core in range(NC):
        b, q = core_bq(core)
        out[b, q * T:(q + 1) * T] = np.asarray(res.results[core]["outT"]).T
    return out
```

```python
import contextlib
import numpy as np
import ml_dtypes
import concourse.bass as bass
import concourse.mybir as mybir
from concourse.bass_utils import run_bass_kernel_spmd

F32 = mybir.dt.float32
BF16 = mybir.dt.bfloat16
AF = mybir.ActivationFunctionType
ALU = mybir.AluOpType
NPBF = ml_dtypes.bfloat16

D = 1024; KD = 8; FF = 2816; KF = 22; T = 4096; CTX = 256; TT = T + CTX; TX = TT + 4
MW = 256; INW = 6656; NC = 8; SEQ = 16384; GRID_W = 64
EPS = 1e-6


class Buf:
    __slots__ = ("name", "lw", "rd", "dsem")

    def __init__(self, name):
        self.name = name
        self.lw = None
        self.rd = []
        self.dsem = None


class Prog:
    ENG = ("pe", "dve", "act", "pool", "sp")

    def __init__(self, nc, stack):
        self.nc = nc
        self.stack = stack
        self.eng = {"pe": nc.tensor, "dve": nc.vector, "act": nc.scalar,
                    "pool": nc.gpsimd, "sp": nc.sync}
        self.sems = {}
        self.cnt = {}
        self.waited = {e: {} for e in self.ENG}
        self.esem = {}
        self.nsem = 0
        self.free_dsems = []
        self.dbufs = []
        self.allbufs = []
        self.pool_fifo = []
        for e in self.ENG:
            key = f"E{e}"
            self._alloc_sem(key)
            self.esem[e] = key
        self.n_inst = 0

    def _alloc_sem(self, key):
        h = self.stack.enter_context(self.nc.semaphore(f"s{self.nsem}_{key}"))
        self.nsem += 1
        self.sems[key] = h
        self.cnt[key] = 0
        return key

    def _need(self, e, deps, key, val):
        if self.waited[e].get(key, 0) >= val:
            return
        deps[key] = max(deps.get(key, 0), val)

    def _emit_waits(self, e, deps):
        eng = self.eng[e]
        for key, val in deps.items():
            eng.wait_ge(self.sems[key], val)
            self.waited[e][key] = val
            self.n_inst += 1

    def _collect(self, e, reads, writes, dma=False):
        deps = {}
        for b in reads:
            if b.lw is not None:
                self._need(e, deps, *b.lw)
        for b in writes:
            if b.lw is not None:
                if not (dma and b.lw[0] == b.dsem and not b.rd):
                    self._need(e, deps, *b.lw)
            for r in b.rd:
                self._need(e, deps, *r)
        return deps

    def _stamp(self, stamp, reads, writes):
        for b in list(reads) + list(writes):
            if b.lw is None and not b.rd:
                self.allbufs.append(b)
        for b in reads:
            b.rd.append(stamp)
            if len(b.rd) > 64:
                m = {}
                for k, v in b.rd:
                    m[k] = max(m.get(k, 0), v)
                b.rd = list(m.items())
        for b in writes:
            b.lw = stamp
            b.rd = []

    def op(self, e, fn, reads=(), writes=(), sig=True):
        deps = self._collect(e, reads, writes)
        if e == "pe":
            deps.pop(self.esem["pe"], None)
        self._emit_waits(e, deps)
        ins = fn()
        self.n_inst += 1
        key = self.esem[e]
        if sig:
            self.cnt[key] += 1
            ins.then_inc(self.sems[key], 1)
            stamp = (key, self.cnt[key])
        else:
            stamp = (key, self.cnt[key] + 1)
        self._stamp(stamp, reads, writes)
        return ins

    def dma(self, q, out_ap, in_ap, reads=(), writes=(), sb=None, ndesc=0, **kw):
        if q == "pool" and ndesc:
            while self.pool_fifo and sum(n for _, _, n in self.pool_fifo) + ndesc > 3000:
                key_, val_, _ = self.pool_fifo.pop(0)
                d_ = {}
                self._need("pool", d_, key_, val_)
                self._emit_waits("pool", d_)
        if sb.dsem is None:
            if self.free_dsems:
                sb.dsem = self.free_dsems.pop()
            else:
                sb.dsem = self._alloc_sem(f"D{self.nsem}")
            self.dbufs.append(sb)
        deps = self._collect(q, reads, writes, dma=True)
        for b in list(reads) + list(writes):
            if b.dsem is not None and b.dsem in deps:
                deps[b.dsem] = self.cnt[b.dsem]
        self._emit_waits(q, deps)
        ins = self.eng[q].dma_start(out=out_ap, in_=in_ap, **kw)
        self.n_inst += 1
        key = sb.dsem
        self.cnt[key] += 16
        ins.then_inc(self.sems[key], 16)
        self._stamp((key, self.cnt[key]), reads, writes)
        if q == "pool" and ndesc:
            self.pool_fifo.append((key, self.cnt[key], ndesc))
        return ins

    def barrier(self):
        for e in self.ENG:
            deps = {}
            for key, v in self.cnt.items():
                if v > 0 and key != self.esem[e]:
                    self._need(e, deps, key, v)
            if e != "pe" and self.cnt[self.esem[e]] > 0:
                self._need(e, deps, self.esem[e], self.cnt[self.esem[e]])
            self._emit_waits(e, deps)
        for b in self.allbufs:
            b.lw = None
            b.rd = []
        self.allbufs = []
        for b in self.dbufs:
            self.free_dsems.append(b.dsem)
            b.dsem = None
        self.dbufs = []
        self.pool_fifo = []
        for e in self.ENG:
            if self.cnt[self.esem[e]] > 20000:
                key = f"E{e}{self.nsem}"
                self._alloc_sem(key)
                self.esem[e] = key

    def collective(self, in_ap, out_ap, groups, reads=(), writes=()):
        deps = self._collect("pool", reads, writes)
        self._emit_waits("pool", deps)
        ins = self.nc.gpsimd.collective_compute("AllGather", ALU.bypass, replica_groups=groups, ins=[in_ap], outs=[out_ap])
        key = self._alloc_sem(f"C{self.nsem}")
        ins.then_inc(self.sems[key])
        self.cnt[key] = 1
        self.n_inst += 1
        self._stamp((key, 1), reads, writes)

    def finish(self):
        deps = {}
        for key, v in self.cnt.items():
            if v > 0:
                self._need("sp", deps, key, v)
        self._emit_waits("sp", deps)


SM_FIELDS = [("bmod", 72), ("normg", 24), ("bgate", 32), ("convaw", 6), ("lcw", 8), ("lcb", 2),
             ("lba", 4), ("lbx", 4), ("llam", 4), ("gqb", 1), ("gkb", 1), ("gqd", 1), ("gkd", 1),
             ("sink", 2), ("cvec", 16), ("hvalid", 4), ("eps", 1),
             ("onesD", 128), ("ones64", 128), ("rmat", 128), ("ones1", 128),
             ("wa00", 128), ("wa01", 128), ("wa10", 128), ("wa11", 128),
             ("wx00", 128), ("wx01", 128), ("wx10", 128), ("wx11", 128),
             ("lsel", 4), ("rsel", 4), ("fsel", 4), ("bsel", 4)]
SM_OFF = {}
_o = 0
for _n, _w in SM_FIELDS:
    SM_OFF[_n] = (_o, _w)
    _o += _w
NSM = _o


def chunked(v, nch):
    return np.ascontiguousarray(np.asarray(v, np.float32).reshape(nch, 128).T)


def blockdiag(a, b):
    m = np.zeros((128, 128), np.float32)
    m[:64, :64] = a
    m[64:, 64:] = b
    return m


def build_sm(inp, l, b, q):
    sm = np.zeros((128, NSM), np.float32)

    def put(name, arr):
        o, w = SM_OFF[name]
        sm[:, o:o + w] = np.asarray(arr, np.float32).reshape(128, w)

    put("bmod", chunked(inp["b_mod"][l], 72))
    put("normg", np.concatenate([chunked(inp["norm_g"][l, s], 8) for s in range(3)], 1))
    put("bgate", chunked(inp["b_gate"][l], 32))
    ca = inp["conv_a_w"][l]
    put("convaw", np.stack([ca[j, c * 128:(c + 1) * 128] for c in range(2) for j in range(3)], 1))
    lw = inp["lru_conv_w"][l]
    put("lcw", np.stack([lw[j, c * 128:(c + 1) * 128] for c in range(2) for j in range(4)], 1))
    put("lcb", chunked(inp["lru_conv_b"][l], 2))
    for nm, key in (("lba", "lru_b_a"), ("lbx", "lru_b_x"), ("llam", "lru_lam")):
        v = inp[key][l]
        put(nm, np.stack([v[d, c * 128:(c + 1) * 128] for d in range(2) for c in range(2)], 1))
    g = inp["qk_norm_g"][l]
    for i, nm in enumerate(("gqb", "gkb", "gqd", "gkd")):
        put(nm, np.tile(g[i], 2)[:, None])
    sk = inp["attn_sink"][l]
    put("sink", np.stack([np.repeat(sk[2 * c:2 * c + 2], 64) for c in range(2)], 1))
    cv = np.stack([inp["c"][b], inp["c_ctx"]], 1)
    put("cvec", cv.reshape(8, 128, 2).transpose(1, 0, 2).reshape(128, 16))
    hv = np.array([1.0 if q > 0 else 0.0, 1.0 if q > 0 else 0.0, 1.0 if q < 3 else 0.0, 0.0], np.float32)
    put("hvalid", np.tile(hv[None, :], (128, 1)))
    put("eps", np.full((128, 1), EPS, np.float32))
    put("onesD", np.full((128, 128), 1.0 / D, np.float32))
    put("ones64", blockdiag(np.full((64, 64), 1.0 / 64, np.float32), np.full((64, 64), 1.0 / 64, np.float32)))
    r = np.zeros((128, 128), np.float32)
    for m in range(128):
        if (m % 32) < 16:
            r[m + 16, m] = -1.0
        else:
            r[m - 16, m] = 1.0
    put("rmat", r)
    put("ones1", np.ones((128, 128), np.float32))
    put("lsel", np.tile(np.array([1.0 if j == q - 1 else 0.0 for j in range(4)], np.float32)[None], (128, 1)))
    put("rsel", np.tile(np.array([1.0 if j == q + 1 else 0.0 for j in range(4)], np.float32)[None], (128, 1)))
    put("fsel", np.tile(np.array([1.0 if j < q else 0.0 for j in range(4)], np.float32)[None], (128, 1)))
    put("bsel", np.tile(np.array([1.0 if j > q else 0.0 for j in range(4)], np.float32)[None], (128, 1)))
    for d in range(2):
        for c in range(2):
            put(f"wa{d}{c}", blockdiag(inp["lru_w_a"][l, d, 2 * c], inp["lru_w_a"][l, d, 2 * c + 1]))
            put(f"wx{d}{c}", blockdiag(inp["lru_w_x"][l, d, 2 * c], inp["lru_w_x"][l, d, 2 * c + 1]))
    return sm


def rope_tables(q):
    pos = q * T + np.arange(T)
    rows = (pos // GRID_W).astype(np.float32)
    cols = (pos % GRID_W).astype(np.float32)
    nf = 16
    inv_freq = (10000.0 ** (-np.arange(nf, dtype=np.float32) / nf)).astype(np.float32)
    p = np.arange(128)
    d = p % 64
    j = d % 16
    use_col = (d // 32) == 1
    posm = np.where(use_col[:, None], cols[None, :], rows[None, :]).astype(np.float32)
    ang = (posm * inv_freq[j][:, None]).astype(np.float32)
    return np.cos(ang).astype(np.float32), np.sin(ang).astype(np.float32)


class Ctx:
    pass


_UNIQ = [0]


def uniq(name):
    _UNIQ[0] += 1
    return f"{name}_sb{_UNIQ[0]}"


def smv(K, name, i=0, n=1):
    o, w = SM_OFF[name]
    return K.sm[:, o + i:o + i + n]


def setup_common(nc, st):
    K = Ctx()
    K.nc = nc
    K.st = st
    st.enter_context(nc.Block())
    K.P = Prog(nc, st)
    K.sbuf = lambda name, shape, dt: st.enter_context(nc.sbuf_tensor(uniq(name), shape, dt))
    K.ps = [st.enter_context(nc.psum_tensor(f"ps{i}", [128, 512], F32)) for i in range(8)]
    K.bps = [Buf(f"ps{i}") for i in range(8)]
    K.sm = K.sbuf("sm_sb", [128, NSM], F32)
    K.bsm = Buf("sm")
    names = ["onesD", "ones64", "rmat", "ones1"] + [f"w{a}{d}{c}" for a in "ax" for d in range(2) for c in range(2)]
    K.cm = K.sbuf("cm_sb", [128, len(names), 128], BF16)
    K.bcm = Buf("cm")
    K.cmi = {n: i for i, n in enumerate(names)}
    return K


def load_sm(K, sm_d):
    nc, P = K.nc, K.P
    P.dma("sp", K.sm[:], sm_d, writes=[K.bsm], sb=K.bsm)
    for n, i in K.cmi.items():
        o, w = SM_OFF[n]
        P.op("dve", lambda: nc.vector.tensor_copy(K.cm[:, i, :], K.sm[:, o:o + 128]), reads=[K.bsm], writes=[K.bcm])


def cmat(K, name):
    return K.cm[:, K.cmi[name], :]


def emit_mod(K, w_mod_d):
    nc, P = K.nc, K.P
    K.modT = K.sbuf("modT", [128, 72, 2], F32)
    K.bmod = Buf("modT")
    K.gsT = K.sbuf("gsT", [128, 3, 8, 2], F32)
    K.gateT = K.sbuf("gateT", [128, 3, 8, 2], F32)
    K.bgs = Buf("gs")
    with contextlib.ExitStack() as s2:
        wm = [s2.enter_context(nc.sbuf_tensor(uniq(f"wm{i}"), [128, 8, 1024], F32)) for i in range(2)]
        bwm = [Buf(f"wm{i}") for i in range(2)]
        sc = s2.enter_context(nc.sbuf_tensor(uniq("silu_c"), [128, 16], F32))
        bsc = Buf("silu_c")
        P.op("act", lambda: nc.scalar.activation(sc[:], smv(K, "cvec", 0, 16), AF.Silu), reads=[K.bsm], writes=[bsc])
        psm = K.ps[7]
        bpsm = K.bps[7]
        for i in range(9):
            s = i % 2
            for k in range(8):
                P.dma("sp", wm[s][:, k, :], w_mod_d[k * 128:(k + 1) * 128, i * 1024:(i + 1) * 1024],
                      writes=[bwm[s]], sb=bwm[s])
            for dk in range(8):
                cidx = i * 8 + dk
                for k in range(8):
                    P.op("pe", lambda: nc.tensor.matmul(psm[:, cidx * 2:cidx * 2 + 2], wm[s][:, k, dk * 128:(dk + 1) * 128],
                                                        sc[:, 2 * k:2 * k + 2], start=(k == 0), stop=(k == 7)),
                         reads=[bwm[s], bsc], writes=[bpsm], sig=(k == 7))
        psv = psm[:, 0:144].rearrange("p (c j) -> p c j", j=2)
        for j in range(2):
            P.op("dve", lambda: nc.vector.tensor_tensor(K.modT[:, :, j], psv[:, :, j], smv(K, "bmod", 0, 72), ALU.add),
                 reads=[bpsm, K.bsm], writes=[K.bmod])
        for s in range(3):
            for j in range(2):
                P.op("dve", lambda: nc.vector.tensor_scalar(K.gsT[:, s, :, j], K.modT[:, (3 * s + 1) * 8:(3 * s + 2) * 8, j],
                                                            1.0, None, ALU.add), reads=[K.bmod], writes=[K.bgs])
                P.op("dve", lambda: nc.vector.tensor_tensor(K.gsT[:, s, :, j], K.gsT[:, s, :, j],
                                                            smv(K, "normg", s * 8, 8), ALU.mult),
                     reads=[K.bgs, K.bsm], writes=[K.bgs])
                P.op("dve", lambda: nc.vector.tensor_scalar(K.gateT[:, s, :, j], K.modT[:, (3 * s + 2) * 8:(3 * s + 3) * 8, j],
                                                            (1.0 if s == 1 else 0.5), None, ALU.mult),
                     reads=[K.bmod], writes=[K.bgs])
        P.barrier()


def shift_col(K, s, k, j):
    return K.modT[:, 3 * s * 8 + k, j:j + 1]


def emit_adaln(K, xt, bxt, N, s, col, hT, bh, sq, bsq, rstd, brs, pst, bpst):
    nc, P = K.nc, K.P
    P.op("act", lambda: nc.scalar.activation(sq[:, :, :N], xt[:, :, :N], AF.Square), reads=[bxt], writes=[bsq])
    for k in range(8):
        P.op("pe", lambda: nc.tensor.matmul(pst[:, :N], cmat(K, "onesD"), sq[:, k, :N], start=(k == 0), stop=(k == 7)),
             reads=[bsq, K.bcm], writes=[bpst], sig=(k == 7))
    P.op("act", lambda: nc.scalar.activation(rstd[:, :N], pst[:, :N], AF.Sqrt, bias=smv(K, "eps"), scale=1.0),
         reads=[bpst, K.bsm], writes=[brs])
    P.op("dve", lambda: nc.vector.reciprocal(rstd[:, :N], rstd[:, :N]), reads=[brs], writes=[brs])
    for k in range(8):
        P.op("dve", lambda: nc.vector.scalar_tensor_tensor(K.tmp32[:, :N], xt[:, k, :N], K.gsT[:, s, k, col:col + 1],
                                                           rstd[:, :N], ALU.mult, ALU.mult),
             reads=[bxt, brs, K.bgs], writes=[K.btmp32])
        P.op("act", lambda: nc.scalar.activation(hT[:, k, :N], K.tmp32[:, :N], AF.Identity,
                                                 bias=shift_col(K, s, k, col), scale=1.0),
             reads=[K.btmp32, K.bmod], writes=[bh])


class WBufs:
    def __init__(self, gw):
        self.gw = gw
        self.bufs = []

    def at(self, col):
        return self.bufs[col // self.gw]


def load_weight_cast(K, dst, wb, src, g0, g1):
    b = Buf("wgrp")
    wb.bufs.append(b)
    K.P.dma("pool", dst[:, :, g0:g1], src.rearrange("(k p) f -> p k f", p=128)[:, :, g0:g1], writes=[b], sb=b,
            ndesc=128 * dst.shape[1])


def load_weight_groups(K, specs):
    pos = [0] * len(specs)
    done = False
    while not done:
        done = True
        for i, (dst, wb, src, width) in enumerate(specs):
            if pos[i] < width:
                g1 = min(width, pos[i] + wb.gw)
                load_weight_cast(K, dst, wb, src, pos[i], g1)
                pos[i] = g1
                done = False


def emit_ffn_phase(K, x_src, x_dst, tiles, s, wg_d, wu_d, wd_d, h_dst=None, s_next=None):
    nc, P = K.nc, K.P
    post = h_dst is not None
    with contextlib.ExitStack() as s2:
        sb = lambda name, shape, dt: s2.enter_context(nc.sbuf_tensor(uniq(name), shape, dt))
        wg = sb("wg", [128, KD, FF], BF16); bwg = WBufs(256)
        wu = sb("wu", [128, KD, FF], BF16); bwu = WBufs(256)
        wd = sb("wd", [128, KF, D], BF16); bwd = WBufs(256)
        load_weight_groups(K, [(wg, bwg, wg_d, FF), (wu, bwu, wu_d, FF)])
        load_weight_groups(K, [(wd, bwd, wd_d, D)])
        xts = [sb(f"xt{i}", [128, 8, 512], F32) for i in range(2)]; bxts = [Buf(f"xt{i}") for i in range(2)]
        hT = sb("hT", [128, 8, 512], BF16); bh = Buf("hT")
        act = sb("actT", [128, KF, 512], BF16); bact = Buf("actT")
        rstd = sb("rstd", [128, 512], F32); brs = Buf("rstd")
        K.tmp32 = sb("tmp32", [128, 512], F32); K.btmp32 = Buf("tmp32")
        sg, bsg = K.tmp32, K.btmp32
        if post:
            h2o = sb("h2o", [128, 8, 512], BF16); bh2o = Buf("h2o")
            rstd2 = sb("rstd2", [128, 512], F32); brs2 = Buf("rstd2")
        xs = x_src.rearrange("(c p) t -> p c t", p=128)
        xd = x_dst.rearrange("(c p) t -> p c t", p=128)
        hd = h_dst.rearrange("(c p) t -> p c t", p=128) if post else None
        n = len(tiles)
        gi = [0]

        def load(i):
            start, N, col = tiles[i]
            P.dma("sp", xts[i % 2][:, :, :N], xs[:, :, start:start + N], writes=[bxts[i % 2]], sb=bxts[i % 2])

        def adaln(i):
            start, N, col = tiles[i]
            emit_adaln(K, xts[i % 2], bxts[i % 2], N, s, col, hT, bh, hT, bh, rstd, brs, K.ps[6], K.bps[6])

        def gateup(i):
            start, N, col = tiles[i]
            for f in range(KF):
                pg, bpg = K.ps[gi[0] % 2], K.bps[gi[0] % 2]
                pu, bpu = K.ps[2 + gi[0] % 2], K.bps[2 + gi[0] % 2]
                gi[0] += 1
                for k in range(8):
                    P.op("pe", lambda: nc.tensor.matmul(pg[:, :N], wg[:, k, f * 128:(f + 1) * 128], hT[:, k, :N],
                                                        start=(k == 0), stop=(k == 7)),
                         reads=[bwg.at(f * 128), bh], writes=[bpg], sig=(k == 7))
                for k in range(8):
                    P.op("pe", lambda: nc.tensor.matmul(pu[:, :N], wu[:, k, f * 128:(f + 1) * 128], hT[:, k, :N],
                                                        start=(k == 0), stop=(k == 7)),
                         reads=[bwu.at(f * 128), bh], writes=[bpu], sig=(k == 7))
                P.op("act", lambda: nc.scalar.activation(sg[:, :N], pg[:, :N], AF.Silu), reads=[bpg], writes=[bsg])
                P.op("dve", lambda: nc.vector.tensor_tensor(act[:, f, :N], pu[:, :N], sg[:, :N], ALU.mult),
                     reads=[bpu, bsg], writes=[bact])

        def down(i, d0, d1):
            start, N, col = tiles[i]
            xt, bxt = xts[i % 2], bxts[i % 2]
            for dk in range(d0, d1):
                py, bpy = K.ps[4 + dk % 2], K.bps[4 + dk % 2]
                for f in range(KF):
                    P.op("pe", lambda: nc.tensor.matmul(py[:, :N], wd[:, f, dk * 128:(dk + 1) * 128], act[:, f, :N],
                                                        start=(f == 0), stop=(f == KF - 1)),
                         reads=[bwd.at(dk * 128), bact], writes=[bpy], sig=(f == KF - 1))
                P.op("dve", lambda: nc.vector.scalar_tensor_tensor(xt[:, dk, :N], py[:, :N], K.gateT[:, s, dk, col:col + 1],
                                                                   xt[:, dk, :N], ALU.mult, ALU.add),
                     reads=[bpy, bxt, K.bgs], writes=[bxt])
            if d1 == 8:
                P.dma("sp", xd[:, :, start:start + N], xt[:, :, :N], reads=[bxt], sb=bxt)

        def postadaln(i):
            start, N, col = tiles[i]
            emit_adaln(K, xts[i % 2], bxts[i % 2], N, s_next, col, h2o, bh2o, h2o, bh2o, rstd2, brs2, K.ps[7], K.bps[7])
            P.dma("sp", hd[:, :, start:start + N], h2o[:, :, :N], reads=[bh2o], sb=bh2o)

        load(0)
        if n > 1:
            load(1)
        adaln(0)
        for i in range(n):
            gateup(i)
            if i >= 1:
                if post:
                    postadaln(i - 1)
                if i + 1 < n:
                    load(i + 1)
            down(i, 0, 4)
            if i + 1 < n:
                adaln(i + 1)
            down(i, 4, 8)
        if post:
            postadaln(n - 1)
        P.barrier()


LAT_TILES = [(i * 512, 512, 0) for i in range(8)]
CTX_TILE = (T, CTX, 1)
HALO_TILE = (TT, 4, 0)


def emit_A(K, I):
    nc, P = K.nc, K.P
    xT_d, cos_d, sin_d = I.xin, I.cos_d, I.sin_d
    wg_d, wu_d, wd_d, win_d = I.wg1, I.wu1, I.wd1, I.win
    O = I.O
    x1full, h2_d = I.x1full, I.h2_d
    with contextlib.ExitStack() as sA:
        K_sbuf = lambda name, shape, dt: sA.enter_context(nc.sbuf_tensor(uniq(name), shape, dt))
        tiles = LAT_TILES + [CTX_TILE, HALO_TILE]
        emit_ffn_phase(K, xT_d, x1full, tiles, 0, wg_d, wu_d, wd_d, h_dst=h2_d, s_next=1)

        cxf = K_sbuf("cxf", [128, 2, T + 4], F32); bcxf = Buf("cxf")
        cxc = K_sbuf("cxc", [128, 2, CTX + 4], F32); bcxc = Buf("cxc")
        P.op("pool", lambda: nc.gpsimd.memset(cxc[:], 0.0), writes=[bcxc])
        gq = K_sbuf("gq", [128, 2], F32); bgq = Buf("gq")
        P.op("dve", lambda: nc.vector.tensor_scalar(gq[:, 0:1], smv(K, "gqb"), 0.125, None, ALU.mult), reads=[K.bsm], writes=[bgq])
        P.op("dve", lambda: nc.vector.tensor_scalar(gq[:, 1:2], smv(K, "gqd"), 0.125, None, ALU.mult), reads=[K.bsm], writes=[bgq])

        with contextlib.ExitStack() as s2:
            sb = lambda name, shape, dt: s2.enter_context(nc.sbuf_tensor(uniq(name), shape, dt))
            win = sb("win", [128, KD, INW], BF16); bwin = WBufs(512)
            load_weight_groups(K, [(win, bwin, win_d, INW)])
            h2t = sb("h2t", [128, 8, 512], BF16); bh2 = Buf("h2t")
            cst = sb("cost", [128, 512], F32); snt = sb("sint", [128, 512], F32); bcs = Buf("cossin")
            axs = sb("axs", [128, 2, 512], F32); baxs = Buf("axs")
            abo = sb("abo", [128, 2, 512], BF16); babo = Buf("abo")
            vo = sb("vo", [128, 2, 512], BF16); bvo = Buf("vo")
            qpl = sb("qpl", [128, 2, 512], BF16); bqpl = Buf("qpl")
            qro = sb("qro", [128, 2, 512], BF16); bqro = Buf("qro")
            ko = sb("ko", [128, 512], BF16); bko = Buf("ko")
            ktmp = sb("ktmp", [128, 512], BF16); bktmp = Buf("ktmp")
            vtok = sb("vtok", [128, 4, 384], BF16); bvtok = Buf("vtok")
            gcg = sb("gcg", [128, 2, 512], BF16); bgcg = Buf("gcg")
            dqo = sb("dqo", [128, 2, 512], BF16); bdqo = Buf("dqo")
            dko = sb("dko", [128, 2, 512], BF16); bdko = Buf("dko")
            gto = sb("gto", [128, 8, 512], BF16); bgto = Buf("gto")
            sqb = [sb(f"sqb{i}", [128, 512], BF16) for i in range(2)]; bsqb = [Buf(f"sqb{i}") for i in range(2)]
            rs = [sb(f"rs{i}", [128, 512], F32) for i in range(2)]; brs = [Buf(f"rs{i}") for i in range(2)]
            qh = [sb(f"qh{i}", [128, 512], F32) for i in range(2)]; bqh = [Buf(f"qh{i}") for i in range(2)]
            t1 = [sb(f"t1{i}", [128, 512], F32) for i in range(2)]; bt1 = [Buf(f"t1{i}") for i in range(2)]
            t2 = [sb(f"t2{i}", [128, 512], F32) for i in range(2)]; bt2 = [Buf(f"t2{i}") for i in range(2)]
            bout = Buf("aout")
            h2s = h2_d.rearrange("(c p) t -> p c t", p=128)
            v3 = lambda ap: ap.rearrange("(c p) t -> p c t", p=128)
            ov = {n: v3(O[n]) for n in ("abT", "gcgT", "dqT", "gT")}
            pctr = [0]

            def proj(c, N):
                i = (0, 1, 2, 6, 7)[pctr[0] % 5]
                pctr[0] += 1
                ps, bps = K.ps[i], K.bps[i]
                for k in range(8):
                    P.op("pe", lambda: nc.tensor.matmul(ps[:, :N], win[:, k, c * 128:(c + 1) * 128], h2t[:, k, :N],
                                                        start=(k == 0), stop=(k == 7)),
                         reads=[bwin.at(c * 128), bh2], writes=[bps], sig=(k == 7))
                step_pending()
                return ps, bps

            pending = []

            def step_pending():
                for g in list(pending):
                    try:
                        next(g)
                    except StopIteration:
                        pending.remove(g)

            def drain():
                while pending:
                    step_pending()

            qn = [0]

            def qknorm(ps, bps, N, gcol, bg, out_plain, bop, rope_out=None, bro=None, after=None):
                i = qn[0] % 2
                qn[0] += 1
                sqb_, bsqb_ = sqb[i], bsqb[i]
                rs_, brs_ = rs[i], brs[i]
                qh_, bqh_ = qh[i], bqh[i]
                t1_, bt1_ = t1[i], bt1[i]
                t2_, bt2_ = t2[i], bt2[i]
                pst, bpst = K.ps[3], K.bps[3]
                P.op("act", lambda: nc.scalar.activation(sqb_[:, :N], ps[:, :N], AF.Square), reads=[bps], writes=[bsqb_])
                yield
                P.op("pe", lambda: nc.tensor.matmul(pst[:, :N], cmat(K, "ones64"), sqb_[:, :N], start=True, stop=True),
                     reads=[bsqb_, K.bcm], writes=[bpst])
                P.op("act", lambda: nc.scalar.activation(rs_[:, :N], pst[:, :N], AF.Sqrt, bias=smv(K, "eps"), scale=1.0),
                     reads=[bpst, K.bsm], writes=[brs_])
                P.op("dve", lambda: nc.vector.reciprocal(rs_[:, :N], rs_[:, :N]), reads=[brs_], writes=[brs_])
                P.op("dve", lambda: nc.vector.scalar_tensor_tensor(qh_[:, :N], ps[:, :N], gcol, rs_[:, :N], ALU.mult, ALU.mult),
                     reads=[bps, brs_, bg], writes=[bqh_])
                P.op("act", lambda: nc.scalar.copy(out_plain[:, :N], qh_[:, :N]), reads=[bqh_], writes=[bop])
                if rope_out is not None:
                    yield
                    psr, bpsr = K.ps[4], K.bps[4]
                    P.op("pe", lambda: nc.tensor.matmul(psr[:, :N], cmat(K, "rmat"), out_plain[:, :N], start=True, stop=True),
                         reads=[bop, K.bcm], writes=[bpsr])
                    P.op("pool", lambda: nc.gpsimd.tensor_tensor(t1_[:, :N], qh_[:, :N], cst[:, :N], ALU.mult),
                         reads=[bqh_, bcs], writes=[bt1_])
                    P.op("dve", lambda: nc.vector.tensor_tensor(t2_[:, :N], psr[:, :N], snt[:, :N], ALU.mult),
                         reads=[bpsr, bcs], writes=[bt2_])
                    P.op("pool", lambda: nc.gpsimd.tensor_tensor(rope_out[:, :N], t1_[:, :N], t2_[:, :N], ALU.add),
                         reads=[bt1_, bt2_], writes=[bro])
                if after is not None:
                    after()

            def qk(*a, **kw):
                g = qknorm(*a, **kw)
                next(g)
                pending.append(g)

            for (start, N, col) in tiles:
                kind = "lat" if start < T else ("ctx" if start == T else "halo")
                P.dma("sp", h2t[:, :, :N], h2s[:, :, start:start + N], writes=[bh2], sb=bh2)
                if kind == "halo":
                    for c in range(2):
                        ps, bps = proj(10 + c, N)
                        P.op("dve", lambda: nc.vector.tensor_tensor(cxf[:, c, 0:2], ps[:, 0:2], smv(K, "hvalid", 0, 2), ALU.mult),
                             reads=[bps, K.bsm], writes=[bcxf])
                        P.op("dve", lambda: nc.vector.tensor_tensor(cxf[:, c, T + 2:T + 4], ps[:, 2:4], smv(K, "hvalid", 2, 2), ALU.mult),
                             reads=[bps, K.bsm], writes=[bcxf])
                    continue
                lat = kind == "lat"
                if lat:
                    P.dma("sp", cst[:, :N], cos_d[:, start:start + N], writes=[bcs], sb=bcs)
                    P.dma("sp", snt[:, :N], sin_d[:, start:start + N], writes=[bcs], sb=bcs)
                sl = slice(start, start + N)
                for c in range(2):
                    ps, bps = proj(0 + c, N)
                    P.op("act", lambda: nc.scalar.copy(axs[:, c, :N], ps[:, :N]), reads=[bps], writes=[baxs])
                for c in range(2):
                    ps, bps = proj(2 + c, N)
                    P.op("act", lambda: nc.scalar.copy(abo[:, c, :N], ps[:, :N]), reads=[bps], writes=[babo])
                for c in range(2):
                    ps, bps = proj(4 + c, N)
                    P.op("dve", lambda: nc.vector.tensor_tensor(vo[:, c, :N], ps[:, :N], axs[:, c, :N], ALU.mult),
                         reads=[bps, baxs], writes=[bvo])
                P.dma("sp", ov["abT"][:, :, sl], abo[:, :, :N], reads=[babo], sb=babo)
                if lat:
                    P.dma("sp", v3(O["vcvext"])[:, :, 1 + start:1 + start + N], vo[:, :, :N], reads=[bvo], sb=bvo)
                else:
                    P.dma("sp", v3(O["vcvcext"])[:, :, 1:1 + N], vo[:, :, :N], reads=[bvo], sb=bvo)
                def store_q(c, sl=sl, N=N, lat=lat):
                    for hh in range(2):
                        r0 = hh * 128 + c * 64
                        P.dma("sp", O["qplT"][r0:r0 + 64, sl], qpl[hh * 64:(hh + 1) * 64, c, :N], reads=[bqpl], sb=bqpl)
                        if lat:
                            P.dma("sp", O["qrotT"][r0:r0 + 64, sl], qro[hh * 64:(hh + 1) * 64, c, :N], reads=[bqro], sb=bqro)
                for c in range(2):
                    ps, bps = proj(6 + c, N)
                    qk(ps, bps, N, gq[:, 0:1], bgq, qpl[:, c, :], bqpl, qro[:, c, :] if lat else None, bqro,
                       after=(lambda c=c, f=store_q: f(c)))
                ps, bps = proj(8, N)
                if lat:
                    qk(ps, bps, N, smv(K, "gkb"), K.bsm, ktmp, bktmp, ko, bko,
                       after=(lambda start=start, N=N: P.dma("sp", O["kext"][:, 128 + start:128 + start + N], ko[:, :N], reads=[bko], sb=bko)))
                else:
                    qk(ps, bps, N, smv(K, "gkb"), K.bsm, ko, bko,
                       after=(lambda N=N: P.dma("sp", O["kc"][:, 0:N], ko[:, :N], reads=[bko], sb=bko)))
                psv, bpsv = K.ps[5], K.bps[5]
                nb = N // 128
                for tb in range(nb):
                    for k in range(8):
                        P.op("pe", lambda: nc.tensor.matmul(psv[:, 0:128], h2t[:, k, tb * 128:(tb + 1) * 128], win[:, k, 9 * 128:10 * 128],
                                                            start=(k == 0), stop=(k == 7)),
                             reads=[bwin.at(9 * 128), bh2], writes=[bpsv], sig=(k == 7))
                    for k in range(8):
                        P.op("pe", lambda: nc.tensor.matmul(psv[:, 128:384], h2t[:, k, tb * 128:(tb + 1) * 128], win[:, k, 18 * 128:20 * 128],
                                                            start=(k == 0), stop=(k == 7)),
                             reads=[bwin.at(18 * 128), bh2], writes=[bpsv], sig=(k == 7))
                    P.op("act", lambda: nc.scalar.copy(vtok[:, tb, :], psv[:, 0:384]), reads=[bpsv], writes=[bvtok])
                tv = lambda ap: ap.rearrange("(b p) f -> p b f", p=128)
                if lat:
                    P.dma("sp", tv(O["vext"][128 + start:128 + start + N, :]), vtok[:, 0:nb, 0:128], reads=[bvtok], sb=bvtok)
                    P.dma("sp", tv(O["vdext"][256 + start:256 + start + N, :]), vtok[:, 0:nb, 128:384], reads=[bvtok], sb=bvtok)
                else:
                    P.dma("sp", tv(O["vc"][0:N, :]), vtok[:, 0:nb, 0:128], reads=[bvtok], sb=bvtok)
                    P.dma("sp", tv(O["vdc"][0:N, :]), vtok[:, 0:nb, 128:384], reads=[bvtok], sb=bvtok)
                for c in range(2):
                    ps, bps = proj(10 + c, N)
                    if lat:
                        P.op("act", lambda: nc.scalar.copy(cxf[:, c, 2 + start:2 + start + N], ps[:, :N]), reads=[bps], writes=[bcxf])
                    else:
                        P.op("act", lambda: nc.scalar.copy(cxc[:, c, 2:2 + N], ps[:, :N]), reads=[bps], writes=[bcxc])
                for c in range(2):
                    ps, bps = proj(12 + c, N)
                    P.op("act", lambda: nc.scalar.activation(gcg[:, c, :N], ps[:, :N], AF.Gelu_apprx_tanh), reads=[bps], writes=[bgcg])
                P.dma("sp", ov["gcgT"][:, :, sl], gcg[:, :, :N], reads=[bgcg], sb=bgcg)
                for c in range(2):
                    ps, bps = proj(14 + c, N)
                    qk(ps, bps, N, gq[:, 1:2], bgq, dqo[:, c, :], bdqo,
                       after=(lambda c=c, sl=sl, N=N: P.dma("sp", ov["dqT"][:, c, sl], dqo[:, c, :N], reads=[bdqo], sb=bdqo)))
                for c in range(2):
                    ps, bps = proj(16 + c, N)
                    if lat:
                        aft = (lambda c=c, start=start, N=N: P.dma("sp", v3(O["kdext"])[:, c, 256 + start:256 + start + N], dko[:, c, :N], reads=[bdko], sb=bdko))
                    else:
                        aft = (lambda c=c, N=N: P.dma("sp", v3(O["kdc"])[:, c, 0:N], dko[:, c, :N], reads=[bdko], sb=bdko))
                    qk(ps, bps, N, smv(K, "gkd"), K.bsm, dko[:, c, :], bdko, after=aft)
                for c in range(32):
                    ps, bps = proj(20 + c, N)
                    P.op("act", lambda: nc.scalar.activation(gto[:, c % 8, :N], ps[:, :N], AF.Sigmoid, bias=smv(K, "bgate", c), scale=1.0),
                         reads=[bps, K.bsm], writes=[bgto])
                    if c % 8 == 7:
                        c0 = c - 7
                        P.dma("sp", ov["gT"][:, c0:c0 + 8, sl], gto[:, :, :N], reads=[bgto], sb=bgto)
                drain()
            P.barrier()

        with contextlib.ExitStack() as s2:
            sb = lambda name, shape, dt: s2.enter_context(nc.sbuf_tensor(uniq(name), shape, dt))
            u = sb("lru_u", [128, T], F32); bu = Buf("lru_u")
            ub = sb("lru_ub", [128, T], BF16); bub = Buf("lru_ub")
            ra = sb("lru_a", [128, T], F32); bra = Buf("lru_a")
            ib = sb("lru_b", [128, T], F32); bib = Buf("lru_b")
            tm = sb("lru_t", [128, T], F32); btm = Buf("lru_t")
            hh = sb("lru_h", [128, T], F32); bhh = Buf("lru_h")
            pc = sb("lru_p", [128, T], F32); bpc = Buf("lru_p")
            zz = sb("lru_z", [128, T], F32); bzz = Buf("lru_z")
            hs = sb("lru_hs", [128, CTX], F32); bhs = Buf("lru_hs")
            nl8 = sb("nl8", [128, 4], F32); bnl = Buf("nl8")
            lsum, bls = K.lsum, K.bls
            blo = Buf("lruout")
            P.op("pool", lambda: nc.gpsimd.memset(zz[:], 0.0), writes=[bzz])
            P.op("act", lambda: nc.scalar.activation(nl8[:], smv(K, "llam", 0, 4), AF.Exp, scale=-1.0), reads=[K.bsm], writes=[bnl])
            P.op("act", lambda: nc.scalar.activation(nl8[:], nl8[:], AF.Ln, bias=1.0, scale=1.0), reads=[bnl], writes=[bnl])
            P.op("dve", lambda: nc.vector.tensor_scalar(nl8[:], nl8[:], -8.0, None, ALU.mult), reads=[bnl], writes=[bnl])
            pi = [0]

            def lru_chain(src, bsrc, Tn, c, is_ctx):
                lw = lambda j: smv(K, "lcw", c * 4 + j)
                P.op("dve", lambda: nc.vector.tensor_scalar(u[:, :Tn], src[:, c, 0:Tn], lw(0), smv(K, "lcb", c), ALU.mult, ALU.add),
                     reads=[bsrc, K.bsm], writes=[bu])
                for j in range(1, 4):
                    P.op("dve", lambda: nc.vector.scalar_tensor_tensor(u[:, :Tn], src[:, c, j:j + Tn], lw(j), u[:, :Tn], ALU.mult, ALU.add),
                         reads=[bsrc, bu, K.bsm], writes=[bu])
                P.op("act", lambda: nc.scalar.copy(ub[:, :Tn], u[:, :Tn]), reads=[bu], writes=[bub])
                for d in range(2):
                    for t0 in range(0, Tn, 512):
                        n = min(512, Tn - t0)
                        for (wn, bn, dst, bdst) in ((f"wa{d}{c}", "lba", ra, bra), (f"wx{d}{c}", "lbx", ib, bib)):
                            ps, bps = K.ps[pi[0] % 4], K.bps[pi[0] % 4]
                            pi[0] += 1
                            P.op("pe", lambda: nc.tensor.matmul(ps[:, :n], cmat(K, wn), ub[:, t0:t0 + n], start=True, stop=True),
                                 reads=[bub, K.bcm], writes=[bps])
                            P.op("act", lambda: nc.scalar.activation(dst[:, t0:t0 + n], ps[:, :n], AF.Sigmoid,
                                                                     bias=smv(K, bn, d * 2 + c), scale=1.0),
                                 reads=[bps, K.bsm], writes=[bdst])
                    P.op("act", lambda: nc.scalar.activation(ra[:, :Tn], ra[:, :Tn], AF.Exp, scale=nl8[:, d * 2 + c:d * 2 + c + 1]),
                         reads=[bra, bnl], writes=[bra])
                    P.op("dve", lambda: nc.vector.tensor_tensor(tm[:, :Tn], ra[:, :Tn], ra[:, :Tn], ALU.mult), reads=[bra], writes=[btm])
                    P.op("act", lambda: nc.scalar.activation(tm[:, :Tn], tm[:, :Tn], AF.Sqrt, bias=1.0000001, scale=-1.0),
                         reads=[btm], writes=[btm])
                    P.op("dve", lambda: nc.vector.tensor_tensor(ib[:, :Tn], ib[:, :Tn], u[:, :Tn], ALU.mult), reads=[bib, bu], writes=[bib])
                    P.op("dve", lambda: nc.vector.tensor_tensor(ib[:, :Tn], ib[:, :Tn], tm[:, :Tn], ALU.mult), reads=[bib, btm], writes=[bib])
                    if d == 0:
                        va, vb, vh, vp, vz = ra[:, 0:Tn], ib[:, 0:Tn], hh[:, 0:Tn], pc[:, 0:Tn], zz[:, 0:Tn]
                        last = Tn - 1
                    else:
                        rv = lambda t_: t_[:, 0:Tn][:, ::-1]
                        va, vb, vh, vp, vz = rv(ra), rv(ib), rv(hh), rv(pc), rv(zz)
                        last = 0
                    P.op("dve", lambda: nc.vector.tensor_tensor_scan(vh, va, vb, 0.0, ALU.mult, ALU.add), reads=[bra, bib], writes=[bhh])
                    if not is_ctx:
                        P.op("dve", lambda: nc.vector.tensor_tensor_scan(vp, va, vz, 1.0, ALU.mult, ALU.add), reads=[bra, bzz], writes=[bpc])
                        hn, pn = ("hlocF", "pcF") if d == 0 else ("hlocB", "pcB")
                        P.dma("sp", O[hn][c * 128:(c + 1) * 128, :], hh[:, :Tn], reads=[bhh], sb=bhh)
                        P.dma("sp", O[pn][c * 128:(c + 1) * 128, :], pc[:, :Tn], reads=[bpc], sb=bpc)
                        P.op("act", lambda: nc.scalar.copy(lsum[:, c * 6 + 2 * d:c * 6 + 2 * d + 1], pc[:, last:last + 1]), reads=[bpc], writes=[bls])
                        P.op("act", lambda: nc.scalar.copy(lsum[:, c * 6 + 2 * d + 1:c * 6 + 2 * d + 2], hh[:, last:last + 1]), reads=[bhh], writes=[bls])
                    else:
                        P.op("act", lambda: nc.scalar.copy(lsum[:, c * 6 + 4 + d:c * 6 + 5 + d], hh[:, last:last + 1]), reads=[bhh], writes=[bls])
                        if d == 0:
                            P.op("pool", lambda: nc.gpsimd.tensor_copy(hs[:, :Tn], hh[:, :Tn]), reads=[bhh], writes=[bhs])
                        else:
                            P.op("pool", lambda: nc.gpsimd.tensor_tensor(hs[:, :Tn], hs[:, :Tn], hh[:, :Tn], ALU.add), reads=[bhh, bhs], writes=[bhs])
                            P.dma("sp", O["hcsum"][c * 128:(c + 1) * 128, :], hs[:, :Tn], reads=[bhs], sb=bhs)

            for c in range(2):
                lru_chain(cxf, bcxf, T, c, False)
                lru_chain(cxc, bcxc, CTX, c, True)
            P.dma("sp", O["lsum"][:, :], lsum[:], reads=[bls], sb=bls)
            P.barrier()


def core_bq(core):
    return core // 4, core % 4


KEXT = T + 256
KDEXT = T + 512
NAJ = {0: list(range(0, 7)), 1: list(range(5)), 2: list(range(5)), 3: list(range(5)), 4: list(range(-2, 5))}
NAOFF = {}
_o = 0
for _v in range(5):
    NAOFF[_v] = _o
    _o += 4 * len(NAJ[_v])
NATILES = _o


def emit_B(K, I, ctx_out):
    nc, P = K.nc, K.P
    O = I.O
    x1_d = I.x1full
    qrot_d, qpl_d = O["qrotT"], O["qplT"]
    kext_d, vext_d, kc_d, vc_d = O["kext"], O["vext"], O["kc"], O["vc"]
    dq_d = O["dqT"]
    kdext_d, vdext_d, kdc_d, vdc_d = O["kdext"], O["vdext"], O["kdc"], O["vdc"]
    vcv_d, ab_d, vcvc_d, gcg_d = O["vcvext"], O["abT"], O["vcvcext"], O["gcgT"]
    hl_d = [O[n] for n in ("hlocF", "pcF", "hlocB", "pcB")]
    hcs_d, g_d = O["hcsum"], O["gT"]
    wmask_d, natab_d = I.wmask, I.natab
    wbr_d, wout_d = I.wbr, I.wout
    wg_d, wu_d, wd_d = I.wg2, I.wu2, I.wd2
    x3_d, x2_d = I.xout, I.x2_d
    if True:
        tiles = LAT_TILES + ([CTX_TILE] if ctx_out else [])
        with contextlib.ExitStack() as s2:
            sb = lambda name, shape, dt: s2.enter_context(nc.sbuf_tensor(uniq(name), shape, dt))
            kext = sb("kext", [128, KEXT], BF16); bkext = Buf("kext")
            vext = sb("vext", [128, KEXT // 128, 128], BF16); bvext = Buf("vext")
            kc = sb("kc", [128, CTX], BF16); bkc = Buf("kc")
            vc = sb("vc", [128, 2, 128], BF16); bvc = Buf("vc")
            kdext = sb("kdext", [128, 2, KDEXT], BF16); bkdext = Buf("kdext")
            vdext = sb("vdext", [128, KDEXT // 128, MW], BF16); bvdext = Buf("vdext")
            kdc = sb("kdc", [128, 2, CTX], BF16); bkdc = Buf("kdc")
            vdc = sb("vdc", [128, 2, MW], BF16); bvdc = Buf("vdc")
            P.dma("sp", kext[:], kext_d[:, :], writes=[bkext], sb=bkext)
            P.dma("sp", vext[:], vext_d.rearrange("(b p) f -> p b f", p=128), writes=[bvext], sb=bvext)
            P.dma("sp", kc[:], kc_d[:, :], writes=[bkc], sb=bkc)
            P.dma("sp", vc[:], vc_d.rearrange("(b p) f -> p b f", p=128), writes=[bvc], sb=bvc)
            P.dma("sp", kdext[:], kdext_d.rearrange("(c p) t -> p c t", p=128), writes=[bkdext], sb=bkdext)
            P.dma("sp", vdext[:], vdext_d.rearrange("(b p) f -> p b f", p=128), writes=[bvdext], sb=bvdext)
            P.dma("sp", kdc[:], kdc_d.rearrange("(c p) t -> p c t", p=128), writes=[bkdc], sb=bkdc)
            P.dma("sp", vdc[:], vdc_d.rearrange("(b p) f -> p b f", p=128), writes=[bvdc], sb=bvdc)
            wbr = sb("wbr", [128, 8, D], BF16); bwbr = WBufs(512)
            wout = sb("wout", [128, 8, D], BF16); bwout = WBufs(512)
            load_weight_groups(K, [(wbr, bwbr, wbr_d, D)])
            load_weight_groups(K, [(wout, bwout, wout_d, D)])
            wm32 = sb("wm32", [128, 4, 128], F32); bwm32 = Buf("wm32")
            wmk = sb("wmk", [128, 4, 128], BF16); bwmk = Buf("wmk")
            P.dma("sp", wm32[:], wmask_d[:, :, :], writes=[bwm32], sb=bwm32)
            P.op("dve", lambda: nc.vector.tensor_copy(wmk[:], wm32[:]), reads=[bwm32], writes=[bwmk])
            etab = sb("etab", [128, NATILES * 128], BF16); betab = Buf("etab")
            with contextlib.ExitStack() as s3:
                tst = s3.enter_context(nc.sbuf_tensor(uniq("tst"), [128, 29, 128], F32)); btst = Buf("tst")
                for v in range(4):
                    P.dma("sp", tst[:], natab_d[v * 29:(v + 1) * 29].rearrange("n k q -> k n q"), writes=[btst], sb=btst)
                    P.op("act", lambda: nc.scalar.activation(etab[:, v * 29 * 128:(v + 1) * 29 * 128], tst[:].rearrange("p c q -> p (c q)"), AF.Exp),
                         reads=[btst], writes=[betab])
                P.barrier()
            esink = sb("esink", [128, 2], F32); besink = Buf("esink")
            P.op("act", lambda: nc.scalar.activation(esink[:], smv(K, "sink", 0, 2), AF.Exp), reads=[K.bsm], writes=[besink])
            carry, bcarry = K.carry, K.bcarry
            qrot = sb("qrot", [128, 2, 512], BF16); bqrot = Buf("qrot")
            qpl = sb("qplb", [128, 2, 512], BF16); bqpl = Buf("qplb")
            dq = sb("dqb", [128, 2, 512], BF16); bdq = Buf("dqb")
            vx = sb("vx", [128, 2, 514], BF16); bvx = Buf("vx")
            ab = sb("abb", [128, 2, 512], BF16); bab = Buf("abb")
            gcg = sb("gcgb", [128, 2, 512], BF16); bgcg = Buf("gcgb")
            hl = sb("hl", [128, 4, 512], F32); bhl = Buf("hl")
            gt = [sb(f"gt{i}", [128, 4, 512], BF16) for i in range(2)]; bgt = [Buf(f"gt{i}") for i in range(2)]
            x1t = sb("x1t", [128, 8, 512], F32); bx1 = Buf("x1t")
            yT = sb("yT", [128, 8, 512], BF16); byT = Buf("yT")
            mT = sb("mT", [128, 8, 512], BF16); bmT = Buf("mT")
            macc = sb("macc", [128, 512], F32); bmacc = Buf("macc")
            mtmp = [sb(f"mtmp{i}", [128, 512], F32) for i in range(2)]; bmtmp = [Buf(f"mtmp{i}") for i in range(2)]
            pex = [sb(f"pex{i}", [128, 9 * 128], BF16) for i in range(3)]; bpex = [Buf(f"pex{i}") for i in range(3)]
            rd = [sb(f"rd{i}", [128, 128], F32) for i in range(3)]; brd = [Buf(f"rd{i}") for i in range(3)]
            ct1 = sb("ct1", [128, 512], F32); bct1 = Buf("ct1")
            ct2 = sb("ct2", [128, 512], F32); bct2 = Buf("ct2")
            ones64 = cmat(K, "ones1")[:, 0:64]
            actr = [0]

            def attn(qT, bq_, qc, qbase, cols, local, ctxk, mask_ops, sink_col, ych, ob):
                chunks = local + ctxk
                n = len(chunks)

                def stage1(i):
                    banks = [(K.ps[i], K.bps[i]), (K.ps[2 + i], K.bps[2 + i])]
                    pe_, bpe = pex[i], bpex[i]
                    for g0 in range(0, n, 4):
                        ps, bps_ = banks[(g0 // 4) % 2]
                        grp = chunks[g0:g0 + 4]
                        for jj, (kap, bk_, qap, bqq, vap, bv_) in enumerate(grp):
                            P.op("pe", lambda: nc.tensor.matmul(ps[:, jj * 128:(jj + 1) * 128], kap, qap, start=True, stop=True),
                                 reads=[bk_, bqq], writes=[bps_])
                        P.op("act", lambda: nc.scalar.activation(pe_[:, g0 * 128:(g0 + len(grp)) * 128], ps[:, 0:len(grp) * 128], AF.Exp),
                             reads=[bps_], writes=[bpe])
                    for (eng, j0, j1, map_, bm_) in mask_ops:
                        if eng == "pool":
                            P.op("pool", lambda: nc.gpsimd.tensor_tensor(pe_[:, j0 * 128:j1 * 128], pe_[:, j0 * 128:j1 * 128], map_, ALU.mult), reads=[bpe, bm_], writes=[bpe])
                        else:
                            P.op("dve", lambda: nc.vector.tensor_tensor(pe_[:, j0 * 128:j1 * 128], pe_[:, j0 * 128:j1 * 128], map_, ALU.mult), reads=[bpe, bm_], writes=[bpe])

                def stage2(i):
                    pe_, bpe = pex[i], bpex[i]
                    psO, bO = K.ps[4 + 2 * i], K.bps[4 + 2 * i]
                    psD, bD = K.ps[5 + 2 * i], K.bps[5 + 2 * i]
                    for j, (kap, bk_, qap, bqq, vap, bv_) in enumerate(chunks):
                        P.op("pe", lambda: nc.tensor.matmul(psO[ob:ob + 64, 0:128], vap, pe_[:, j * 128:(j + 1) * 128], start=(j == 0), stop=(j == n - 1)),
                             reads=[bv_, bpe], writes=[bO], sig=(j == n - 1))
                    for j in range(n):
                        P.op("pe", lambda: nc.tensor.matmul(psD[ob:ob + 64, 0:128], ones64, pe_[:, j * 128:(j + 1) * 128], start=(j == 0), stop=(j == n - 1)),
                             reads=[K.bcm, bpe], writes=[bD], sig=(j == n - 1))
                    r_, br_ = rd[i], brd[i]
                    if sink_col is not None:
                        P.op("dve", lambda: nc.vector.tensor_scalar(r_[ob:ob + 64, :], psD[ob:ob + 64, 0:128], sink_col, None, ALU.add),
                             reads=[bD, besink], writes=[br_])
                        P.op("dve", lambda: nc.vector.reciprocal(r_[ob:ob + 64, :], r_[ob:ob + 64, :]), reads=[br_], writes=[br_])
                    else:
                        P.op("dve", lambda: nc.vector.reciprocal(r_[ob:ob + 64, :], psD[ob:ob + 64, 0:128]), reads=[bD], writes=[br_])
                    P.op("dve", lambda: nc.vector.tensor_tensor(yT[ob:ob + 64, ych, cols], psO[ob:ob + 64, 0:128], r_[ob:ob + 64, :], ALU.mult),
                         reads=[bO, br_], writes=[byT])

                items.append((stage1, stage2))

            def run_items(mid):
                if items:
                    items[0][0](0)
                half = len(items) // 2
                for i in range(len(items)):
                    if i == half:
                        mid()
                    if i + 1 < len(items):
                        items[i + 1][0]((i + 1) % 2)
                    items[i][1](i % 2)
                if not items:
                    mid()
                del items[:]

            items = []
            x1s = x1_d.rearrange("(c p) t -> p c t", p=128)
            x2s = x2_d.rearrange("(c p) t -> p c t", p=128)
            v3 = lambda ap: ap.rearrange("(c p) t -> p c t", p=128)
            g4 = g_d.rearrange("(n k p) t -> p n k t", p=128, k=8)
            gctr = [0]
            for (start, N, col) in tiles:
                lat = start < T
                sl = slice(start, start + N)
                P.dma("sp", x1t[:, :, :N], x1s[:, :, sl], writes=[bx1], sb=bx1)
                P.dma("sp", qpl[:, :, :N], v3(qpl_d)[:, :, sl], writes=[bqpl], sb=bqpl)
                P.dma("sp", dq[:, :, :N], v3(dq_d)[:, :, sl], writes=[bdq], sb=bdq)
                P.dma("sp", ab[:, :, :N], v3(ab_d)[:, :, sl], writes=[bab], sb=bab)
                P.dma("sp", gcg[:, :, :N], v3(gcg_d)[:, :, sl], writes=[bgcg], sb=bgcg)
                if lat:
                    P.dma("sp", qrot[:, :, :N], v3(qrot_d)[:, :, sl], writes=[bqrot], sb=bqrot)
                    P.dma("sp", vx[:, :, :N + 2], v3(vcv_d)[:, :, start:start + N + 2], writes=[bvx], sb=bvx)
                else:
                    P.dma("sp", vx[:, :, :N + 2], v3(vcvc_d)[:, :, 0:N + 2], writes=[bvx], sb=bvx)
                def elementwise_mixers(start=start, N=N, lat=lat, sl=sl):
                    for c in range(2):
                        cw = lambda j: smv(K, "convaw", c * 3 + j)
                        P.op("dve", lambda: nc.vector.tensor_scalar(ct1[:, :N], vx[:, c, 0:N], cw(0), None, ALU.mult), reads=[bvx, K.bsm], writes=[bct1])
                        P.op("dve", lambda: nc.vector.scalar_tensor_tensor(ct1[:, :N], vx[:, c, 1:N + 1], cw(1), ct1[:, :N], ALU.mult, ALU.add),
                             reads=[bvx, K.bsm, bct1], writes=[bct1])
                        P.op("dve", lambda: nc.vector.scalar_tensor_tensor(ct1[:, :N], vx[:, c, 2:N + 2], cw(2), ct1[:, :N], ALU.mult, ALU.add),
                             reads=[bvx, K.bsm, bct1], writes=[bct1])
                        P.op("pool", lambda: nc.gpsimd.tensor_tensor(yT[:, 0 + c, :N], ct1[:, :N], ab[:, c, :N], ALU.mult), reads=[bct1, bab], writes=[byT])
                    for c in range(2):
                        if lat:
                            for a_ in range(4):
                                P.dma("sp", hl[:, a_, :N], hl_d[a_][c * 128:(c + 1) * 128, sl], writes=[bhl], sb=bhl)
                        else:
                            P.dma("sp", hl[:, 0, :N], hcs_d[c * 128:(c + 1) * 128, 0:N], writes=[bhl], sb=bhl)
                        if lat:
                            P.op("dve", lambda: nc.vector.scalar_tensor_tensor(ct1[:, :N], hl[:, 1, :N], carry[:, c * 2:c * 2 + 1], hl[:, 0, :N], ALU.mult, ALU.add),
                                 reads=[bhl, bcarry], writes=[bct1])
                            P.op("dve", lambda: nc.vector.scalar_tensor_tensor(ct2[:, :N], hl[:, 3, :N], carry[:, c * 2 + 1:c * 2 + 2], hl[:, 2, :N], ALU.mult, ALU.add),
                                 reads=[bhl, bcarry], writes=[bct2])
                            P.op("pool", lambda: nc.gpsimd.tensor_tensor(ct1[:, :N], ct1[:, :N], ct2[:, :N], ALU.add), reads=[bct1, bct2], writes=[bct1])
                            P.op("pool", lambda: nc.gpsimd.tensor_tensor(yT[:, 4 + c, :N], ct1[:, :N], gcg[:, c, :N], ALU.mult), reads=[bct1, bgcg], writes=[byT])
                        else:
                            P.op("pool", lambda: nc.gpsimd.tensor_tensor(yT[:, 4 + c, :N], hl[:, 0, :N], gcg[:, c, :N], ALU.mult), reads=[bhl, bgcg], writes=[byT])
                for blk in range(N // 128):
                    cols = slice(blk * 128, (blk + 1) * 128)
                    n_ = (start // 128) + blk
                    for h in range(4):
                        kv = h // 2
                        qc, qb_ = h % 2, kv * 64
                        ob = (h % 2) * 64
                        ctxk = [(kc[qb_:qb_ + 64, j * 128:(j + 1) * 128], bkc, qpl[qb_:qb_ + 64, qc, cols], bqpl,
                                 vc[:, j, kv * 64:(kv + 1) * 64], bvc) for j in range(2)]
                        if lat:
                            local = [(kext[qb_:qb_ + 64, (n_ + j) * 128:(n_ + j + 1) * 128], bkext, qrot[qb_:qb_ + 64, qc, cols], bqrot,
                                      vext[:, n_ + j, kv * 64:(kv + 1) * 64], bvext) for j in range(3)]
                            mp = wmk[:, 2, :] if n_ == 0 else wmk[:, 0, :]
                            mn = wmk[:, 3, :] if n_ == 31 else wmk[:, 1, :]
                            mops = [("pool", 0, 1, mp, bwmk), ("pool", 2, 3, mn, bwmk)]
                        else:
                            local, mops = [], []
                        attn(None, None, qc, qb_, cols, local, ctxk, mops, esink[ob:ob + 64, h // 2:h // 2 + 1], 2 + h // 2, ob)
                    for h in range(4):
                        hc_, hb_ = h // 2, (h % 2) * 64
                        ob = hb_
                        ctxk = [(kdc[hb_:hb_ + 64, hc_, j * 128:(j + 1) * 128], bkdc, dq[hb_:hb_ + 64, hc_, cols], bdq,
                                 vdc[:, j, h * 64:(h + 1) * 64], bvdc) for j in range(2)]
                        if lat:
                            var = 0 if n_ == 0 else 1 if n_ == 1 else 3 if n_ == 30 else 4 if n_ == 31 else 2
                            jl = NAJ[var]
                            local = [(kdext[hb_:hb_ + 64, hc_, (n_ + j) * 128:(n_ + j + 1) * 128], bkdext, dq[hb_:hb_ + 64, hc_, cols], bdq,
                                      vdext[:, n_ + j, h * 64:(h + 1) * 64], bvdext) for j in jl]
                            e0 = NAOFF[var] + h * len(jl)
                            mops = [("dve", 0, len(jl), etab[:, e0 * 128:(e0 + len(jl)) * 128], betab)]
                        else:
                            local, mops = [], []
                        attn(None, None, None, None, cols, local, ctxk, mops, None, 6 + h // 2, ob)
                run_items(elementwise_mixers)
                for dk in range(8):
                    gi_ = gctr[0] % 2
                    gctr[0] += 1
                    P.dma("sp", gt[gi_][:, :, :N], g4[:, :, dk, sl], writes=[bgt[gi_]], sb=bgt[gi_])
                    for n in range(4):
                        ps, bps_ = K.ps[(dk * 4 + n) % 4], K.bps[(dk * 4 + n) % 4]
                        for cc in range(2):
                            P.op("pe", lambda: nc.tensor.matmul(ps[:, :N], wbr[:, n * 2 + cc, dk * 128:(dk + 1) * 128], yT[:, n * 2 + cc, :N],
                                                                start=(cc == 0), stop=(cc == 1)),
                                 reads=[bwbr.at(dk * 128), byT], writes=[bps_], sig=(cc == 1))
                        if n == 0:
                            P.op("dve", lambda: nc.vector.tensor_tensor(macc[:, :N], ps[:, :N], gt[gi_][:, 0, :N], ALU.mult), reads=[bps_, bgt[gi_]], writes=[bmacc])
                        else:
                            mt_, bmt_ = mtmp[n % 2], bmtmp[n % 2]
                            P.op("dve", lambda: nc.vector.tensor_tensor(mt_[:, :N], ps[:, :N], gt[gi_][:, n, :N], ALU.mult), reads=[bps_, bgt[gi_]], writes=[bmt_])
                            if n < 3:
                                P.op("pool", lambda: nc.gpsimd.tensor_tensor(macc[:, :N], macc[:, :N], mt_[:, :N], ALU.add), reads=[bmacc, bmt_], writes=[bmacc])
                            else:
                                P.op("pool", lambda: nc.gpsimd.tensor_tensor(mT[:, dk, :N], macc[:, :N], mt_[:, :N], ALU.add), reads=[bmacc, bmt_], writes=[bmT])
                for dk in range(8):
                    ps, bps_ = K.ps[4 + dk % 4], K.bps[4 + dk % 4]
                    for k in range(8):
                        P.op("pe", lambda: nc.tensor.matmul(ps[:, :N], wout[:, k, dk * 128:(dk + 1) * 128], mT[:, k, :N], start=(k == 0), stop=(k == 7)),
                             reads=[bwout.at(dk * 128), bmT], writes=[bps_], sig=(k == 7))
                    P.op("dve", lambda: nc.vector.scalar_tensor_tensor(x1t[:, dk, :N], ps[:, :N], K.gateT[:, 1, dk, col:col + 1], x1t[:, dk, :N], ALU.mult, ALU.add),
                         reads=[bps_, bx1, K.bgs], writes=[bx1])
                P.dma("sp", x2s[:, :, sl], x1t[:, :, :N], reads=[bx1], sb=bx1)
            P.barrier()
        emit_ffn_phase(K, x2_d, x3_d, tiles, 2, wg_d, wu_d, wd_d)


def window_masks(q):
    j = np.arange(128)[:, None]
    i = np.arange(128)[None, :]
    tp = (j >= i).astype(np.float32)
    tn = (j <= i).astype(np.float32)
    z = np.zeros((128, 128), np.float32)
    return np.ascontiguousarray(np.stack([tp, tn, tp if q > 0 else z, tn if q < 3 else z], 1))


def na_index(gm, jl):
    rows_total = SEQ // GRID_W
    kk = np.arange(128)
    qq = np.arange(128)
    qrow = 2 * gm + qq // 64
    qcol = qq % 64
    n = len(jl)
    dr = np.zeros((n, 128, 128), np.int64)
    dc = np.zeros((n, 128, 128), np.int64)
    valid = np.zeros((n, 128, 128), bool)
    rstart = np.clip(qrow - 4, 0, rows_total - 8)
    cstart = np.clip(qcol - 8, 0, GRID_W - 16)
    for i, j in enumerate(jl):
        krow = (2 * gm - 4 + 2 * j + kk // 64)[:, None]
        kcol = (kk % 64)[:, None]
        v = (krow >= rstart[None, :]) & (krow < rstart[None, :] + 8) & (kcol >= cstart[None, :]) & (kcol < cstart[None, :] + 16)
        valid[i] = v
        dr[i] = np.clip(krow - qrow[None, :] + 7, 0, 14)
        dc[i] = np.clip(kcol - qcol[None, :], -15, 15) + 15
    return dr, dc, valid


def na_tables(rel_bias, q):
    out = np.empty((NATILES, 128, 128), np.float32)
    for v, m in enumerate((0, 1, 15, 30, 31)):
        gm = q * 32 + m
        jl = NAJ[v]
        dr, dc, valid = na_index(gm, jl)
        for h in range(4):
            o = NAOFF[v] + h * len(jl)
            out[o:o + len(jl)] = np.where(valid, rel_bias[h][dr, dc], np.float32(-30000.0))
    return out


XP = 1282
GROUPS = [[0, 1, 2, 3], [4, 5, 6, 7]]


def emit_exchange(K, I):
    nc, P, O = K.nc, K.P, I.O
    pack, packall, lpack, lall = I.pack.ap(), I.packall.ap(), I.lpack.ap(), I.lall.ap()
    bpk, bpa, blp, bla = Buf("pack"), Buf("packall"), Buf("lpack"), Buf("lall")
    v3 = lambda ap: ap.rearrange("(c p) t -> p c t", p=128)
    tv = lambda ap: ap.rearrange("(b p) f -> p b f", p=128)
    w3 = lambda ap, w: ap.rearrange("p (a w) -> p a w", w=w)
    kext, vext, kdext, vdext, vcv = O["kext"], O["vext"], O["kdext"], O["vdext"], O["vcvext"]
    for side in range(2):
        o = side * XP
        if side == 0:
            srcs = [kext[:, T:T + 128], vext[T:T + 128, :], v3(kdext)[:, :, T:T + 256], tv(vdext[T:T + 256, :]), v3(vcv)[:, :, T:T + 1]]
        else:
            srcs = [kext[:, 128:256], vext[128:256, :], v3(kdext)[:, :, 256:512], tv(vdext[256:512, :]), v3(vcv)[:, :, 1:2]]
        dsts = [pack[:, o:o + 128], pack[:, o + 128:o + 256], w3(pack[:, o + 256:o + 768], 256), w3(pack[:, o + 768:o + 1280], 256),
                w3(pack[:, o + 1280:o + 1282], 1)]
        for d_, s_ in zip(dsts, srcs):
            P.dma("sp", d_, s_, writes=[bpk], sb=bpk, allow_slow_non_contiguous=True)
    P.collective(I.pack.ap().opt(), I.packall.ap().opt(), GROUPS, reads=[bpk], writes=[bpa])
    with contextlib.ExitStack() as s2:
        sb = lambda name, shape, dt: s2.enter_context(nc.sbuf_tensor(uniq(name), shape, dt))
        cand = sb("cand", [128, 4, 2 * XP], BF16); bcand = Buf("cand")
        hl = sb("halo_l", [128, XP], BF16); bhl = Buf("halo_l")
        hr = sb("halo_r", [128, XP], BF16); bhr = Buf("halo_r")
        P.dma("sp", cand[:], packall.rearrange("(j p) x -> p j x", p=128), reads=[bpa], writes=[bcand], sb=bcand)
        for (dst, bdst, c0, sel) in ((hl, bhl, 0, "lsel"), (hr, bhr, XP, "rsel")):
            P.op("dve", lambda: nc.vector.tensor_scalar(dst[:], cand[:, 0, c0:c0 + XP], smv(K, sel, 0), None, ALU.mult),
                 reads=[bcand, K.bsm], writes=[bdst])
            for j in range(1, 4):
                P.op("dve", lambda: nc.vector.scalar_tensor_tensor(dst[:], cand[:, j, c0:c0 + XP], smv(K, sel, j), dst[:], ALU.mult, ALU.add),
                     reads=[bcand, K.bsm, bdst], writes=[bdst])
        for (src, bsrc, kc, vr, dk, dv, cv) in ((hl, bhl, slice(0, 128), slice(0, 128), slice(0, 256), slice(0, 256), slice(0, 1)),
                                              (hr, bhr, slice(128 + T, 256 + T), slice(128 + T, 256 + T), slice(256 + T, 512 + T),
                                               slice(256 + T, 512 + T), slice(T + 1, T + 2))):
            P.dma("sp", kext[:, kc], src[:, 0:128], reads=[bsrc], sb=bsrc)
            P.dma("sp", vext[vr, :], src[:, 128:256], reads=[bsrc], sb=bsrc)
            P.dma("sp", v3(kdext)[:, :, dk], w3(src[:, 256:768], 256), reads=[bsrc], sb=bsrc)
            P.dma("sp", tv(vdext[dv, :]), w3(src[:, 768:1280], 256), reads=[bsrc], sb=bsrc)
            P.dma("sp", v3(vcv)[:, :, cv], w3(src[:, 1280:1282], 1), reads=[bsrc], sb=bsrc, allow_slow_non_contiguous=True)
        P.dma("sp", lpack[:, :], K.lsum[:], reads=[K.bls], writes=[blp], sb=K.bls)
        P.collective(I.lpack.ap().opt(), I.lall.ap().opt(), GROUPS, reads=[blp], writes=[bla])
        lc = sb("lcand", [128, 4, 12], F32); blc = Buf("lcand")
        tt_ = sb("ltmp", [128, 2], F32); btt = Buf("ltmp")
        P.dma("sp", lc[:], lall.rearrange("(j p) x -> p j x", p=128), reads=[bla], writes=[blc], sb=blc)
        carry, bcarry = K.carry, K.bcarry
        for c in range(2):
            for d in range(2):
                cc = carry[:, c * 2 + d:c * 2 + d + 1]
                P.op("dve", lambda: nc.vector.tensor_copy(cc, K.lsum[:, c * 6 + 4 + d:c * 6 + 5 + d]), reads=[K.bls], writes=[bcarry])
                order = [0, 1, 2, 3] if d == 0 else [3, 2, 1, 0]
                sel = "fsel" if d == 0 else "bsel"
                for j in order:
                    a_ = lc[:, j, c * 6 + 2 * d:c * 6 + 2 * d + 1]
                    h_ = lc[:, j, c * 6 + 2 * d + 1:c * 6 + 2 * d + 2]
                    P.op("dve", lambda: nc.vector.scalar_tensor_tensor(tt_[:, 0:1], cc, a_, h_, ALU.mult, ALU.add),
                         reads=[bcarry, blc], writes=[btt])
                    P.op("dve", lambda: nc.vector.tensor_tensor(tt_[:, 1:2], tt_[:, 0:1], cc, ALU.subtract), reads=[btt, bcarry], writes=[btt])
                    P.op("dve", lambda: nc.vector.scalar_tensor_tensor(cc, tt_[:, 1:2], smv(K, sel, j), cc, ALU.mult, ALU.add),
                         reads=[btt, K.bsm, bcarry], writes=[bcarry])
        P.barrier()


def emit_xhalo(K, I):
    nc, P = K.nc, K.P
    xin1 = I.xin1
    xp, xall = I.xpack.ap(), I.xall.ap()
    bxp, bxa = Buf("xpack"), Buf("xall")
    v3 = lambda ap: ap.rearrange("(c p) t -> p c t", p=128)
    w3 = lambda ap, w: ap.rearrange("p (a w) -> p a w", w=w)
    P.dma("sp", w3(xp[:, 0:16], 2), v3(xin1)[:, :, T - 2:T], writes=[bxp], sb=bxp)
    P.dma("sp", w3(xp[:, 16:24], 1), v3(xin1)[:, :, 0:1], writes=[bxp], sb=bxp, allow_slow_non_contiguous=True)
    P.collective(I.xpack.ap().opt(), I.xall.ap().opt(), GROUPS, reads=[bxp], writes=[bxa])
    with contextlib.ExitStack() as s2:
        sb = lambda name, shape, dt: s2.enter_context(nc.sbuf_tensor(uniq(name), shape, dt))
        xc = sb("xcand", [128, 4, 24], F32); bxc = Buf("xcand")
        hx = sb("xhalo", [128, 32], F32); bhx = Buf("xhalo")
        P.dma("sp", xc[:], xall.rearrange("(j p) x -> p j x", p=128), reads=[bxa], writes=[bxc], sb=bxc)
        P.op("dve", lambda: nc.vector.memset(hx[:], 0.0), writes=[bhx])
        for (c0, n, o, sel) in ((0, 16, 0, "lsel"), (16, 8, 16, "rsel")):
            for j in range(4):
                P.op("dve", lambda: nc.vector.scalar_tensor_tensor(hx[:, o:o + n], xc[:, j, c0:c0 + n], smv(K, sel, j), hx[:, o:o + n], ALU.mult, ALU.add),
                     reads=[bxc, K.bsm, bhx], writes=[bhx])
        P.dma("sp", v3(xin1)[:, :, TT:TT + 2], w3(hx[:, 0:16], 2), reads=[bhx], sb=bhx)
        P.dma("sp", v3(xin1)[:, :, TT + 2:TT + 3], w3(hx[:, 16:24], 1), reads=[bhx], sb=bhx, allow_slow_non_contiguous=True)
        P.dma("sp", v3(xin1)[:, :, TT + 3:TT + 4], w3(hx[:, 24:32], 1), reads=[bhx], sb=bhx, allow_slow_non_contiguous=True)
        P.barrier()


INTERNALS = [("abT", [MW, TT], BF16), ("qrotT", [MW, T], BF16), ("qplT", [MW, TT], BF16),
             ("kext", [128, KEXT], BF16), ("vext", [KEXT, 128], BF16), ("kc", [128, CTX], BF16), ("vc", [CTX, 128], BF16),
             ("gcgT", [MW, TT], BF16), ("dqT", [MW, TT], BF16),
             ("kdext", [MW, KDEXT], BF16), ("vdext", [KDEXT, MW], BF16), ("kdc", [MW, CTX], BF16), ("vdc", [CTX, MW], BF16),
             ("vcvext", [MW, T + 2], BF16), ("vcvcext", [MW, CTX + 2], BF16), ("gT", [4 * D, TT], BF16),
             ("hlocF", [MW, T], F32), ("pcF", [MW, T], F32), ("hlocB", [MW, T], F32), ("pcB", [MW, T], F32),
             ("hcsum", [MW, CTX], F32), ("lsum", [128, 12], F32)]


def build_fused():
    nc = bass.Bass("TRN2", target_bir_lowering=False, dynamic_dma_scratch_size=4096)
    din = lambda n, sh, dt=F32: nc.dram_tensor(n, sh, dt, kind="ExternalInput").ap()
    dint = lambda n, sh, dt=F32: nc.dram_tensor(n, sh, dt, kind="Internal").ap()
    I = Ctx()
    xT_d = din("xT", [D, TX])
    I.cos_d = din("cosT", [128, T]); I.sin_d = din("sinT", [128, T])
    sm_d = [din(f"sm{l}", [128, NSM]) for l in range(2)]
    I.wmask = din("wmask", [128, 4, 128])
    natab_d = din("natab", [2, NATILES, 128, 128])
    wmod_d = din("w_mod", [2, D, 9 * D])
    wg_d = din("wg", [2, 2, D, FF]); wu_d = din("wu", [2, 2, D, FF]); wd_d = din("wd", [2, 2, FF, D])
    win_d = din("w_in", [2, D, INW])
    wbr_d = din("w_branch", [2, D, D]); wout_d = din("w_out", [2, D, D])
    out_d = nc.dram_tensor("outT", [D, T], F32, kind="ExternalOutput").ap()
    I.O = {n: dint(n, sh, dt) for n, sh, dt in INTERNALS}
    I.x1full = dint("x1full", [D, TX]); I.h2_d = dint("h2T", [D, TX], BF16)
    I.x2_d = dint("x2T", [D, TT]); I.xin1 = dint("xin1", [D, TX])
    I.pack = nc.dram_tensor("pack", [128, 2 * XP], BF16); I.packall = nc.dram_tensor("packall", [512, 2 * XP], BF16)
    I.lpack = nc.dram_tensor("lpack", [128, 12], F32); I.lall = nc.dram_tensor("lall", [512, 12], F32)
    I.xpack = nc.dram_tensor("xpack", [128, 24], F32); I.xall = nc.dram_tensor("xall", [512, 24], F32)

    with contextlib.ExitStack() as st:
        K = setup_common(nc, st)
        P = K.P
        K.lsum = K.sbuf("lsum_sb", [128, 12], F32); K.bls = Buf("lsum")
        K.carry = K.sbuf("carry", [128, 4], F32); K.bcarry = Buf("carry")
        zt = K.sbuf("zeros_sb", [128, 2, 1], BF16); bzt = Buf("zeros")
        P.op("dve", lambda: nc.vector.memset(zt[:], 0.0), writes=[bzt])
        vcc = I.O["vcvcext"].rearrange("(c p) t -> p c t", p=128)
        P.dma("sp", vcc[:, :, 0:1], zt[:], reads=[bzt], sb=bzt, allow_slow_non_contiguous=True)
        P.dma("sp", vcc[:, :, CTX + 1:CTX + 2], zt[:], reads=[bzt], sb=bzt, allow_slow_non_contiguous=True)
        for l in range(2):
            load_sm(K, sm_d[l][:, :])
            emit_mod(K, wmod_d[l])
            I.xin = xT_d if l == 0 else I.xin1
            I.wg1, I.wu1, I.wd1, I.win = wg_d[l, 0], wu_d[l, 0], wd_d[l, 0], win_d[l]
            I.wg2, I.wu2, I.wd2 = wg_d[l, 1], wu_d[l, 1], wd_d[l, 1]
            I.wbr, I.wout, I.natab = wbr_d[l], wout_d[l], natab_d[l]
            I.xout = I.xin1 if l == 0 else out_d
            emit_A(K, I)
            emit_exchange(K, I)
            emit_B(K, I, ctx_out=(l == 0))
            if l == 0:
                emit_xhalo(K, I)
        P.finish()
    return nc


_NC = []


def kernel(**inp):
    inp = {k: np.asarray(v) for k, v in inp.items()}
    if not _NC:
        _NC.append(build_fused())
    nc = _NC[0]
    x = np.asarray(inp["x"], np.float32)
    shared = {
        "w_mod": np.ascontiguousarray(inp["w_mod"], np.float32),
        "wg": np.ascontiguousarray(inp["ffn_w_gate"], np.float32), "wu": np.ascontiguousarray(inp["ffn_w_up"], np.float32),
        "wd": np.ascontiguousarray(inp["ffn_w_down"], np.float32), "w_in": np.ascontiguousarray(inp["w_in"], np.float32),
        "w_branch": np.ascontiguousarray(np.asarray(inp["w_branch"], np.float32).reshape(2, D, D)),
        "w_out": np.ascontiguousarray(inp["w_out"], np.float32),
    }
    in_maps = []
    for core in range(NC):
        b, q = core_bq(core)
        xs = x[b]
        lo, hi = q * T, (q + 1) * T
        halo = np.zeros((4, D), np.float32)
        if q > 0:
            halo[0] = xs[lo - 2]; halo[1] = xs[lo - 1]
        if q < 3:
            halo[2] = xs[hi]
        xT = np.ascontiguousarray(np.concatenate([xs[lo:hi], np.asarray(inp["ctx"][b], np.float32), halo], 0).T)
        cs, sn = rope_tables(q)
        m = {"xT": xT, "cosT": cs, "sinT": sn, "sm0": build_sm(inp, 0, b, q), "sm1": build_sm(inp, 1, b, q),
             "wmask": window_masks(q),
             "natab": np.stack([na_tables(np.asarray(inp["na_rel_bias"][l], np.float32), q) for l in range(2)], 0)}
        m.update(shared)
        in_maps.append(m)
    res = run_bass_kernel_spmd(nc, in_maps, core_ids=list(range(NC)))
    out = np.empty_like(x)
    for core in range(NC):
        b, q = core_bq(core)
        out[b, q * T:(q + 1) * T] = np.asarray(res.results[core]["outT"]).T
    return out
```

```python
import contextlib
import numpy as np
import ml_dtypes
import concourse.bass as bass
import concourse.mybir as mybir
from concourse.bass_utils import run_bass_kernel_spmd

F32 = mybir.dt.float32
BF16 = mybir.dt.bfloat16
AF = mybir.ActivationFunctionType
ALU = mybir.AluOpType
NPBF = ml_dtypes.bfloat16

D = 1024; KD = 8; FF = 2816; KF = 22; T = 4096; CTX = 256; TT = T + CTX; TX = TT + 4
MW = 256; INW = 6656; NC = 8; SEQ = 16384; GRID_W = 64
EPS = 1e-6


class Buf:
    __slots__ = ("name", "lw", "rd", "dsem")

    def __init__(self, name):
        self.name = name
        self.lw = None
        self.rd = []
        self.dsem = None


class Prog:
    ENG = ("pe", "dve", "act", "pool", "sp")

    def __init__(self, nc, stack):
        self.nc = nc
        self.stack = stack
        self.eng = {"pe": nc.tensor, "dve": nc.vector, "act": nc.scalar,
                    "pool": nc.gpsimd, "sp": nc.sync}
        self.sems = {}
        self.cnt = {}
        self.waited = {e: {} for e in self.ENG}
        self.esem = {}
        self.nsem = 0
        self.free_dsems = []
        self.dbufs = []
        self.allbufs = []
        self.pool_fifo = []
        for e in self.ENG:
            key = f"E{e}"
            self._alloc_sem(key)
            self.esem[e] = key
        self.n_inst = 0

    def _alloc_sem(self, key):
        h = self.stack.enter_context(self.nc.semaphore(f"s{self.nsem}_{key}"))
        self.nsem += 1
        self.sems[key] = h
        self.cnt[key] = 0
        return key

    def _need(self, e, deps, key, val):
        if self.waited[e].get(key, 0) >= val:
            return
        deps[key] = max(deps.get(key, 0), val)

    def _emit_waits(self, e, deps):
        eng = self.eng[e]
        for key, val in deps.items():
            eng.wait_ge(self.sems[key], val)
            self.waited[e][key] = val
            self.n_inst += 1

    def _collect(self, e, reads, writes, dma=False):
        deps = {}
        for b in reads:
            if b.lw is not None:
                self._need(e, deps, *b.lw)
        for b in writes:
            if b.lw is not None:
                if not (dma and b.lw[0] == b.dsem and not b.rd):
                    self._need(e, deps, *b.lw)
            for r in b.rd:
                self._need(e, deps, *r)
        return deps

    def _stamp(self, stamp, reads, writes):
        for b in list(reads) + list(writes):
            if b.lw is None and not b.rd:
                self.allbufs.append(b)
        for b in reads:
            b.rd.append(stamp)
            if len(b.rd) > 64:
                m = {}
                for k, v in b.rd:
                    m[k] = max(m.get(k, 0), v)
                b.rd = list(m.items())
        for b in writes:
            b.lw = stamp
            b.rd = []

    def op(self, e, fn, reads=(), writes=(), sig=True):
        deps = self._collect(e, reads, writes)
        if e == "pe":
            deps.pop(self.esem["pe"], None)
        self._emit_waits(e, deps)
        ins = fn()
        self.n_inst += 1
        key = self.esem[e]
        if sig:
            self.cnt[key] += 1
            ins.then_inc(self.sems[key], 1)
            stamp = (key, self.cnt[key])
        else:
            stamp = (key, self.cnt[key] + 1)
        self._stamp(stamp, reads, writes)
        return ins

    def dma(self, q, out_ap, in_ap, reads=(), writes=(), sb=None, ndesc=0, **kw):
        if q == "pool" and ndesc:
            while self.pool_fifo and sum(n for _, _, n in self.pool_fifo) + ndesc > 3000:
                key_, val_, _ = self.pool_fifo.pop(0)
                d_ = {}
                self._need("pool", d_, key_, val_)
                self._emit_waits("pool", d_)
        if sb.dsem is None:
            if self.free_dsems:
                sb.dsem = self.free_dsems.pop()
            else:
                sb.dsem = self._alloc_sem(f"D{self.nsem}")
            self.dbufs.append(sb)
        deps = self._collect(q, reads, writes, dma=True)
        for b in list(reads) + list(writes):
            if b.dsem is not None and b.dsem in deps:
                deps[b.dsem] = self.cnt[b.dsem]
        self._emit_waits(q, deps)
        ins = self.eng[q].dma_start(out=out_ap, in_=in_ap, **kw)
        self.n_inst += 1
        key = sb.dsem
        self.cnt[key] += 16
        ins.then_inc(self.sems[key], 16)
        self._stamp((key, self.cnt[key]), reads, writes)
        if q == "pool" and ndesc:
            self.pool_fifo.append((key, self.cnt[key], ndesc))
        return ins

    def barrier(self):
        for e in self.ENG:
            deps = {}
            for key, v in self.cnt.items():
                if v > 0 and key != self.esem[e]:
                    self._need(e, deps, key, v)
            if e != "pe" and self.cnt[self.esem[e]] > 0:
                self._need(e, deps, self.esem[e], self.cnt[self.esem[e]])
            self._emit_waits(e, deps)
        for b in self.allbufs:
            b.lw = None
            b.rd = []
        self.allbufs = []
        for b in self.dbufs:
            self.free_dsems.append(b.dsem)
            b.dsem = None
        self.dbufs = []
        self.pool_fifo = []
        for e in self.ENG:
            if self.cnt[self.esem[e]] > 20000:
                key = f"E{e}{self.nsem}"
                self._alloc_sem(key)
                self.esem[e] = key

    def collective(self, in_ap, out_ap, groups, reads=(), writes=()):
        deps = self._collect("pool", reads, writes)
        self._emit_waits("pool", deps)
        ins = self.nc.gpsimd.collective_compute("AllGather", ALU.bypass, replica_groups=groups, ins=[in_ap], outs=[out_ap])
        key = self._alloc_sem(f"C{self.nsem}")
        ins.then_inc(self.sems[key])
        self.cnt[key] = 1
        self.n_inst += 1
        self._stamp((key, 1), reads, writes)

    def finish(self):
        deps = {}
        for key, v in self.cnt.items():
            if v > 0:
                self._need("sp", deps, key, v)
        self._emit_waits("sp", deps)


SM_FIELDS = [("bmod", 72), ("normg", 24), ("bgate", 32), ("convaw", 6), ("lcw", 8), ("lcb", 2),
             ("lba", 4), ("lbx", 4), ("llam", 4), ("gqb", 1), ("gkb", 1), ("gqd", 1), ("gkd", 1),
             ("sink", 2), ("cvec", 16), ("hvalid", 4), ("eps", 1),
             ("onesD", 128), ("ones64", 128), ("rmat", 128), ("ones1", 128),
             ("wa00", 128), ("wa01", 128), ("wa10", 128), ("wa11", 128),
             ("wx00", 128), ("wx01", 128), ("wx10", 128), ("wx11", 128),
             ("lsel", 4), ("rsel", 4), ("fsel", 4), ("bsel", 4)]
SM_OFF = {}
_o = 0
for _n, _w in SM_FIELDS:
    SM_OFF[_n] = (_o, _w)
    _o += _w
NSM = _o


def chunked(v, nch):
    return np.ascontiguousarray(np.asarray(v, np.float32).reshape(nch, 128).T)


def blockdiag(a, b):
    m = np.zeros((128, 128), np.float32)
    m[:64, :64] = a
    m[64:, 64:] = b
    return m


def build_sm(inp, l, b, q):
    sm = np.zeros((128, NSM), np.float32)

    def put(name, arr):
        o, w = SM_OFF[name]
        sm[:, o:o + w] = np.asarray(arr, np.float32).reshape(128, w)

    put("bmod", chunked(inp["b_mod"][l], 72))
    put("normg", np.concatenate([chunked(inp["norm_g"][l, s], 8) for s in range(3)], 1))
    put("bgate", chunked(inp["b_gate"][l], 32))
    ca = inp["conv_a_w"][l]
    put("convaw", np.stack([ca[j, c * 128:(c + 1) * 128] for c in range(2) for j in range(3)], 1))
    lw = inp["lru_conv_w"][l]
    put("lcw", np.stack([lw[j, c * 128:(c + 1) * 128] for c in range(2) for j in range(4)], 1))
    put("lcb", chunked(inp["lru_conv_b"][l], 2))
    for nm, key in (("lba", "lru_b_a"), ("lbx", "lru_b_x"), ("llam", "lru_lam")):
        v = inp[key][l]
        put(nm, np.stack([v[d, c * 128:(c + 1) * 128] for d in range(2) for c in range(2)], 1))
    g = inp["qk_norm_g"][l]
    for i, nm in enumerate(("gqb", "gkb", "gqd", "gkd")):
        put(nm, np.tile(g[i], 2)[:, None])
    sk = inp["attn_sink"][l]
    put("sink", np.stack([np.repeat(sk[2 * c:2 * c + 2], 64) for c in range(2)], 1))
    cv = np.stack([inp["c"][b], inp["c_ctx"]], 1)
    put("cvec", cv.reshape(8, 128, 2).transpose(1, 0, 2).reshape(128, 16))
    hv = np.array([1.0 if q > 0 else 0.0, 1.0 if q > 0 else 0.0, 1.0 if q < 3 else 0.0, 0.0], np.float32)
    put("hvalid", np.tile(hv[None, :], (128, 1)))
    put("eps", np.full((128, 1), EPS, np.float32))
    put("onesD", np.full((128, 128), 1.0 / D, np.float32))
    put("ones64", blockdiag(np.full((64, 64), 1.0 / 64, np.float32), np.full((64, 64), 1.0 / 64, np.float32)))
    r = np.zeros((128, 128), np.float32)
    for m in range(128):
        if (m % 32) < 16:
            r[m + 16, m] = -1.0
        else:
            r[m - 16, m] = 1.0
    put("rmat", r)
    put("ones1", np.ones((128, 128), np.float32))
    put("lsel", np.tile(np.array([1.0 if j == q - 1 else 0.0 for j in range(4)], np.float32)[None], (128, 1)))
    put("rsel", np.tile(np.array([1.0 if j == q + 1 else 0.0 for j in range(4)], np.float32)[None], (128, 1)))
    put("fsel", np.tile(np.array([1.0 if j < q else 0.0 for j in range(4)], np.float32)[None], (128, 1)))
    put("bsel", np.tile(np.array([1.0 if j > q else 0.0 for j in range(4)], np.float32)[None], (128, 1)))
    for d in range(2):
        for c in range(2):
            put(f"wa{d}{c}", blockdiag(inp["lru_w_a"][l, d, 2 * c], inp["lru_w_a"][l, d, 2 * c + 1]))
            put(f"wx{d}{c}", blockdiag(inp["lru_w_x"][l, d, 2 * c], inp["lru_w_x"][l, d, 2 * c + 1]))
    return sm


def rope_tables(q):
    pos = q * T + np.arange(T)
    rows = (pos // GRID_W).astype(np.float32)
    cols = (pos % GRID_W).astype(np.float32)
    nf = 16
    inv_freq = (10000.0 ** (-np.arange(nf, dtype=np.float32) / nf)).astype(np.float32)
    p = np.arange(128)
    d = p % 64
    j = d % 16
    use_col = (d // 32) == 1
    posm = np.where(use_col[:, None], cols[None, :], rows[None, :]).astype(np.float32)
    ang = (posm * inv_freq[j][:, None]).astype(np.float32)
    return np.cos(ang).astype(np.float32), np.sin(ang).astype(np.float32)


class Ctx:
    pass


_UNIQ = [0]


def uniq(name):
    _UNIQ[0] += 1
    return f"{name}_sb{_UNIQ[0]}"


def smv(K, name, i=0, n=1):
    o, w = SM_OFF[name]
    return K.sm[:, o + i:o + i + n]


def setup_common(nc, st):
    K = Ctx()
    K.nc = nc
    K.st = st
    st.enter_context(nc.Block())
    K.P = Prog(nc, st)
    K.sbuf = lambda name, shape, dt: st.enter_context(nc.sbuf_tensor(uniq(name), shape, dt))
    K.ps = [st.enter_context(nc.psum_tensor(f"ps{i}", [128, 512], F32)) for i in range(8)]
    K.bps = [Buf(f"ps{i}") for i in range(8)]
    K.sm = K.sbuf("sm_sb", [128, NSM], F32)
    K.bsm = Buf("sm")
    names = ["onesD", "ones64", "rmat", "ones1"] + [f"w{a}{d}{c}" for a in "ax" for d in range(2) for c in range(2)]
    K.cm = K.sbuf("cm_sb", [128, len(names), 128], BF16)
    K.bcm = Buf("cm")
    K.cmi = {n: i for i, n in enumerate(names)}
    return K


def load_sm(K, sm_d):
    nc, P = K.nc, K.P
    P.dma("sp", K.sm[:], sm_d, writes=[K.bsm], sb=K.bsm)
    for n, i in K.cmi.items():
        o, w = SM_OFF[n]
        P.op("dve", lambda: nc.vector.tensor_copy(K.cm[:, i, :], K.sm[:, o:o + 128]), reads=[K.bsm], writes=[K.bcm])


def cmat(K, name):
    return K.cm[:, K.cmi[name], :]


def emit_mod(K, w_mod_d):
    nc, P = K.nc, K.P
    K.modT = K.sbuf("modT", [128, 72, 2], F32)
    K.bmod = Buf("modT")
    K.gsT = K.sbuf("gsT", [128, 3, 8, 2], F32)
    K.gateT = K.sbuf("gateT", [128, 3, 8, 2], F32)
    K.bgs = Buf("gs")
    with contextlib.ExitStack() as s2:
        wm = [s2.enter_context(nc.sbuf_tensor(uniq(f"wm{i}"), [128, 8, 1024], F32)) for i in range(2)]
        bwm = [Buf(f"wm{i}") for i in range(2)]
        sc = s2.enter_context(nc.sbuf_tensor(uniq("silu_c"), [128, 16], F32))
        bsc = Buf("silu_c")
        P.op("act", lambda: nc.scalar.activation(sc[:], smv(K, "cvec", 0, 16), AF.Silu), reads=[K.bsm], writes=[bsc])
        psm = K.ps[7]
        bpsm = K.bps[7]
        for i in range(9):
            s = i % 2
            for k in range(8):
                P.dma("sp", wm[s][:, k, :], w_mod_d[k * 128:(k + 1) * 128, i * 1024:(i + 1) * 1024],
                      writes=[bwm[s]], sb=bwm[s])
            for dk in range(8):
                cidx = i * 8 + dk
                for k in range(8):
                    P.op("pe", lambda: nc.tensor.matmul(psm[:, cidx * 2:cidx * 2 + 2], wm[s][:, k, dk * 128:(dk + 1) * 128],
                                                        sc[:, 2 * k:2 * k + 2], start=(k == 0), stop=(k == 7)),
                         reads=[bwm[s], bsc], writes=[bpsm], sig=(k == 7))
        psv = psm[:, 0:144].rearrange("p (c j) -> p c j", j=2)
        for j in range(2):
            P.op("dve", lambda: nc.vector.tensor_tensor(K.modT[:, :, j], psv[:, :, j], smv(K, "bmod", 0, 72), ALU.add),
                 reads=[bpsm, K.bsm], writes=[K.bmod])
        for s in range(3):
            for j in range(2):
                P.op("dve", lambda: nc.vector.tensor_scalar(K.gsT[:, s, :, j], K.modT[:, (3 * s + 1) * 8:(3 * s + 2) * 8, j],
                                                            1.0, None, ALU.add), reads=[K.bmod], writes=[K.bgs])
                P.op("dve", lambda: nc.vector.tensor_tensor(K.gsT[:, s, :, j], K.gsT[:, s, :, j],
                                                            smv(K, "normg", s * 8, 8), ALU.mult),
                     reads=[K.bgs, K.bsm], writes=[K.bgs])
                P.op("dve", lambda: nc.vector.tensor_scalar(K.gateT[:, s, :, j], K.modT[:, (3 * s + 2) * 8:(3 * s + 3) * 8, j],
                                                            (1.0 if s == 1 else 0.5), None, ALU.mult),
                     reads=[K.bmod], writes=[K.bgs])
        P.barrier()


def shift_col(K, s, k, j):
    return K.modT[:, 3 * s * 8 + k, j:j + 1]


def emit_adaln(K, xt, bxt, N, s, col, hT, bh, sq, bsq, rstd, brs, pst, bpst):
    nc, P = K.nc, K.P
    P.op("act", lambda: nc.scalar.activation(sq[:, :, :N], xt[:, :, :N], AF.Square), reads=[bxt], writes=[bsq])
    for k in range(8):
        P.op("pe", lambda: nc.tensor.matmul(pst[:, :N], cmat(K, "onesD"), sq[:, k, :N], start=(k == 0), stop=(k == 7)),
             reads=[bsq, K.bcm], writes=[bpst], sig=(k == 7))
    P.op("act", lambda: nc.scalar.activation(rstd[:, :N], pst[:, :N], AF.Sqrt, bias=smv(K, "eps"), scale=1.0),
         reads=[bpst, K.bsm], writes=[brs])
    P.op("dve", lambda: nc.vector.reciprocal(rstd[:, :N], rstd[:, :N]), reads=[brs], writes=[brs])
    for k in range(8):
        P.op("dve", lambda: nc.vector.tensor_tensor(K.tmp32[:, :N], xt[:, k, :N], rstd[:, :N], ALU.mult),
             reads=[bxt, brs], writes=[K.btmp32])
        P.op("dve", lambda: nc.vector.tensor_scalar(hT[:, k, :N], K.tmp32[:, :N], K.gsT[:, s, k, col:col + 1],
                                                    shift_col(K, s, k, col), ALU.mult, ALU.add),
             reads=[K.btmp32, K.bgs, K.bmod], writes=[bh])


class WBufs:
    def __init__(self, gw):
        self.gw = gw
        self.bufs = []

    def at(self, col):
        return self.bufs[col // self.gw]


def load_weight_cast(K, dst, wb, src, g0, g1):
    b = Buf("wgrp")
    wb.bufs.append(b)
    K.P.dma("pool", dst[:, :, g0:g1], src.rearrange("(k p) f -> p k f", p=128)[:, :, g0:g1], writes=[b], sb=b,
            ndesc=128 * dst.shape[1])


def load_weight_groups(K, specs):
    pos = [0] * len(specs)
    done = False
    while not done:
        done = True
        for i, (dst, wb, src, width) in enumerate(specs):
            if pos[i] < width:
                g1 = min(width, pos[i] + wb.gw)
                load_weight_cast(K, dst, wb, src, pos[i], g1)
                pos[i] = g1
                done = False


def emit_ffn_phase(K, x_src, x_dst, tiles, s, wg_d, wu_d, wd_d, h_dst=None, s_next=None):
    nc, P = K.nc, K.P
    post = h_dst is not None
    with contextlib.ExitStack() as s2:
        sb = lambda name, shape, dt: s2.enter_context(nc.sbuf_tensor(uniq(name), shape, dt))
        wg = sb("wg", [128, KD, FF], BF16); bwg = WBufs(256)
        wu = sb("wu", [128, KD, FF], BF16); bwu = WBufs(256)
        wd = sb("wd", [128, KF, D], BF16); bwd = WBufs(256)
        load_weight_groups(K, [(wg, bwg, wg_d, FF), (wu, bwu, wu_d, FF)])
        load_weight_groups(K, [(wd, bwd, wd_d, D)])
        xts = [sb(f"xt{i}", [128, 8, 512], F32) for i in range(2)]; bxts = [Buf(f"xt{i}") for i in range(2)]
        hT = sb("hT", [128, 8, 512], BF16); bh = Buf("hT")
        act = sb("actT", [128, KF, 512], BF16); bact = Buf("actT")
        rstd = sb("rstd", [128, 512], F32); brs = Buf("rstd")
        K.tmp32 = sb("tmp32", [128, 512], F32); K.btmp32 = Buf("tmp32")
        sg, bsg = K.tmp32, K.btmp32
        if post:
            h2o = sb("h2o", [128, 8, 512], BF16); bh2o = Buf("h2o")
            rstd2 = sb("rstd2", [128, 512], F32); brs2 = Buf("rstd2")
        xs = x_src.rearrange("(c p) t -> p c t", p=128)
        xd = x_dst.rearrange("(c p) t -> p c t", p=128)
        hd = h_dst.rearrange("(c p) t -> p c t", p=128) if post else None
        n = len(tiles)
        gi = [0]

        def load(i):
            start, N, col = tiles[i]
            P.dma("sp", xts[i % 2][:, :, :N], xs[:, :, start:start + N], writes=[bxts[i % 2]], sb=bxts[i % 2])

        def adaln(i):
            start, N, col = tiles[i]
            emit_adaln(K, xts[i % 2], bxts[i % 2], N, s, col, hT, bh, hT, bh, rstd, brs, K.ps[6], K.bps[6])

        def gateup(i):
            start, N, col = tiles[i]
            for f in range(KF):
                pg, bpg = K.ps[gi[0] % 2], K.bps[gi[0] % 2]
                pu, bpu = K.ps[2 + gi[0] % 2], K.bps[2 + gi[0] % 2]
                gi[0] += 1
                for k in range(8):
                    P.op("pe", lambda: nc.tensor.matmul(pg[:, :N], wg[:, k, f * 128:(f + 1) * 128], hT[:, k, :N],
                                                        start=(k == 0), stop=(k == 7)),
                         reads=[bwg.at(f * 128), bh], writes=[bpg], sig=(k == 7))
                for k in range(8):
                    P.op("pe", lambda: nc.tensor.matmul(pu[:, :N], wu[:, k, f * 128:(f + 1) * 128], hT[:, k, :N],
                                                        start=(k == 0), stop=(k == 7)),
                         reads=[bwu.at(f * 128), bh], writes=[bpu], sig=(k == 7))
                P.op("act", lambda: nc.scalar.activation(sg[:, :N], pg[:, :N], AF.Silu), reads=[bpg], writes=[bsg])
                P.op("dve", lambda: nc.vector.tensor_tensor(act[:, f, :N], pu[:, :N], sg[:, :N], ALU.mult),
                     reads=[bpu, bsg], writes=[bact])

        def down(i, d0, d1):
            start, N, col = tiles[i]
            xt, bxt = xts[i % 2], bxts[i % 2]
            for dk in range(d0, d1):
                py, bpy = K.ps[4 + dk % 2], K.bps[4 + dk % 2]
                for f in range(KF):
                    P.op("pe", lambda: nc.tensor.matmul(py[:, :N], wd[:, f, dk * 128:(dk + 1) * 128], act[:, f, :N],
                                                        start=(f == 0), stop=(f == KF - 1)),
                         reads=[bwd.at(dk * 128), bact], writes=[bpy], sig=(f == KF - 1))
                P.op("dve", lambda: nc.vector.scalar_tensor_tensor(xt[:, dk, :N], py[:, :N], K.gateT[:, s, dk, col:col + 1],
                                                                   xt[:, dk, :N], ALU.mult, ALU.add),
                     reads=[bpy, bxt, K.bgs], writes=[bxt])
            if d1 == 8:
                P.dma("sp", xd[:, :, start:start + N], xt[:, :, :N], reads=[bxt], sb=bxt)

        def postadaln(i):
            start, N, col = tiles[i]
            emit_adaln(K, xts[i % 2], bxts[i % 2], N, s_next, col, h2o, bh2o, h2o, bh2o, rstd2, brs2, K.ps[7], K.bps[7])
            P.dma("sp", hd[:, :, start:start + N], h2o[:, :, :N], reads=[bh2o], sb=bh2o)

        load(0)
        if n > 1:
            load(1)
        adaln(0)
        for i in range(n):
            gateup(i)
            if i >= 1:
                if post:
                    postadaln(i - 1)
                if i + 1 < n:
                    load(i + 1)
            down(i, 0, 4)
            if i + 1 < n:
                adaln(i + 1)
            down(i, 4, 8)
        if post:
            postadaln(n - 1)
        P.barrier()


LAT_TILES = [(i * 512, 512, 0) for i in range(8)]
CTX_TILE = (T, CTX, 1)
HALO_TILE = (TT, 4, 0)


def emit_A(K, I):
    nc, P = K.nc, K.P
    xT_d, cos_d, sin_d = I.xin, I.cos_d, I.sin_d
    wg_d, wu_d, wd_d, win_d = I.wg1, I.wu1, I.wd1, I.win
    O = I.O
    x1full, h2_d = I.x1full, I.h2_d
    with contextlib.ExitStack() as sA:
        K_sbuf = lambda name, shape, dt: sA.enter_context(nc.sbuf_tensor(uniq(name), shape, dt))
        tiles = LAT_TILES + [CTX_TILE, HALO_TILE]
        emit_ffn_phase(K, xT_d, x1full, tiles, 0, wg_d, wu_d, wd_d, h_dst=h2_d, s_next=1)

        cxf = K_sbuf("cxf", [128, 2, T + 4], F32); bcxf = Buf("cxf")
        cxc = K_sbuf("cxc", [128, 2, CTX + 4], F32); bcxc = Buf("cxc")
        P.op("pool", lambda: nc.gpsimd.memset(cxc[:], 0.0), writes=[bcxc])
        gq = K_sbuf("gq", [128, 2], F32); bgq = Buf("gq")
        P.op("dve", lambda: nc.vector.tensor_scalar(gq[:, 0:1], smv(K, "gqb"), 0.125, None, ALU.mult), reads=[K.bsm], writes=[bgq])
        P.op("dve", lambda: nc.vector.tensor_scalar(gq[:, 1:2], smv(K, "gqd"), 0.125, None, ALU.mult), reads=[K.bsm], writes=[bgq])

        with contextlib.ExitStack() as s2:
            sb = lambda name, shape, dt: s2.enter_context(nc.sbuf_tensor(uniq(name), shape, dt))
            win = sb("win", [128, KD, INW], BF16); bwin = WBufs(512)
            load_weight_groups(K, [(win, bwin, win_d, INW)])
            h2t = sb("h2t", [128, 8, 512], BF16); bh2 = Buf("h2t")
            cst = sb("cost", [128, 512], F32); snt = sb("sint", [128, 512], F32); bcs = Buf("cossin")
            axs = sb("axs", [128, 2, 512], F32); baxs = Buf("axs")
            abo = sb("abo", [128, 2, 512], BF16); babo = Buf("abo")
            vo = sb("vo", [128, 2, 512], BF16); bvo = Buf("vo")
            qpl = sb("qpl", [128, 2, 512], BF16); bqpl = Buf("qpl")
            qro = sb("qro", [128, 2, 512], BF16); bqro = Buf("qro")
            ko = sb("ko", [128, 512], BF16); bko = Buf("ko")
            ktmp = sb("ktmp", [128, 512], BF16); bktmp = Buf("ktmp")
            vtok = sb("vtok", [128, 4, 384], BF16); bvtok = Buf("vtok")
            gcg = sb("gcg", [128, 2, 512], BF16); bgcg = Buf("gcg")
            dqo = sb("dqo", [128, 2, 512], BF16); bdqo = Buf("dqo")
            dko = sb("dko", [128, 2, 512], BF16); bdko = Buf("dko")
            gto = sb("gto", [128, 8, 512], BF16); bgto = Buf("gto")
            sqb = [sb(f"sqb{i}", [128, 512], BF16) for i in range(2)]; bsqb = [Buf(f"sqb{i}") for i in range(2)]
            rs = [sb(f"rs{i}", [128, 512], F32) for i in range(2)]; brs = [Buf(f"rs{i}") for i in range(2)]
            qh = [sb(f"qh{i}", [128, 512], F32) for i in range(2)]; bqh = [Buf(f"qh{i}") for i in range(2)]
            t1 = [sb(f"t1{i}", [128, 512], F32) for i in range(2)]; bt1 = [Buf(f"t1{i}") for i in range(2)]
            t2 = [sb(f"t2{i}", [128, 512], F32) for i in range(2)]; bt2 = [Buf(f"t2{i}") for i in range(2)]
            bout = Buf("aout")
            h2s = h2_d.rearrange("(c p) t -> p c t", p=128)
            v3 = lambda ap: ap.rearrange("(c p) t -> p c t", p=128)
            ov = {n: v3(O[n]) for n in ("abT", "gcgT", "dqT", "gT")}
            pctr = [0]

            def proj(c, N):
                i = (0, 1, 2, 6, 7)[pctr[0] % 5]
                pctr[0] += 1
                ps, bps = K.ps[i], K.bps[i]
                for k in range(8):
                    P.op("pe", lambda: nc.tensor.matmul(ps[:, :N], win[:, k, c * 128:(c + 1) * 128], h2t[:, k, :N],
                                                        start=(k == 0), stop=(k == 7)),
                         reads=[bwin.at(c * 128), bh2], writes=[bps], sig=(k == 7))
                step_pending()
                return ps, bps

            pending = []

            def step_pending():
                for g in list(pending):
                    try:
                        next(g)
                    except StopIteration:
                        pending.remove(g)

            def drain():
                while pending:
                    step_pending()

            qn = [0]

            def qknorm(ps, bps, N, gcol, bg, out_plain, bop, rope_out=None, bro=None, after=None):
                i = qn[0] % 2
                qn[0] += 1
                sqb_, bsqb_ = sqb[i], bsqb[i]
                rs_, brs_ = rs[i], brs[i]
                qh_, bqh_ = qh[i], bqh[i]
                t1_, bt1_ = t1[i], bt1[i]
                t2_, bt2_ = t2[i], bt2[i]
                pst, bpst = K.ps[3], K.bps[3]
                P.op("act", lambda: nc.scalar.activation(sqb_[:, :N], ps[:, :N], AF.Square), reads=[bps], writes=[bsqb_])
                yield
                P.op("pe", lambda: nc.tensor.matmul(pst[:, :N], cmat(K, "ones64"), sqb_[:, :N], start=True, stop=True),
                     reads=[bsqb_, K.bcm], writes=[bpst])
                P.op("act", lambda: nc.scalar.activation(rs_[:, :N], pst[:, :N], AF.Sqrt, bias=smv(K, "eps"), scale=1.0),
                     reads=[bpst, K.bsm], writes=[brs_])
                P.op("dve", lambda: nc.vector.reciprocal(rs_[:, :N], rs_[:, :N]), reads=[brs_], writes=[brs_])
                P.op("dve", lambda: nc.vector.scalar_tensor_tensor(qh_[:, :N], ps[:, :N], gcol, rs_[:, :N], ALU.mult, ALU.mult),
                     reads=[bps, brs_, bg], writes=[bqh_])
                P.op("act", lambda: nc.scalar.copy(out_plain[:, :N], qh_[:, :N]), reads=[bqh_], writes=[bop])
                if rope_out is not None:
                    yield
                    psr, bpsr = K.ps[4], K.bps[4]
                    P.op("pe", lambda: nc.tensor.matmul(psr[:, :N], cmat(K, "rmat"), out_plain[:, :N], start=True, stop=True),
                         reads=[bop, K.bcm], writes=[bpsr])
                    P.op("pool", lambda: nc.gpsimd.tensor_tensor(t1_[:, :N], qh_[:, :N], cst[:, :N], ALU.mult),
                         reads=[bqh_, bcs], writes=[bt1_])
                    P.op("dve", lambda: nc.vector.tensor_tensor(t2_[:, :N], psr[:, :N], snt[:, :N], ALU.mult),
                         reads=[bpsr, bcs], writes=[bt2_])
                    P.op("pool", lambda: nc.gpsimd.tensor_tensor(rope_out[:, :N], t1_[:, :N], t2_[:, :N], ALU.add),
                         reads=[bt1_, bt2_], writes=[bro])
                if after is not None:
                    after()

            def qk(*a, **kw):
                g = qknorm(*a, **kw)
                next(g)
                pending.append(g)

            for (start, N, col) in tiles:
                kind = "lat" if start < T else ("ctx" if start == T else "halo")
                P.dma("sp", h2t[:, :, :N], h2s[:, :, start:start + N], writes=[bh2], sb=bh2)
                if kind == "halo":
                    for c in range(2):
                        ps, bps = proj(10 + c, N)
                        P.op("dve", lambda: nc.vector.tensor_tensor(cxf[:, c, 0:2], ps[:, 0:2], smv(K, "hvalid", 0, 2), ALU.mult),
                             reads=[bps, K.bsm], writes=[bcxf])
                        P.op("dve", lambda: nc.vector.tensor_tensor(cxf[:, c, T + 2:T + 4], ps[:, 2:4], smv(K, "hvalid", 2, 2), ALU.mult),
                             reads=[bps, K.bsm], writes=[bcxf])
                    continue
                lat = kind == "lat"
                if lat:
                    P.dma("sp", cst[:, :N], cos_d[:, start:start + N], writes=[bcs], sb=bcs)
                    P.dma("sp", snt[:, :N], sin_d[:, start:start + N], writes=[bcs], sb=bcs)
                sl = slice(start, start + N)
                for c in range(2):
                    ps, bps = proj(0 + c, N)
                    P.op("act", lambda: nc.scalar.copy(axs[:, c, :N], ps[:, :N]), reads=[bps], writes=[baxs])
                for c in range(2):
                    ps, bps = proj(2 + c, N)
                    P.op("act", lambda: nc.scalar.copy(abo[:, c, :N], ps[:, :N]), reads=[bps], writes=[babo])
                for c in range(2):
                    ps, bps = proj(4 + c, N)
                    P.op("dve", lambda: nc.vector.tensor_tensor(vo[:, c, :N], ps[:, :N], axs[:, c, :N], ALU.mult),
                         reads=[bps, baxs], writes=[bvo])
                P.dma("sp", ov["abT"][:, :, sl], abo[:, :, :N], reads=[babo], sb=babo)
                if lat:
                    P.dma("sp", v3(O["vcvext"])[:, :, 1 + start:1 + start + N], vo[:, :, :N], reads=[bvo], sb=bvo)
                else:
                    P.dma("sp", v3(O["vcvcext"])[:, :, 1:1 + N], vo[:, :, :N], reads=[bvo], sb=bvo)
                def store_q(c, sl=sl, N=N, lat=lat):
                    for hh in range(2):
                        r0 = hh * 128 + c * 64
                        P.dma("sp", O["qplT"][r0:r0 + 64, sl], qpl[hh * 64:(hh + 1) * 64, c, :N], reads=[bqpl], sb=bqpl)
                        if lat:
                            P.dma("sp", O["qrotT"][r0:r0 + 64, sl], qro[hh * 64:(hh + 1) * 64, c, :N], reads=[bqro], sb=bqro)
                for c in range(2):
                    ps, bps = proj(6 + c, N)
                    qk(ps, bps, N, gq[:, 0:1], bgq, qpl[:, c, :], bqpl, qro[:, c, :] if lat else None, bqro,
                       after=(lambda c=c, f=store_q: f(c)))
                ps, bps = proj(8, N)
                if lat:
                    qk(ps, bps, N, smv(K, "gkb"), K.bsm, ktmp, bktmp, ko, bko,
                       after=(lambda start=start, N=N: P.dma("sp", O["kext"][:, 128 + start:128 + start + N], ko[:, :N], reads=[bko], sb=bko)))
                else:
                    qk(ps, bps, N, smv(K, "gkb"), K.bsm, ko, bko,
                       after=(lambda N=N: P.dma("sp", O["kc"][:, 0:N], ko[:, :N], reads=[bko], sb=bko)))
                psv, bpsv = K.ps[5], K.bps[5]
                nb = N // 128
                for tb in range(nb):
                    for k in range(8):
                        P.op("pe", lambda: nc.tensor.matmul(psv[:, 0:128], h2t[:, k, tb * 128:(tb + 1) * 128], win[:, k, 9 * 128:10 * 128],
                                                            start=(k == 0), stop=(k == 7)),
                             reads=[bwin.at(9 * 128), bh2], writes=[bpsv], sig=(k == 7))
                    for k in range(8):
                        P.op("pe", lambda: nc.tensor.matmul(psv[:, 128:384], h2t[:, k, tb * 128:(tb + 1) * 128], win[:, k, 18 * 128:20 * 128],
                                                            start=(k == 0), stop=(k == 7)),
                             reads=[bwin.at(18 * 128), bh2], writes=[bpsv], sig=(k == 7))
                    P.op("act", lambda: nc.scalar.copy(vtok[:, tb, :], psv[:, 0:384]), reads=[bpsv], writes=[bvtok])
                tv = lambda ap: ap.rearrange("(b p) f -> p b f", p=128)
                if lat:
                    P.dma("sp", tv(O["vext"][128 + start:128 + start + N, :]), vtok[:, 0:nb, 0:128], reads=[bvtok], sb=bvtok)
                    P.dma("sp", tv(O["vdext"][256 + start:256 + start + N, :]), vtok[:, 0:nb, 128:384], reads=[bvtok], sb=bvtok)
                else:
                    P.dma("sp", tv(O["vc"][0:N, :]), vtok[:, 0:nb, 0:128], reads=[bvtok], sb=bvtok)
                    P.dma("sp", tv(O["vdc"][0:N, :]), vtok[:, 0:nb, 128:384], reads=[bvtok], sb=bvtok)
                for c in range(2):
                    ps, bps = proj(10 + c, N)
                    if lat:
                        P.op("act", lambda: nc.scalar.copy(cxf[:, c, 2 + start:2 + start + N], ps[:, :N]), reads=[bps], writes=[bcxf])
                    else:
                        P.op("act", lambda: nc.scalar.copy(cxc[:, c, 2:2 + N], ps[:, :N]), reads=[bps], writes=[bcxc])
                for c in range(2):
                    ps, bps = proj(12 + c, N)
                    P.op("act", lambda: nc.scalar.activation(gcg[:, c, :N], ps[:, :N], AF.Gelu_apprx_tanh), reads=[bps], writes=[bgcg])
                P.dma("sp", ov["gcgT"][:, :, sl], gcg[:, :, :N], reads=[bgcg], sb=bgcg)
                for c in range(2):
                    ps, bps = proj(14 + c, N)
                    qk(ps, bps, N, gq[:, 1:2], bgq, dqo[:, c, :], bdqo,
                       after=(lambda c=c, sl=sl, N=N: P.dma("sp", ov["dqT"][:, c, sl], dqo[:, c, :N], reads=[bdqo], sb=bdqo)))
                for c in range(2):
                    ps, bps = proj(16 + c, N)
                    if lat:
                        aft = (lambda c=c, start=start, N=N: P.dma("sp", v3(O["kdext"])[:, c, 256 + start:256 + start + N], dko[:, c, :N], reads=[bdko], sb=bdko))
                    else:
                        aft = (lambda c=c, N=N: P.dma("sp", v3(O["kdc"])[:, c, 0:N], dko[:, c, :N], reads=[bdko], sb=bdko))
                    qk(ps, bps, N, smv(K, "gkd"), K.bsm, dko[:, c, :], bdko, after=aft)
                for c in range(32):
                    ps, bps = proj(20 + c, N)
                    P.op("act", lambda: nc.scalar.activation(gto[:, c % 8, :N], ps[:, :N], AF.Sigmoid, bias=smv(K, "bgate", c), scale=1.0),
                         reads=[bps, K.bsm], writes=[bgto])
                    if c % 8 == 7:
                        c0 = c - 7
                        P.dma("sp", ov["gT"][:, c0:c0 + 8, sl], gto[:, :, :N], reads=[bgto], sb=bgto)
                drain()
            P.barrier()

        with contextlib.ExitStack() as s2:
            sb = lambda name, shape, dt: s2.enter_context(nc.sbuf_tensor(uniq(name), shape, dt))
            u = sb("lru_u", [128, T], F32); bu = Buf("lru_u")
            ub = sb("lru_ub", [128, T], BF16); bub = Buf("lru_ub")
            ra = sb("lru_a", [128, T], F32); bra = Buf("lru_a")
            ib = sb("lru_b", [128, T], F32); bib = Buf("lru_b")
            tm = sb("lru_t", [128, T], F32); btm = Buf("lru_t")
            hh = sb("lru_h", [128, T], F32); bhh = Buf("lru_h")
            pc = sb("lru_p", [128, T], F32); bpc = Buf("lru_p")
            zz = sb("lru_z", [128, T], F32); bzz = Buf("lru_z")
            hs = sb("lru_hs", [128, CTX], F32); bhs = Buf("lru_hs")
            nl8 = sb("nl8", [128, 4], F32); bnl = Buf("nl8")
            lsum, bls = K.lsum, K.bls
            blo = Buf("lruout")
            P.op("pool", lambda: nc.gpsimd.memset(zz[:], 0.0), writes=[bzz])
            P.op("act", lambda: nc.scalar.activation(nl8[:], smv(K, "llam", 0, 4), AF.Exp, scale=-1.0), reads=[K.bsm], writes=[bnl])
            P.op("act", lambda: nc.scalar.activation(nl8[:], nl8[:], AF.Ln, bias=1.0, scale=1.0), reads=[bnl], writes=[bnl])
            P.op("dve", lambda: nc.vector.tensor_scalar(nl8[:], nl8[:], -8.0, None, ALU.mult), reads=[bnl], writes=[bnl])
            pi = [0]

            def lru_chain(src, bsrc, Tn, c, is_ctx):
                lw = lambda j: smv(K, "lcw", c * 4 + j)
                P.op("dve", lambda: nc.vector.tensor_scalar(u[:, :Tn], src[:, c, 0:Tn], lw(0), smv(K, "lcb", c), ALU.mult, ALU.add),
                     reads=[bsrc, K.bsm], writes=[bu])
                for j in range(1, 4):
                    P.op("dve", lambda: nc.vector.scalar_tensor_tensor(u[:, :Tn], src[:, c, j:j + Tn], lw(j), u[:, :Tn], ALU.mult, ALU.add),
                         reads=[bsrc, bu, K.bsm], writes=[bu])
                P.op("act", lambda: nc.scalar.copy(ub[:, :Tn], u[:, :Tn]), reads=[bu], writes=[bub])
                for d in range(2):
                    for t0 in range(0, Tn, 512):
                        n = min(512, Tn - t0)
                        for (wn, bn, dst, bdst) in ((f"wa{d}{c}", "lba", ra, bra), (f"wx{d}{c}", "lbx", ib, bib)):
                            ps, bps = K.ps[pi[0] % 4], K.bps[pi[0] % 4]
                            pi[0] += 1
                            P.op("pe", lambda: nc.tensor.matmul(ps[:, :n], cmat(K, wn), ub[:, t0:t0 + n], start=True, stop=True),
                                 reads=[bub, K.bcm], writes=[bps])
                            P.op("act", lambda: nc.scalar.activation(dst[:, t0:t0 + n], ps[:, :n], AF.Sigmoid,
                                                                     bias=smv(K, bn, d * 2 + c), scale=1.0),
                                 reads=[bps, K.bsm], writes=[bdst])
                    P.op("act", lambda: nc.scalar.activation(ra[:, :Tn], ra[:, :Tn], AF.Exp, scale=nl8[:, d * 2 + c:d * 2 + c + 1]),
                         reads=[bra, bnl], writes=[bra])
                    P.op("dve", lambda: nc.vector.tensor_tensor(tm[:, :Tn], ra[:, :Tn], ra[:, :Tn], ALU.mult), reads=[bra], writes=[btm])
                    P.op("act", lambda: nc.scalar.activation(tm[:, :Tn], tm[:, :Tn], AF.Sqrt, bias=1.0000001, scale=-1.0),
                         reads=[btm], writes=[btm])
                    P.op("dve", lambda: nc.vector.tensor_tensor(ib[:, :Tn], ib[:, :Tn], u[:, :Tn], ALU.mult), reads=[bib, bu], writes=[bib])
                    P.op("dve", lambda: nc.vector.tensor_tensor(ib[:, :Tn], ib[:, :Tn], tm[:, :Tn], ALU.mult), reads=[bib, btm], writes=[bib])
                    if d == 0:
                        va, vb, vh, vp, vz = ra[:, 0:Tn], ib[:, 0:Tn], hh[:, 0:Tn], pc[:, 0:Tn], zz[:, 0:Tn]
                        last = Tn - 1
                    else:
                        rv = lambda t_: t_[:, 0:Tn][:, ::-1]
                        va, vb, vh, vp, vz = rv(ra), rv(ib), rv(hh), rv(pc), rv(zz)
                        last = 0
                    P.op("dve", lambda: nc.vector.tensor_tensor_scan(vh, va, vb, 0.0, ALU.mult, ALU.add), reads=[bra, bib], writes=[bhh])
                    if not is_ctx:
                        P.op("dve", lambda: nc.vector.tensor_tensor_scan(vp, va, vz, 1.0, ALU.mult, ALU.add), reads=[bra, bzz], writes=[bpc])
                        hn, pn = ("hlocF", "pcF") if d == 0 else ("hlocB", "pcB")
                        P.dma("sp", O[hn][c * 128:(c + 1) * 128, :], hh[:, :Tn], reads=[bhh], sb=bhh)
                        P.dma("sp", O[pn][c * 128:(c + 1) * 128, :], pc[:, :Tn], reads=[bpc], sb=bpc)
                        P.op("act", lambda: nc.scalar.copy(lsum[:, c * 6 + 2 * d:c * 6 + 2 * d + 1], pc[:, last:last + 1]), reads=[bpc], writes=[bls])
                        P.op("act", lambda: nc.scalar.copy(lsum[:, c * 6 + 2 * d + 1:c * 6 + 2 * d + 2], hh[:, last:last + 1]), reads=[bhh], writes=[bls])
                    else:
                        P.op("act", lambda: nc.scalar.copy(lsum[:, c * 6 + 4 + d:c * 6 + 5 + d], hh[:, last:last + 1]), reads=[bhh], writes=[bls])
                        if d == 0:
                            P.op("pool", lambda: nc.gpsimd.tensor_copy(hs[:, :Tn], hh[:, :Tn]), reads=[bhh], writes=[bhs])
                        else:
                            P.op("pool", lambda: nc.gpsimd.tensor_tensor(hs[:, :Tn], hs[:, :Tn], hh[:, :Tn], ALU.add), reads=[bhh, bhs], writes=[bhs])
                            P.dma("sp", O["hcsum"][c * 128:(c + 1) * 128, :], hs[:, :Tn], reads=[bhs], sb=bhs)

            for c in range(2):
                lru_chain(cxf, bcxf, T, c, False)
                lru_chain(cxc, bcxc, CTX, c, True)
            P.dma("sp", O["lsum"][:, :], lsum[:], reads=[bls], sb=bls)
            P.barrier()


def core_bq(core):
    return core // 4, core % 4


KEXT = T + 256
KDEXT = T + 512
NAJ = {0: list(range(0, 7)), 1: list(range(5)), 2: list(range(5)), 3: list(range(5)), 4: list(range(-2, 5))}
NAOFF = {}
_o = 0
for _v in range(5):
    NAOFF[_v] = _o
    _o += 4 * len(NAJ[_v])
NATILES = _o


def emit_B(K, I, ctx_out):
    nc, P = K.nc, K.P
    O = I.O
    x1_d = I.x1full
    qrot_d, qpl_d = O["qrotT"], O["qplT"]
    kext_d, vext_d, kc_d, vc_d = O["kext"], O["vext"], O["kc"], O["vc"]
    dq_d = O["dqT"]
    kdext_d, vdext_d, kdc_d, vdc_d = O["kdext"], O["vdext"], O["kdc"], O["vdc"]
    vcv_d, ab_d, vcvc_d, gcg_d = O["vcvext"], O["abT"], O["vcvcext"], O["gcgT"]
    hl_d = [O[n] for n in ("hlocF", "pcF", "hlocB", "pcB")]
    hcs_d, g_d = O["hcsum"], O["gT"]
    wmask_d, natab_d = I.wmask, I.natab
    wbr_d, wout_d = I.wbr, I.wout
    wg_d, wu_d, wd_d = I.wg2, I.wu2, I.wd2
    x3_d, x2_d = I.xout, I.x2_d
    if True:
        tiles = LAT_TILES + ([CTX_TILE] if ctx_out else [])
        with contextlib.ExitStack() as s2:
            sb = lambda name, shape, dt: s2.enter_context(nc.sbuf_tensor(uniq(name), shape, dt))
            kext = sb("kext", [128, KEXT], BF16); bkext = Buf("kext")
            vext = sb("vext", [128, KEXT // 128, 128], BF16); bvext = Buf("vext")
            kc = sb("kc", [128, CTX], BF16); bkc = Buf("kc")
            vc = sb("vc", [128, 2, 128], BF16); bvc = Buf("vc")
            kdext = sb("kdext", [128, 2, KDEXT], BF16); bkdext = Buf("kdext")
            vdext = sb("vdext", [128, KDEXT // 128, MW], BF16); bvdext = Buf("vdext")
            kdc = sb("kdc", [128, 2, CTX], BF16); bkdc = Buf("kdc")
            vdc = sb("vdc", [128, 2, MW], BF16); bvdc = Buf("vdc")
            P.dma("sp", kext[:], kext_d[:, :], writes=[bkext], sb=bkext)
            P.dma("sp", vext[:], vext_d.rearrange("(b p) f -> p b f", p=128), writes=[bvext], sb=bvext)
            P.dma("sp", kc[:], kc_d[:, :], writes=[bkc], sb=bkc)
            P.dma("sp", vc[:], vc_d.rearrange("(b p) f -> p b f", p=128), writes=[bvc], sb=bvc)
            P.dma("sp", kdext[:], kdext_d.rearrange("(c p) t -> p c t", p=128), writes=[bkdext], sb=bkdext)
            P.dma("sp", vdext[:], vdext_d.rearrange("(b p) f -> p b f", p=128), writes=[bvdext], sb=bvdext)
            P.dma("sp", kdc[:], kdc_d.rearrange("(c p) t -> p c t", p=128), writes=[bkdc], sb=bkdc)
            P.dma("sp", vdc[:], vdc_d.rearrange("(b p) f -> p b f", p=128), writes=[bvdc], sb=bvdc)
            wbr = sb("wbr", [128, 8, D], BF16); bwbr = WBufs(512)
            wout = sb("wout", [128, 8, D], BF16); bwout = WBufs(512)
            load_weight_groups(K, [(wbr, bwbr, wbr_d, D)])
            load_weight_groups(K, [(wout, bwout, wout_d, D)])
            wm32 = sb("wm32", [128, 4, 128], F32); bwm32 = Buf("wm32")
            wmk = sb("wmk", [128, 4, 128], BF16); bwmk = Buf("wmk")
            P.dma("sp", wm32[:], wmask_d[:, :, :], writes=[bwm32], sb=bwm32)
            P.op("dve", lambda: nc.vector.tensor_copy(wmk[:], wm32[:]), reads=[bwm32], writes=[bwmk])
            etab = sb("etab", [128, NATILES * 128], BF16); betab = Buf("etab")
            with contextlib.ExitStack() as s3:
                tst = s3.enter_context(nc.sbuf_tensor(uniq("tst"), [128, 29, 128], F32)); btst = Buf("tst")
                for v in range(4):
                    P.dma("sp", tst[:], natab_d[v * 29:(v + 1) * 29].rearrange("n k q -> k n q"), writes=[btst], sb=btst)
                    P.op("act", lambda: nc.scalar.activation(etab[:, v * 29 * 128:(v + 1) * 29 * 128], tst[:].rearrange("p c q -> p (c q)"), AF.Exp),
                         reads=[btst], writes=[betab])
                P.barrier()
            esink = sb("esink", [128, 2], F32); besink = Buf("esink")
            P.op("act", lambda: nc.scalar.activation(esink[:], smv(K, "sink", 0, 2), AF.Exp), reads=[K.bsm], writes=[besink])
            carry, bcarry = K.carry, K.bcarry
            qrot = sb("qrot", [128, 2, 512], BF16); bqrot = Buf("qrot")
            qpl = sb("qplb", [128, 2, 512], BF16); bqpl = Buf("qplb")
            dq = sb("dqb", [128, 2, 512], BF16); bdq = Buf("dqb")
            vx = sb("vx", [128, 2, 514], BF16); bvx = Buf("vx")
            ab = sb("abb", [128, 2, 512], BF16); bab = Buf("abb")
            gcg = sb("gcgb", [128, 2, 512], BF16); bgcg = Buf("gcgb")
            hl = sb("hl", [128, 4, 512], F32); bhl = Buf("hl")
            gt = [sb(f"gt{i}", [128, 4, 512], BF16) for i in range(2)]; bgt = [Buf(f"gt{i}") for i in range(2)]
            x1t = sb("x1t", [128, 8, 512], F32); bx1 = Buf("x1t")
            yT = sb("yT", [128, 8, 512], BF16); byT = Buf("yT")
            mT = sb("mT", [128, 8, 512], BF16); bmT = Buf("mT")
            macc = sb("macc", [128, 512], F32); bmacc = Buf("macc")
            mtmp = [sb(f"mtmp{i}", [128, 512], F32) for i in range(2)]; bmtmp = [Buf(f"mtmp{i}") for i in range(2)]
            pex = [sb(f"pex{i}", [128, 9 * 128], BF16) for i in range(3)]; bpex = [Buf(f"pex{i}") for i in range(3)]
            rd = [sb(f"rd{i}", [128, 128], F32) for i in range(3)]; brd = [Buf(f"rd{i}") for i in range(3)]
            ct1 = sb("ct1", [128, 512], F32); bct1 = Buf("ct1")
            ct2 = sb("ct2", [128, 512], F32); bct2 = Buf("ct2")
            ones64 = cmat(K, "ones1")[:, 0:64]
            actr = [0]

            def attn(qT, bq_, qc, qbase, cols, local, ctxk, mask_ops, sink_col, ych, ob):
                chunks = local + ctxk
                n = len(chunks)

                def stage1(i):
                    banks = [(K.ps[i], K.bps[i]), (K.ps[2 + i], K.bps[2 + i])]
                    pe_, bpe = pex[i], bpex[i]
                    for g0 in range(0, n, 4):
                        ps, bps_ = banks[(g0 // 4) % 2]
                        grp = chunks[g0:g0 + 4]
                        for jj, (kap, bk_, qap, bqq, vap, bv_) in enumerate(grp):
                            P.op("pe", lambda: nc.tensor.matmul(ps[:, jj * 128:(jj + 1) * 128], kap, qap, start=True, stop=True),
                                 reads=[bk_, bqq], writes=[bps_])
                        P.op("act", lambda: nc.scalar.activation(pe_[:, g0 * 128:(g0 + len(grp)) * 128], ps[:, 0:len(grp) * 128], AF.Exp),
                             reads=[bps_], writes=[bpe])
                    for (eng, j0, j1, map_, bm_) in mask_ops:
                        if eng == "pool":
                            P.op("pool", lambda: nc.gpsimd.tensor_tensor(pe_[:, j0 * 128:j1 * 128], pe_[:, j0 * 128:j1 * 128], map_, ALU.mult), reads=[bpe, bm_], writes=[bpe])
                        else:
                            P.op("dve", lambda: nc.vector.tensor_tensor(pe_[:, j0 * 128:j1 * 128], pe_[:, j0 * 128:j1 * 128], map_, ALU.mult), reads=[bpe, bm_], writes=[bpe])

                def stage2(i):
                    pe_, bpe = pex[i], bpex[i]
                    psO, bO = K.ps[4 + 2 * i], K.bps[4 + 2 * i]
                    psD, bD = K.ps[5 + 2 * i], K.bps[5 + 2 * i]
                    for j, (kap, bk_, qap, bqq, vap, bv_) in enumerate(chunks):
                        P.op("pe", lambda: nc.tensor.matmul(psO[ob:ob + 64, 0:128], vap, pe_[:, j * 128:(j + 1) * 128], start=(j == 0), stop=(j == n - 1)),
                             reads=[bv_, bpe], writes=[bO], sig=(j == n - 1))
                    for j in range(n):
                        P.op("pe", lambda: nc.tensor.matmul(psD[ob:ob + 64, 0:128], ones64, pe_[:, j * 128:(j + 1) * 128], start=(j == 0), stop=(j == n - 1)),
                             reads=[K.bcm, bpe], writes=[bD], sig=(j == n - 1))
                    r_, br_ = rd[i], brd[i]
                    if sink_col is not None:
                        P.op("dve", lambda: nc.vector.tensor_scalar(r_[ob:ob + 64, :], psD[ob:ob + 64, 0:128], sink_col, None, ALU.add),
                             reads=[bD, besink], writes=[br_])
                        P.op("dve", lambda: nc.vector.reciprocal(r_[ob:ob + 64, :], r_[ob:ob + 64, :]), reads=[br_], writes=[br_])
                    else:
                        P.op("dve", lambda: nc.vector.reciprocal(r_[ob:ob + 64, :], psD[ob:ob + 64, 0:128]), reads=[bD], writes=[br_])
                    P.op("dve", lambda: nc.vector.tensor_tensor(yT[ob:ob + 64, ych, cols], psO[ob:ob + 64, 0:128], r_[ob:ob + 64, :], ALU.mult),
                         reads=[bO, br_], writes=[byT])

                items.append((stage1, stage2))

            def run_items():
                if items:
                    items[0][0](0)
                for i in range(len(items)):
                    if i + 1 < len(items):
                        items[i + 1][0]((i + 1) % 2)
                    items[i][1](i % 2)
                del items[:]

            items = []
            x1s = x1_d.rearrange("(c p) t -> p c t", p=128)
            x2s = x2_d.rearrange("(c p) t -> p c t", p=128)
            v3 = lambda ap: ap.rearrange("(c p) t -> p c t", p=128)
            g4 = g_d.rearrange("(n k p) t -> p n k t", p=128, k=8)
            gctr = [0]
            for (start, N, col) in tiles:
                lat = start < T
                sl = slice(start, start + N)
                P.dma("sp", x1t[:, :, :N], x1s[:, :, sl], writes=[bx1], sb=bx1)
                P.dma("sp", qpl[:, :, :N], v3(qpl_d)[:, :, sl], writes=[bqpl], sb=bqpl)
                P.dma("sp", dq[:, :, :N], v3(dq_d)[:, :, sl], writes=[bdq], sb=bdq)
                P.dma("sp", ab[:, :, :N], v3(ab_d)[:, :, sl], writes=[bab], sb=bab)
                P.dma("sp", gcg[:, :, :N], v3(gcg_d)[:, :, sl], writes=[bgcg], sb=bgcg)
                if lat:
                    P.dma("sp", qrot[:, :, :N], v3(qrot_d)[:, :, sl], writes=[bqrot], sb=bqrot)
                    P.dma("sp", vx[:, :, :N + 2], v3(vcv_d)[:, :, start:start + N + 2], writes=[bvx], sb=bvx)
                else:
                    P.dma("sp", vx[:, :, :N + 2], v3(vcvc_d)[:, :, 0:N + 2], writes=[bvx], sb=bvx)
                for c in range(2):
                    cw = lambda j: smv(K, "convaw", c * 3 + j)
                    P.op("dve", lambda: nc.vector.tensor_scalar(ct1[:, :N], vx[:, c, 0:N], cw(0), None, ALU.mult), reads=[bvx, K.bsm], writes=[bct1])
                    P.op("dve", lambda: nc.vector.scalar_tensor_tensor(ct1[:, :N], vx[:, c, 1:N + 1], cw(1), ct1[:, :N], ALU.mult, ALU.add),
                         reads=[bvx, K.bsm, bct1], writes=[bct1])
                    P.op("dve", lambda: nc.vector.scalar_tensor_tensor(ct1[:, :N], vx[:, c, 2:N + 2], cw(2), ct1[:, :N], ALU.mult, ALU.add),
                         reads=[bvx, K.bsm, bct1], writes=[bct1])
                    P.op("pool", lambda: nc.gpsimd.tensor_tensor(yT[:, 0 + c, :N], ct1[:, :N], ab[:, c, :N], ALU.mult), reads=[bct1, bab], writes=[byT])
                for c in range(2):
                    if lat:
                        for a_ in range(4):
                            P.dma("sp", hl[:, a_, :N], hl_d[a_][c * 128:(c + 1) * 128, sl], writes=[bhl], sb=bhl)
                    else:
                        P.dma("sp", hl[:, 0, :N], hcs_d[c * 128:(c + 1) * 128, 0:N], writes=[bhl], sb=bhl)
                    if lat:
                        P.op("dve", lambda: nc.vector.scalar_tensor_tensor(ct1[:, :N], hl[:, 1, :N], carry[:, c * 2:c * 2 + 1], hl[:, 0, :N], ALU.mult, ALU.add),
                             reads=[bhl, bcarry], writes=[bct1])
                        P.op("dve", lambda: nc.vector.scalar_tensor_tensor(ct2[:, :N], hl[:, 3, :N], carry[:, c * 2 + 1:c * 2 + 2], hl[:, 2, :N], ALU.mult, ALU.add),
                             reads=[bhl, bcarry], writes=[bct2])
                        P.op("pool", lambda: nc.gpsimd.tensor_tensor(ct1[:, :N], ct1[:, :N], ct2[:, :N], ALU.add), reads=[bct1, bct2], writes=[bct1])
                        P.op("pool", lambda: nc.gpsimd.tensor_tensor(yT[:, 4 + c, :N], ct1[:, :N], gcg[:, c, :N], ALU.mult), reads=[bct1, bgcg], writes=[byT])
                    else:
                        P.op("pool", lambda: nc.gpsimd.tensor_tensor(yT[:, 4 + c, :N], hl[:, 0, :N], gcg[:, c, :N], ALU.mult), reads=[bhl, bgcg], writes=[byT])
                for blk in range(N // 128):
                    cols = slice(blk * 128, (blk + 1) * 128)
                    n_ = (start // 128) + blk
                    for h in range(4):
                        kv = h // 2
                        qc, qb_ = h % 2, kv * 64
                        ob = (h % 2) * 64
                        ctxk = [(kc[qb_:qb_ + 64, j * 128:(j + 1) * 128], bkc, qpl[qb_:qb_ + 64, qc, cols], bqpl,
                                 vc[:, j, kv * 64:(kv + 1) * 64], bvc) for j in range(2)]
                        if lat:
                            local = [(kext[qb_:qb_ + 64, (n_ + j) * 128:(n_ + j + 1) * 128], bkext, qrot[qb_:qb_ + 64, qc, cols], bqrot,
                                      vext[:, n_ + j, kv * 64:(kv + 1) * 64], bvext) for j in range(3)]
                            mp = wmk[:, 2, :] if n_ == 0 else wmk[:, 0, :]
                            mn = wmk[:, 3, :] if n_ == 31 else wmk[:, 1, :]
                            mops = [("pool", 0, 1, mp, bwmk), ("pool", 2, 3, mn, bwmk)]
                        else:
                            local, mops = [], []
                        attn(None, None, qc, qb_, cols, local, ctxk, mops, esink[ob:ob + 64, h // 2:h // 2 + 1], 2 + h // 2, ob)
                    for h in range(4):
                        hc_, hb_ = h // 2, (h % 2) * 64
                        ob = hb_
                        ctxk = [(kdc[hb_:hb_ + 64, hc_, j * 128:(j + 1) * 128], bkdc, dq[hb_:hb_ + 64, hc_, cols], bdq,
                                 vdc[:, j, h * 64:(h + 1) * 64], bvdc) for j in range(2)]
                        if lat:
                            var = 0 if n_ == 0 else 1 if n_ == 1 else 3 if n_ == 30 else 4 if n_ == 31 else 2
                            jl = NAJ[var]
                            local = [(kdext[hb_:hb_ + 64, hc_, (n_ + j) * 128:(n_ + j + 1) * 128], bkdext, dq[hb_:hb_ + 64, hc_, cols], bdq,
                                      vdext[:, n_ + j, h * 64:(h + 1) * 64], bvdext) for j in jl]
                            e0 = NAOFF[var] + h * len(jl)
                            mops = [("dve", 0, len(jl), etab[:, e0 * 128:(e0 + len(jl)) * 128], betab)]
                        else:
                            local, mops = [], []
                        attn(None, None, None, None, cols, local, ctxk, mops, None, 6 + h // 2, ob)
                run_items()
                for dk in range(8):
                    gi_ = gctr[0] % 2
                    gctr[0] += 1
                    P.dma("sp", gt[gi_][:, :, :N], g4[:, :, dk, sl], writes=[bgt[gi_]], sb=bgt[gi_])
                    for n in range(4):
                        ps, bps_ = K.ps[(dk * 4 + n) % 4], K.bps[(dk * 4 + n) % 4]
                        for cc in range(2):
                            P.op("pe", lambda: nc.tensor.matmul(ps[:, :N], wbr[:, n * 2 + cc, dk * 128:(dk + 1) * 128], yT[:, n * 2 + cc, :N],
                                                                start=(cc == 0), stop=(cc == 1)),
                                 reads=[bwbr.at(dk * 128), byT], writes=[bps_], sig=(cc == 1))
                        if n == 0:
                            P.op("dve", lambda: nc.vector.tensor_tensor(macc[:, :N], ps[:, :N], gt[gi_][:, 0, :N], ALU.mult), reads=[bps_, bgt[gi_]], writes=[bmacc])
                        else:
                            mt_, bmt_ = mtmp[n % 2], bmtmp[n % 2]
                            P.op("dve", lambda: nc.vector.tensor_tensor(mt_[:, :N], ps[:, :N], gt[gi_][:, n, :N], ALU.mult), reads=[bps_, bgt[gi_]], writes=[bmt_])
                            if n < 3:
                                P.op("pool", lambda: nc.gpsimd.tensor_tensor(macc[:, :N], macc[:, :N], mt_[:, :N], ALU.add), reads=[bmacc, bmt_], writes=[bmacc])
                            else:
                                P.op("pool", lambda: nc.gpsimd.tensor_tensor(mT[:, dk, :N], macc[:, :N], mt_[:, :N], ALU.add), reads=[bmacc, bmt_], writes=[bmT])
                for dk in range(8):
                    ps, bps_ = K.ps[4 + dk % 4], K.bps[4 + dk % 4]
                    for k in range(8):
                        P.op("pe", lambda: nc.tensor.matmul(ps[:, :N], wout[:, k, dk * 128:(dk + 1) * 128], mT[:, k, :N], start=(k == 0), stop=(k == 7)),
                             reads=[bwout.at(dk * 128), bmT], writes=[bps_], sig=(k == 7))
                    P.op("dve", lambda: nc.vector.scalar_tensor_tensor(x1t[:, dk, :N], ps[:, :N], K.gateT[:, 1, dk, col:col + 1], x1t[:, dk, :N], ALU.mult, ALU.add),
                         reads=[bps_, bx1, K.bgs], writes=[bx1])
                P.dma("sp", x2s[:, :, sl], x1t[:, :, :N], reads=[bx1], sb=bx1)
            P.barrier()
        emit_ffn_phase(K, x2_d, x3_d, tiles, 2, wg_d, wu_d, wd_d)


def window_masks(q):
    j = np.arange(128)[:, None]
    i = np.arange(128)[None, :]
    tp = (j >= i).astype(np.float32)
    tn = (j <= i).astype(np.float32)
    z = np.zeros((128, 128), np.float32)
    return np.ascontiguousarray(np.stack([tp, tn, tp if q > 0 else z, tn if q < 3 else z], 1))


def na_index(gm, jl):
    rows_total = SEQ // GRID_W
    kk = np.arange(128)
    qq = np.arange(128)
    qrow = 2 * gm + qq // 64
    qcol = qq % 64
    n = len(jl)
    dr = np.zeros((n, 128, 128), np.int64)
    dc = np.zeros((n, 128, 128), np.int64)
    valid = np.zeros((n, 128, 128), bool)
    rstart = np.clip(qrow - 4, 0, rows_total - 8)
    cstart = np.clip(qcol - 8, 0, GRID_W - 16)
    for i, j in enumerate(jl):
        krow = (2 * gm - 4 + 2 * j + kk // 64)[:, None]
        kcol = (kk % 64)[:, None]
        v = (krow >= rstart[None, :]) & (krow < rstart[None, :] + 8) & (kcol >= cstart[None, :]) & (kcol < cstart[None, :] + 16)
        valid[i] = v
        dr[i] = np.clip(krow - qrow[None, :] + 7, 0, 14)
        dc[i] = np.clip(kcol - qcol[None, :], -15, 15) + 15
    return dr, dc, valid


def na_tables(rel_bias, q):
    out = np.empty((NATILES, 128, 128), np.float32)
    for v, m in enumerate((0, 1, 15, 30, 31)):
        gm = q * 32 + m
        jl = NAJ[v]
        dr, dc, valid = na_index(gm, jl)
        for h in range(4):
            o = NAOFF[v] + h * len(jl)
            out[o:o + len(jl)] = np.where(valid, rel_bias[h][dr, dc], np.float32(-30000.0))
    return out


XP = 1282
GROUPS = [[0, 1, 2, 3], [4, 5, 6, 7]]


def emit_exchange(K, I):
    nc, P, O = K.nc, K.P, I.O
    pack, packall, lpack, lall = I.pack.ap(), I.packall.ap(), I.lpack.ap(), I.lall.ap()
    bpk, bpa, blp, bla = Buf("pack"), Buf("packall"), Buf("lpack"), Buf("lall")
    v3 = lambda ap: ap.rearrange("(c p) t -> p c t", p=128)
    tv = lambda ap: ap.rearrange("(b p) f -> p b f", p=128)
    w3 = lambda ap, w: ap.rearrange("p (a w) -> p a w", w=w)
    kext, vext, kdext, vdext, vcv = O["kext"], O["vext"], O["kdext"], O["vdext"], O["vcvext"]
    for side in range(2):
        o = side * XP
        if side == 0:
            srcs = [kext[:, T:T + 128], vext[T:T + 128, :], v3(kdext)[:, :, T:T + 256], tv(vdext[T:T + 256, :]), v3(vcv)[:, :, T:T + 1]]
        else:
            srcs = [kext[:, 128:256], vext[128:256, :], v3(kdext)[:, :, 256:512], tv(vdext[256:512, :]), v3(vcv)[:, :, 1:2]]
        dsts = [pack[:, o:o + 128], pack[:, o + 128:o + 256], w3(pack[:, o + 256:o + 768], 256), w3(pack[:, o + 768:o + 1280], 256),
                w3(pack[:, o + 1280:o + 1282], 1)]
        for d_, s_ in zip(dsts, srcs):
            P.dma("sp", d_, s_, writes=[bpk], sb=bpk, allow_slow_non_contiguous=True)
    P.collective(I.pack.ap().opt(), I.packall.ap().opt(), GROUPS, reads=[bpk], writes=[bpa])
    with contextlib.ExitStack() as s2:
        sb = lambda name, shape, dt: s2.enter_context(nc.sbuf_tensor(uniq(name), shape, dt))
        cand = sb("cand", [128, 4, 2 * XP], BF16); bcand = Buf("cand")
        hl = sb("halo_l", [128, XP], BF16); bhl = Buf("halo_l")
        hr = sb("halo_r", [128, XP], BF16); bhr = Buf("halo_r")
        P.dma("sp", cand[:], packall.rearrange("(j p) x -> p j x", p=128), reads=[bpa], writes=[bcand], sb=bcand)
        for (dst, bdst, c0, sel) in ((hl, bhl, 0, "lsel"), (hr, bhr, XP, "rsel")):
            P.op("dve", lambda: nc.vector.tensor_scalar(dst[:], cand[:, 0, c0:c0 + XP], smv(K, sel, 0), None, ALU.mult),
                 reads=[bcand, K.bsm], writes=[bdst])
            for j in range(1, 4):
                P.op("dve", lambda: nc.vector.scalar_tensor_tensor(dst[:], cand[:, j, c0:c0 + XP], smv(K, sel, j), dst[:], ALU.mult, ALU.add),
                     reads=[bcand, K.bsm, bdst], writes=[bdst])
        for (src, bsrc, kc, vr, dk, dv, cv) in ((hl, bhl, slice(0, 128), slice(0, 128), slice(0, 256), slice(0, 256), slice(0, 1)),
                                              (hr, bhr, slice(128 + T, 256 + T), slice(128 + T, 256 + T), slice(256 + T, 512 + T),
                                               slice(256 + T, 512 + T), slice(T + 1, T + 2))):
            P.dma("sp", kext[:, kc], src[:, 0:128], reads=[bsrc], sb=bsrc)
            P.dma("sp", vext[vr, :], src[:, 128:256], reads=[bsrc], sb=bsrc)
            P.dma("sp", v3(kdext)[:, :, dk], w3(src[:, 256:768], 256), reads=[bsrc], sb=bsrc)
            P.dma("sp", tv(vdext[dv, :]), w3(src[:, 768:1280], 256), reads=[bsrc], sb=bsrc)
            P.dma("sp", v3(vcv)[:, :, cv], w3(src[:, 1280:1282], 1), reads=[bsrc], sb=bsrc, allow_slow_non_contiguous=True)
        P.dma("sp", lpack[:, :], K.lsum[:], reads=[K.bls], writes=[blp], sb=K.bls)
        P.collective(I.lpack.ap().opt(), I.lall.ap().opt(), GROUPS, reads=[blp], writes=[bla])
        lc = sb("lcand", [128, 4, 12], F32); blc = Buf("lcand")
        tt_ = sb("ltmp", [128, 2], F32); btt = Buf("ltmp")
        P.dma("sp", lc[:], lall.rearrange("(j p) x -> p j x", p=128), reads=[bla], writes=[blc], sb=blc)
        carry, bcarry = K.carry, K.bcarry
        for c in range(2):
            for d in range(2):
                cc = carry[:, c * 2 + d:c * 2 + d + 1]
                P.op("dve", lambda: nc.vector.tensor_copy(cc, K.lsum[:, c * 6 + 4 + d:c * 6 + 5 + d]), reads=[K.bls], writes=[bcarry])
                order = [0, 1, 2, 3] if d == 0 else [3, 2, 1, 0]
                sel = "fsel" if d == 0 else "bsel"
                for j in order:
                    a_ = lc[:, j, c * 6 + 2 * d:c * 6 + 2 * d + 1]
                    h_ = lc[:, j, c * 6 + 2 * d + 1:c * 6 + 2 * d + 2]
                    P.op("dve", lambda: nc.vector.scalar_tensor_tensor(tt_[:, 0:1], cc, a_, h_, ALU.mult, ALU.add),
                         reads=[bcarry, blc], writes=[btt])
                    P.op("dve", lambda: nc.vector.tensor_tensor(tt_[:, 1:2], tt_[:, 0:1], cc, ALU.subtract), reads=[btt, bcarry], writes=[btt])
                    P.op("dve", lambda: nc.vector.scalar_tensor_tensor(cc, tt_[:, 1:2], smv(K, sel, j), cc, ALU.mult, ALU.add),
                         reads=[btt, K.bsm, bcarry], writes=[bcarry])
        P.barrier()


def emit_xhalo(K, I):
    nc, P = K.nc, K.P
    xin1 = I.xin1
    xp, xall = I.xpack.ap(), I.xall.ap()
    bxp, bxa = Buf("xpack"), Buf("xall")
    v3 = lambda ap: ap.rearrange("(c p) t -> p c t", p=128)
    w3 = lambda ap, w: ap.rearrange("p (a w) -> p a w", w=w)
    P.dma("sp", w3(xp[:, 0:16], 2), v3(xin1)[:, :, T - 2:T], writes=[bxp], sb=bxp)
    P.dma("sp", w3(xp[:, 16:24], 1), v3(xin1)[:, :, 0:1], writes=[bxp], sb=bxp, allow_slow_non_contiguous=True)
    P.collective(I.xpack.ap().opt(), I.xall.ap().opt(), GROUPS, reads=[bxp], writes=[bxa])
    with contextlib.ExitStack() as s2:
        sb = lambda name, shape, dt: s2.enter_context(nc.sbuf_tensor(uniq(name), shape, dt))
        xc = sb("xcand", [128, 4, 24], F32); bxc = Buf("xcand")
        hx = sb("xhalo", [128, 32], F32); bhx = Buf("xhalo")
        P.dma("sp", xc[:], xall.rearrange("(j p) x -> p j x", p=128), reads=[bxa], writes=[bxc], sb=bxc)
        P.op("dve", lambda: nc.vector.memset(hx[:], 0.0), writes=[bhx])
        for (c0, n, o, sel) in ((0, 16, 0, "lsel"), (16, 8, 16, "rsel")):
            for j in range(4):
                P.op("dve", lambda: nc.vector.scalar_tensor_tensor(hx[:, o:o + n], xc[:, j, c0:c0 + n], smv(K, sel, j), hx[:, o:o + n], ALU.mult, ALU.add),
                     reads=[bxc, K.bsm, bhx], writes=[bhx])
        P.dma("sp", v3(xin1)[:, :, TT:TT + 2], w3(hx[:, 0:16], 2), reads=[bhx], sb=bhx)
        P.dma("sp", v3(xin1)[:, :, TT + 2:TT + 3], w3(hx[:, 16:24], 1), reads=[bhx], sb=bhx, allow_slow_non_contiguous=True)
        P.dma("sp", v3(xin1)[:, :, TT + 3:TT + 4], w3(hx[:, 24:32], 1), reads=[bhx], sb=bhx, allow_slow_non_contiguous=True)
        P.barrier()


INTERNALS = [("abT", [MW, TT], BF16), ("qrotT", [MW, T], BF16), ("qplT", [MW, TT], BF16),
             ("kext", [128, KEXT], BF16), ("vext", [KEXT, 128], BF16), ("kc", [128, CTX], BF16), ("vc", [CTX, 128], BF16),
             ("gcgT", [MW, TT], BF16), ("dqT", [MW, TT], BF16),
             ("kdext", [MW, KDEXT], BF16), ("vdext", [KDEXT, MW], BF16), ("kdc", [MW, CTX], BF16), ("vdc", [CTX, MW], BF16),
             ("vcvext", [MW, T + 2], BF16), ("vcvcext", [MW, CTX + 2], BF16), ("gT", [4 * D, TT], BF16),
             ("hlocF", [MW, T], F32), ("pcF", [MW, T], F32), ("hlocB", [MW, T], F32), ("pcB", [MW, T], F32),
             ("hcsum", [MW, CTX], F32), ("lsum", [128, 12], F32)]


def build_fused():
    nc = bass.Bass("TRN2", target_bir_lowering=False, dynamic_dma_scratch_size=4096)
    din = lambda n, sh, dt=F32: nc.dram_tensor(n, sh, dt, kind="ExternalInput").ap()
    dint = lambda n, sh, dt=F32: nc.dram_tensor(n, sh, dt, kind="Internal").ap()
    I = Ctx()
    xT_d = din("xT", [D, TX])
    I.cos_d = din("cosT", [128, T]); I.sin_d = din("sinT", [128, T])
    sm_d = [din(f"sm{l}", [128, NSM]) for l in range(2)]
    I.wmask = din("wmask", [128, 4, 128])
    natab_d = din("natab", [2, NATILES, 128, 128])
    wmod_d = din("w_mod", [2, D, 9 * D])
    wg_d = din("wg", [2, 2, D, FF]); wu_d = din("wu", [2, 2, D, FF]); wd_d = din("wd", [2, 2, FF, D])
    win_d = din("w_in", [2, D, INW])
    wbr_d = din("w_branch", [2, D, D]); wout_d = din("w_out", [2, D, D])
    out_d = nc.dram_tensor("outT", [D, T], F32, kind="ExternalOutput").ap()
    I.O = {n: dint(n, sh, dt) for n, sh, dt in INTERNALS}
    I.x1full = dint("x1full", [D, TX]); I.h2_d = dint("h2T", [D, TX], BF16)
    I.x2_d = dint("x2T", [D, TT]); I.xin1 = dint("xin1", [D, TX])
    I.pack = nc.dram_tensor("pack", [128, 2 * XP], BF16); I.packall = nc.dram_tensor("packall", [512, 2 * XP], BF16)
    I.lpack = nc.dram_tensor("lpack", [128, 12], F32); I.lall = nc.dram_tensor("lall", [512, 12], F32)
    I.xpack = nc.dram_tensor("xpack", [128, 24], F32); I.xall = nc.dram_tensor("xall", [512, 24], F32)

    with contextlib.ExitStack() as st:
        K = setup_common(nc, st)
        P = K.P
        K.lsum = K.sbuf("lsum_sb", [128, 12], F32); K.bls = Buf("lsum")
        K.carry = K.sbuf("carry", [128, 4], F32); K.bcarry = Buf("carry")
        zt = K.sbuf("zeros_sb", [128, 2, 1], BF16); bzt = Buf("zeros")
        P.op("dve", lambda: nc.vector.memset(zt[:], 0.0), writes=[bzt])
        vcc = I.O["vcvcext"].rearrange("(c p) t -> p c t", p=128)
        P.dma("sp", vcc[:, :, 0:1], zt[:], reads=[bzt], sb=bzt, allow_slow_non_contiguous=True)
        P.dma("sp", vcc[:, :, CTX + 1:CTX + 2], zt[:], reads=[bzt], sb=bzt, allow_slow_non_contiguous=True)
        for l in range(2):
            load_sm(K, sm_d[l][:, :])
            emit_mod(K, wmod_d[l])
            I.xin = xT_d if l == 0 else I.xin1
            I.wg1, I.wu1, I.wd1, I.win = wg_d[l, 0], wu_d[l, 0], wd_d[l, 0], win_d[l]
            I.wg2, I.wu2, I.wd2 = wg_d[l, 1], wu_d[l, 1], wd_d[l, 1]
            I.wbr, I.wout, I.natab = wbr_d[l], wout_d[l], natab_d[l]
            I.xout = I.xin1 if l == 0 else out_d
            emit_A(K, I)
            emit_exchange(K, I)
            emit_B(K, I, ctx_out=(l == 0))
            if l == 0:
                emit_xhalo(K, I)
        P.finish()
    return nc


_NC = []


def kernel(**inp):
    inp = {k: np.asarray(v) for k, v in inp.items()}
    if not _NC:
        _NC.append(build_fused())
    nc = _NC[0]
    x = np.asarray(inp["x"], np.float32)
    shared = {
        "w_mod": np.ascontiguousarray(inp["w_mod"], np.float32),
        "wg": np.ascontiguousarray(inp["ffn_w_gate"], np.float32), "wu": np.ascontiguousarray(inp["ffn_w_up"], np.float32),
        "wd": np.ascontiguousarray(inp["ffn_w_down"], np.float32), "w_in": np.ascontiguousarray(inp["w_in"], np.float32),
        "w_branch": np.ascontiguousarray(np.asarray(inp["w_branch"], np.float32).reshape(2, D, D)),
        "w_out": np.ascontiguousarray(inp["w_out"], np.float32),
    }
    in_maps = []
    for core in range(NC):
        b, q = core_bq(core)
        xs = x[b]
        lo, hi = q * T, (q + 1) * T
        halo = np.zeros((4, D), np.float32)
        if q > 0:
            halo[0] = xs[lo - 2]; halo[1] = xs[lo - 1]
        if q < 3:
            halo[2] = xs[hi]
        xT = np.ascontiguousarray(np.concatenate([xs[lo:hi], np.asarray(inp["ctx"][b], np.float32), halo], 0).T)
        cs, sn = rope_tables(q)
        m = {"xT": xT, "cosT": cs, "sinT": sn, "sm0": build_sm(inp, 0, b, q), "sm1": build_sm(inp, 1, b, q),
             "wmask": window_masks(q),
             "natab": np.stack([na_tables(np.asarray(inp["na_rel_bias"][l], np.float32), q) for l in range(2)], 0)}
        m.update(shared)
        in_maps.append(m)
    res = run_bass_kernel_spmd(nc, in_maps, core_ids=list(range(NC)))
    out = np.empty_like(x)
    for core in range(NC):
        b, q = core_bq(core)
        out[b, q * T:(q + 1) * T] = np.asarray(res.results[core]["outT"]).T
    return out
```
